# Optimizing a Trainium2 kernel written in Bass

```python
import math
import jax
import jax.numpy as jnp
from jax import lax
import numpy as np

D_MODEL = 1024
BATCH = 32
SEQ = 256
DEPTH = 2
DEC_BATCH = 4
DEC_SEQ = 1024
PAST_LEN = 256

GRID_W = 64
N_BRANCH = 4
BRANCH_W = D_MODEL // 2
D_FF = 4 * D_MODEL
EPS = 1e-6
NEG_INF = -1e30
QBLK = 128
ROPE_BASE = 10000.0

SSD_HEAD_DIM = 64
SSD_HEADS = BRANCH_W // SSD_HEAD_DIM
SSD_GROUPS = 2
SSD_STATE = 64
SSD_CONV_W = 7
SSD_CHUNK = 128
SSD_XBC = BRANCH_W + 2 * SSD_GROUPS * SSD_STATE

S5_GROUP = 16
S5_GROUPS = BRANCH_W // S5_GROUP
S5_STATE = 64

DIFF_HEAD_DIM = 64
DIFF_HEADS = BRANCH_W // (2 * DIFF_HEAD_DIM)

WIN_HEAD_DIM = 64
WIN_HEADS = BRANCH_W // WIN_HEAD_DIM
WIN_KV_HEADS = 2
WIN_GROUP = WIN_HEADS // WIN_KV_HEADS
WINDOW = 128
BAND_BLK = 128

IN_SIZES = (BRANCH_W, SSD_XBC, 2 * SSD_HEADS, BRANCH_W,
            2 * DIFF_HEADS * DIFF_HEAD_DIM, 2 * DIFF_HEADS * DIFF_HEAD_DIM, 2 * DIFF_HEADS * DIFF_HEAD_DIM,
            WIN_HEADS * WIN_HEAD_DIM, WIN_KV_HEADS * WIN_HEAD_DIM, WIN_KV_HEADS * WIN_HEAD_DIM)
D_IN = sum(IN_SIZES)
IN_SPLITS = tuple(int(v) for v in np.cumsum(IN_SIZES)[:-1])

kernel_name = 'hybrid_diffusion_parallel_step'


def rmsnorm(x, g):
    xf = x.astype(jnp.float32)
    y = xf * lax.rsqrt(jnp.mean(xf * xf, axis=-1, keepdims=True) + EPS)
    return (y * g.astype(jnp.float32)).astype(x.dtype)


def adaln(cvec, w, b):
    m = jax.nn.silu(cvec) @ w + b
    return jnp.split(m[:, None, :], 6, axis=-1)


def axial_rope(length, head_dim):
    rows = length // GRID_W
    row = jnp.repeat(jnp.arange(rows), GRID_W).astype(jnp.float32)
    col = jnp.tile(jnp.arange(GRID_W), rows).astype(jnp.float32)
    nf = head_dim // 4
    inv = ROPE_BASE ** (-jnp.arange(nf, dtype=jnp.float32) / nf)
    ang = jnp.concatenate([row[:, None] * inv, col[:, None] * inv], axis=-1)
    return jnp.cos(ang), jnp.sin(ang)


def apply_rope(x, cos, sin):
    shp = x.shape
    nf = shp[-1] // 4
    xr = x.astype(jnp.float32).reshape(shp[:-1] + (2, 2, nf))
    bshape = (1, shp[1]) + (1,) * (len(shp) - 3) + (2, nf)
    c = cos.reshape(bshape)
    s = sin.reshape(bshape)
    x1 = xr[..., 0, :]
    x2 = xr[..., 1, :]
    out = jnp.stack([x1 * c - x2 * s, x2 * c + x1 * s], axis=-2)
    return out.reshape(shp).astype(x.dtype)


def map_query_blocks(fn, q):
    bsz, length = q.shape[0], q.shape[1]
    nb = length // QBLK
    qb = jnp.moveaxis(q.reshape((bsz, nb, QBLK) + q.shape[2:]), 1, 0)
    out = jnp.moveaxis(lax.map(fn, qb), 0, 1)
    return out.reshape((bsz, length) + out.shape[3:])


def dwconv_centred(x, w, b):
    width, ch = w.shape
    y = lax.conv_general_dilated(x, w[:, None, :], window_strides=(1,),
                                 padding=[(width // 2, width // 2)],
                                 dimension_numbers=('NWC', 'WIO', 'NWC'),
                                 feature_group_count=ch)
    return y + b


def ssd_chunked_scan(x, dt, a_log, bm, cm, s0):
    f32 = jnp.float32
    bsz, length, nh, hp = x.shape
    nc = length // SSD_CHUNK

    def chunks(t):
        return t.astype(f32).reshape((bsz, nc, SSD_CHUNK) + t.shape[2:])

    xc, dtc, bc, cc = chunks(x), chunks(dt), chunks(bm), chunks(cm)
    acum = jnp.cumsum(dtc * (-jnp.exp(a_log.astype(f32))), axis=2)
    seg = acum[:, :, :, None, :] - acum[:, :, None, :, :]
    lower = jnp.tril(jnp.ones((SSD_CHUNK, SSD_CHUNK), dtype=bool))[None, None, :, :, None]
    lmat = jnp.exp(jnp.where(lower, seg, NEG_INF))
    xdt = xc * dtc[..., None]
    scores = jnp.einsum('bcthn,bcshn->bctsh', cc, bc) * lmat
    y_diag = jnp.einsum('bctsh,bcshp->bcthp', scores, xdt)
    decay_end = jnp.exp(acum[:, :, -1:, :] - acum)
    chunk_states = jnp.einsum('bcshn,bcshp->bchpn', bc * decay_end[..., None], xdt)
    chunk_decay = jnp.exp(acum[:, :, -1, :])

    def step(s, inp):
        st, dec = inp
        return dec[:, :, None, None] * s + st, s

    final, s_start = lax.scan(step, s0.astype(f32),
                              (jnp.moveaxis(chunk_states, 1, 0), jnp.moveaxis(chunk_decay, 1, 0)))
    s_start = jnp.moveaxis(s_start, 0, 1)
    y_off = jnp.einsum('bcthn,bchpn->bcthp', cc, s_start) * jnp.exp(acum)[..., None]
    return (y_diag + y_off).reshape(bsz, length, nh, hp), final


def ssd_mixer(z, xbc, dt_raw, conv_w, conv_b, dt_bias, a_log, d_skip, norm_g, s0):
    f32 = jnp.float32
    bsz, length, _ = z.shape
    xbc = jax.nn.silu(dwconv_centred(xbc, conv_w, conv_b))
    xs, bm, cm = jnp.split(xbc, [BRANCH_W, BRANCH_W + SSD_GROUPS * SSD_STATE], axis=-1)
    xs = xs.reshape(bsz, length, SSD_HEADS, SSD_HEAD_DIM)
    rep = SSD_HEADS // SSD_GROUPS
    bm = jnp.repeat(bm.reshape(bsz, length, SSD_GROUPS, SSD_STATE), rep, axis=2)
    cm = jnp.repeat(cm.reshape(bsz, length, SSD_GROUPS, SSD_STATE), rep, axis=2)
    dt = jax.nn.softplus(dt_raw.reshape(bsz, length, 2, SSD_HEADS).astype(f32) + dt_bias.astype(f32))

    def flip(t):
        return jnp.flip(t, axis=1)

    y_f, s_f = ssd_chunked_scan(xs, dt[:, :, 0], a_log[0], bm, cm, s0[:, 0])
    y_b, s_b = ssd_chunked_scan(flip(xs), flip(dt[:, :, 1]), a_log[1], flip(bm), flip(cm), s0[:, 1])
    y = y_f + flip(y_b) + d_skip.astype(f32)[:, None] * xs.astype(f32)
    y = rmsnorm(y.reshape(bsz, length, BRANCH_W) * jax.nn.silu(z.astype(f32)), norm_g)
    return y, jnp.stack([s_f, s_b], axis=1)


def s5_direction(u, lam_re, lam_im, log_step, b_re, b_im, c_re, c_im, s0_re, s0_im):
    f32 = jnp.float32
    lam_re = lam_re.astype(f32)
    lam_im = lam_im.astype(f32)
    step = jnp.exp(log_step.astype(f32))[:, None]
    mag = jnp.exp(lam_re * step)
    ab_re = mag * jnp.cos(lam_im * step)
    ab_im = mag * jnp.sin(lam_im * step)
    den = lam_re * lam_re + lam_im * lam_im
    coef_re = ((ab_re - 1.0) * lam_re + ab_im * lam_im) / den
    coef_im = (ab_im * lam_re - (ab_re - 1.0) * lam_im) / den
    bu_re = jnp.einsum('blgm,gnm->blgn', u, b_re.astype(f32))
    bu_im = jnp.einsum('blgm,gnm->blgn', u, b_im.astype(f32))
    s0_re = s0_re.astype(f32)
    s0_im = s0_im.astype(f32)
    x_re = (coef_re * bu_re - coef_im * bu_im).at[:, 0].add(ab_re * s0_re - ab_im * s0_im)
    x_im = (coef_re * bu_im + coef_im * bu_re).at[:, 0].add(ab_re * s0_im + ab_im * s0_re)
    length = u.shape[1]
    a_re = jnp.broadcast_to(ab_re, (1, length) + ab_re.shape)
    a_im = jnp.broadcast_to(ab_im, (1, length) + ab_im.shape)

    def combine(e1, e2):
        a1r, a1i, b1r, b1i = e1
        a2r, a2i, b2r, b2i = e2
        return (a2r * a1r - a2i * a1i, a2r * a1i + a2i * a1r,
                a2r * b1r - a2i * b1i + b2r, a2r * b1i + a2i * b1r + b2i)

    _, _, s_re, s_im = lax.associative_scan(combine, (a_re, a_im, x_re, x_im), axis=1)
    y = (jnp.einsum('blgn,gmn->blgm', s_re, c_re.astype(f32))
         - jnp.einsum('blgn,gmn->blgm', s_im, c_im.astype(f32)))
    return y, s_re[:, -1], s_im[:, -1]


def s5_mixer(u, lam_re, lam_im, log_step, b_re, b_im, c_re, c_im, d_skip, w_glu, b_glu, s0):
    f32 = jnp.float32
    bsz, length, _ = u.shape
    uf = u.astype(f32).reshape(bsz, length, S5_GROUPS, S5_GROUP)

    def flip(t):
        return jnp.flip(t, axis=1)

    y_f, f_re, f_im = s5_direction(uf, lam_re[0], lam_im[0], log_step[0], b_re[0], b_im[0],
                                   c_re[0], c_im[0], s0[:, 0, 0], s0[:, 0, 1])
    y_b, r_re, r_im = s5_direction(flip(uf), lam_re[1], lam_im[1], log_step[1], b_re[1], b_im[1],
                                   c_re[1], c_im[1], s0[:, 1, 0], s0[:, 1, 1])
    y = y_f + flip(y_b) + d_skip.astype(f32).reshape(S5_GROUPS, S5_GROUP) * uf
    g = y.reshape(bsz, length, BRANCH_W) @ w_glu.astype(f32) + b_glu.astype(f32)
    out = g[..., :BRANCH_W] * jax.nn.sigmoid(g[..., BRANCH_W:])
    state = jnp.stack([jnp.stack([f_re, f_im], axis=1), jnp.stack([r_re, r_im], axis=1)], axis=1)
    return out, state


def diff_block(qb, k, v, lam):
    f32 = jnp.float32
    s = jnp.einsum('bqhcd,bshcd->bhcqs', qb.astype(f32), k.astype(f32)) * (DIFF_HEAD_DIM ** -0.5)
    p = jax.nn.softmax(s, axis=-1)
    a = p[:, :, 0] - lam * p[:, :, 1]
    return jnp.einsum('bhqs,bshe->bqhe', a, v.astype(f32))


def diff_attention(q, k, v, lam, lam_init, subln_g):
    bsz, length = q.shape[0], q.shape[1]
    o = map_query_blocks(lambda qb: diff_block(qb, k, v, lam), q)
    return (rmsnorm(o, subln_g) * (1.0 - lam_init)).reshape(bsz, length, BRANCH_W)


def win_dense_block(qb, k, v, sink):
    f32 = jnp.float32
    s = jnp.einsum('bqngd,bsnd->bngqs', qb.astype(f32), k.astype(f32)) * (WIN_HEAD_DIM ** -0.5)
    sk = jnp.broadcast_to(sink.astype(f32)[None, :, :, None, None], s.shape[:-1] + (1,))
    p = jax.nn.softmax(jnp.concatenate([sk, s], axis=-1), axis=-1)[..., 1:]
    return jnp.einsum('bngqs,bsnd->bqngd', p, v.astype(f32))


def win_banded(q, k, v, ctx_k, ctx_v, sink):
    f32 = jnp.float32
    bsz, length = q.shape[0], q.shape[1]
    nb = length // BAND_BLK
    scale = WIN_HEAD_DIM ** -0.5
    qb = q.astype(f32).reshape(bsz, nb, BAND_BLK, WIN_KV_HEADS, WIN_GROUP, WIN_HEAD_DIM)
    pad = ((0, 0), (BAND_BLK, BAND_BLK), (0, 0), (0, 0))
    kp = jnp.pad(k.astype(f32), pad).reshape(bsz, nb + 2, BAND_BLK, WIN_KV_HEADS, WIN_HEAD_DIM)
    vp = jnp.pad(v.astype(f32), pad).reshape(bsz, nb + 2, BAND_BLK, WIN_KV_HEADS, WIN_HEAD_DIM)
    kband = jnp.concatenate([kp[:, :-2], kp[:, 1:-1], kp[:, 2:]], axis=2)
    vband = jnp.concatenate([vp[:, :-2], vp[:, 1:-1], vp[:, 2:]], axis=2)
    blk = jnp.arange(nb)[:, None, None]
    qpos = blk * BAND_BLK + jnp.arange(BAND_BLK)[None, :, None]
    kpos = (blk - 1) * BAND_BLK + jnp.arange(3 * BAND_BLK)[None, None, :]
    valid = (jnp.abs(qpos - kpos) <= WINDOW) & (kpos >= 0) & (kpos < length)
    s_band = jnp.einsum('bjqngd,bjknd->bjngqk', qb, kband) * scale
    s_band = jnp.where(valid[None, :, None, None], s_band, NEG_INF)
    s_ctx = jnp.einsum('bjqngd,bsnd->bjngqs', qb, ctx_k.astype(f32)) * scale
    sk = jnp.broadcast_to(sink.astype(f32)[None, None, :, :, None, None], s_ctx.shape[:-1] + (1,))
    p = jax.nn.softmax(jnp.concatenate([sk, s_ctx, s_band], axis=-1), axis=-1)
    n_ctx = ctx_k.shape[1]
    o = (jnp.einsum('bjngqs,bsnd->bjqngd', p[..., 1:1 + n_ctx], ctx_v.astype(f32))
         + jnp.einsum('bjngqk,bjknd->bjqngd', p[..., 1 + n_ctx:], vband))
    return o.reshape(bsz, length, BRANCH_W)


def trunk_layer(x, cmod, P, l, cache):
    f32 = jnp.float32
    bsz, length, _ = x.shape
    sh1, sc1, g1, sh2, sc2, g2 = adaln(cmod, P['w_mod'][l], P['b_mod'][l])
    h = rmsnorm(x, P['g_norm1'][l]) * (1.0 + sc1) + sh1
    z, xbc, dt_raw, u, dq, dk, dv, wq, wk, wv = jnp.split(h @ P['w_in'][l], IN_SPLITS, axis=-1)
    if cache is None:
        ssd_s0 = jnp.zeros((bsz, 2, SSD_HEADS, SSD_HEAD_DIM, SSD_STATE), f32)
        s5_s0 = jnp.zeros((bsz, 2, 2, S5_GROUPS, S5_STATE), f32)
    else:
        ssd_s0, s5_s0, dk_ctx, dv_ctx, wk_ctx, wv_ctx = cache
    y_a, ssd_fin = ssd_mixer(z, xbc, dt_raw, P['ssd_conv_w'][l], P['ssd_conv_b'][l], P['ssd_dt_bias'][l],
                             P['ssd_a_log'][l], P['ssd_d'][l], P['ssd_norm_g'][l], ssd_s0)
    y_b, s5_fin = s5_mixer(u, P['s5_lam_re'][l], P['s5_lam_im'][l], P['s5_log_step'][l], P['s5_b_re'][l],
                           P['s5_b_im'][l], P['s5_c_re'][l], P['s5_c_im'][l], P['s5_d'][l],
                           P['s5_w_glu'][l], P['s5_b_glu'][l], s5_s0)
    q = rmsnorm(dq.reshape(bsz, length, DIFF_HEADS, 2, DIFF_HEAD_DIM), P['diff_qn_g'][l])
    k = rmsnorm(dk.reshape(bsz, length, DIFF_HEADS, 2, DIFF_HEAD_DIM), P['diff_kn_g'][l])
    v = dv.reshape(bsz, length, DIFF_HEADS, 2 * DIFF_HEAD_DIM)
    lam_init = 0.8 - 0.6 * math.exp(-0.3 * l)
    lv = P['diff_lambda'][l].astype(f32)
    lam = jnp.exp(jnp.sum(lv[0] * lv[1])) - jnp.exp(jnp.sum(lv[2] * lv[3])) + lam_init
    wq_ = rmsnorm(wq.reshape(bsz, length, WIN_KV_HEADS, WIN_GROUP, WIN_HEAD_DIM), P['win_qn_g'][l])
    wk_ = rmsnorm(wk.reshape(bsz, length, WIN_KV_HEADS, WIN_HEAD_DIM), P['win_kn_g'][l])
    wv_ = wv.reshape(bsz, length, WIN_KV_HEADS, WIN_HEAD_DIM)
    sink = P['win_sink'][l].reshape(WIN_KV_HEADS, WIN_GROUP)
    if cache is None:
        y_c = diff_attention(q, k, v, lam, lam_init, P['diff_subln_g'][l])
        y_d = map_query_blocks(lambda qb: win_dense_block(qb, wk_, wv_, sink), wq_).reshape(bsz, length, BRANCH_W)
        new_ctx = (ssd_fin, s5_fin, k, v, wk_, wv_)
    else:
        cos, sin = axial_rope(length, DIFF_HEAD_DIM)
        y_c = diff_attention(apply_rope(q, cos, sin),
                             jnp.concatenate([dk_ctx.astype(k.dtype), apply_rope(k, cos, sin)], axis=1),
                             jnp.concatenate([dv_ctx.astype(v.dtype), v], axis=1),
                             lam, lam_init, P['diff_subln_g'][l])
        y_d = win_banded(apply_rope(wq_, cos, sin), apply_rope(wk_, cos, sin), wv_, wk_ctx, wv_ctx, sink)
        new_ctx = None
    branches = jnp.stack([y_a.astype(x.dtype), y_b.astype(x.dtype), y_c.astype(x.dtype),
                          y_d.astype(x.dtype)], axis=2)
    br = jnp.einsum('blie,ied->blid', branches, P['w_branch'][l])
    gates = jax.nn.sigmoid((h @ P['w_gate'][l] + P['b_gate'][l]).astype(f32))
    gates = gates.reshape(bsz, length, N_BRANCH, D_MODEL)
    merged = jnp.einsum('blid,blid->bld', gates, br.astype(f32)).astype(x.dtype) @ P['w_out'][l]
    x = x + g1 * merged
    h2 = rmsnorm(x, P['g_norm2'][l]) * (1.0 + sc2) + sh2
    x = x + g2 * (jnp.square(jax.nn.relu(h2 @ P['w_fc1'][l])) @ P['w_fc2'][l])
    return x, new_ctx


def setup_inputs(seed: int = 0) -> dict:
    key = jax.random.key(seed)
    ks = iter(jax.random.split(key, 64))
    f32 = jnp.float32

    def nrm(shape, scale=1.0):
        return jax.random.normal(next(ks), shape, f32) * scale

    def gain(shape):
        return 1.0 + nrm(shape, 0.02)

    D = D_MODEL
    H = SSD_HEADS
    x_prompt = nrm((BATCH, SEQ, D))
    x_sample = nrm((DEC_BATCH, DEC_SEQ, D))
    state_ssd = nrm((DEC_BATCH, DEPTH, 2, H, SSD_HEAD_DIM, SSD_STATE), 0.5)
    state_s5 = nrm((DEC_BATCH, DEPTH, 2, 2, S5_GROUPS, S5_STATE), 0.5)
    cache_diff_k = nrm((DEC_BATCH, DEPTH, PAST_LEN, DIFF_HEADS, 2, DIFF_HEAD_DIM))
    cache_diff_v = nrm((DEC_BATCH, DEPTH, PAST_LEN, DIFF_HEADS, 2 * DIFF_HEAD_DIM))
    cache_win_k = nrm((DEC_BATCH, DEPTH, PAST_LEN, WIN_KV_HEADS, WIN_HEAD_DIM))
    cache_win_v = nrm((DEC_BATCH, DEPTH, PAST_LEN, WIN_KV_HEADS, WIN_HEAD_DIM))
    c = nrm((DEC_BATCH, D))
    c_ctx = nrm((D,))
    w_mod = nrm((DEPTH, D, 6 * D), 0.5 * D ** -0.5)
    b_mod = nrm((DEPTH, 6 * D), 0.02)
    g_norm1 = gain((DEPTH, D))
    g_norm2 = gain((DEPTH, D))
    w_in = nrm((DEPTH, D, D_IN), D ** -0.5)
    ssd_conv_w = nrm((DEPTH, SSD_CONV_W, SSD_XBC), SSD_CONV_W ** -0.5)
    ssd_conv_b = nrm((DEPTH, SSD_XBC), 0.02)
    dt0 = jnp.exp(jax.random.uniform(next(ks), (DEPTH, 2, H), f32, math.log(1e-3), math.log(1e-1)))
    ssd_dt_bias = dt0 + jnp.log(-jnp.expm1(-dt0))
    ssd_a_log = jnp.log(jax.random.uniform(next(ks), (DEPTH, 2, H), f32, 1.0, 16.0))
    ssd_d = 1.0 + nrm((DEPTH, H), 0.1)
    ssd_norm_g = gain((DEPTH, BRANCH_W))
    n_idx = jnp.arange(S5_STATE, dtype=f32)
    s5_lam_re = -0.5 + nrm((DEPTH, 2, S5_GROUPS, S5_STATE), 0.01)
    s5_lam_im = jnp.pi * n_idx + nrm((DEPTH, 2, S5_GROUPS, S5_STATE), 0.01)
    s5_log_step = jax.random.uniform(next(ks), (DEPTH, 2, S5_GROUPS), f32, math.log(1e-3), math.log(1e-1))
    s5_b_re = nrm((DEPTH, 2, S5_GROUPS, S5_STATE, S5_GROUP), (2 * S5_GROUP) ** -0.5)
    s5_b_im = nrm((DEPTH, 2, S5_GROUPS, S5_STATE, S5_GROUP), (2 * S5_GROUP) ** -0.5)
    s5_c_re = nrm((DEPTH, 2, S5_GROUPS, S5_GROUP, S5_STATE), (2 * S5_STATE) ** -0.5)
    s5_c_im = nrm((DEPTH, 2, S5_GROUPS, S5_GROUP, S5_STATE), (2 * S5_STATE) ** -0.5)
    s5_d = nrm((DEPTH, BRANCH_W), 0.5)
    s5_w_glu = nrm((DEPTH, BRANCH_W, 2 * BRANCH_W), BRANCH_W ** -0.5)
    s5_b_glu = nrm((DEPTH, 2 * BRANCH_W), 0.02)
    diff_qn_g = gain((DEPTH, DIFF_HEAD_DIM))
    diff_kn_g = gain((DEPTH, DIFF_HEAD_DIM))
    diff_lambda = nrm((DEPTH, 4, DIFF_HEAD_DIM), 0.1)
    diff_subln_g = gain((DEPTH, 2 * DIFF_HEAD_DIM))
    win_qn_g = gain((DEPTH, WIN_HEAD_DIM))
    win_kn_g = gain((DEPTH, WIN_HEAD_DIM))
    win_sink = nrm((DEPTH, WIN_HEADS), 0.5)
    w_branch = nrm((DEPTH, N_BRANCH, BRANCH_W, D), BRANCH_W ** -0.5)
    w_gate = nrm((DEPTH, D, N_BRANCH * D), D ** -0.5)
    b_gate = nrm((DEPTH, N_BRANCH * D), 0.02)
    w_out = nrm((DEPTH, D, D), D ** -0.5)
    w_fc1 = nrm((DEPTH, D, D_FF), D ** -0.5)
    w_fc2 = nrm((DEPTH, D_FF, D), D_FF ** -0.5)
    return {'x_prompt': x_prompt, 'x_sample': x_sample, 'state_ssd': state_ssd, 'state_s5': state_s5,
            'cache_diff_k': cache_diff_k, 'cache_diff_v': cache_diff_v, 'cache_win_k': cache_win_k,
            'cache_win_v': cache_win_v, 'c': c, 'c_ctx': c_ctx, 'w_mod': w_mod, 'b_mod': b_mod,
            'g_norm1': g_norm1, 'g_norm2': g_norm2, 'w_in': w_in, 'ssd_conv_w': ssd_conv_w,
            'ssd_conv_b': ssd_conv_b, 'ssd_dt_bias': ssd_dt_bias, 'ssd_a_log': ssd_a_log, 'ssd_d': ssd_d,
            'ssd_norm_g': ssd_norm_g, 's5_lam_re': s5_lam_re, 's5_lam_im': s5_lam_im,
            's5_log_step': s5_log_step, 's5_b_re': s5_b_re, 's5_b_im': s5_b_im, 's5_c_re': s5_c_re,
            's5_c_im': s5_c_im, 's5_d': s5_d, 's5_w_glu': s5_w_glu, 's5_b_glu': s5_b_glu,
            'diff_qn_g': diff_qn_g, 'diff_kn_g': diff_kn_g, 'diff_lambda': diff_lambda,
            'diff_subln_g': diff_subln_g, 'win_qn_g': win_qn_g, 'win_kn_g': win_kn_g, 'win_sink': win_sink,
            'w_branch': w_branch, 'w_gate': w_gate, 'b_gate': b_gate, 'w_out': w_out,
            'w_fc1': w_fc1, 'w_fc2': w_fc2}


def reference(x_prompt, x_sample, state_ssd, state_s5, cache_diff_k, cache_diff_v, cache_win_k,
              cache_win_v, c, c_ctx, w_mod, b_mod, g_norm1, g_norm2, w_in, ssd_conv_w, ssd_conv_b,
              ssd_dt_bias, ssd_a_log, ssd_d, ssd_norm_g, s5_lam_re, s5_lam_im, s5_log_step, s5_b_re,
              s5_b_im, s5_c_re, s5_c_im, s5_d, s5_w_glu, s5_b_glu, diff_qn_g, diff_kn_g, diff_lambda,
              diff_subln_g, win_qn_g, win_kn_g, win_sink, w_branch, w_gate, b_gate, w_out, w_fc1, w_fc2):
    P = {'w_mod': w_mod, 'b_mod': b_mod, 'g_norm1': g_norm1, 'g_norm2': g_norm2, 'w_in': w_in,
         'ssd_conv_w': ssd_conv_w, 'ssd_conv_b': ssd_conv_b, 'ssd_dt_bias': ssd_dt_bias,
         'ssd_a_log': ssd_a_log, 'ssd_d': ssd_d, 'ssd_norm_g': ssd_norm_g, 's5_lam_re': s5_lam_re,
         's5_lam_im': s5_lam_im, 's5_log_step': s5_log_step, 's5_b_re': s5_b_re, 's5_b_im': s5_b_im,
         's5_c_re': s5_c_re, 's5_c_im': s5_c_im, 's5_d': s5_d, 's5_w_glu': s5_w_glu, 's5_b_glu': s5_b_glu,
         'diff_qn_g': diff_qn_g, 'diff_kn_g': diff_kn_g, 'diff_lambda': diff_lambda,
         'diff_subln_g': diff_subln_g, 'win_qn_g': win_qn_g, 'win_kn_g': win_kn_g, 'win_sink': win_sink,
         'w_branch': w_branch, 'w_gate': w_gate, 'b_gate': b_gate, 'w_out': w_out,
         'w_fc1': w_fc1, 'w_fc2': w_fc2}
    xp = x_prompt
    ctx_out = []
    for l in range(DEPTH):
        xp, new_ctx = trunk_layer(xp, c_ctx[None, :], P, l, None)
        ctx_out.append(new_ctx)
    xs = x_sample
    for l in range(DEPTH):
        layer_cache = (state_ssd[:, l], state_s5[:, l], cache_diff_k[:, l], cache_diff_v[:, l],
                       cache_win_k[:, l], cache_win_v[:, l])
        xs, _ = trunk_layer(xs, c, P, l, layer_cache)
    new_state_ssd = jnp.stack([t[0] for t in ctx_out], axis=1)
    new_state_s5 = jnp.stack([t[1] for t in ctx_out], axis=1)
    new_cache_diff_k = jnp.stack([t[2] for t in ctx_out], axis=1)
    new_cache_diff_v = jnp.stack([t[3] for t in ctx_out], axis=1)
    new_cache_win_k = jnp.stack([t[4] for t in ctx_out], axis=1)
    new_cache_win_v = jnp.stack([t[5] for t in ctx_out], axis=1)
    return (xp, xs, new_state_ssd, new_state_s5, new_cache_diff_k, new_cache_diff_v, new_cache_win_k, new_cache_win_v)
```

```python
import math
import numpy as np
from contextlib import ExitStack
import ml_dtypes
import concourse.bass as bass
import concourse.mybir as mybir
from concourse.bass_utils import run_bass_kernel_spmd

F32 = mybir.dt.float32
BF16 = mybir.dt.bfloat16
I32 = mybir.dt.int32
ALU = mybir.AluOpType
AF = mybir.ActivationFunctionType
AX = mybir.AxisListType
ENGS = ("pe", "dve", "act", "pool", "sp")
EPS = 1e-6


class Buf:
    __slots__ = ("name", "last_w", "readers", "load_sem", "load_cnt", "store_sem", "store_cnt", "excl")

    def __init__(self, name, excl=False):
        self.name = name
        self.excl = excl
        self.last_w = None
        self.readers = []
        self.load_sem = None
        self.load_cnt = 0
        self.store_sem = None
        self.store_cnt = 0


class Op:
    __slots__ = ("eng", "fn", "deps", "signal", "semval", "is_dma", "dsem", "dval", "phase")

    def __init__(self, eng, fn):
        self.phase = Sched.PHASE
        self.eng = eng
        self.fn = fn
        self.deps = []
        self.signal = False
        self.semval = 0
        self.is_dma = False
        self.dsem = None
        self.dval = 0


class Sched:
    PHASE = ""

    def __init__(self, nc, es):
        self.nc = nc
        self.es = es
        self.ops = {e: [] for e in ENGS}
        self.sems = {e: es.enter_context(nc.semaphore("c_" + e)) for e in ENGS}
        self.store_bufs = []
        self.nsem = 5
        self.pool = {}

    def new_sem(self, name):
        self.nsem += 1
        return self.es.enter_context(self.nc.semaphore(f"{name}_{self.nsem}"))

    def _track(self, op, reads, writes, skip_waw=False):
        deps = op.deps
        for r in reads:
            if r.last_w is not None and r.last_w is not op:
                deps.append(r.last_w)
            if r.excl:
                deps.extend(x for x in r.readers if x is not op and x.eng != op.eng)
            r.readers.append(op)
        for w in writes:
            if w.last_w is not None and w.last_w is not op and not skip_waw:
                deps.append(w.last_w)
            deps.extend(r for r in w.readers if r is not op)
            w.last_w = op
            w.readers = []

    def op(self, eng, fn, reads=(), writes=()):
        o = Op(eng, fn)
        self._track(o, reads, writes)
        self.ops[eng].append(o)
        return o

    def dma(self, q, out, in_, reads=(), writes=(), group=False, sbuf=None, **kw):
        o = Op(q, lambda e: e.dma_start(out=out, in_=in_, **kw))
        o.is_dma = True
        self._track(o, reads, writes, skip_waw=group)
        if sbuf is None:
            sbuf = writes[0] if writes else reads[0]
        key = ("l_" if sbuf in writes else "s_") + sbuf.name
        ent = self.pool.get(key)
        if ent is None:
            ent = [self.new_sem(key), 0]
            self.pool[key] = ent
        ent[1] += 16
        o.dsem, o.dval = ent[0], ent[1]
        self.ops[q].append(o)
        return o

    def emit(self, block):
        for e in ENGS:
            for o in self.ops[e]:
                for d in o.deps:
                    if not d.is_dma and not (d.eng == "pe" and o.eng == "pe"):
                        d.signal = True
        for e in ENGS:
            v = 0
            for o in self.ops[e]:
                if o.signal and not o.is_dma:
                    v += 1
                    o.semval = v
        engmap = {"pe": block.tensor, "dve": block.vector, "act": block.scalar,
                  "pool": block.gpsimd, "sp": block.sync}
        sems = self.sems
        store_bufs = self.store_bufs
        for e in ENGS:
            def body(eng, ops=self.ops[e], e=e):
                known = {}
                for o in ops:
                    need = {}
                    for d in o.deps:
                        if d.is_dma:
                            key, val = d.dsem, d.dval
                        else:
                            if d.eng == "pe" and e == "pe":
                                continue
                            key, val = sems[d.eng], d.semval
                        if need.get(key, 0) < val:
                            need[key] = val
                    for key, val in need.items():
                        if known.get(key, 0) < val:
                            eng.wait_ge(key, val)
                            known[key] = val
                    inst = o.fn(eng)
                    if o.is_dma:
                        inst.then_inc(o.dsem, 16)
                    elif o.signal:
                        inst.then_inc(sems[e], 1)
                if e == "sp":
                    for key, ent in self.pool.items():
                        if key.startswith("s_"):
                            eng.wait_ge(ent[0], ent[1])
            engmap[e](body)


D = 1024
T = 1024
W_IN = 4112
C_Z, C_XBC, C_DT, C_U, C_DQ, C_DK, C_DV, C_WQ, C_WK, C_WV = 0, 512, 1280, 1296, 1808, 2320, 2832, 3344, 3856, 3984


class _Stop(Exception):
    pass


def build_program(stop=None, sub=None):
    nc = bass.Bass("TRN2", target_bir_lowering=False)
    es = ExitStack()
    S = Sched(nc, es)

    def din(name, shape, dt=F32):
        return nc.dram_tensor(name, list(shape), dt, kind="ExternalInput").ap()

    def dout(name, shape):
        return nc.dram_tensor(name, list(shape), F32, kind="ExternalOutput").ap()

    cnt = [0]

    def sb(shape, dt=F32, name=None):
        cnt[0] += 1
        return es.enter_context(nc.sbuf_tensor(name or f"t{cnt[0]}", list(shape), dt))

    xin = [din("xp", [T, D]), din("xs", [T, D])]
    yout = [dout("yp", [T, D]), dout("ys", [T, D])]
    cvT_d = din("cvT", [128, 8, 2])
    st_ssd = din("st_ssd", [2, 2, 8, 64, 64])
    st_s5 = din("st_s5", [2, 64, 128])
    cdk = din("cdk", [2, 256, 512]); cdv = din("cdv", [2, 256, 512])
    cwk = din("cwk", [2, 256, 128]); cwv = din("cwv", [2, 256, 128])
    w_mod = din("w_mod", [2, D, 6 * D]); w_in = din("w_in", [2, D, W_IN]); w_gate = din("w_gate", [2, D, 4 * D])
    w_out = din("w_out", [2, D, D]); w_fc1 = din("w_fc1", [2, D, 4 * D]); w_fc2 = din("w_fc2", [2, 4 * D, D])
    w_glu = din("w_glu", [2, 512, 1024]); w_br = din("w_br", [2, 2048, 1024])
    bmodc = din("bmodc", [2, 128, 48]); g1c = din("g1c", [2, 128, 8]); g2c = din("g2c", [2, 128, 8])
    convw = din("convw", [2, 128, 6, 7]); convb = din("convb", [2, 128, 6])
    dtb = din("dtb", [2, 16]); alog = din("alog", [2, 16]); ssdd = din("ssdd", [2, 8]); normgc = din("normgc", [2, 128, 4])
    lamre = din("lamre", [2, 32, 128]); lamim = din("lamim", [2, 32, 128]); lsx = din("lsx", [2, 32, 128])
    s5bre = din("s5bre", [2, 2, 2048, 16]); s5bim = din("s5bim", [2, 2, 2048, 16])
    s5cre = din("s5cre", [2, 2, 512, 64]); s5cim = din("s5cim", [2, 2, 512, 64])
    s5dc = din("s5dc", [2, 128, 4]); bgluc = din("bgluc", [2, 128, 8])
    dqg = din("dqg", [2, 64]); dkg = din("dkg", [2, 64]); dlam = din("dlam", [2, 256]); dsubc = din("dsubc", [2, 128, 1])
    wqg = din("wqg", [2, 64]); wkg = din("wkg", [2, 64]); wsink = din("wsink", [2, 8]); bgatec = din("bgatec", [2, 128, 32])
    c_identb = din("c_identb", [128, 128], BF16); c_identf = din("c_identf", [128, 128]); c_ones = din("c_ones", [128, 128])
    c_triu = din("c_triu", [128, 128]); c_tril = din("c_tril", [128, 128])
    c_mnegF = din("c_mnegF", [128, 128]); c_mnegB = din("c_mnegB", [128, 128])
    c_bprev = din("c_bprev", [128, 128]); c_bnext = din("c_bnext", [128, 128])
    c_maskB = din("c_maskB", [128, 4, 128]); c_maskC = din("c_maskC", [128, 4, 128])
    c_iota = din("c_iota", [128, 1024]); c_ropeC = din("c_ropeC", [128, 8, 64]); c_ropeS = din("c_ropeS", [128, 8, 64])
    c_rmF = din("c_rmF", [128, 1024], BF16); c_rmB = din("c_rmB", [128, 1024], BF16)
    o_ssd = dout("o_ssd", [4, 2, 2, 8, 64, 64]); o_s5 = dout("o_s5", [4, 2, 64, 128])
    o_dk = dout("o_dk", [4, 2, 256, 512]); o_dv = dout("o_dv", [4, 2, 256, 512])
    o_wk = dout("o_wk", [4, 2, 256, 128]); o_wv = dout("o_wv", [4, 2, 256, 128])

    def TT(eng, out, in0, in1, op, r, w):
        S.op(eng, lambda e: e.tensor_tensor(out=out, in0=in0, in1=in1, op=op), r, w)

    def TS(eng, out, in0, s1, op0, r, w, s2=None, op1=None):
        if op1 is None:
            S.op(eng, lambda e: e.tensor_scalar(out=out, in0=in0, scalar1=s1, scalar2=None, op0=op0), r, w)
        else:
            S.op(eng, lambda e: e.tensor_scalar(out=out, in0=in0, scalar1=s1, scalar2=s2, op0=op0, op1=op1), r, w)

    def STT(out, in0, scalar, in1, op0, op1, r, w):
        S.op("dve", lambda e: e.scalar_tensor_tensor(out=out, in0=in0, scalar=scalar, in1=in1, op0=op0, op1=op1), r, w)

    def ACT(out, in_, func, r, w, scale=1.0, bias=None, accum=None):
        kw = {}
        if bias is not None:
            kw["bias"] = bias
        if accum is not None:
            kw["accum_out"] = accum
        S.op("act", lambda e: e.activation(out=out, in_=in_, func=func, scale=scale, **kw), r, w)

    def CP(eng, out, in_, r, w):
        if eng == "act":
            S.op("act", lambda e: e.copy(out=out, in_=in_), r, w)
        else:
            S.op(eng, lambda e: e.tensor_copy(out=out, in_=in_), r, w)

    def MM(out, lhsT, rhs, start, stop, r, w):
        S.op("pe", lambda e: e.matmul(out, lhsT=lhsT, rhs=rhs, start=start, stop=stop), r, w)

    def MSET(eng, out, val, w):
        S.op(eng, lambda e: e.memset(out, val), (), w)

    def LD(out, in_, b, q="sp", group=False):
        S.dma(q, out, in_, writes=[b], group=group)

    def STO(out, in_, b, q="sp"):
        S.dma(q, out, in_, reads=[b])

    def const(src, shape, dt=F32):
        t = sb(shape, dt)
        b = Buf(f"c{cnt[0]}")
        LD(t[:], src, b)
        return t, b

    identb, b_identb = const(c_identb[:, :], [128, 128], BF16)
    identf, b_identf = const(c_identf[:, :], [128, 128])
    onesf, b_ones = const(c_ones[:, :], [128, 128])
    triu, b_triu = const(c_triu[:, :], [128, 128]); tril, b_tril = const(c_tril[:, :], [128, 128])
    mnegF, b_mnegF = const(c_mnegF[:, :], [128, 128]); mnegB, b_mnegB = const(c_mnegB[:, :], [128, 128])
    maskB, b_maskB = const(c_maskB[:, :, :], [128, 4, 128]); maskC, b_maskC = const(c_maskC[:, :, :], [128, 4, 128])
    ropeC, b_ropeC = const(c_ropeC[:, :, :], [128, 8, 64]); ropeS, b_ropeS = const(c_ropeS[:, :, :], [128, 8, 64])
    CONSTB = [b_identb, b_identf, b_ones]

    psum = [es.enter_context(nc.psum_tensor(f"ps{i}", [128, 512], F32)) for i in range(8)]
    psb = [Buf(f"ps{i}", excl=True) for i in range(8)]

    class RR:
        def __init__(self, ids):
            self.ids = list(ids); self.i = 0

        def get(self):
            k = self.ids[self.i % len(self.ids)]; self.i += 1
            return psum[k], psb[k]

        def set(self, ids):
            self.ids = list(ids)

    rr = RR(range(0, 4))

    xres = sb([128, 8, D]); b_xres = [Buf(f"xres{t}") for t in range(8)]
    hT = sb([128, 8, T], BF16); b_hT = Buf("hT")
    NST, NBF = 3, 3
    wst = [sb([128, 8, 256]) for _ in range(NST)]; b_wst = [Buf(f"wst{i}") for i in range(NST)]
    wbf = [sb([128, 8, 256], BF16) for _ in range(NBF)]; b_wbf = [Buf(f"wbf{i}") for i in range(NBF)]
    wctr = [0, 0]
    big = sb([128, 16, 1024], BF16)
    b_big = [Buf(f"big{i}") for i in range(16)]
    modc = sb([128, 48]); b_modc = Buf("modc")
    scol = sb([128, 8, 2]); b_scol = Buf("scol")
    G1 = sb([128, 8]); SH1 = sb([128, 8]); G2 = sb([128, 8]); SH2 = sb([128, 8]); b_G = Buf("G")
    gbc = sb([128, 2, D]); b_gbc = [Buf("gbc0"), Buf("gbc1")]
    small = sb([128, 64]); b_small = Buf("small")
    junk = sb([128, 768]); b_junk = Buf("junk")
    SCR_BYTES = 60 * 1024
    scr = sb([128, SCR_BYTES // 4])

    def wload(src, nk, ncols, cast=True):
        i = wctr[0] % NST; wctr[0] += 1
        st, bs = wst[i], b_wst[i]
        LD(st[:, 0:nk, 0:ncols], src.rearrange("(k p) c -> p k c", p=128), bs)
        if not cast:
            return st, bs
        j = wctr[1] % NBF; wctr[1] += 1
        wb, bb = wbf[j], b_wbf[j]
        heavy = any(k in Sched.PHASE for k in ("prologue", "merge", "mlp"))
        eng = "act" if (wctr[1] % 2 == 0 or not heavy) else "dve"
        CP(eng, wb[:, 0:nk, 0:ncols], st[:, 0:nk, 0:ncols], [bs], [bb])
        return wb, bb

    def proj_tm(src, bsrc, nk, w, bw, ncols, tiles, evac):
        for t in tiles:
            ps, bp = rr.get()
            for k in range(nk):
                MM(ps[:, 0:ncols], src[:, k, t * 128:(t + 1) * 128], w[:, k, 0:ncols], k == 0, k == nk - 1, [bsrc, bw], [bp])
            evac(t, ps, bp)

    def proj_fm(src, bsrc, nk, w, bw, ncols, evac, halves=(0, 1)):
        for cc in range((ncols + 127) // 128):
            m = min(128, ncols - cc * 128)
            for h in halves:
                ps, bp = rr.get()
                for k in range(nk):
                    MM(ps[0:m, :], w[:, k, cc * 128:cc * 128 + m], src[:, k, h * 512:(h + 1) * 512], k == 0, k == nk - 1, [bsrc, bw], [bp])
                evac(cc, h, ps, bp)

    def transpose_to(ps_out, in_, r, w, dt=BF16, np_=128):
        idn = identb if dt == BF16 else identf
        S.op("pe", lambda e: e.transpose(out=ps_out, in_=in_, identity=idn[0:np_, 0:np_]), list(r) + CONSTB, w)

    def bcast_rows(col_ap, bcol, ps_out, bp):
        dg = sb_diag[dgc[0] % 4]; bd = b_diag[dgc[0] % 4]; dgc[0] += 1
        TS("dve", dg[:], identf[:], col_ap, ALU.mult, [b_identf, bcol], [bd])
        MM(ps_out, onesf[:], dg[:], True, True, [b_ones, bd], [bp])

    sb_diag = [sb([128, 128]) for _ in range(4)]; b_diag = [Buf(f"dg{i}") for i in range(4)]; dgc = [0]

    def rstd_from_ss(ss_ap, n, out_ap, r, w, ncols=1):
        TS("dve", out_ap, ss_ap, 1.0 / n, ALU.mult, r, w, s2=EPS, op1=ALU.add)
        ACT(out_ap, out_ap, AF.Sqrt, w, w)
        S.op("dve", lambda e: e.reciprocal(out=out_ap, in_=out_ap), w, w)

    LD(scol[:], cvT_d[:, :, :], b_scol)
    ACT(scol[:], scol[:], AF.Silu, [b_scol], [b_scol])
    scolb = sb([128, 8, 2], BF16)
    CP("dve", scolb[:], scol[:], [b_scol], [b_scol])

    modall = sb([128, 2, 2, 48]); b_modall = Buf("modall")

    def adaln_weights(l):
        rr.set(range(8))
        bm = sb_bm; LD(bm[:], bmodc[l], b_bm)
        for blk in range(24):
            w, bw = wload(w_mod[l][:, blk * 256:(blk + 1) * 256], 8, 256)
            for cc in range(2):
                ps, bp = rr.get()
                for k in range(8):
                    MM(ps[:, 0:2], w[:, k, cc * 128:(cc + 1) * 128], scolb[:, k, 0:2], k == 0, k == 7, [bw, b_scol], [bp])
                c = blk * 2 + cc
                TT("dve", modall[:, l, :, c], ps[:, 0:2], bm[:, c:c + 1].broadcast_to([128, 2]), ALU.add, [bp, b_bm], [b_modall])
        rr.set(range(4))

    def adaln(l, path):
        rr.set(range(8))
        CP("dve", modc[:], modall[:, l, path, :], [b_modall], [b_modc])
        gt = sb_gt; LD(gt[:, 0:8], g1c[l], b_gt); LD(gt[:, 8:16], g2c[l], b_gt, group=True)
        STT(G1[:], modc[:, 8:16], 1.0, gt[:, 0:8], ALU.add, ALU.mult, [b_modc, b_gt], [b_G])
        STT(G2[:], modc[:, 32:40], 1.0, gt[:, 8:16], ALU.add, ALU.mult, [b_modc, b_gt], [b_G])
        CP("dve", SH1[:], modc[:, 0:8], [b_modc], [b_G])
        CP("dve", SH2[:], modc[:, 24:32], [b_modc], [b_G])
        for gi, base in enumerate((16, 40)):
            for c in range(8):
                ps, bp = rr.get()
                bcast_rows(modc[:, base + c:base + c + 1], b_modc, ps[:, 0:128], bp)
                CP("act", gbc[:, gi, c * 128:(c + 1) * 128], ps[:, 0:128], [bp], [b_gbc[gi]])
        rr.set(range(4))

    sb_bm = sb([128, 48]); b_bm = Buf("bm"); sb_gt = sb([128, 16]); b_gt = Buf("gt")

    xn = [sb([128, D], BF16), sb([128, D], BF16)]; b_xn = [Buf("xn0"), Buf("xn1")]

    def norm_mod(Gc, SHc):
        rr.set(range(8))
        stg = []
        for t in range(8):
            def FA(t=t):
                ss = small[:, t:t + 1]
                x_, bx_ = xn[t % 2], b_xn[t % 2]
                ACT(x_[:], xres[:, t, :], AF.Square, [b_xres[t]], [bx_, b_small], accum=ss)
                rstd_from_ss(ss, D, small[:, 8 + t:9 + t], [b_small], [b_small])
                ACT(x_[:], xres[:, t, :], AF.Copy, [b_xres[t], b_small], [bx_], scale=small[:, 8 + t:9 + t])

            def FB(t=t):
                x_, bx_ = xn[t % 2], b_xn[t % 2]
                for c in range(8):
                    ps, bp = rr.get()
                    pv = ps[:].bitcast(BF16)[:, 0:128]
                    transpose_to(pv, x_[:, c * 128:(c + 1) * 128], [bx_], [bp])
                    if c % 2 == 0:
                        ACT(hT[:, c, t * 128:(t + 1) * 128], pv, AF.Identity, [bp, b_G], [b_hT], scale=Gc[:, c:c + 1], bias=SHc[:, c:c + 1])
                    else:
                        TS("dve", hT[:, c, t * 128:(t + 1) * 128], pv, Gc[:, c:c + 1], ALU.mult, [bp, b_G], [b_hT], s2=SHc[:, c:c + 1], op1=ALU.add)
            stg.append((FA, FB))
        stg[0][0]()
        for k in range(8):
            if k + 1 < 8:
                stg[k + 1][0]()
            stg[k][1]()
        rr.set(range(4))

    def yT(br, fc):
        return big[:, br * 4 + fc, :], b_big[br * 4 + fc]

    def run_pass(path):
        nseq, L = (4, 256) if path == 0 else (1, 1024)
        nt = L // 128
        is_s = path == 1
        for t in range(8):
            LD(xres[:, t, :], xin[path][t * 128:(t + 1) * 128, :], b_xres[t])
        def chk(stage, l):
            if stop is not None and stop == (path, l, stage):
                raise _Stop()
        for l in range(2):
            def ph(n):
                Sched.PHASE = f"{'PS'[path]}{l}_{n}"
            ph("adaln"); adaln(l, path); chk("adaln", l)
            ph("norm1"); norm_mod(G1, SH1); chk("norm1", l)
            ph("ssd"); branch_ssd(l, path, nseq, L, nt, is_s); chk("ssd", l)
            ph("s5"); branch_s5(l, path, nseq, L, nt, is_s); chk("s5", l)
            ph("diff"); branch_diff(l, path, nseq, L, nt, is_s); chk("diff", l)
            ph("win"); branch_win(l, path, nseq, L, nt, is_s); chk("win", l)
            ph("merge"); merge(l); chk("merge", l)
            ph("norm2"); norm_mod(G2, SH2); chk("norm2", l)
            ph("mlp"); mlp(l); chk("mlp", l)
        for t in range(8):
            STO(yout[path][t * 128:(t + 1) * 128, :], xres[:, t, :], b_xres[t])

    class Carve:
        def __init__(self):
            self.off = 0

        def take(self, shape, dt=F32):
            n = int(np.prod(shape))
            nbytes = n * (4 if dt in (F32, I32) else 2)
            nbytes = (nbytes + 31) // 32 * 32
            assert self.off + nbytes <= SCR_BYTES, (self.off, nbytes)
            v = scr[:, self.off // 4:(self.off + nbytes) // 4]
            self.off += nbytes
            if dt != F32:
                v = v.bitcast(dt)
            v = v[:, 0:n]
            if len(shape) == 2:
                return v.rearrange("p (a b) -> p a b", a=shape[0])
            if len(shape) == 3:
                return v.rearrange("p (a b c) -> p a b c", a=shape[0], b=shape[1])
            return v

    b_scr_all = Buf("scrall")
    bar = [None]

    def NB(name):
        x = Buf(name)
        x.last_w = bar[0]
        return x

    def barrier_begin():
        bar[0] = S.op("pool", lambda e: e.memset(small[:, 63:64], 0.0), [b_scr_all], [b_scr_all])


    def branch_ssd(l, path, nseq, L, nt, is_s):
        barrier_begin()
        cv = Carve()
        zs = cv.take([8, 512], BF16); b_zs = NB("zs")
        dt = cv.take([8, 16]); dtA = cv.take([8, 16]); ainc = cv.take([8, 16]); arest = cv.take([8, 16]); edt = cv.take([8, 16]); einc = cv.take([8, 16])
        nainc = cv.take([8, 16])
        b_dt = NB("dt"); b_cum = NB("cum")
        cumP = cv.take([8, 32]); b_cumP = NB("cumP")
        Lp = L + 6
        raw = cv.take([nseq * Lp]); b_raw = NB("raw")
        acc = cv.take([T]); b_acc = NB("acc")
        xrot = [cv.take([T], BF16), cv.take([T], BF16)]; b_xrot = [NB("xrot0"), NB("xrot1")]
        xB = cv.take([T], BF16); xC = cv.take([T], BF16); b_xB = NB("xB"); b_xC = NB("xC")
        xs_tok = cv.take([8, 512], BF16); b_xs = NB("xs_tok")
        B_tok = cv.take([8, 128], BF16); b_Btok = NB("Btok")
        NDP = 12
        GS = 4
        b_seg = []
        Lt = [cv.take([128]) for _ in range(NDP)]; b_Lt = [NB(f"Lt{i}") for i in range(NDP)]
        sc = [cv.take([128], BF16) for _ in range(NDP)]; b_sc = [NB(f"sc{i}") for i in range(NDP)]
        yacc = cv.take([512]); b_yacc = NB("yacc")
        ytmp = cv.take([512]); b_ytmp = NB("ytmp")
        ynb = cv.take([512], BF16); b_ynb = NB("ynb")
        prm = cv.take([64]); b_prm = NB("ssdprm")
        cw = cv.take([6, 7]); cb = cv.take([6]); ngc = cv.take([4]); b_cw = NB("cw")
        Bw = [cv.take([64], BF16), cv.take([64], BF16)]; b_Bw = [NB("Bw0"), NB("Bw1")]
        fin = None; s0T = None; st_ld = None
        b_fin = NB("fin"); b_s0T = NB("s0T"); b_stld = NB("stld")
        if is_s:
            s0T = cv.take([8, 128], BF16); st_ld = cv.take([8, 128])
        else:
            fin = cv.take([16, 64])
        _p0 = Sched.PHASE
        S.op("pool", lambda e: e.memset(prm[:, 0:64], 0.0), [b_scr_all], [b_prm, b_scr_all])
        LD(prm[:, 0:16], dtb[l:l + 1, :].partition_broadcast(128), b_prm)
        LD(prm[:, 16:32], alog[l:l + 1, :].partition_broadcast(128), b_prm, group=True)
        LD(prm[:, 32:40], ssdd[l:l + 1, :].partition_broadcast(128), b_prm, group=True)
        ACT(prm[:, 16:32], prm[:, 16:32], AF.Exp, [b_prm], [b_prm])
        TS("dve", prm[:, 16:32], prm[:, 16:32], -1.0, ALU.mult, [b_prm], [b_prm])
        LD(cw[:], convw[l], b_cw); LD(cb[:], convb[l], b_cw, group=True); LD(ngc[:], normgc[l], b_cw, group=True)
        Sched.PHASE = _p0 + 'A'
        for blk in range(2):
            w, bw = wload(w_in[l][:, C_Z + blk * 256:C_Z + (blk + 1) * 256], 8, 256)
            proj_tm(hT, b_hT, 8, w, bw, 256, range(8),
                    lambda t, ps, bp, blk=blk: ACT(zs[:, t, blk * 256:(blk + 1) * 256], ps[:, 0:256], AF.Silu, [bp], [b_zs]))
        Sched.PHASE = _p0 + 'B'
        w, bw = wload(w_in[l][:, C_DT:C_DT + 16], 8, 16)

        def ev_dt(t, ps, bp):
            TT("dve", dt[:, t, :], ps[:, 0:16], prm[:, 0:16], ALU.add, [bp, b_prm], [b_dt])
            ACT(dt[:, t, :], dt[:, t, :], AF.Exp, [b_dt], [b_dt])
            ACT(dt[:, t, :], dt[:, t, :], AF.Ln, [b_dt], [b_dt], bias=1.0)
            TT("dve", dtA[:, t, :], dt[:, t, :], prm[:, 16:32], ALU.mult, [b_dt, b_prm], [b_dt])
        proj_tm(hT, b_hT, 8, w, bw, 16, range(8), ev_dt)
        Sched.PHASE = _p0 + 'C'
        for blk in range(3):
            w, bw = wload(w_in[l][:, C_XBC + blk * 256:C_XBC + (blk + 1) * 256], 8, 256)
            for c2 in range(2):
                cc = blk * 2 + c2
                if cc < 4:
                    xa, bxa = xrot[cc % 2], b_xrot[cc % 2]
                elif cc == 4:
                    xa, bxa = xB, b_xB
                else:
                    xa, bxa = xC, b_xC
                MSET("pool", raw[:], 0.0, [b_raw])
                rw3 = raw.rearrange("p (s x) -> p s x", s=nseq)
                for h in range(2):
                    ps, bp = rr.get()
                    for k in range(8):
                        MM(ps[:, :], w[:, k, c2 * 128:(c2 + 1) * 128], hT[:, k, h * 512:(h + 1) * 512], k == 0, k == 7, [b_hT, bw], [bp])
                    if is_s:
                        CP("act", raw[:, 3 + h * 512:3 + (h + 1) * 512], ps[:, :], [bp], [b_raw])
                    else:
                        CP("act", rw3[:, 2 * h:2 * h + 2, 3:3 + L], ps[:, :].rearrange("p (s x) -> p s x", s=2), [bp], [b_raw])
                ac3 = acc.rearrange("p (s x) -> p s x", s=nseq)
                TS("dve", ac3, rw3[:, :, 0:L], cw[:, cc, 0:1], ALU.mult, [b_raw, b_cw], [b_acc])
                for k in range(1, 7):
                    STT(ac3, rw3[:, :, k:k + L], cw[:, cc, k:k + 1], ac3, ALU.mult, ALU.add, [b_raw, b_cw, b_acc], [b_acc])
                ACT(xa[:], acc[:], AF.Silu, [b_acc, b_cw], [bxa], bias=cb[:, cc:cc + 1])
                if cc < 5:
                    for t in range(8):
                        ps, bp = rr.get()
                        pv = ps[:].bitcast(BF16)[:, 0:128]
                        transpose_to(pv, xa[:, t * 128:(t + 1) * 128], [bxa], [bp])
                        if cc < 4:
                            CP("act", xs_tok[:, t, cc * 128:(cc + 1) * 128], pv, [bp], [b_xs])
                        else:
                            CP("act", B_tok[:, t, :], pv, [bp], [b_Btok])
        Sched.PHASE = _p0 + 'E'
        for s in range(nseq):
            for j in range(nt):
                tj = s * nt + j
                ps, bp = rr.get()
                for i in range(j + 1):
                    MM(ps[:, 0:16], (triu if i == j else onesf)[:], dtA[:, s * nt + i, :], i == 0, i == j, [b_triu, b_ones, b_dt], [bp])
                for i in range(nt - 1, j - 1, -1):
                    MM(ps[:, 16:32], (tril if i == j else onesf)[:], dtA[:, s * nt + i, :], i == nt - 1, i == j, [b_tril, b_ones, b_dt], [bp])
                CP("act", cumP[:, tj, :], ps[:, 0:32], [bp], [b_cumP])
                CP("dve", ainc[:, tj, 0:8], cumP[:, tj, 0:8], [b_cumP], [b_cum])
                CP("dve", ainc[:, tj, 8:16], cumP[:, tj, 24:32], [b_cumP], [b_cum])
                TT("dve", arest[:, tj, 0:8], cumP[:, tj, 16:24], dtA[:, tj, 0:8], ALU.subtract, [b_cumP, b_dt], [b_cum])
                TT("dve", arest[:, tj, 8:16], cumP[:, tj, 8:16], dtA[:, tj, 8:16], ALU.subtract, [b_cumP, b_dt], [b_cum])
                ACT(edt[:, tj, :], arest[:, tj, :], AF.Exp, [b_cum], [b_cum])
                TT("dve", edt[:, tj, :], edt[:, tj, :], dt[:, tj, :], ALU.mult, [b_cum, b_dt], [b_cum])
                ACT(einc[:, tj, :], ainc[:, tj, :], AF.Exp, [b_cum], [b_cum])
                TS("dve", nainc[:, tj, :], ainc[:, tj, :], -1.0, ALU.mult, [b_cum], [b_cum])
        if is_s:
            stv = st_ssd[l].rearrange("d h p n -> (d h p) n").rearrange("(j q) n -> q j n", q=128)
            LD(st_ld[:, :, 0:64], stv, b_stld); LD(st_ld[:, :, 64:128], stv, b_stld, group=True)
            for j8 in range(8):
                ps, bp = rr.get()
                transpose_to(ps[:, 0:128], st_ld[:, j8, :], [b_stld], [bp], dt=F32)
                CP("act", s0T[:, j8, :], ps[:, 0:128], [bp], [b_s0T])
        Sched.PHASE = _p0 + 'F'
        ybanks = [(psum[4], psb[4]), (psum[7], psb[7])]
        pa_ = [0]
        k_ = [0]
        stages = []
        for s in range(nseq):
            for j in range(nt):
                tj = s * nt + j
                ybank, b_yb = ybanks[tj % 2]
                for h in range(8):
                    g = h // 4
                    gsl = slice(g * 64, (g + 1) * 64)
                    units = [(0, i) for i in range(j + 1)] + [(1, i) for i in range(j, nt)]
                    cur = {"psA": None}
                    for g0 in range(0, len(units), GS):
                        grp = list(enumerate(units))[g0:g0 + GS]
                        st = {}

                        def A(st=st, grp=grp, units=units, s=s, j=j, tj=tj, h=h, gsl=gsl, cur=cur):
                            for ui, (d, i) in grp:
                                ti = s * nt + i
                                dh = d * 8 + h
                                if ui == 0 or units[ui - 1][0] != d:
                                    cur["psA"] = (psum[5 + pa_[0] % 2], psb[5 + pa_[0] % 2]); pa_[0] += 1
                                    bcast_rows(ainc[:, tj, dh:dh + 1], b_cum, cur["psA"][0][:, 0:128], cur["psA"][1])
                                psA_t, b_psA = cur["psA"]
                                q = k_[0] % NDP; k_[0] += 1
                                st[ui] = q
                                if i == j:
                                    STT(Lt[q][:], psA_t[:, 0:128], ainc[:, ti, dh:dh + 1], (mnegF if d == 0 else mnegB)[:], ALU.subtract, ALU.add,
                                        [b_psA, b_cum, b_mnegF, b_mnegB], [b_Lt[q]])
                                    ACT(Lt[q][:], Lt[q][:], AF.Exp, [b_Lt[q]], [b_Lt[q]])
                                else:
                                    ACT(Lt[q][:], psA_t[:, 0:128], AF.Exp, [b_psA, b_cum], [b_Lt[q]], bias=nainc[:, ti, dh:dh + 1])
                            for ui, (d, i) in grp:
                                ti = s * nt + i
                                dh = d * 8 + h
                                q = st[ui]
                                psG, bpG = rr.get()
                                MM(psG[:, 0:128], xB[gsl, ti * 128:(ti + 1) * 128], xC[gsl, tj * 128:(tj + 1) * 128], True, True, [b_xB, b_xC], [bpG])
                                STT(sc[q][:], psG[:, 0:128], dt[:, ti, dh:dh + 1], Lt[q][:], ALU.mult, ALU.mult, [bpG, b_dt, b_Lt[q]], [b_sc[q]])

                        def B(st=st, grp=grp, units=units, s=s, tj=tj, h=h, ybank=ybank, b_yb=b_yb, last_grp=(g0 + GS >= len(units))):
                            for ui, (d, i) in grp:
                                ti = s * nt + i
                                q = st[ui]
                                MM(ybank[:, h * 64:(h + 1) * 64], sc[q][:], xs_tok[:, ti, h * 64:(h + 1) * 64], ui == 0, ui == len(units) - 1, [b_sc[q], b_xs], [b_yb])
                            if h == 7 and last_grp:
                                finalize(tj, ybank, b_yb)
                        stages.append((A, B))

        def finalize(tj, ybank, b_yb):
            if True:
                TT("dve", ytmp.rearrange("p (h x) -> p h x", h=8), xs_tok[:, tj, :].rearrange("p (h x) -> p h x", h=8),
                   prm[:, 32:40].unsqueeze(2).broadcast_to([128, 8, 64]), ALU.mult, [b_xs, b_prm], [b_ytmp])
                TT("dve", yacc[:], ybank[:, :], ytmp[:], ALU.add, [b_yb, b_ytmp], [b_yacc])
                if is_s:
                    for d in range(2):
                        for h in range(8):
                            g = h // 4
                            gsl = slice(g * 64, (g + 1) * 64)
                            j8 = (d * 8 + h) // 2
                            h2 = (d * 8 + h) % 2
                            psO, bpO = rr.get()
                            MM(psO[:, 0:64], xC[gsl, tj * 128:(tj + 1) * 128], s0T[gsl, j8, h2 * 64:(h2 + 1) * 64], True, True, [b_xC, b_s0T], [bpO])
                            STT(yacc[:, h * 64:(h + 1) * 64], psO[:, 0:64], einc[:, tj, d * 8 + h:d * 8 + h + 1], yacc[:, h * 64:(h + 1) * 64], ALU.mult, ALU.add,
                                [bpO, b_cum, b_yacc], [b_yacc])
                TT("dve", yacc[:], yacc[:], zs[:, tj, :], ALU.mult, [b_yacc, b_zs], [b_yacc])
                ACT(ytmp[:], yacc[:], AF.Square, [b_yacc], [b_ytmp, b_small], accum=small[:, 16:17])
                rstd_from_ss(small[:, 16:17], 512, small[:, 17:18], [b_small], [b_small])
                ACT(ynb[:], yacc[:], AF.Copy, [b_yacc, b_small], [b_ynb], scale=small[:, 17:18])
                for c4 in range(4):
                    ps, bp = rr.get()
                    pv = ps[:].bitcast(BF16)[:, 0:128]
                    transpose_to(pv, ynb[:, c4 * 128:(c4 + 1) * 128], [b_ynb], [bp])
                    yt_, by_ = yT(0, c4)
                    ACT(yt_[:, tj * 128:(tj + 1) * 128], pv, AF.Copy, [bp, b_cw], [by_], scale=ngc[:, c4:c4 + 1])
        LA = 2
        for k in range(min(LA, len(stages))):
            stages[k][0]()
        for k in range(len(stages)):
            if k + LA < len(stages):
                stages[k + LA][0]()
            stages[k][1]()
        Sched.PHASE = _p0 + 'G'
        if not is_s:
            for s in range(nseq):
                for d in range(2):
                    for h in range(8):
                        g = h // 4
                        psF, bpF = rr.get()
                        for i in range(nt):
                            ti = s * nt + i
                            q = k_[0] % 2; k_[0] += 1
                            TS("dve", Bw[q][:], B_tok[:, ti, g * 64:(g + 1) * 64], edt[:, ti, d * 8 + h:d * 8 + h + 1], ALU.mult, [b_Btok, b_cum], [b_Bw[q]])
                            MM(psF[0:64, 0:64], xs_tok[:, ti, h * 64:(h + 1) * 64], Bw[q][:], i == 0, i == nt - 1, [b_xs, b_Bw[q]], [bpF])
                        CP("act", fin[0:64, d * 8 + h, :], psF[0:64, 0:64], [bpF], [b_fin])
                STO(o_ssd[s, l].rearrange("d h p n -> p (d h) n"), fin[0:64, :, :], b_fin)
        S.op("pool", lambda e: e.memset(prm[:, 0:1], 0.0), [], ([b_fin, b_yacc, b_ynb, b_Btok, b_xs, b_cum, b_dt, b_zs, b_cumP, b_xB, b_xC, b_raw, b_acc, b_ytmp, b_cw, b_s0T, b_stld, b_prm]
             + b_xrot + b_seg + b_Lt + b_sc + b_Bw) + [b_scr_all])

    def branch_s5(l, path, nseq, L, nt, is_s):
        barrier_begin()
        _p0 = Sched.PHASE
        cv = Carve()
        uT = cv.take([4, T], BF16); b_uT = NB("uT")
        y5T = uT; b_y5 = b_uT
        prow = cv.take([128]); b_prow = NB("prow")
        pc = cv.take([12, 32]); b_pc = NB("pc")
        pci = cv.take([32], I32); b_pci = NB("pci")
        Bst = cv.take([4, 4, 16]); b_Bst = NB("Bst")
        Cn = cv.take([4, 64]); b_Cn = NB("Cn")
        Bc = cv.take([2, 16]); b_Bc = NB("Bc"); Bt = cv.take([16]); b_Bt = NB("Bt")
        Bx = [cv.take([128], BF16), cv.take([128], BF16)]; b_Bx = [NB("Bx0"), NB("Bx1")]
        BcL = [cv.take([128], BF16), cv.take([128], BF16)]; b_BcL = [NB("BcL0"), NB("BcL1")]
        Cx = [cv.take([128], BF16), cv.take([128], BF16)]; b_Cx = [NB("Cx0"), NB("Cx1")]
        CL = cv.take([4, 128], BF16); b_CL = NB("CL")
        Lt_ = L
        cosT = cv.take([Lt_]); sinT = cv.take([Lt_]); b_tab = NB("tab")
        xr = [cv.take([T]), cv.take([T])]; b_xr = [NB("xr0"), NB("xr1")]
        prR = cv.take([2 * T])
        prb = prR.bitcast(BF16)
        pr = [prb[:, k * T:(k + 1) * T] for k in range(4)]; b_pr = [NB(f"pr{i}") for i in range(4)]
        argF = prR[:, 0:Lt_]; argI = prR[:, T:T + Lt_].bitcast(I32)
        tmpx = prR[:, 0:T]; rmt = prR[:, T:2 * T]
        bA = [b_pr[0], b_pr[1]]; bB = [b_pr[2], b_pr[3]]
        if not is_s:
            tmpy = cv.take([T]); bY = [NB("tmpy")]
        else:
            tmpy = tmpx; bY = bA
        d5 = cv.take([4]); bg = cv.take([8]); b_d5 = NB("d5")
        finS = cv.take([256]); b_finS = NB("finS")
        b_wcap = NB("wcap")
        if not is_s:
            wcap = cv.take([2, 32, 4]); tcap = cv.take([2, 32]); wtmp = cv.take([4, 32, 4])
        s0c = cv.take([64]); b_s0c = NB("s0c")
        sg = cv.take([512]); b_sg = NB("sg")
        fT = cv.take([128]); b_fT = NB("fT")
        iota = cv.take([Lt_]); b_iota = NB("iota")
        LD(iota[:], c_iota[:, 0:Lt_], b_iota)
        if not is_s:
            rmF = cv.take([T], BF16); rmB = cv.take([T], BF16); b_rmF = NB("rmF"); b_rmB = NB("rmB")
            LD(rmF[:], c_rmF[:, :], b_rmF); LD(rmB[:], c_rmB[:, :], b_rmB)
        else:
            rmF = rmB = None; b_rmF = b_rmB = b_iota
        S.op("pool", lambda e: e.memset(prow[:], 0.0), [b_scr_all], [b_prow, b_scr_all])
        LD(prow[0:32, :], lamre[l], b_prow); LD(prow[32:64, :], lamim[l], b_prow, group=True); LD(prow[64:96, :], lsx[l], b_prow, group=True)
        ps, bp = rr.get()
        transpose_to(ps[:, 0:96], prow[0:96, :], [b_prow], [bp], dt=F32, np_=96)
        CP("act", pc[:, 0:3, :].rearrange("p a b -> p (a b)"), ps[:, 0:96], [bp], [b_pc])
        P_ = lambda i: pc[:, i, :]
        R, W_ = [b_pc], [b_pc]
        ACT(P_(2), P_(2), AF.Exp, R, W_)
        TT("dve", P_(3), P_(0), P_(2), ALU.mult, R, W_)
        TT("dve", P_(4), P_(1), P_(2), ALU.mult, R, W_)
        ACT(P_(5), P_(3), AF.Exp, R, W_)
        TS("dve", pci[:], P_(4), 1.0 / (2 * math.pi), ALU.mult, R, [b_pci])
        CP("dve", P_(10), pci[:], [b_pci], W_)
        STT(P_(11), P_(10), -2 * math.pi, P_(4), ALU.mult, ALU.add, R, W_)
        TS("dve", P_(11), P_(11), 3.14159, ALU.min, R, W_, s2=-3.14159, op1=ALU.max)
        ACT(P_(7), P_(11), AF.Sin, R, W_)
        ACT(P_(10), P_(11), AF.Abs, R, W_)
        ACT(P_(6), P_(10), AF.Sin, R, W_, scale=-1.0, bias=math.pi / 2)
        TT("dve", P_(6), P_(6), P_(5), ALU.mult, R, W_)
        TT("dve", P_(7), P_(7), P_(5), ALU.mult, R, W_)
        TT("dve", P_(10), P_(0), P_(0), ALU.mult, R, W_)
        TT("dve", P_(11), P_(1), P_(1), ALU.mult, R, W_)
        TT("dve", P_(10), P_(10), P_(11), ALU.add, R, W_)
        S.op("dve", lambda e: e.reciprocal(out=P_(10), in_=P_(10)), R, W_)
        TS("dve", P_(11), P_(6), -1.0, ALU.add, R, W_)
        TT("dve", P_(8), P_(11), P_(0), ALU.mult, R, W_)
        TT("dve", P_(9), P_(7), P_(1), ALU.mult, R, W_)
        TT("dve", P_(8), P_(8), P_(9), ALU.add, R, W_)
        TT("dve", P_(8), P_(8), P_(10), ALU.mult, R, W_)
        TT("dve", P_(9), P_(7), P_(0), ALU.mult, R, W_)
        TT("dve", P_(11), P_(11), P_(1), ALU.mult, R, W_)
        TT("dve", P_(9), P_(9), P_(11), ALU.subtract, R, W_)
        TT("dve", P_(9), P_(9), P_(10), ALU.mult, R, W_)
        LD(d5[:], s5dc[l], b_d5); LD(bg[:], bgluc[l], b_d5, group=True)
        if is_s:
            LD(fT[0:64, :], st_s5[l], b_fT)
            ps, bp = rr.get()
            transpose_to(ps[:, 0:64], fT[0:64, :], [b_fT], [bp], dt=F32, np_=64)
            CP("act", s0c[:], ps[:, 0:64], [bp], [b_s0c])
        Sched.PHASE = _p0 + 'u'
        for blk in range(2):
            w, bw = wload(w_in[l][:, C_U + blk * 256:C_U + (blk + 1) * 256], 8, 256)
            proj_fm(hT, b_hT, 8, w, bw, 256,
                    lambda cc, h, ps, bp, blk=blk: CP("act", uT[:, blk * 2 + cc, h * 512:(h + 1) * 512], ps[:, :], [bp], [b_uT]))
        ybk = [(psum[4], psb[4]), (psum[5], psb[5])]
        xbk = [(psum[6], psb[6]), (psum[7], psb[7])]
        nrep = T // Lt_
        v3 = (lambda a: a.rearrange("p (s x) -> p s x", s=nrep)) if nrep > 1 else (lambda a: a)
        Bc2 = [Bc, cv.take([2, 16])]; b_Bc2 = [b_Bc, NB("Bc_1")]; Bt2 = [Bt, cv.take([16])]; b_Bt2 = [b_Bt, NB("Bt_1")]
        Bx2 = [Bx, [cv.take([128], BF16), cv.take([128], BF16)]]; b_Bx2 = [b_Bx, [NB("Bx0_1"), NB("Bx1_1")]]
        BcL2 = [BcL, [cv.take([128], BF16), cv.take([128], BF16)]]; b_BcL2 = [b_BcL, [NB("BcL0_1"), NB("BcL1_1")]]
        Cx2 = [Cx, [cv.take([128], BF16), cv.take([128], BF16)]]; b_Cx2 = [b_Cx, [NB("Cx0_1"), NB("Cx1_1")]]
        CL2 = [CL, cv.take([4, 128], BF16)]; b_CL2 = [b_CL, NB("CL_1")]
        cos2 = [cosT, cv.take([Lt_])]; sin2 = [sinT, cv.take([Lt_])]; b_tab2 = [b_tab, NB("tab_1")]
        its = [(fc, d, q4) for fc in range(4) for d in range(2) for q4 in range(4)]
        NI = len(its)

        def stB(k):
            fc, d, q4 = its[k]
            z = k % 2
            if d == 0 and q4 == 0:
                for dd in range(2):
                    LD(Bst[:, dd * 2 + 0, :, :], s5bre[l, dd][fc * 512:(fc + 1) * 512, :].rearrange("(c p) m -> p c m", p=128), b_Bst, group=(dd > 0))
                    LD(Bst[:, dd * 2 + 1, :, :], s5bim[l, dd][fc * 512:(fc + 1) * 512, :].rearrange("(c p) m -> p c m", p=128), b_Bst, group=True)
                    LD(Cn[:, dd * 2 + 0, :], s5cre[l, dd][fc * 128:(fc + 1) * 128, :], b_Cn, group=(dd > 0))
                    LD(Cn[:, dd * 2 + 1, :], s5cim[l, dd][fc * 128:(fc + 1) * 128, :], b_Cn, group=True)
            c = fc * 4 + q4
            dc = d * 16 + c
            cre, cim = pc[:, 8, dc:dc + 1], pc[:, 9, dc:dc + 1]
            Bre, Bim = Bst[:, d * 2 + 0, q4, :], Bst[:, d * 2 + 1, q4, :]
            Bc_, bBc_, Bt_, bBt_ = Bc2[z], b_Bc2[z], Bt2[z], b_Bt2[z]
            TS("dve", Bt_[:], Bim, cim, ALU.mult, [b_Bst, b_pc], [bBt_])
            STT(Bc_[:, 0, :], Bre, cre, Bt_[:], ALU.mult, ALU.subtract, [b_Bst, b_pc, bBt_], [bBc_])
            TS("dve", Bt_[:], Bre, cim, ALU.mult, [b_Bst, b_pc], [bBt_])
            STT(Bc_[:, 1, :], Bim, cre, Bt_[:], ALU.mult, ALU.add, [b_Bst, b_pc, bBt_], [bBc_])
            for ri in range(2):
                TT("pool", Bx2[z][ri].rearrange("p (g m) -> p g m", g=8), maskB[:, q4, :].rearrange("p (g m) -> p g m", g=8),
                   Bc_[:, ri, :].unsqueeze(1).broadcast_to([128, 8, 16]), ALU.mult, [b_maskB, bBc_], [b_Bx2[z][ri]])
                ps, bp = rr.get()
                pv = ps[:].bitcast(BF16)[:, 0:128]
                transpose_to(pv, Bx2[z][ri][:], [b_Bx2[z][ri]], [bp])
                CP("act", BcL2[z][ri][:], pv, [bp], [b_BcL2[z][ri]])
            for ri in range(2):
                TT("pool", Cx2[z][ri].rearrange("p (g n) -> p g n", g=2), maskC[:, q4, :].rearrange("p (g n) -> p g n", g=2),
                   Cn[:, d * 2 + ri, :].unsqueeze(1).broadcast_to([128, 2, 64]), ALU.mult, [b_maskC, b_Cn], [b_Cx2[z][ri]])
                ps, bp = rr.get()
                pv = ps[:].bitcast(BF16)[:, 0:128]
                transpose_to(pv, Cx2[z][ri][:], [b_Cx2[z][ri]], [bp])
                CP("act", CL2[z][:, 2 * ri, :], pv, [bp], [b_CL2[z]])
                ACT(CL2[z][:, 2 * ri + 1, :], pv, AF.Copy, [bp], [b_CL2[z]], scale=-1.0)

        def stT1(k):
            fc, d, q4 = its[k]
            z = k % 2
            dc = d * 16 + fc * 4 + q4
            cT, sT, bt = cos2[z], sin2[z], [b_tab2[z]]
            ACT(cT[:], iota[:, 0:Lt_], AF.Copy, [b_iota, b_pc], bt, scale=pc[:, 4, dc:dc + 1])
            TS("dve", sT[:].bitcast(I32), cT[:], 1.0 / (2 * math.pi), ALU.mult, bt, bt)
            CP("act", sT[:], sT[:].bitcast(I32), bt, bt)

        def stT2(k):
            z = k % 2
            cT, sT, bt = cos2[z], sin2[z], [b_tab2[z]]
            STT(cT[:], sT[:], -2 * math.pi, cT[:], ALU.mult, ALU.add, bt, bt)
            TS("dve", cT[:], cT[:], 3.14159, ALU.min, bt, bt, s2=-3.14159, op1=ALU.max)
            ACT(sT[:], cT[:], AF.Sin, bt, bt)
            ACT(cT[:], cT[:], AF.Abs, bt, bt)
            ACT(cT[:], cT[:], AF.Sin, bt, bt, scale=-1.0, bias=math.pi / 2)

        def stX(k):
            fc, d, q4 = its[k]
            z = k % 2
            c = fc * 4 + q4
            dc = d * 16 + c
            cosT_, sinT_, bt = cos2[z], sin2[z], b_tab2[z]
            for h in range(2):
                hs = slice(h * 512, (h + 1) * 512)
                for ri in range(2):
                    MM(xbk[ri][0][:, :], BcL2[z][ri][:], uT[:, fc, hs], True, True, [b_BcL2[z][ri], b_uT], [xbk[ri][1]])
                xre, xim = xbk[0][0][:, :], xbk[1][0][:, :]
                if Lt_ < 512:
                    nr2 = 512 // Lt_
                    cB = cosT_.unsqueeze(1).broadcast_to([128, nr2, Lt_]); sB = sinT_.unsqueeze(1).broadcast_to([128, nr2, Lt_])
                    vv = lambda a, nr2=nr2: a.rearrange("p (s x) -> p s x", s=nr2)
                else:
                    cB, sB = cosT_[:, hs], sinT_[:, hs]
                    vv = lambda a: a
                TT("dve", vv(xr[1][:, hs]), vv(xim), cB, ALU.mult, [xbk[1][1], bt], [b_xr[1]])
                TT("dve", vv(tmpy[:, hs]), vv(xre), sB, ALU.mult, [xbk[0][1], bt], bY)
                TT("pool", xr[1][:, hs], xr[1][:, hs], tmpy[:, hs], ALU.subtract if d == 0 else ALU.add, [b_xr[1]] + bY, [b_xr[1]])
                TT("dve", vv(xr[0][:, hs]), vv(xre), cB, ALU.mult, [xbk[0][1], bt], [b_xr[0]])
                TT("dve", vv(tmpx[:, hs]), vv(xim), sB, ALU.mult, [xbk[1][1], bt], bA)
                TT("pool", xr[0][:, hs], xr[0][:, hs], tmpx[:, hs], ALU.add if d == 0 else ALU.subtract, [b_xr[0]] + bA, [b_xr[0]])
            if is_s:
                sre = s0c[:, (d * 2 + 0) * 16 + c:(d * 2 + 0) * 16 + c + 1]; sim = s0c[:, (d * 2 + 1) * 16 + c:(d * 2 + 1) * 16 + c + 1]
                abre, abim = pc[:, 6, dc:dc + 1], pc[:, 7, dc:dc + 1]
                sm = small
                RS, WS_ = [b_small, b_s0c, b_pc, bt], [b_small]
                TT("dve", sm[:, 20:21], sre, abre, ALU.mult, RS, WS_); TT("dve", sm[:, 21:22], sim, abim, ALU.mult, RS, WS_)
                TT("dve", sm[:, 22:23], sm[:, 20:21], sm[:, 21:22], ALU.subtract, RS, WS_)
                TT("dve", sm[:, 20:21], sre, abim, ALU.mult, RS, WS_); TT("dve", sm[:, 21:22], sim, abre, ALU.mult, RS, WS_)
                TT("dve", sm[:, 23:24], sm[:, 20:21], sm[:, 21:22], ALU.add, RS, WS_)
                if d == 0:
                    TT("dve", xr[0][:, 0:1], xr[0][:, 0:1], sm[:, 22:23], ALU.add, [b_xr[0], b_small], [b_xr[0]])
                    TT("dve", xr[1][:, 0:1], xr[1][:, 0:1], sm[:, 23:24], ALU.add, [b_xr[1], b_small], [b_xr[1]])
                else:
                    cl, sl = cosT_[:, L - 1:L], sinT_[:, L - 1:L]
                    TT("dve", sm[:, 20:21], sm[:, 22:23], cl, ALU.mult, RS, WS_); TT("dve", sm[:, 21:22], sm[:, 23:24], sl, ALU.mult, RS, WS_)
                    TT("dve", sm[:, 24:25], sm[:, 20:21], sm[:, 21:22], ALU.subtract, RS, WS_)
                    TT("dve", sm[:, 20:21], sm[:, 22:23], sl, ALU.mult, RS, WS_); TT("dve", sm[:, 21:22], sm[:, 23:24], cl, ALU.mult, RS, WS_)
                    TT("dve", sm[:, 25:26], sm[:, 20:21], sm[:, 21:22], ALU.add, RS, WS_)
                    TT("dve", xr[0][:, L - 1:L], xr[0][:, L - 1:L], sm[:, 24:25], ALU.add, [b_xr[0], b_small], [b_xr[0]])
                    TT("dve", xr[1][:, L - 1:L], xr[1][:, L - 1:L], sm[:, 25:26], ALU.add, [b_xr[1], b_small], [b_xr[1]])

        def stS(k):
            fc, d, q4 = its[k]
            z = k % 2
            dc = d * 16 + fc * 4 + q4
            cosT_, sinT_, bt = cos2[z], sin2[z], b_tab2[z]
            if is_s:
                rm_ = pc[:, 5, dc:dc + 1].broadcast_to([128, T]); rm_r = rm_; brm = [b_pc]
            else:
                TS("dve", rmt, (rmF if d == 0 else rmB)[:], pc[:, 5, dc:dc + 1], ALU.mult, [b_rmF, b_rmB, b_pc], bB)
                rm_ = rmt; rm_r = rmt[:, ::-1]; brm = bB
            for ri in (1, 0):
                if d == 0:
                    S.op("dve", lambda e, ri=ri, rm_=rm_: e.tensor_tensor_scan(out=xr[ri][:], data0=rm_, data1=xr[ri][:], initial=0.0, op0=ALU.mult, op1=ALU.add),
                         brm + [b_xr[ri]], [b_xr[ri]])
                else:
                    S.op("dve", lambda e, ri=ri, rm_r=rm_r: e.tensor_tensor_scan(out=xr[ri][:, ::-1], data0=rm_r, data1=xr[ri][:, ::-1], initial=0.0, op0=ALU.mult, op1=ALU.add),
                         brm + [b_xr[ri]], [b_xr[ri]])
            if not is_s:
                lpos = L - 1 if d == 0 else 0
                for ri in range(2):
                    w3 = xr[ri].rearrange("p (s x) -> p s x", s=nseq)[:, :, lpos:lpos + 1].rearrange("p s x -> p (s x)")
                    CP("act", wcap[:, ri, dc, :], w3, [b_xr[ri]], [b_wcap])
                CP("act", tcap[:, 0, dc:dc + 1], cosT_[:, lpos:lpos + 1], [bt], [b_wcap])
                CP("act", tcap[:, 1, dc:dc + 1], sinT_[:, lpos:lpos + 1], [bt], [b_wcap])

        def stP(k):
            fc, d, q4 = its[k]
            z = k % 2
            cosT_, sinT_, bt = cos2[z], sin2[z], b_tab2[z]
            cosB = cosT_.unsqueeze(1).broadcast_to([128, nrep, Lt_]) if nrep > 1 else cosT_
            sinB = sinT_.unsqueeze(1).broadcast_to([128, nrep, Lt_]) if nrep > 1 else sinT_
            TT("dve", v3(pr[1]), v3(xr[1][:]), sinB, ALU.mult, [b_xr[1], bt], [b_pr[1]])
            TT("pool", v3(pr[0]), v3(xr[0][:]), cosB, ALU.mult, [b_xr[0], bt], [b_pr[0]])
            TT("dve", v3(pr[3]), v3(xr[1][:]), cosB, ALU.mult, [b_xr[1], bt], [b_pr[3]])
            TT("pool", v3(pr[2]), v3(xr[0][:]), sinB, ALU.mult, [b_xr[0], bt], [b_pr[2]])
            sel = [0, 1, 3, 3] if d == 0 else [0, 0, 2, 3]
            first_y = (d == 0 and q4 == 0)
            last_y = (d == 1 and q4 == 3)
            for h in range(2):
                hs = slice(h * 512, (h + 1) * 512)
                for k4 in range(4):
                    MM(ybk[h][0][:, :], CL2[z][:, sel[k4], :], pr[k4][:, hs], first_y and k4 == 0, last_y and k4 == 3, [b_CL2[z], b_pr[k4]], [ybk[h][1]])
            if last_y:
                for h in range(2):
                    hs = slice(h * 512, (h + 1) * 512)
                    STT(uT[:, fc, hs], uT[:, fc, hs], d5[:, fc:fc + 1], ybk[h][0][:, :], ALU.mult, ALU.add, [b_uT, b_d5, ybk[h][1]], [b_uT])

        Sched.PHASE = _p0 + 'L'
        stB(0); stT1(0); stT2(0)
        for k in range(NI):
            if k + 1 < NI:
                stB(k + 1); stT1(k + 1)
            stX(k)
            if k + 1 < NI:
                stT2(k + 1)
            stS(k)
            stP(k)
        if not is_s:
            cB_ = tcap[:, 0, :].unsqueeze(2).broadcast_to([128, 32, 4]); sB_ = tcap[:, 1, :].unsqueeze(2).broadcast_to([128, 32, 4])
            RW = [b_wcap]
            TT("dve", wtmp[:, 0, :, :], wcap[:, 0, :, :], cB_, ALU.mult, RW, RW)
            TT("dve", wtmp[:, 1, :, :], wcap[:, 1, :, :], sB_, ALU.mult, RW, RW)
            TT("dve", wtmp[:, 2, :, :], wcap[:, 0, :, :], sB_, ALU.mult, RW, RW)
            TT("dve", wtmp[:, 3, :, :], wcap[:, 1, :, :], cB_, ALU.mult, RW, RW)
            f4 = finS.rearrange("p (s d r c) -> p s d r c", s=4, d=2, r=2)
            for d in range(2):
                src = lambda k, d=d: wtmp[:, k, d * 16:(d + 1) * 16, :].rearrange("p c s -> p s c")
                TT("dve", f4[:, :, d, 0, :], src(0), src(1), ALU.subtract if d == 0 else ALU.add, RW, [b_finS])
                TT("dve", f4[:, :, d, 1, :], src(3), src(2), ALU.add if d == 0 else ALU.subtract, RW, [b_finS])
            for half in range(2):
                ps, bp = rr.get()
                transpose_to(ps[:, 0:128], finS[:, half * 128:(half + 1) * 128], [b_finS], [bp], dt=F32)
                CP("act", fT[:], ps[:, 0:128], [bp], [b_fT])
                for s2 in range(2):
                    STO(o_s5[half * 2 + s2, l], fT[s2 * 64:(s2 + 1) * 64, :], b_fT)
        Sched.PHASE = _p0 + 'G'
        for blk in range(2):
            w, bw = wload(w_glu[l][:, blk * 256:(blk + 1) * 256], 4, 256)
            w2, bw2 = wload(w_glu[l][:, (blk + 2) * 256:(blk + 3) * 256], 4, 256)
            for c2 in range(2):
                cc = blk * 2 + c2
                for h in range(2):
                    hs = slice(h * 512, (h + 1) * 512)
                    psg, bpg = rr.get()
                    for k in range(4):
                        MM(psg[:, :], w2[:, k, c2 * 128:(c2 + 1) * 128], y5T[:, k, hs], k == 0, k == 3, [b_y5, bw2], [bpg])
                    ACT(sg[:], psg[:, :], AF.Sigmoid, [bpg, b_d5], [b_sg], bias=bg[:, 4 + cc:5 + cc])
                    psv, bpv = rr.get()
                    for k in range(4):
                        MM(psv[:, :], w[:, k, c2 * 128:(c2 + 1) * 128], y5T[:, k, hs], k == 0, k == 3, [b_y5, bw], [bpv])
                    yt_, by_ = yT(1, cc)
                    STT(yt_[:, hs], psv[:, :], bg[:, cc:cc + 1], sg[:], ALU.add, ALU.mult, [bpv, b_d5, b_sg], [by_])
        S.op("pool", lambda e: e.memset(prow[:, 0:1], 0.0), [], ([b_uT, b_y5, b_prow, b_pc, b_pci, b_Bst, b_Cn, b_Bc, b_Bt, b_CL, b_tab, b_d5, b_finS, b_s0c, b_sg, b_fT]
             + b_Bx + b_BcL + b_Cx + b_xr + b_pr + (bY if not is_s else []) + [b_iota, b_rmF, b_rmB, b_wcap]
             + [b_Bc2[1], b_Bt2[1], b_CL2[1], b_tab2[1]] + b_Bx2[1] + b_BcL2[1] + b_Cx2[1]) + [b_scr_all])

    def rms_groups(ps, bp, ncols, gain_bc, b_gain, qf, b_qf, sq, b_sq, rs, b_rs, t, rope):
        ng = ncols // 64
        CP("act", qf[:, 0:ncols], ps[:, 0:ncols], [bp], [b_qf])
        TT("dve", sq[:, 0:ncols], qf[:, 0:ncols], qf[:, 0:ncols], ALU.mult, [b_qf], [b_sq])
        S.op("dve", lambda e: e.tensor_reduce(out=rs[:, 0:ng], in_=sq[:, 0:ncols].rearrange("p (g x) -> p g x", g=ng), op=ALU.add, axis=AX.X), [b_sq], [b_rs])
        rstd_from_ss(rs[:, 0:ng], 64, rs[:, 0:ng], [b_rs], [b_rs])
        q3 = qf[:, 0:ncols].rearrange("p (g x) -> p g x", g=ng)
        TT("dve", q3, q3, rs[:, 0:ng].unsqueeze(2).broadcast_to([128, ng, 64]), ALU.mult, [b_qf, b_rs], [b_qf])
        TT("dve", q3, q3, gain_bc.unsqueeze(1).broadcast_to([128, ng, 64]), ALU.mult, [b_qf, b_gain], [b_qf])
        if rope:
            s3 = sq[:, 0:ncols].rearrange("p (g a q f) -> p (g a) q f", g=ng, a=2, q=2)
            x4 = qf[:, 0:ncols].rearrange("p (g a q f) -> p (g a) q f", g=ng, a=2, q=2)
            S4 = ropeS[:, t, :].rearrange("p (a q f) -> p a q f", a=2, q=2)
            for pz in range(2):
                TT("dve", s3[:, :, pz, :].rearrange("p (g a) f -> p g a f", g=ng), x4[:, :, 1 - pz, :].rearrange("p (g a) f -> p g a f", g=ng),
                   S4[:, :, pz, :].unsqueeze(1).broadcast_to([128, ng, 2, 16]), ALU.mult, [b_qf, b_ropeS], [b_sq])
            TT("dve", q3, q3, ropeC[:, t, :].unsqueeze(1).broadcast_to([128, ng, 64]), ALU.mult, [b_qf, b_ropeC], [b_qf])
            TT("dve", qf[:, 0:ncols], qf[:, 0:ncols], sq[:, 0:ncols], ALU.add, [b_qf, b_sq], [b_qf])

    def branch_diff(l, path, nseq, L, nt, is_s):
        barrier_begin()
        _p0 = Sched.PHASE
        cv = Carve()
        nk_ctx = 2 if is_s else 0
        NKT = 8 + nk_ctx
        qT = cv.take([4, T], BF16); b_qT = NB("qT")
        kT = cv.take([4, NKT * 128], BF16); b_kT = NB("kT")
        vaug = cv.take([NKT, 4, 130], BF16); b_va = NB("vaug")
        qfL = [cv.take([512]) for _ in range(2)]; b_qfL = [NB(f"qf{i}") for i in range(2)]
        sqL = [cv.take([512]) for _ in range(2)]; b_sqL = [NB(f"sq{i}") for i in range(2)]
        rsL = [cv.take([16]) for _ in range(2)]; b_rsL = [NB(f"rs{i}") for i in range(2)]
        rot_ = [0]

        def nxt():
            i = rot_[0] % 2; rot_[0] += 1
            return qfL[i], b_qfL[i], sqL[i], b_sqL[i], rsL[i], b_rsL[i]
        qb = [cv.take([512], BF16), cv.take([512], BF16)]; b_qb = [NB("qb0"), NB("qb1")]
        gq = cv.take([64]); gk = cv.take([64]); b_g = NB("dg")
        lamt = cv.take([4, 64]); lamc = cv.take([8]); b_lam = NB("lam")
        o1 = cv.take([4, 128]); b_o1 = NB("o1")
        odn = cv.take([8, 512], BF16); b_odn = NB("odn")
        pT = [cv.take([512], BF16) for _ in range(4)]; b_pT = [NB(f"pT{i}") for i in range(4)]
        rd = cv.take([8]); b_rd = NB("rd")
        oh = cv.take([128]); b_oh = NB("oh")
        gsub = cv.take([1]); b_gsub = NB("gsub")
        kstL = [cv.take([512]) for _ in range(2)]; b_kstL = [NB(f"kst{i}") for i in range(2)]
        kst, b_kst = kstL[0], b_kstL[0]
        lam_init = 0.8 - 0.6 * math.exp(-0.3 * l)
        S.op("pool", lambda e: e.memset(gq[:], 0.0), [b_scr_all], [b_g, b_scr_all])
        LD(gq[:], dqg[l:l + 1, :].partition_broadcast(128), b_g); LD(gk[:], dkg[l:l + 1, :].partition_broadcast(128), b_g, group=True)
        TS("dve", gq[:], gq[:], 0.125, ALU.mult, [b_g], [b_g])
        LD(lamt[:].rearrange("p a b -> p (a b)"), dlam[l:l + 1, :].partition_broadcast(128), b_lam)
        LD(gsub[:], dsubc[l], b_gsub)
        TS("dve", gsub[:], gsub[:], 1.0 - lam_init, ALU.mult, [b_gsub], [b_gsub])
        TT("dve", lamt[:, 0, :], lamt[:, 0, :], lamt[:, 1, :], ALU.mult, [b_lam], [b_lam])
        TT("dve", lamt[:, 2, :], lamt[:, 2, :], lamt[:, 3, :], ALU.mult, [b_lam], [b_lam])
        S.op("dve", lambda e: e.tensor_reduce(out=lamc[:, 0:1], in_=lamt[:, 0, :], op=ALU.add, axis=AX.X), [b_lam], [b_lam])
        S.op("dve", lambda e: e.tensor_reduce(out=lamc[:, 1:2], in_=lamt[:, 2, :], op=ALU.add, axis=AX.X), [b_lam], [b_lam])
        ACT(lamc[:, 0:2], lamc[:, 0:2], AF.Exp, [b_lam], [b_lam])
        TT("dve", lamc[:, 2:3], lamc[:, 0:1], lamc[:, 1:2], ALU.subtract, [b_lam], [b_lam])
        TS("dve", lamc[:, 2:3], lamc[:, 2:3], lam_init, ALU.add, [b_lam], [b_lam], s2=-1.0, op1=ALU.mult)
        MSET("pool", vaug[:].rearrange("p a b c -> p (a b c)"), 1.0, [b_va])
        if sub == 1:
            raise _Stop()
        Sched.PHASE = _p0 + 'A'
        stagesA = []
        for which in range(2):
            col0 = C_DQ if which == 0 else C_DK
            wd = {}
            for t in range(8):
                st = {}

                def FA(st=st, which=which, t=t, wd=wd, col0=col0):
                    if t == 0:
                        wd["A"] = wload(w_in[l][:, col0:col0 + 256], 8, 256)
                        wd["B"] = wload(w_in[l][:, col0 + 256:col0 + 512], 8, 256)
                    wA, bwA = wd["A"]; wB, bwB = wd["B"]
                    ps, bp = rr.get()
                    for k in range(8):
                        MM(ps[:, 0:256], hT[:, k, t * 128:(t + 1) * 128], wA[:, k, :], k == 0, k == 7, [b_hT, bwA], [bp])
                    for k in range(8):
                        MM(ps[:, 256:512], hT[:, k, t * 128:(t + 1) * 128], wB[:, k, :], k == 0, k == 7, [b_hT, bwB], [bp])
                    qf, b_qf, sq, b_sq, rs, b_rs = nxt()
                    kst, b_kst = kstL[t % 2], b_kstL[t % 2]
                    if which == 1 and not is_s:
                        rms_groups(ps, bp, 512, gk[:], b_g, kst, b_kst, sq, b_sq, rs, b_rs, t, False)
                        STO(o_dk[t // 2, l, (t % 2) * 128:(t % 2 + 1) * 128, :], kst[:], b_kst)
                        st["src"] = (kst, b_kst)
                    else:
                        rms_groups(ps, bp, 512, (gq if which == 0 else gk)[:], b_g, qf, b_qf, sq, b_sq, rs, b_rs, t, is_s)
                        st["src"] = (qf, b_qf)

                def FB(st=st, which=which, t=t):
                    src, bsrc = st["src"]
                    qb_, bqb_ = qb[t % 2], b_qb[t % 2]
                    CP("pool", qb_[:], src[:], [bsrc], [bqb_])
                    for j4 in range(4):
                        ps2, bp2 = rr.get()
                        pv = ps2[:].bitcast(BF16)[:, 0:128]
                        transpose_to(pv, qb_[:, j4 * 128:(j4 + 1) * 128], [bqb_], [bp2])
                        if which == 0:
                            CP("act", qT[:, j4, t * 128:(t + 1) * 128], pv, [bp2], [b_qT])
                        else:
                            CP("act", kT[:, j4, (nk_ctx + t) * 128:(nk_ctx + t + 1) * 128], pv, [bp2], [b_kT])
                stagesA.append((FA, FB))
        stagesA[0][0]()
        for k in range(len(stagesA)):
            if k + 1 < len(stagesA):
                stagesA[k + 1][0]()
            stagesA[k][1]()
        if is_s:
            for kt in range(2):
                LD(kst[:], cdk[l, kt * 128:(kt + 1) * 128, :], b_kst)
                CP("pool", qb[0][:], kst[:], [b_kst], [b_qb[0]])
                for j4 in range(4):
                    ps2, bp2 = rr.get()
                    pv = ps2[:].bitcast(BF16)[:, 0:128]
                    transpose_to(pv, qb[0][:, j4 * 128:(j4 + 1) * 128], [b_qb[0]], [bp2])
                    CP("act", kT[:, j4, kt * 128:(kt + 1) * 128], pv, [bp2], [b_kT])
                LD(kst[:], cdv[l, kt * 128:(kt + 1) * 128, :], b_kst)
                CP("pool", vaug[:, kt, :, 0:128], kst[:].rearrange("p (h e) -> p h e", h=4), [b_kst], [b_va])
        Sched.PHASE = _p0 + 'B'
        wA, bwA = wload(w_in[l][:, C_DV:C_DV + 256], 8, 256)
        wB, bwB = wload(w_in[l][:, C_DV + 256:C_DV + 512], 8, 256)
        for t in range(8):
            ps, bp = rr.get()
            for k in range(8):
                MM(ps[:, 0:256], hT[:, k, t * 128:(t + 1) * 128], wA[:, k, :], k == 0, k == 7, [b_hT, bwA], [bp])
            for k in range(8):
                MM(ps[:, 256:512], hT[:, k, t * 128:(t + 1) * 128], wB[:, k, :], k == 0, k == 7, [b_hT, bwB], [bp])
            if sub != 31:
                kst, b_kst = kstL[t % 2], b_kstL[t % 2]
            CP("act", vaug[:, nk_ctx + t, :, 0:128], ps[:, :].rearrange("p (h e) -> p h e", h=4), [bp], [b_va])
            if not is_s and sub != 32:
                CP("dve", kst[:], ps[:, :], [bp], [b_kst])
                STO(o_dv[t // 2, l, (t % 2) * 128:(t % 2 + 1) * 128, :], kst[:], b_kst)
        if sub in (3, 31, 32):
            raise _Stop()
        Sched.PHASE = _p0 + 'C'
        obk = [(psum[4 + i], psb[4 + i]) for i in range(4)]
        pc_ = [0]
        stages = []
        for s in range(nseq):
            keyt = list(range(nk_ctx)) + [nk_ctx + s * nt + i for i in range(nt)]
            nq = min(L, 512)
            for qc in range(L // nq):
                q0 = s * L + qc * nq
                nqt = nq // 128
                for h in range(4):
                    for c in range(2):
                        ksl = slice(c * 64, (c + 1) * 64)
                        for ki, kt in enumerate(keyt):
                            st = {}

                            def A(st=st, ksl=ksl, h=h, kt=kt, q0=q0, nq=nq):
                                psS, bpS = rr.get()
                                MM(psS[:, 0:nq], kT[ksl, h, kt * 128:(kt + 1) * 128], qT[ksl, h, q0:q0 + nq], True, True, [b_kT, b_qT], [bpS])
                                z = pc_[0] % 4; pc_[0] += 1
                                st["z"] = z
                                ACT(pT[z][:, 0:nq], psS[:, 0:nq], AF.Exp, [bpS], [b_pT[z]])

                            def B(st=st, h=h, c=c, kt=kt, ki=ki, nk=len(keyt), nqt=nqt, q0=q0):
                                z = st["z"]
                                for qt in range(nqt):
                                    MM(obk[qt][0][:, 0:129], pT[z][:, qt * 128:(qt + 1) * 128], vaug[:, kt, h, 0:129], ki == 0, ki == nk - 1, [b_pT[z], b_va], [obk[qt][1]])
                                if ki != nk - 1:
                                    return
                                for qt in range(nqt):
                                    tq = q0 // 128 + qt
                                    ob, bob = obk[qt]
                                    S.op("dve", lambda e, ob=ob, qt=qt, c=c: e.reciprocal(out=rd[:, qt * 2 + c:qt * 2 + c + 1], in_=ob[:, 128:129]), [bob], [b_rd])
                                    if c == 0:
                                        TS("dve", o1[:, qt, :], ob[:, 0:128], rd[:, qt * 2:qt * 2 + 1], ALU.mult, [bob, b_rd], [b_o1])
                                    else:
                                        TT("dve", rd[:, qt * 2 + 1:qt * 2 + 2], rd[:, qt * 2 + 1:qt * 2 + 2], lamc[:, 2:3], ALU.mult, [b_rd, b_lam], [b_rd])
                                        STT(oh[:], ob[:, 0:128], rd[:, qt * 2 + 1:qt * 2 + 2], o1[:, qt, :], ALU.mult, ALU.add, [bob, b_rd, b_o1], [b_oh])
                                        ACT(junk[:, 0:128], oh[:], AF.Square, [b_oh], [b_junk, b_small], accum=small[:, 40:41])
                                        rstd_from_ss(small[:, 40:41], 128, small[:, 41:42], [b_small], [b_small])
                                        TS("dve", odn[:, tq, h * 128:(h + 1) * 128], oh[:], small[:, 41:42], ALU.mult, [b_oh, b_small], [b_odn])
                            stages.append((A, B))
        LA = 3
        for k in range(min(LA, len(stages))):
            stages[k][0]()
        for k in range(len(stages)):
            if k + LA < len(stages):
                stages[k + LA][0]()
            stages[k][1]()
        if sub == 4:
            raise _Stop()
        Sched.PHASE = _p0 + 'D'
        for t in range(8):
            for h in range(4):
                ps2, bp2 = rr.get()
                pv = ps2[:].bitcast(BF16)[:, 0:128]
                transpose_to(pv, odn[:, t, h * 128:(h + 1) * 128], [b_odn], [bp2])
                yt_, by_ = yT(2, h)
                ACT(yt_[:, t * 128:(t + 1) * 128], pv, AF.Copy, [bp2, b_gsub], [by_], scale=gsub[:, 0:1])
        S.op("pool", lambda e: e.memset(gq[:, 0:1], 0.0), [], ([b_qT, b_kT, b_va, b_g, b_lam, b_o1, b_odn, b_rd, b_oh, b_gsub] + b_kstL + b_qb + b_pT + b_qfL + b_sqL + b_rsL) + [b_scr_all])

    def branch_win(l, path, nseq, L, nt, is_s):
        barrier_begin()
        cv = Carve()
        nk_ctx = 2 if is_s else 0
        NKT = 8 + nk_ctx
        qT = cv.take([4, T], BF16); b_qT = NB("wqT")
        kT = cv.take([2, NKT * 128], BF16); b_kT = NB("wkT")
        vaug = cv.take([NKT, 2, 66], BF16); b_va = NB("wvaug")
        qfL = [cv.take([512]) for _ in range(2)]; b_qfL = [NB(f"wqf{i}") for i in range(2)]
        sqL = [cv.take([512]) for _ in range(2)]; b_sqL = [NB(f"wsq{i}") for i in range(2)]
        rsL = [cv.take([16]) for _ in range(2)]; b_rsL = [NB(f"wrs{i}") for i in range(2)]
        rot_ = [0]

        def nxt():
            i = rot_[0] % 2; rot_[0] += 1
            return qfL[i], b_qfL[i], sqL[i], b_sqL[i], rsL[i], b_rsL[i]
        qb = [cv.take([512], BF16), cv.take([512], BF16)]; b_qb = [NB("wqb0"), NB("wqb1")]
        gq = cv.take([64]); gk = cv.take([64]); b_g = NB("wg")
        snk = cv.take([8]); b_snk = NB("snk")
        on = cv.take([8, 512], BF16); b_on = NB("won")
        pT = [cv.take([512], BF16) for _ in range(4)]; b_pT = [NB(f"wpT{i}") for i in range(4)]
        rd = cv.take([8]); b_rd = NB("wrd")
        kst = cv.take([256]); b_kst = NB("wkst")
        S.op("pool", lambda e: e.memset(gq[:], 0.0), [b_scr_all], [b_g, b_scr_all])
        LD(gq[:], wqg[l:l + 1, :].partition_broadcast(128), b_g); LD(gk[:], wkg[l:l + 1, :].partition_broadcast(128), b_g, group=True)
        TS("dve", gq[:], gq[:], 0.125, ALU.mult, [b_g], [b_g])
        LD(snk[:], wsink[l:l + 1, :].partition_broadcast(128), b_snk)
        ACT(snk[:], snk[:], AF.Exp, [b_snk], [b_snk])
        MSET("pool", vaug[:].rearrange("p a b c -> p (a b c)"), 1.0, [b_va])
        wA, bwA = wload(w_in[l][:, C_WQ:C_WQ + 256], 8, 256)
        wB, bwB = wload(w_in[l][:, C_WQ + 256:C_WQ + 512], 8, 256)
        stq = []
        for t in range(8):
            st = {}

            def QA(st=st, t=t):
                ps, bp = rr.get()
                for k in range(8):
                    MM(ps[:, 0:256], hT[:, k, t * 128:(t + 1) * 128], wA[:, k, :], k == 0, k == 7, [b_hT, bwA], [bp])
                for k in range(8):
                    MM(ps[:, 256:512], hT[:, k, t * 128:(t + 1) * 128], wB[:, k, :], k == 0, k == 7, [b_hT, bwB], [bp])
                qf, b_qf, sq, b_sq, rs, b_rs = nxt()
                rms_groups(ps, bp, 512, gq[:], b_g, qf, b_qf, sq, b_sq, rs, b_rs, t, is_s)
                st["q"] = (qf, b_qf)

            def QB(st=st, t=t):
                qf, b_qf = st["q"]
                qb_, bqb_ = qb[t % 2], b_qb[t % 2]
                CP("pool", qb_[:], qf[:], [b_qf], [bqb_])
                for j4 in range(4):
                    ps2, bp2 = rr.get()
                    pv = ps2[:].bitcast(BF16)[:, 0:128]
                    transpose_to(pv, qb_[:, j4 * 128:(j4 + 1) * 128], [bqb_], [bp2])
                    CP("act", qT[:, j4, t * 128:(t + 1) * 128], pv, [bp2], [b_qT])
            stq.append((QA, QB))
        stq[0][0]()
        for k in range(8):
            if k + 1 < 8:
                stq[k + 1][0]()
            stq[k][1]()
        wK, bwK = wload(w_in[l][:, C_WK:C_WK + 256], 8, 256)

        def put_k(src_f32, bsrc, ktile):
            for n in range(2):
                CP("pool", qb[0][:, 0:64], src_f32[:, n * 64:(n + 1) * 64], [bsrc], [b_qb[0]])
                CP("pool", qb[0][:, 64:128], src_f32[:, n * 64:(n + 1) * 64], [bsrc], [b_qb[0]])
                ps2, bp2 = rr.get()
                pv = ps2[:].bitcast(BF16)[:, 0:128]
                transpose_to(pv, qb[0][:, 0:128], [b_qb[0]], [bp2])
                CP("act", kT[:, n, ktile * 128:(ktile + 1) * 128], pv, [bp2], [b_kT])
        for t in range(8):
            ps, bp = rr.get()
            for k in range(8):
                MM(ps[:, 0:256], hT[:, k, t * 128:(t + 1) * 128], wK[:, k, :], k == 0, k == 7, [b_hT, bwK], [bp])
            CP("act", vaug[:, nk_ctx + t, :, 0:64], ps[:, 128:256].rearrange("p (n e) -> p n e", n=2), [bp], [b_va])
            qf, b_qf, sq, b_sq, rs, b_rs = nxt()
            if not is_s:
                CP("dve", kst[:, 128:256], ps[:, 128:256], [bp], [b_kst])
                STO(o_wv[t // 2, l, (t % 2) * 128:(t % 2 + 1) * 128, :], kst[:, 128:256], b_kst)
                rms_groups(ps, bp, 128, gk[:], b_g, kst, b_kst, sq, b_sq, rs, b_rs, t, False)
                STO(o_wk[t // 2, l, (t % 2) * 128:(t % 2 + 1) * 128, :], kst[:, 0:128], b_kst)
                put_k(kst, b_kst, nk_ctx + t)
            else:
                rms_groups(ps, bp, 128, gk[:], b_g, qf, b_qf, sq, b_sq, rs, b_rs, t, True)
                put_k(qf, b_qf, nk_ctx + t)
        if is_s:
            for kt in range(2):
                LD(kst[:, 0:128], cwk[l, kt * 128:(kt + 1) * 128, :], b_kst)
                put_k(kst, b_kst, kt)
                LD(kst[:, 128:256], cwv[l, kt * 128:(kt + 1) * 128, :], b_kst)
                CP("pool", vaug[:, kt, :, 0:64], kst[:, 128:256].rearrange("p (n e) -> p n e", n=2), [b_kst], [b_va])
        obk = [(psum[4 + i], psb[4 + i]) for i in range(4)]
        pc_ = [0]
        stages = []

        def evac(ob, bob, qt, tq, h):
            TT("dve", rd[:, qt:qt + 1], ob[:, 64:65], snk[:, h:h + 1], ALU.add, [bob, b_snk], [b_rd])
            S.op("dve", lambda e, qt=qt: e.reciprocal(out=rd[:, qt:qt + 1], in_=rd[:, qt:qt + 1]), [b_rd], [b_rd])
            TS("dve", on[:, tq, h * 64:(h + 1) * 64], ob[:, 0:64], rd[:, qt:qt + 1], ALU.mult, [bob, b_rd], [b_on])
        for h in range(8):
            n = h // 4
            j4 = h // 2
            bsl = slice((h % 2) * 64, (h % 2 + 1) * 64)
            if not is_s:
                for s in range(nseq):
                    q0 = s * L
                    keyt = [s * nt + i for i in range(nt)]
                    for ki, kt in enumerate(keyt):
                        st = {}

                        def A(st=st, bsl=bsl, n=n, j4=j4, kt=kt, q0=q0):
                            psS, bpS = rr.get()
                            MM(psS[:, 0:L], kT[bsl, n, kt * 128:(kt + 1) * 128], qT[bsl, j4, q0:q0 + L], True, True, [b_kT, b_qT], [bpS])
                            z = pc_[0] % 4; pc_[0] += 1
                            st["z"] = z
                            ACT(pT[z][:, 0:L], psS[:, 0:L], AF.Exp, [bpS], [b_pT[z]])

                        def B(st=st, n=n, kt=kt, ki=ki, nk=len(keyt), s=s, h=h):
                            z = st["z"]
                            for qt in range(nt):
                                MM(obk[qt][0][:, 0:65], pT[z][:, qt * 128:(qt + 1) * 128], vaug[:, kt, n, 0:65], ki == 0, ki == nk - 1, [b_pT[z], b_va], [obk[qt][1]])
                            if ki == nk - 1:
                                for qt in range(nt):
                                    evac(obk[qt][0], obk[qt][1], qt, s * nt + qt, h)
                        stages.append((A, B))
            else:
                for tq in range(8):
                    qt = tq % 4
                    keys = [(0, None), (1, None)]
                    if tq > 0:
                        keys.append((nk_ctx + tq - 1, "prev"))
                    keys.append((nk_ctx + tq, None))
                    if tq < 7:
                        keys.append((nk_ctx + tq + 1, "next"))
                    for ki, (kt, msk) in enumerate(keys):
                        st = {}

                        def A(st=st, bsl=bsl, n=n, j4=j4, kt=kt, tq=tq, msk=msk):
                            psS, bpS = rr.get()
                            MM(psS[:, 0:128], kT[bsl, n, kt * 128:(kt + 1) * 128], qT[bsl, j4, tq * 128:(tq + 1) * 128], True, True, [b_kT, b_qT], [bpS])
                            z = pc_[0] % 4; pc_[0] += 1
                            st["z"] = z
                            ACT(pT[z][:, 0:128], psS[:, 0:128], AF.Exp, [bpS], [b_pT[z]])
                            if msk is not None:
                                TT("dve", pT[z][:, 0:128], pT[z][:, 0:128], (tril if msk == "prev" else triu)[:], ALU.mult, [b_pT[z], b_tril, b_triu], [b_pT[z]])

                        def B(st=st, n=n, kt=kt, ki=ki, nk=len(keys), qt=qt, tq=tq, h=h):
                            z = st["z"]
                            ob, bob = obk[qt]
                            MM(ob[:, 0:65], pT[z][:, 0:128], vaug[:, kt, n, 0:65], ki == 0, ki == nk - 1, [b_pT[z], b_va], [bob])
                            if ki == nk - 1:
                                evac(ob, bob, qt, tq, h)
                        stages.append((A, B))
        LA = 3
        for k in range(min(LA, len(stages))):
            stages[k][0]()
        for k in range(len(stages)):
            if k + LA < len(stages):
                stages[k + LA][0]()
            stages[k][1]()
        for t in range(8):
            for j4 in range(4):
                ps2, bp2 = rr.get()
                pv = ps2[:].bitcast(BF16)[:, 0:128]
                transpose_to(pv, on[:, t, j4 * 128:(j4 + 1) * 128], [b_on], [bp2])
                yt_, by_ = yT(3, j4)
                CP("act", yt_[:, t * 128:(t + 1) * 128], pv, [bp2], [by_])
        S.op("pool", lambda e: e.memset(gq[:, 0:1], 0.0), [], ([b_qT, b_kT, b_va, b_g, b_snk, b_on, b_rd, b_kst] + b_qb + b_pT + b_qfL + b_sqL + b_rsL) + [b_scr_all])

    def merge(l):
        rr.set(range(8))
        barrier_begin()
        cv = Carve()
        mT = cv.take([8, T], BF16); b_mT = [NB(f"mT{c}") for c in range(8)]
        gs = [cv.take([512]) for _ in range(4)]; b_gs = [NB(f"gs{i}") for i in range(4)]
        tmp = [cv.take([512]) for _ in range(4)]; b_tmp = [NB(f"mt{i}") for i in range(4)]
        bgt = cv.take([32]); b_bgt = NB("bgt")
        S.op("pool", lambda e: e.memset(bgt[:], 0.0), [b_scr_all], [b_bgt, b_scr_all])
        LD(bgt[:], bgatec[l], b_bgt)
        kq = [0]
        for dcp in range(4):
            for br in range(4):
                wg, bwg = wload(w_gate[l][:, br * 1024 + dcp * 256: br * 1024 + (dcp + 1) * 256], 8, 256)
                wb_, bwb_ = wload(w_br[l][br * 512:(br + 1) * 512, dcp * 256:(dcp + 1) * 256], 4, 256)
                for c2 in range(2):
                    dc = dcp * 2 + c2
                    for h in range(2):
                        hs = slice(h * 512, (h + 1) * 512)
                        ti_ = c2 * 2 + h
                        psg, bpg = rr.get()
                        for k in range(8):
                            MM(psg[:, :], wg[:, k, c2 * 128:(c2 + 1) * 128], hT[:, k, hs], k == 0, k == 7, [b_hT, bwg], [bpg])
                        z = kq[0] % 4; kq[0] += 1
                        ACT(gs[z][:], psg[:, :], AF.Sigmoid, [bpg, b_bgt], [b_gs[z]], bias=bgt[:, br * 8 + dc:br * 8 + dc + 1])
                        psb_, bpb_ = rr.get()
                        for k in range(4):
                            yt_, by_ = yT(br, k)
                            MM(psb_[:, :], wb_[:, k, c2 * 128:(c2 + 1) * 128], yt_[:, hs], k == 0, k == 3, [by_, bwb_], [bpb_])
                        if br == 0:
                            TT("dve", tmp[ti_][:], psb_[:, :], gs[z][:], ALU.mult, [bpb_, b_gs[z]], [b_tmp[ti_]])
                        else:
                            TT("dve", gs[z][:], psb_[:, :], gs[z][:], ALU.mult, [bpb_, b_gs[z]], [b_gs[z]])
                            if br < 3:
                                TT("pool", tmp[ti_][:], tmp[ti_][:], gs[z][:], ALU.add, [b_tmp[ti_], b_gs[z]], [b_tmp[ti_]])
                            else:
                                TT("pool", mT[:, dc, hs], tmp[ti_][:], gs[z][:], ALU.add, [b_tmp[ti_], b_gs[z]], [b_mT[dc]])
        if sub == 54:
            raise _Stop()
        for cb4 in range(4):
            w, bw = wload(w_out[l][:, cb4 * 256:(cb4 + 1) * 256], 8, 256)
            for t in range(8):
                ps, bp = rr.get()
                for k in range(8):
                    MM(ps[:, 0:256], mT[:, k, t * 128:(t + 1) * 128], w[:, k, :], k == 0, k == 7, [b_mT[k], bw], [bp])
                cs = slice(cb4 * 256, (cb4 + 1) * 256)
                zz = kq[0] % 4; kq[0] += 1
                TT("dve", gs[zz][:, 0:256], ps[:, 0:256], gbc[:, 0, cs], ALU.mult, [bp, b_gbc[0]], [b_gs[zz]])
                TT("pool", xres[:, t, cs], xres[:, t, cs], gs[zz][:, 0:256], ALU.add, [b_xres[t], b_gs[zz]], [b_xres[t]])
        S.op("pool", lambda e: e.memset(bgt[:, 0:1], 0.0), [], (b_mT + b_gs + b_tmp + [b_bgt]) + [b_scr_all])
        rr.set(range(4))

    def mlp(l):
        barrier_begin()
        cvm = Carve()
        rl = [cvm.take([512]) for _ in range(4)]; b_rl = [NB(f"rl{i}") for i in range(4)]
        rs_ = [cvm.take([256]) for _ in range(4)]; b_rs_ = [NB(f"rsd{i}") for i in range(4)]
        mk = [0, 0]
        for h in range(2):
            hs = slice(h * 512, (h + 1) * 512)

            def aTv(kc):
                return big[:, kc // 2, (kc % 2) * 512:(kc % 2 + 1) * 512], b_big[kc // 2]
            for blk in range(16):
                w, bw = wload(w_fc1[l][:, blk * 256:(blk + 1) * 256], 8, 256)
                for c2 in range(2):
                    kc = blk * 2 + c2
                    ps, bp = rr.get()
                    for k in range(8):
                        MM(ps[:, :], w[:, k, c2 * 128:(c2 + 1) * 128], hT[:, k, hs], k == 0, k == 7, [b_hT, bw], [bp])
                    a_, ba_ = aTv(kc)
                    zr = mk[0] % 4; mk[0] += 1
                    ACT(rl[zr][:], ps[:, :], AF.Relu, [bp], [b_rl[zr]])
                    TT("dve", a_, rl[zr][:], rl[zr][:], ALU.mult, [b_rl[zr]], [ba_])
            for cb4 in range(4):
                cs = slice(cb4 * 256, (cb4 + 1) * 256)
                accb = [(psum[4 + i], psb[4 + i]) for i in range(4)]
                for kg in range(4):
                    w, bw = wload(w_fc2[l][kg * 1024:(kg + 1) * 1024, cs], 8, 256)
                    for tt in range(4):
                        for k in range(8):
                            kc = kg * 8 + k
                            a_, ba_ = aTv(kc)
                            MM(accb[tt][0][:, 0:256], a_[:, tt * 128:(tt + 1) * 128], w[:, k, :], kc == 0, kc == 31, [ba_, bw], [accb[tt][1]])
                for tt in range(4):
                    t = h * 4 + tt
                    zq = mk[1] % 4; mk[1] += 1
                    TT("dve", rs_[zq][:], accb[tt][0][:, 0:256], gbc[:, 1, cs], ALU.mult, [accb[tt][1], b_gbc[1]], [b_rs_[zq]])
                    TT("pool", xres[:, t, cs], xres[:, t, cs], rs_[zq][:], ALU.add, [b_xres[t], b_rs_[zq]], [b_xres[t]])

        S.op("pool", lambda e: e.memset(small[:, 62:63], 0.0), [], b_rl + b_rs_ + [b_scr_all])

    try:
        Sched.PHASE = "prologue"
        adaln_weights(0)
        adaln_weights(1)
        run_pass(0)
        run_pass(1)
    except _Stop:
        pass
    if stop is not None:
        d_hT = nc.dram_tensor("dbg_hT", [128, 8, T], BF16, kind="ExternalOutput").ap()
        d_big = nc.dram_tensor("dbg_big", [128, 16, 1024], BF16, kind="ExternalOutput").ap()
        d_x = nc.dram_tensor("dbg_x", [128, 8, D], F32, kind="ExternalOutput").ap()
        d_modc = nc.dram_tensor("dbg_modc", [128, 48], F32, kind="ExternalOutput").ap()
        S.dma("sp", d_hT[:, :, :], hT[:], reads=[b_hT])
        S.dma("sp", d_big[:, :, :], big[:], reads=b_big, sbuf=b_big[0])
        S.dma("sp", d_x[:, :, :], xres[:], reads=b_xres, sbuf=b_xres[0])
        S.dma("sp", d_modc[:, :], modc[:], reads=[b_modc])
    with nc.Block() as block:
        S.emit(block)
    es.close()
    nc._phases = {e: [o.phase for o in S.ops[e]] for e in ENGS}
    return nc


_NC_CACHE = {}


def _consts():
    bf = ml_dtypes.bfloat16
    c = {}
    c["c_identb"] = np.eye(128, dtype=np.float32).astype(bf)
    c["c_identf"] = np.eye(128, dtype=np.float32)
    c["c_ones"] = np.ones((128, 128), np.float32)
    k = np.arange(128)[:, None]; t = np.arange(128)[None, :]
    c["c_triu"] = (k <= t).astype(np.float32)
    c["c_tril"] = (k >= t).astype(np.float32)
    c["c_mnegF"] = np.where(k <= t, 0.0, -1e30).astype(np.float32)
    c["c_mnegB"] = np.where(k >= t, 0.0, -1e30).astype(np.float32)
    c["c_bprev"] = (t <= k).astype(np.float32)
    c["c_bnext"] = (k <= t).astype(np.float32)
    mB = np.zeros((128, 4, 128), np.float32)
    mC = np.zeros((128, 4, 128), np.float32)
    for q in range(4):
        for gl in range(2):
            g8 = 2 * q + gl
            mB[gl * 64:(gl + 1) * 64, q, g8 * 16:(g8 + 1) * 16] = 1.0
            mC[g8 * 16:(g8 + 1) * 16, q, gl * 64:(gl + 1) * 64] = 1.0
    c["c_maskB"] = mB; c["c_maskC"] = mC
    c["c_iota"] = np.broadcast_to(np.arange(1024, dtype=np.float32)[None, :], (128, 1024)).copy()
    Ls = 1024
    row = np.repeat(np.arange(Ls // 64), 64).astype(np.float32); col = np.tile(np.arange(64), Ls // 64).astype(np.float32)
    nf = 16
    inv = (10000.0 ** (-np.arange(nf, dtype=np.float32) / nf)).astype(np.float32)
    ang = np.concatenate([row[:, None] * inv, col[:, None] * inv], axis=-1).astype(np.float32)
    cs, sn = np.cos(ang).astype(np.float32), np.sin(ang).astype(np.float32)
    C64 = np.zeros((Ls, 2, 2, 16), np.float32); S64 = np.zeros((Ls, 2, 2, 16), np.float32)
    for a in range(2):
        for p in range(2):
            C64[:, a, p, :] = cs[:, a * 16:(a + 1) * 16]
            S64[:, a, p, :] = (-1.0 if p == 0 else 1.0) * sn[:, a * 16:(a + 1) * 16]
    c["c_ropeC"] = C64.reshape(8, 128, 64).transpose(1, 0, 2).copy()
    c["c_ropeS"] = S64.reshape(8, 128, 64).transpose(1, 0, 2).copy()
    lidx = np.arange(1024)
    c["c_rmF"] = np.broadcast_to((lidx % 256 != 0).astype(np.float32)[None, :], (128, 1024)).astype(bf)
    c["c_rmB"] = np.broadcast_to((lidx % 256 != 255).astype(np.float32)[None, :], (128, 1024)).astype(bf)
    return c


def _colmajor(v, nchunk):
    return np.ascontiguousarray(np.swapaxes(v.reshape(v.shape[:-1] + (nchunk, 128)), -1, -2))


def make_in_maps(inp):
    f = lambda a: np.ascontiguousarray(np.asarray(a, dtype=np.float32))
    I = {k: f(v) for k, v in inp.items()}
    shared = dict(_consts())
    shared.update({
        "w_mod": I["w_mod"], "w_in": I["w_in"], "w_gate": I["w_gate"], "w_out": I["w_out"], "w_fc1": I["w_fc1"], "w_fc2": I["w_fc2"],
        "w_glu": I["s5_w_glu"], "w_br": I["w_branch"].reshape(2, 2048, 1024),
        "bmodc": _colmajor(I["b_mod"], 48), "g1c": _colmajor(I["g_norm1"], 8), "g2c": _colmajor(I["g_norm2"], 8),
        "convw": np.ascontiguousarray(I["ssd_conv_w"].transpose(0, 2, 1).reshape(2, 6, 128, 7).transpose(0, 2, 1, 3)),
        "convb": _colmajor(I["ssd_conv_b"], 6),
        "dtb": I["ssd_dt_bias"].reshape(2, 16), "alog": I["ssd_a_log"].reshape(2, 16), "ssdd": I["ssd_d"], "normgc": _colmajor(I["ssd_norm_g"], 4),
        "lamre": I["s5_lam_re"].reshape(2, 32, 128), "lamim": I["s5_lam_im"].reshape(2, 32, 128),
        "lsx": np.ascontiguousarray(np.repeat(I["s5_log_step"].reshape(2, 2, 32, 1), 64, axis=-1).reshape(2, 32, 128)),
        "s5bre": I["s5_b_re"].reshape(2, 2, 2048, 16), "s5bim": I["s5_b_im"].reshape(2, 2, 2048, 16),
        "s5cre": I["s5_c_re"].reshape(2, 2, 512, 64), "s5cim": I["s5_c_im"].reshape(2, 2, 512, 64),
        "s5dc": _colmajor(I["s5_d"], 4), "bgluc": _colmajor(I["s5_b_glu"], 8),
        "dqg": I["diff_qn_g"], "dkg": I["diff_kn_g"], "dlam": I["diff_lambda"].reshape(2, 256), "dsubc": I["diff_subln_g"].reshape(2, 128, 1),
        "wqg": I["win_qn_g"], "wkg": I["win_kn_g"], "wsink": I["win_sink"], "bgatec": _colmajor(I["b_gate"], 32),
    })
    in_maps = []
    for i in range(8):
        b = i // 2
        cv = np.stack([I["c_ctx"], I["c"][b]], axis=0)
        m = dict(shared)
        m.update({
            "xp": I["x_prompt"][4 * i:4 * i + 4].reshape(1024, 1024), "xs": I["x_sample"][b],
            "cvT": np.ascontiguousarray(cv.reshape(2, 8, 128).transpose(2, 1, 0)),
            "st_ssd": I["state_ssd"][b], "st_s5": I["state_s5"][b].reshape(2, 64, 128),
            "cdk": I["cache_diff_k"][b].reshape(2, 256, 512), "cdv": I["cache_diff_v"][b].reshape(2, 256, 512),
            "cwk": I["cache_win_k"][b].reshape(2, 256, 128), "cwv": I["cache_win_v"][b].reshape(2, 256, 128),
        })
        in_maps.append({k: np.ascontiguousarray(v) for k, v in m.items()})
    return in_maps


def kernel(**inp):
    if "nc" not in _NC_CACHE:
        _NC_CACHE["nc"] = build_program()
    nc = _NC_CACHE["nc"]
    in_maps = make_in_maps(inp)
    res = run_bass_kernel_spmd(nc, in_maps, core_ids=list(range(8)))
    R = res.results
    yp = np.concatenate([R[i]["yp"].reshape(4, 256, 1024) for i in range(8)], axis=0)
    ys = np.stack([R[2 * b]["ys"] for b in range(4)], axis=0)
    ssd = np.concatenate([R[i]["o_ssd"] for i in range(8)], axis=0)
    s5 = np.concatenate([R[i]["o_s5"].reshape(4, 2, 2, 2, 32, 64) for i in range(8)], axis=0)
    dk = np.concatenate([R[i]["o_dk"].reshape(4, 2, 256, 4, 2, 64) for i in range(8)], axis=0)
    dv = np.concatenate([R[i]["o_dv"].reshape(4, 2, 256, 4, 128) for i in range(8)], axis=0)
    wk = np.concatenate([R[i]["o_wk"].reshape(4, 2, 256, 2, 64) for i in range(8)], axis=0)
    wv = np.concatenate([R[i]["o_wv"].reshape(4, 2, 256, 2, 64) for i in range(8)], axis=0)
    return tuple(np.ascontiguousarray(a.astype(np.float32)) for a in (yp, ys, ssd, s5, dk, dv, wk, wv))
```

```python
import math
import numpy as np
from contextlib import ExitStack
import ml_dtypes
import concourse.bass as bass
import concourse.mybir as mybir
from concourse.bass_utils import run_bass_kernel_spmd

F32 = mybir.dt.float32
BF16 = mybir.dt.bfloat16
I32 = mybir.dt.int32
ALU = mybir.AluOpType
AF = mybir.ActivationFunctionType
AX = mybir.AxisListType
ENGS = ("pe", "dve", "act", "pool", "sp")
EPS = 1e-6


class Buf:
    __slots__ = ("name", "last_w", "readers", "load_sem", "load_cnt", "store_sem", "store_cnt", "excl")

    def __init__(self, name, excl=False):
        self.name = name
        self.excl = excl
        self.last_w = None
        self.readers = []
        self.load_sem = None
        self.load_cnt = 0
        self.store_sem = None
        self.store_cnt = 0


class Op:
    __slots__ = ("eng", "fn", "deps", "signal", "semval", "is_dma", "dsem", "dval", "phase")

    def __init__(self, eng, fn):
        self.phase = Sched.PHASE
        self.eng = eng
        self.fn = fn
        self.deps = []
        self.signal = False
        self.semval = 0
        self.is_dma = False
        self.dsem = None
        self.dval = 0


class Sched:
    PHASE = ""

    def __init__(self, nc, es):
        self.nc = nc
        self.es = es
        self.ops = {e: [] for e in ENGS}
        self.sems = {e: es.enter_context(nc.semaphore("c_" + e)) for e in ENGS}
        self.store_bufs = []
        self.nsem = 5
        self.pool = {}

    def new_sem(self, name):
        self.nsem += 1
        return self.es.enter_context(self.nc.semaphore(f"{name}_{self.nsem}"))

    def _track(self, op, reads, writes, skip_waw=False):
        deps = op.deps
        for r in reads:
            if r.last_w is not None and r.last_w is not op:
                deps.append(r.last_w)
            if r.excl:
                deps.extend(x for x in r.readers if x is not op and x.eng != op.eng)
            r.readers.append(op)
        for w in writes:
            if w.last_w is not None and w.last_w is not op and not skip_waw:
                deps.append(w.last_w)
            deps.extend(r for r in w.readers if r is not op)
            w.last_w = op
            w.readers = []

    def op(self, eng, fn, reads=(), writes=()):
        o = Op(eng, fn)
        self._track(o, reads, writes)
        self.ops[eng].append(o)
        return o

    def dma(self, q, out, in_, reads=(), writes=(), group=False, sbuf=None, **kw):
        o = Op(q, lambda e: e.dma_start(out=out, in_=in_, **kw))
        o.is_dma = True
        self._track(o, reads, writes, skip_waw=group)
        if sbuf is None:
            sbuf = writes[0] if writes else reads[0]
        key = ("l_" if sbuf in writes else "s_") + sbuf.name
        ent = self.pool.get(key)
        if ent is None:
            ent = [self.new_sem(key), 0]
            self.pool[key] = ent
        ent[1] += 16
        o.dsem, o.dval = ent[0], ent[1]
        self.ops[q].append(o)
        return o

    def emit(self, block):
        for e in ENGS:
            for o in self.ops[e]:
                for d in o.deps:
                    if not d.is_dma and not (d.eng == "pe" and o.eng == "pe"):
                        d.signal = True
        for e in ENGS:
            v = 0
            for o in self.ops[e]:
                if o.signal and not o.is_dma:
                    v += 1
                    o.semval = v
        engmap = {"pe": block.tensor, "dve": block.vector, "act": block.scalar,
                  "pool": block.gpsimd, "sp": block.sync}
        sems = self.sems
        store_bufs = self.store_bufs
        for e in ENGS:
            def body(eng, ops=self.ops[e], e=e):
                known = {}
                for o in ops:
                    need = {}
                    for d in o.deps:
                        if d.is_dma:
                            key, val = d.dsem, d.dval
                        else:
                            if d.eng == "pe" and e == "pe":
                                continue
                            key, val = sems[d.eng], d.semval
                        if need.get(key, 0) < val:
                            need[key] = val
                    for key, val in need.items():
                        if known.get(key, 0) < val:
                            eng.wait_ge(key, val)
                            known[key] = val
                    inst = o.fn(eng)
                    if o.is_dma:
                        inst.then_inc(o.dsem, 16)
                    elif o.signal:
                        inst.then_inc(sems[e], 1)
                if e == "sp":
                    for key, ent in self.pool.items():
                        if key.startswith("s_"):
                            eng.wait_ge(ent[0], ent[1])
            engmap[e](body)


D = 1024
T = 1024
W_IN = 4112
C_Z, C_XBC, C_DT, C_U, C_DQ, C_DK, C_DV, C_WQ, C_WK, C_WV = 0, 512, 1280, 1296, 1808, 2320, 2832, 3344, 3856, 3984


class _Stop(Exception):
    pass


def build_program(stop=None, sub=None):
    nc = bass.Bass("TRN2", target_bir_lowering=False)
    es = ExitStack()
    S = Sched(nc, es)

    def din(name, shape, dt=F32):
        return nc.dram_tensor(name, list(shape), dt, kind="ExternalInput").ap()

    def dout(name, shape):
        return nc.dram_tensor(name, list(shape), F32, kind="ExternalOutput").ap()

    cnt = [0]

    def sb(shape, dt=F32, name=None):
        cnt[0] += 1
        return es.enter_context(nc.sbuf_tensor(name or f"t{cnt[0]}", list(shape), dt))

    xin = [din("xp", [T, D]), din("xs", [T, D])]
    yout = [dout("yp", [T, D]), dout("ys", [T, D])]
    cvT_d = din("cvT", [128, 8, 2])
    st_ssd = din("st_ssd", [2, 2, 8, 64, 64])
    st_s5 = din("st_s5", [2, 64, 128])
    cdk = din("cdk", [2, 256, 512]); cdv = din("cdv", [2, 256, 512])
    cwk = din("cwk", [2, 256, 128]); cwv = din("cwv", [2, 256, 128])
    w_mod = din("w_mod", [2, D, 6 * D]); w_in = din("w_in", [2, D, W_IN]); w_gate = din("w_gate", [2, D, 4 * D])
    w_out = din("w_out", [2, D, D]); w_fc1 = din("w_fc1", [2, D, 4 * D]); w_fc2 = din("w_fc2", [2, 4 * D, D])
    w_glu = din("w_glu", [2, 512, 1024]); w_br = din("w_br", [2, 2048, 1024])
    bmodc = din("bmodc", [2, 128, 48]); g1c = din("g1c", [2, 128, 8]); g2c = din("g2c", [2, 128, 8])
    convw = din("convw", [2, 128, 6, 7]); convb = din("convb", [2, 128, 6])
    dtb = din("dtb", [2, 16]); alog = din("alog", [2, 16]); ssdd = din("ssdd", [2, 8]); normgc = din("normgc", [2, 128, 4])
    lamre = din("lamre", [2, 32, 128]); lamim = din("lamim", [2, 32, 128]); lsx = din("lsx", [2, 32, 128])
    s5bre = din("s5bre", [2, 2, 2048, 16]); s5bim = din("s5bim", [2, 2, 2048, 16])
    s5cre = din("s5cre", [2, 2, 512, 64]); s5cim = din("s5cim", [2, 2, 512, 64])
    s5dc = din("s5dc", [2, 128, 4]); bgluc = din("bgluc", [2, 128, 8])
    dqg = din("dqg", [2, 64]); dkg = din("dkg", [2, 64]); dlam = din("dlam", [2, 256]); dsubc = din("dsubc", [2, 128, 1])
    wqg = din("wqg", [2, 64]); wkg = din("wkg", [2, 64]); wsink = din("wsink", [2, 8]); bgatec = din("bgatec", [2, 128, 32])
    c_identb = din("c_identb", [128, 128], BF16); c_identf = din("c_identf", [128, 128]); c_ones = din("c_ones", [128, 128])
    c_triu = din("c_triu", [128, 128]); c_tril = din("c_tril", [128, 128])
    c_mnegF = din("c_mnegF", [128, 128]); c_mnegB = din("c_mnegB", [128, 128])
    c_bprev = din("c_bprev", [128, 128]); c_bnext = din("c_bnext", [128, 128])
    c_maskB = din("c_maskB", [128, 4, 128]); c_maskC = din("c_maskC", [128, 4, 128])
    c_iota = din("c_iota", [128, 1024]); c_ropeC = din("c_ropeC", [128, 8, 64]); c_ropeS = din("c_ropeS", [128, 8, 64])
    c_rmF = din("c_rmF", [128, 1024], BF16); c_rmB = din("c_rmB", [128, 1024], BF16)
    o_ssd = dout("o_ssd", [4, 2, 2, 8, 64, 64]); o_s5 = dout("o_s5", [4, 2, 64, 128])
    o_dk = dout("o_dk", [4, 2, 256, 512]); o_dv = dout("o_dv", [4, 2, 256, 512])
    o_wk = dout("o_wk", [4, 2, 256, 128]); o_wv = dout("o_wv", [4, 2, 256, 128])

    def TT(eng, out, in0, in1, op, r, w):
        S.op(eng, lambda e: e.tensor_tensor(out=out, in0=in0, in1=in1, op=op), r, w)

    def TS(eng, out, in0, s1, op0, r, w, s2=None, op1=None):
        if op1 is None:
            S.op(eng, lambda e: e.tensor_scalar(out=out, in0=in0, scalar1=s1, scalar2=None, op0=op0), r, w)
        else:
            S.op(eng, lambda e: e.tensor_scalar(out=out, in0=in0, scalar1=s1, scalar2=s2, op0=op0, op1=op1), r, w)

    def STT(out, in0, scalar, in1, op0, op1, r, w):
        S.op("dve", lambda e: e.scalar_tensor_tensor(out=out, in0=in0, scalar=scalar, in1=in1, op0=op0, op1=op1), r, w)

    def ACT(out, in_, func, r, w, scale=1.0, bias=None, accum=None):
        kw = {}
        if bias is not None:
            kw["bias"] = bias
        if accum is not None:
            kw["accum_out"] = accum
        S.op("act", lambda e: e.activation(out=out, in_=in_, func=func, scale=scale, **kw), r, w)

    def CP(eng, out, in_, r, w):
        if eng == "act":
            S.op("act", lambda e: e.copy(out=out, in_=in_), r, w)
        else:
            S.op(eng, lambda e: e.tensor_copy(out=out, in_=in_), r, w)

    def MM(out, lhsT, rhs, start, stop, r, w):
        S.op("pe", lambda e: e.matmul(out, lhsT=lhsT, rhs=rhs, start=start, stop=stop), r, w)

    def MSET(eng, out, val, w):
        S.op(eng, lambda e: e.memset(out, val), (), w)

    def LD(out, in_, b, q="sp", group=False):
        S.dma(q, out, in_, writes=[b], group=group)

    def STO(out, in_, b, q="sp"):
        S.dma(q, out, in_, reads=[b])

    def const(src, shape, dt=F32):
        t = sb(shape, dt)
        b = Buf(f"c{cnt[0]}")
        LD(t[:], src, b)
        return t, b

    identb, b_identb = const(c_identb[:, :], [128, 128], BF16)
    identf, b_identf = const(c_identf[:, :], [128, 128])
    onesf, b_ones = const(c_ones[:, :], [128, 128])
    triu, b_triu = const(c_triu[:, :], [128, 128]); tril, b_tril = const(c_tril[:, :], [128, 128])
    mnegF, b_mnegF = const(c_mnegF[:, :], [128, 128]); mnegB, b_mnegB = const(c_mnegB[:, :], [128, 128])
    maskB, b_maskB = const(c_maskB[:, :, :], [128, 4, 128]); maskC, b_maskC = const(c_maskC[:, :, :], [128, 4, 128])
    ropeC, b_ropeC = const(c_ropeC[:, :, :], [128, 8, 64]); ropeS, b_ropeS = const(c_ropeS[:, :, :], [128, 8, 64])
    CONSTB = [b_identb, b_identf, b_ones]

    psum = [es.enter_context(nc.psum_tensor(f"ps{i}", [128, 512], F32)) for i in range(8)]
    psb = [Buf(f"ps{i}", excl=True) for i in range(8)]

    class RR:
        def __init__(self, ids):
            self.ids = list(ids); self.i = 0

        def get(self):
            k = self.ids[self.i % len(self.ids)]; self.i += 1
            return psum[k], psb[k]

        def set(self, ids):
            self.ids = list(ids)

    rr = RR(range(0, 4))

    xres = sb([128, 8, D]); b_xres = [Buf(f"xres{t}") for t in range(8)]
    hT = sb([128, 8, T], BF16); b_hT = Buf("hT")
    NST, NBF = 3, 3
    wst = [sb([128, 8, 256]) for _ in range(NST)]; b_wst = [Buf(f"wst{i}") for i in range(NST)]
    wbf = [sb([128, 8, 256], BF16) for _ in range(NBF)]; b_wbf = [Buf(f"wbf{i}") for i in range(NBF)]
    wctr = [0, 0]
    big = sb([128, 16, 1024], BF16)
    b_big = [Buf(f"big{i}") for i in range(16)]
    modc = sb([128, 48]); b_modc = Buf("modc")
    scol = sb([128, 8, 2]); b_scol = Buf("scol")
    G1 = sb([128, 8]); SH1 = sb([128, 8]); G2 = sb([128, 8]); SH2 = sb([128, 8]); b_G = Buf("G")
    gbc = sb([128, 2, D]); b_gbc = [Buf("gbc0"), Buf("gbc1")]
    small = sb([128, 64]); b_small = Buf("small")
    junk = sb([128, 768]); b_junk = Buf("junk")
    SCR_BYTES = 60 * 1024
    scr = sb([128, SCR_BYTES // 4])

    def wload(src, nk, ncols, cast=True):
        i = wctr[0] % NST; wctr[0] += 1
        st, bs = wst[i], b_wst[i]
        LD(st[:, 0:nk, 0:ncols], src.rearrange("(k p) c -> p k c", p=128), bs)
        if not cast:
            return st, bs
        j = wctr[1] % NBF; wctr[1] += 1
        wb, bb = wbf[j], b_wbf[j]
        heavy = any(k in Sched.PHASE for k in ("prologue", "merge", "mlp"))
        eng = "act" if (wctr[1] % 2 == 0 or not heavy) else "dve"
        CP(eng, wb[:, 0:nk, 0:ncols], st[:, 0:nk, 0:ncols], [bs], [bb])
        return wb, bb

    def proj_tm(src, bsrc, nk, w, bw, ncols, tiles, evac):
        for t in tiles:
            ps, bp = rr.get()
            for k in range(nk):
                MM(ps[:, 0:ncols], src[:, k, t * 128:(t + 1) * 128], w[:, k, 0:ncols], k == 0, k == nk - 1, [bsrc, bw], [bp])
            evac(t, ps, bp)

    def proj_fm(src, bsrc, nk, w, bw, ncols, evac, halves=(0, 1)):
        for cc in range((ncols + 127) // 128):
            m = min(128, ncols - cc * 128)
            for h in halves:
                ps, bp = rr.get()
                for k in range(nk):
                    MM(ps[0:m, :], w[:, k, cc * 128:cc * 128 + m], src[:, k, h * 512:(h + 1) * 512], k == 0, k == nk - 1, [bsrc, bw], [bp])
                evac(cc, h, ps, bp)

    def transpose_to(ps_out, in_, r, w, dt=BF16, np_=128):
        idn = identb if dt == BF16 else identf
        S.op("pe", lambda e: e.transpose(out=ps_out, in_=in_, identity=idn[0:np_, 0:np_]), list(r) + CONSTB, w)

    def bcast_rows(col_ap, bcol, ps_out, bp):
        dg = sb_diag[dgc[0] % 4]; bd = b_diag[dgc[0] % 4]; dgc[0] += 1
        TS("dve", dg[:], identf[:], col_ap, ALU.mult, [b_identf, bcol], [bd])
        MM(ps_out, onesf[:], dg[:], True, True, [b_ones, bd], [bp])

    sb_diag = [sb([128, 128]) for _ in range(4)]; b_diag = [Buf(f"dg{i}") for i in range(4)]; dgc = [0]

    def rstd_from_ss(ss_ap, n, out_ap, r, w, ncols=1):
        TS("dve", out_ap, ss_ap, 1.0 / n, ALU.mult, r, w, s2=EPS, op1=ALU.add)
        ACT(out_ap, out_ap, AF.Sqrt, w, w)
        S.op("dve", lambda e: e.reciprocal(out=out_ap, in_=out_ap), w, w)

    LD(scol[:], cvT_d[:, :, :], b_scol)
    ACT(scol[:], scol[:], AF.Silu, [b_scol], [b_scol])
    scolb = sb([128, 8, 2], BF16)
    CP("dve", scolb[:], scol[:], [b_scol], [b_scol])

    modall = sb([128, 2, 2, 48]); b_modall = Buf("modall")

    def adaln_weights(l):
        rr.set(range(8))
        bm = sb_bm; LD(bm[:], bmodc[l], b_bm)
        for blk in range(24):
            w, bw = wload(w_mod[l][:, blk * 256:(blk + 1) * 256], 8, 256)
            for cc in range(2):
                ps, bp = rr.get()
                for k in range(8):
                    MM(ps[:, 0:2], w[:, k, cc * 128:(cc + 1) * 128], scolb[:, k, 0:2], k == 0, k == 7, [bw, b_scol], [bp])
                c = blk * 2 + cc
                TT("dve", modall[:, l, :, c], ps[:, 0:2], bm[:, c:c + 1].broadcast_to([128, 2]), ALU.add, [bp, b_bm], [b_modall])
        rr.set(range(4))

    def adaln(l, path):
        rr.set(range(8))
        CP("dve", modc[:], modall[:, l, path, :], [b_modall], [b_modc])
        gt = sb_gt; LD(gt[:, 0:8], g1c[l], b_gt); LD(gt[:, 8:16], g2c[l], b_gt, group=True)
        STT(G1[:], modc[:, 8:16], 1.0, gt[:, 0:8], ALU.add, ALU.mult, [b_modc, b_gt], [b_G])
        STT(G2[:], modc[:, 32:40], 1.0, gt[:, 8:16], ALU.add, ALU.mult, [b_modc, b_gt], [b_G])
        CP("dve", SH1[:], modc[:, 0:8], [b_modc], [b_G])
        CP("dve", SH2[:], modc[:, 24:32], [b_modc], [b_G])
        for gi, base in enumerate((16, 40)):
            for c in range(8):
                ps, bp = rr.get()
                bcast_rows(modc[:, base + c:base + c + 1], b_modc, ps[:, 0:128], bp)
                CP("act", gbc[:, gi, c * 128:(c + 1) * 128], ps[:, 0:128], [bp], [b_gbc[gi]])
        rr.set(range(4))

    sb_bm = sb([128, 48]); b_bm = Buf("bm"); sb_gt = sb([128, 16]); b_gt = Buf("gt")

    xn = [sb([128, D], BF16), sb([128, D], BF16)]; b_xn = [Buf("xn0"), Buf("xn1")]

    def norm_mod(Gc, SHc):
        rr.set(range(8))
        stg = []
        for t in range(8):
            def FA(t=t):
                ss = small[:, t:t + 1]
                x_, bx_ = xn[t % 2], b_xn[t % 2]
                ACT(x_[:], xres[:, t, :], AF.Square, [b_xres[t]], [bx_, b_small], accum=ss)
                rstd_from_ss(ss, D, small[:, 8 + t:9 + t], [b_small], [b_small])
                ACT(x_[:], xres[:, t, :], AF.Copy, [b_xres[t], b_small], [bx_], scale=small[:, 8 + t:9 + t])

            def FB(t=t):
                x_, bx_ = xn[t % 2], b_xn[t % 2]
                for c in range(8):
                    ps, bp = rr.get()
                    pv = ps[:].bitcast(BF16)[:, 0:128]
                    transpose_to(pv, x_[:, c * 128:(c + 1) * 128], [bx_], [bp])
                    if c % 2 == 0:
                        ACT(hT[:, c, t * 128:(t + 1) * 128], pv, AF.Identity, [bp, b_G], [b_hT], scale=Gc[:, c:c + 1], bias=SHc[:, c:c + 1])
                    else:
                        TS("dve", hT[:, c, t * 128:(t + 1) * 128], pv, Gc[:, c:c + 1], ALU.mult, [bp, b_G], [b_hT], s2=SHc[:, c:c + 1], op1=ALU.add)
            stg.append((FA, FB))
        stg[0][0]()
        for k in range(8):
            if k + 1 < 8:
                stg[k + 1][0]()
            stg[k][1]()
        rr.set(range(4))

    def yT(br, fc):
        return big[:, br * 4 + fc, :], b_big[br * 4 + fc]

    def run_pass(path):
        nseq, L = (4, 256) if path == 0 else (1, 1024)
        nt = L // 128
        is_s = path == 1
        for t in range(8):
            LD(xres[:, t, :], xin[path][t * 128:(t + 1) * 128, :], b_xres[t])
        def chk(stage, l):
            if stop is not None and stop == (path, l, stage):
                raise _Stop()
        for l in range(2):
            def ph(n):
                Sched.PHASE = f"{'PS'[path]}{l}_{n}"
            ph("adaln"); adaln(l, path); chk("adaln", l)
            ph("norm1"); norm_mod(G1, SH1); chk("norm1", l)
            ph("ssd"); branch_ssd(l, path, nseq, L, nt, is_s); chk("ssd", l)
            ph("s5"); branch_s5(l, path, nseq, L, nt, is_s); chk("s5", l)
            ph("diff"); branch_diff(l, path, nseq, L, nt, is_s); chk("diff", l)
            ph("win"); branch_win(l, path, nseq, L, nt, is_s); chk("win", l)
            ph("merge"); merge(l); chk("merge", l)
            ph("norm2"); norm_mod(G2, SH2); chk("norm2", l)
            ph("mlp"); mlp(l); chk("mlp", l)
        for t in range(8):
            STO(yout[path][t * 128:(t + 1) * 128, :], xres[:, t, :], b_xres[t])

    class Carve:
        def __init__(self):
            self.off = 0

        def take(self, shape, dt=F32):
            n = int(np.prod(shape))
            nbytes = n * (4 if dt in (F32, I32) else 2)
            nbytes = (nbytes + 31) // 32 * 32
            assert self.off + nbytes <= SCR_BYTES, (self.off, nbytes)
            v = scr[:, self.off // 4:(self.off + nbytes) // 4]
            self.off += nbytes
            if dt != F32:
                v = v.bitcast(dt)
            v = v[:, 0:n]
            if len(shape) == 2:
                return v.rearrange("p (a b) -> p a b", a=shape[0])
            if len(shape) == 3:
                return v.rearrange("p (a b c) -> p a b c", a=shape[0], b=shape[1])
            return v

    b_scr_all = Buf("scrall")
    bar = [None]

    def NB(name):
        x = Buf(name)
        x.last_w = bar[0]
        return x

    def barrier_begin():
        bar[0] = S.op("pool", lambda e: e.memset(small[:, 63:64], 0.0), [b_scr_all], [b_scr_all])


    def branch_ssd(l, path, nseq, L, nt, is_s):
        barrier_begin()
        cv = Carve()
        zs = cv.take([8, 512], BF16); b_zs = NB("zs")
        dt = cv.take([8, 16]); dtA = cv.take([8, 16]); ainc = cv.take([8, 16]); arest = cv.take([8, 16]); edt = cv.take([8, 16]); einc = cv.take([8, 16])
        nainc = cv.take([8, 16])
        b_dt = NB("dt"); b_cum = NB("cum")
        cumP = cv.take([8, 32]); b_cumP = NB("cumP")
        Lp = L + 6
        raw = cv.take([nseq * Lp]); b_raw = NB("raw")
        acc = cv.take([T]); b_acc = NB("acc")
        xrot = [cv.take([T], BF16), cv.take([T], BF16)]; b_xrot = [NB("xrot0"), NB("xrot1")]
        xB = cv.take([T], BF16); xC = cv.take([T], BF16); b_xB = NB("xB"); b_xC = NB("xC")
        xs_tok = cv.take([8, 512], BF16); b_xs = NB("xs_tok")
        B_tok = cv.take([8, 128], BF16); b_Btok = NB("Btok")
        NDP = 12
        GS = 4
        b_seg = []
        Lt = [cv.take([128]) for _ in range(NDP)]; b_Lt = [NB(f"Lt{i}") for i in range(NDP)]
        sc = [cv.take([128], BF16) for _ in range(NDP)]; b_sc = [NB(f"sc{i}") for i in range(NDP)]
        yacc = cv.take([512]); b_yacc = NB("yacc")
        ytmp = cv.take([512]); b_ytmp = NB("ytmp")
        ynb = cv.take([512], BF16); b_ynb = NB("ynb")
        prm = cv.take([64]); b_prm = NB("ssdprm")
        cw = cv.take([6, 7]); cb = cv.take([6]); ngc = cv.take([4]); b_cw = NB("cw")
        Bw = [cv.take([64], BF16), cv.take([64], BF16)]; b_Bw = [NB("Bw0"), NB("Bw1")]
        fin = None; s0T = None; st_ld = None
        b_fin = NB("fin"); b_s0T = NB("s0T"); b_stld = NB("stld")
        if is_s:
            s0T = cv.take([8, 128], BF16); st_ld = cv.take([8, 128])
        else:
            fin = cv.take([16, 64])
        _p0 = Sched.PHASE
        S.op("pool", lambda e: e.memset(prm[:, 0:64], 0.0), [b_scr_all], [b_prm, b_scr_all])
        LD(prm[:, 0:16], dtb[l:l + 1, :].partition_broadcast(128), b_prm)
        LD(prm[:, 16:32], alog[l:l + 1, :].partition_broadcast(128), b_prm, group=True)
        LD(prm[:, 32:40], ssdd[l:l + 1, :].partition_broadcast(128), b_prm, group=True)
        ACT(prm[:, 16:32], prm[:, 16:32], AF.Exp, [b_prm], [b_prm])
        TS("dve", prm[:, 16:32], prm[:, 16:32], -1.0, ALU.mult, [b_prm], [b_prm])
        LD(cw[:], convw[l], b_cw); LD(cb[:], convb[l], b_cw, group=True); LD(ngc[:], normgc[l], b_cw, group=True)
        Sched.PHASE = _p0 + 'A'
        for blk in range(2):
            w, bw = wload(w_in[l][:, C_Z + blk * 256:C_Z + (blk + 1) * 256], 8, 256)
            proj_tm(hT, b_hT, 8, w, bw, 256, range(8),
                    lambda t, ps, bp, blk=blk: ACT(zs[:, t, blk * 256:(blk + 1) * 256], ps[:, 0:256], AF.Silu, [bp], [b_zs]))
        Sched.PHASE = _p0 + 'B'
        w, bw = wload(w_in[l][:, C_DT:C_DT + 16], 8, 16)

        def ev_dt(t, ps, bp):
            TT("dve", dt[:, t, :], ps[:, 0:16], prm[:, 0:16], ALU.add, [bp, b_prm], [b_dt])
            ACT(dt[:, t, :], dt[:, t, :], AF.Exp, [b_dt], [b_dt])
            ACT(dt[:, t, :], dt[:, t, :], AF.Ln, [b_dt], [b_dt], bias=1.0)
            TT("dve", dtA[:, t, :], dt[:, t, :], prm[:, 16:32], ALU.mult, [b_dt, b_prm], [b_dt])
        proj_tm(hT, b_hT, 8, w, bw, 16, range(8), ev_dt)
        Sched.PHASE = _p0 + 'C'
        for blk in range(3):
            w, bw = wload(w_in[l][:, C_XBC + blk * 256:C_XBC + (blk + 1) * 256], 8, 256)
            for c2 in range(2):
                cc = blk * 2 + c2
                if cc < 4:
                    xa, bxa = xrot[cc % 2], b_xrot[cc % 2]
                elif cc == 4:
                    xa, bxa = xB, b_xB
                else:
                    xa, bxa = xC, b_xC
                MSET("pool", raw[:], 0.0, [b_raw])
                rw3 = raw.rearrange("p (s x) -> p s x", s=nseq)
                for h in range(2):
                    ps, bp = rr.get()
                    for k in range(8):
                        MM(ps[:, :], w[:, k, c2 * 128:(c2 + 1) * 128], hT[:, k, h * 512:(h + 1) * 512], k == 0, k == 7, [b_hT, bw], [bp])
                    if is_s:
                        CP("act", raw[:, 3 + h * 512:3 + (h + 1) * 512], ps[:, :], [bp], [b_raw])
                    else:
                        CP("act", rw3[:, 2 * h:2 * h + 2, 3:3 + L], ps[:, :].rearrange("p (s x) -> p s x", s=2), [bp], [b_raw])
                ac3 = acc.rearrange("p (s x) -> p s x", s=nseq)
                TS("dve", ac3, rw3[:, :, 0:L], cw[:, cc, 0:1], ALU.mult, [b_raw, b_cw], [b_acc])
                for k in range(1, 7):
                    STT(ac3, rw3[:, :, k:k + L], cw[:, cc, k:k + 1], ac3, ALU.mult, ALU.add, [b_raw, b_cw, b_acc], [b_acc])
                ACT(xa[:], acc[:], AF.Silu, [b_acc, b_cw], [bxa], bias=cb[:, cc:cc + 1])
                if cc < 5:
                    for t in range(8):
                        ps, bp = rr.get()
                        pv = ps[:].bitcast(BF16)[:, 0:128]
                        transpose_to(pv, xa[:, t * 128:(t + 1) * 128], [bxa], [bp])
                        if cc < 4:
                            CP("act", xs_tok[:, t, cc * 128:(cc + 1) * 128], pv, [bp], [b_xs])
                        else:
                            CP("act", B_tok[:, t, :], pv, [bp], [b_Btok])
        Sched.PHASE = _p0 + 'E'
        for s in range(nseq):
            for j in range(nt):
                tj = s * nt + j
                ps, bp = rr.get()
                for i in range(j + 1):
                    MM(ps[:, 0:16], (triu if i == j else onesf)[:], dtA[:, s * nt + i, :], i == 0, i == j, [b_triu, b_ones, b_dt], [bp])
                for i in range(nt - 1, j - 1, -1):
                    MM(ps[:, 16:32], (tril if i == j else onesf)[:], dtA[:, s * nt + i, :], i == nt - 1, i == j, [b_tril, b_ones, b_dt], [bp])
                CP("act", cumP[:, tj, :], ps[:, 0:32], [bp], [b_cumP])
                CP("dve", ainc[:, tj, 0:8], cumP[:, tj, 0:8], [b_cumP], [b_cum])
                CP("dve", ainc[:, tj, 8:16], cumP[:, tj, 24:32], [b_cumP], [b_cum])
                TT("dve", arest[:, tj, 0:8], cumP[:, tj, 16:24], dtA[:, tj, 0:8], ALU.subtract, [b_cumP, b_dt], [b_cum])
                TT("dve", arest[:, tj, 8:16], cumP[:, tj, 8:16], dtA[:, tj, 8:16], ALU.subtract, [b_cumP, b_dt], [b_cum])
                ACT(edt[:, tj, :], arest[:, tj, :], AF.Exp, [b_cum], [b_cum])
                TT("dve", edt[:, tj, :], edt[:, tj, :], dt[:, tj, :], ALU.mult, [b_cum, b_dt], [b_cum])
                ACT(einc[:, tj, :], ainc[:, tj, :], AF.Exp, [b_cum], [b_cum])
                TS("dve", nainc[:, tj, :], ainc[:, tj, :], -1.0, ALU.mult, [b_cum], [b_cum])
        if is_s:
            stv = st_ssd[l].rearrange("d h p n -> (d h p) n").rearrange("(j q) n -> q j n", q=128)
            LD(st_ld[:, :, 0:64], stv, b_stld); LD(st_ld[:, :, 64:128], stv, b_stld, group=True)
            for j8 in range(8):
                ps, bp = rr.get()
                transpose_to(ps[:, 0:128], st_ld[:, j8, :], [b_stld], [bp], dt=F32)
                CP("act", s0T[:, j8, :], ps[:, 0:128], [bp], [b_s0T])
        Sched.PHASE = _p0 + 'F'
        ybanks = [(psum[4], psb[4]), (psum[7], psb[7])]
        pa_ = [0]
        k_ = [0]
        stages = []
        for s in range(nseq):
            for j in range(nt):
                tj = s * nt + j
                ybank, b_yb = ybanks[tj % 2]
                for h in range(8):
                    g = h // 4
                    gsl = slice(g * 64, (g + 1) * 64)
                    units = [(0, i) for i in range(j + 1)] + [(1, i) for i in range(j, nt)]
                    cur = {"psA": None}
                    for g0 in range(0, len(units), GS):
                        grp = list(enumerate(units))[g0:g0 + GS]
                        st = {}

                        def A(st=st, grp=grp, units=units, s=s, j=j, tj=tj, h=h, gsl=gsl, cur=cur):
                            for ui, (d, i) in grp:
                                ti = s * nt + i
                                dh = d * 8 + h
                                if ui == 0 or units[ui - 1][0] != d:
                                    cur["psA"] = (psum[5 + pa_[0] % 2], psb[5 + pa_[0] % 2]); pa_[0] += 1
                                    bcast_rows(ainc[:, tj, dh:dh + 1], b_cum, cur["psA"][0][:, 0:128], cur["psA"][1])
                                psA_t, b_psA = cur["psA"]
                                q = k_[0] % NDP; k_[0] += 1
                                st[ui] = q
                                if i == j:
                                    STT(Lt[q][:], psA_t[:, 0:128], ainc[:, ti, dh:dh + 1], (mnegF if d == 0 else mnegB)[:], ALU.subtract, ALU.add,
                                        [b_psA, b_cum, b_mnegF, b_mnegB], [b_Lt[q]])
                                    ACT(Lt[q][:], Lt[q][:], AF.Exp, [b_Lt[q]], [b_Lt[q]])
                                elif is_s:
                                    ACT(Lt[q][:], psA_t[:, 0:128], AF.Exp, [b_psA, b_cum], [b_Lt[q]], bias=nainc[:, ti, dh:dh + 1])
                                else:
                                    TS("dve", Lt[q][:], psA_t[:, 0:128], ainc[:, ti, dh:dh + 1], ALU.subtract, [b_psA, b_cum], [b_Lt[q]], s2=0.0, op1=ALU.min)
                                    ACT(Lt[q][:], Lt[q][:], AF.Exp, [b_Lt[q]], [b_Lt[q]])
                            for ui, (d, i) in grp:
                                ti = s * nt + i
                                dh = d * 8 + h
                                q = st[ui]
                                psG, bpG = rr.get()
                                MM(psG[:, 0:128], xB[gsl, ti * 128:(ti + 1) * 128], xC[gsl, tj * 128:(tj + 1) * 128], True, True, [b_xB, b_xC], [bpG])
                                STT(sc[q][:], psG[:, 0:128], dt[:, ti, dh:dh + 1], Lt[q][:], ALU.mult, ALU.mult, [bpG, b_dt, b_Lt[q]], [b_sc[q]])

                        def B(st=st, grp=grp, units=units, s=s, tj=tj, h=h, ybank=ybank, b_yb=b_yb, last_grp=(g0 + GS >= len(units))):
                            for ui, (d, i) in grp:
                                ti = s * nt + i
                                q = st[ui]
                                MM(ybank[:, h * 64:(h + 1) * 64], sc[q][:], xs_tok[:, ti, h * 64:(h + 1) * 64], ui == 0, ui == len(units) - 1, [b_sc[q], b_xs], [b_yb])
                            if h == 7 and last_grp:
                                finalize(tj, ybank, b_yb)
                        stages.append((A, B))

        def finalize(tj, ybank, b_yb):
            if True:
                TT("dve", ytmp.rearrange("p (h x) -> p h x", h=8), xs_tok[:, tj, :].rearrange("p (h x) -> p h x", h=8),
                   prm[:, 32:40].unsqueeze(2).broadcast_to([128, 8, 64]), ALU.mult, [b_xs, b_prm], [b_ytmp])
                TT("dve", yacc[:], ybank[:, :], ytmp[:], ALU.add, [b_yb, b_ytmp], [b_yacc])
                if is_s:
                    for d in range(2):
                        for h in range(8):
                            g = h // 4
                            gsl = slice(g * 64, (g + 1) * 64)
                            j8 = (d * 8 + h) // 2
                            h2 = (d * 8 + h) % 2
                            psO, bpO = rr.get()
                            MM(psO[:, 0:64], xC[gsl, tj * 128:(tj + 1) * 128], s0T[gsl, j8, h2 * 64:(h2 + 1) * 64], True, True, [b_xC, b_s0T], [bpO])
                            STT(yacc[:, h * 64:(h + 1) * 64], psO[:, 0:64], einc[:, tj, d * 8 + h:d * 8 + h + 1], yacc[:, h * 64:(h + 1) * 64], ALU.mult, ALU.add,
                                [bpO, b_cum, b_yacc], [b_yacc])
                TT("dve", yacc[:], yacc[:], zs[:, tj, :], ALU.mult, [b_yacc, b_zs], [b_yacc])
                ACT(ytmp[:], yacc[:], AF.Square, [b_yacc], [b_ytmp, b_small], accum=small[:, 16:17])
                rstd_from_ss(small[:, 16:17], 512, small[:, 17:18], [b_small], [b_small])
                ACT(ynb[:], yacc[:], AF.Copy, [b_yacc, b_small], [b_ynb], scale=small[:, 17:18])
                for c4 in range(4):
                    ps, bp = rr.get()
                    pv = ps[:].bitcast(BF16)[:, 0:128]
                    transpose_to(pv, ynb[:, c4 * 128:(c4 + 1) * 128], [b_ynb], [bp])
                    yt_, by_ = yT(0, c4)
                    ACT(yt_[:, tj * 128:(tj + 1) * 128], pv, AF.Copy, [bp, b_cw], [by_], scale=ngc[:, c4:c4 + 1])
        LA = 2
        for k in range(min(LA, len(stages))):
            stages[k][0]()
        for k in range(len(stages)):
            if k + LA < len(stages):
                stages[k + LA][0]()
            stages[k][1]()
        Sched.PHASE = _p0 + 'G'
        if not is_s:
            for s in range(nseq):
                for d in range(2):
                    for h in range(8):
                        g = h // 4
                        psF, bpF = rr.get()
                        for i in range(nt):
                            ti = s * nt + i
                            q = k_[0] % 2; k_[0] += 1
                            TS("dve", Bw[q][:], B_tok[:, ti, g * 64:(g + 1) * 64], edt[:, ti, d * 8 + h:d * 8 + h + 1], ALU.mult, [b_Btok, b_cum], [b_Bw[q]])
                            MM(psF[0:64, 0:64], xs_tok[:, ti, h * 64:(h + 1) * 64], Bw[q][:], i == 0, i == nt - 1, [b_xs, b_Bw[q]], [bpF])
                        CP("act", fin[0:64, d * 8 + h, :], psF[0:64, 0:64], [bpF], [b_fin])
                STO(o_ssd[s, l].rearrange("d h p n -> p (d h) n"), fin[0:64, :, :], b_fin)
        S.op("pool", lambda e: e.memset(prm[:, 0:1], 0.0), [], ([b_fin, b_yacc, b_ynb, b_Btok, b_xs, b_cum, b_dt, b_zs, b_cumP, b_xB, b_xC, b_raw, b_acc, b_ytmp, b_cw, b_s0T, b_stld, b_prm]
             + b_xrot + b_seg + b_Lt + b_sc + b_Bw) + [b_scr_all])

    def branch_s5(l, path, nseq, L, nt, is_s):
        barrier_begin()
        _p0 = Sched.PHASE
        cv = Carve()
        uT = cv.take([4, T], BF16); b_uT = NB("uT")
        y5T = uT; b_y5 = b_uT
        prow = cv.take([128]); b_prow = NB("prow")
        pc = cv.take([12, 32]); b_pc = NB("pc")
        pci = cv.take([32], I32); b_pci = NB("pci")
        Bst = cv.take([4, 4, 16]); b_Bst = NB("Bst")
        Cn = cv.take([4, 64]); b_Cn = NB("Cn")
        Bc = cv.take([2, 16]); b_Bc = NB("Bc"); Bt = cv.take([16]); b_Bt = NB("Bt")
        Bx = [cv.take([128], BF16), cv.take([128], BF16)]; b_Bx = [NB("Bx0"), NB("Bx1")]
        BcL = [cv.take([128], BF16), cv.take([128], BF16)]; b_BcL = [NB("BcL0"), NB("BcL1")]
        Cx = [cv.take([128], BF16), cv.take([128], BF16)]; b_Cx = [NB("Cx0"), NB("Cx1")]
        CL = cv.take([4, 128], BF16); b_CL = NB("CL")
        Lt_ = L
        cosT = cv.take([Lt_]); sinT = cv.take([Lt_]); b_tab = NB("tab")
        xr = [cv.take([T]), cv.take([T])]; b_xr = [NB("xr0"), NB("xr1")]
        prR = cv.take([2 * T])
        prb = prR.bitcast(BF16)
        pr = [prb[:, k * T:(k + 1) * T] for k in range(4)]; b_pr = [NB(f"pr{i}") for i in range(4)]
        argF = prR[:, 0:Lt_]; argI = prR[:, T:T + Lt_].bitcast(I32)
        tmpx = prR[:, 0:T]; rmt = prR[:, T:2 * T]
        bA = [b_pr[0], b_pr[1]]; bB = [b_pr[2], b_pr[3]]
        if not is_s:
            tmpy = cv.take([T]); bY = [NB("tmpy")]
        else:
            tmpy = tmpx; bY = bA
        d5 = cv.take([4]); bg = cv.take([8]); b_d5 = NB("d5")
        finS = cv.take([256]); b_finS = NB("finS")
        b_wcap = NB("wcap")
        if not is_s:
            wcap = cv.take([2, 32, 4]); tcap = cv.take([2, 32]); wtmp = cv.take([4, 32, 4])
        s0c = cv.take([64]); b_s0c = NB("s0c")
        sg = cv.take([512]); b_sg = NB("sg")
        fT = cv.take([128]); b_fT = NB("fT")
        iota = cv.take([Lt_]); b_iota = NB("iota")
        LD(iota[:], c_iota[:, 0:Lt_], b_iota)
        if not is_s:
            rmF = cv.take([T], BF16); rmB = cv.take([T], BF16); b_rmF = NB("rmF"); b_rmB = NB("rmB")
            LD(rmF[:], c_rmF[:, :], b_rmF); LD(rmB[:], c_rmB[:, :], b_rmB)
        else:
            rmF = rmB = None; b_rmF = b_rmB = b_iota
        S.op("pool", lambda e: e.memset(prow[:], 0.0), [b_scr_all], [b_prow, b_scr_all])
        LD(prow[0:32, :], lamre[l], b_prow); LD(prow[32:64, :], lamim[l], b_prow, group=True); LD(prow[64:96, :], lsx[l], b_prow, group=True)
        ps, bp = rr.get()
        transpose_to(ps[:, 0:96], prow[0:96, :], [b_prow], [bp], dt=F32, np_=96)
        CP("act", pc[:, 0:3, :].rearrange("p a b -> p (a b)"), ps[:, 0:96], [bp], [b_pc])
        P_ = lambda i: pc[:, i, :]
        R, W_ = [b_pc], [b_pc]
        ACT(P_(2), P_(2), AF.Exp, R, W_)
        TT("dve", P_(3), P_(0), P_(2), ALU.mult, R, W_)
        TT("dve", P_(4), P_(1), P_(2), ALU.mult, R, W_)
        ACT(P_(5), P_(3), AF.Exp, R, W_)
        TS("dve", pci[:], P_(4), 1.0 / (2 * math.pi), ALU.mult, R, [b_pci])
        CP("dve", P_(10), pci[:], [b_pci], W_)
        STT(P_(11), P_(10), -2 * math.pi, P_(4), ALU.mult, ALU.add, R, W_)
        TS("dve", P_(11), P_(11), 3.14159, ALU.min, R, W_, s2=-3.14159, op1=ALU.max)
        ACT(P_(7), P_(11), AF.Sin, R, W_)
        ACT(P_(10), P_(11), AF.Abs, R, W_)
        ACT(P_(6), P_(10), AF.Sin, R, W_, scale=-1.0, bias=math.pi / 2)
        TT("dve", P_(6), P_(6), P_(5), ALU.mult, R, W_)
        TT("dve", P_(7), P_(7), P_(5), ALU.mult, R, W_)
        TT("dve", P_(10), P_(0), P_(0), ALU.mult, R, W_)
        TT("dve", P_(11), P_(1), P_(1), ALU.mult, R, W_)
        TT("dve", P_(10), P_(10), P_(11), ALU.add, R, W_)
        S.op("dve", lambda e: e.reciprocal(out=P_(10), in_=P_(10)), R, W_)
        TS("dve", P_(11), P_(6), -1.0, ALU.add, R, W_)
        TT("dve", P_(8), P_(11), P_(0), ALU.mult, R, W_)
        TT("dve", P_(9), P_(7), P_(1), ALU.mult, R, W_)
        TT("dve", P_(8), P_(8), P_(9), ALU.add, R, W_)
        TT("dve", P_(8), P_(8), P_(10), ALU.mult, R, W_)
        TT("dve", P_(9), P_(7), P_(0), ALU.mult, R, W_)
        TT("dve", P_(11), P_(11), P_(1), ALU.mult, R, W_)
        TT("dve", P_(9), P_(9), P_(11), ALU.subtract, R, W_)
        TT("dve", P_(9), P_(9), P_(10), ALU.mult, R, W_)
        LD(d5[:], s5dc[l], b_d5); LD(bg[:], bgluc[l], b_d5, group=True)
        if is_s:
            LD(fT[0:64, :], st_s5[l], b_fT)
            ps, bp = rr.get()
            transpose_to(ps[:, 0:64], fT[0:64, :], [b_fT], [bp], dt=F32, np_=64)
            CP("act", s0c[:], ps[:, 0:64], [bp], [b_s0c])
        Sched.PHASE = _p0 + 'u'
        for blk in range(2):
            w, bw = wload(w_in[l][:, C_U + blk * 256:C_U + (blk + 1) * 256], 8, 256)
            proj_fm(hT, b_hT, 8, w, bw, 256,
                    lambda cc, h, ps, bp, blk=blk: CP("act", uT[:, blk * 2 + cc, h * 512:(h + 1) * 512], ps[:, :], [bp], [b_uT]))
        ybk = [(psum[4], psb[4]), (psum[5], psb[5])]
        xbk = [(psum[6], psb[6]), (psum[7], psb[7])]
        nrep = T // Lt_
        v3 = (lambda a: a.rearrange("p (s x) -> p s x", s=nrep)) if nrep > 1 else (lambda a: a)
        Bc2 = [Bc, cv.take([2, 16])]; b_Bc2 = [b_Bc, NB("Bc_1")]; Bt2 = [Bt, cv.take([16])]; b_Bt2 = [b_Bt, NB("Bt_1")]
        Bx2 = [Bx, [cv.take([128], BF16), cv.take([128], BF16)]]; b_Bx2 = [b_Bx, [NB("Bx0_1"), NB("Bx1_1")]]
        BcL2 = [BcL, [cv.take([128], BF16), cv.take([128], BF16)]]; b_BcL2 = [b_BcL, [NB("BcL0_1"), NB("BcL1_1")]]
        Cx2 = [Cx, [cv.take([128], BF16), cv.take([128], BF16)]]; b_Cx2 = [b_Cx, [NB("Cx0_1"), NB("Cx1_1")]]
        CL2 = [CL, cv.take([4, 128], BF16)]; b_CL2 = [b_CL, NB("CL_1")]
        cos2 = [cosT, cv.take([Lt_])]; sin2 = [sinT, cv.take([Lt_])]; b_tab2 = [b_tab, NB("tab_1")]
        its = [(fc, d, q4) for fc in range(4) for d in range(2) for q4 in range(4)]
        NI = len(its)

        def stB(k):
            fc, d, q4 = its[k]
            z = k % 2
            if d == 0 and q4 == 0:
                for dd in range(2):
                    LD(Bst[:, dd * 2 + 0, :, :], s5bre[l, dd][fc * 512:(fc + 1) * 512, :].rearrange("(c p) m -> p c m", p=128), b_Bst, group=(dd > 0))
                    LD(Bst[:, dd * 2 + 1, :, :], s5bim[l, dd][fc * 512:(fc + 1) * 512, :].rearrange("(c p) m -> p c m", p=128), b_Bst, group=True)
                    LD(Cn[:, dd * 2 + 0, :], s5cre[l, dd][fc * 128:(fc + 1) * 128, :], b_Cn, group=(dd > 0))
                    LD(Cn[:, dd * 2 + 1, :], s5cim[l, dd][fc * 128:(fc + 1) * 128, :], b_Cn, group=True)
            c = fc * 4 + q4
            dc = d * 16 + c
            cre, cim = pc[:, 8, dc:dc + 1], pc[:, 9, dc:dc + 1]
            Bre, Bim = Bst[:, d * 2 + 0, q4, :], Bst[:, d * 2 + 1, q4, :]
            Bc_, bBc_, Bt_, bBt_ = Bc2[z], b_Bc2[z], Bt2[z], b_Bt2[z]
            TS("dve", Bt_[:], Bim, cim, ALU.mult, [b_Bst, b_pc], [bBt_])
            STT(Bc_[:, 0, :], Bre, cre, Bt_[:], ALU.mult, ALU.subtract, [b_Bst, b_pc, bBt_], [bBc_])
            TS("dve", Bt_[:], Bre, cim, ALU.mult, [b_Bst, b_pc], [bBt_])
            STT(Bc_[:, 1, :], Bim, cre, Bt_[:], ALU.mult, ALU.add, [b_Bst, b_pc, bBt_], [bBc_])
            for ri in range(2):
                TT("pool", Bx2[z][ri].rearrange("p (g m) -> p g m", g=8), maskB[:, q4, :].rearrange("p (g m) -> p g m", g=8),
                   Bc_[:, ri, :].unsqueeze(1).broadcast_to([128, 8, 16]), ALU.mult, [b_maskB, bBc_], [b_Bx2[z][ri]])
                ps, bp = rr.get()
                pv = ps[:].bitcast(BF16)[:, 0:128]
                transpose_to(pv, Bx2[z][ri][:], [b_Bx2[z][ri]], [bp])
                CP("act", BcL2[z][ri][:], pv, [bp], [b_BcL2[z][ri]])
            for ri in range(2):
                TT("pool", Cx2[z][ri].rearrange("p (g n) -> p g n", g=2), maskC[:, q4, :].rearrange("p (g n) -> p g n", g=2),
                   Cn[:, d * 2 + ri, :].unsqueeze(1).broadcast_to([128, 2, 64]), ALU.mult, [b_maskC, b_Cn], [b_Cx2[z][ri]])
                ps, bp = rr.get()
                pv = ps[:].bitcast(BF16)[:, 0:128]
                transpose_to(pv, Cx2[z][ri][:], [b_Cx2[z][ri]], [bp])
                CP("act", CL2[z][:, 2 * ri, :], pv, [bp], [b_CL2[z]])
                ACT(CL2[z][:, 2 * ri + 1, :], pv, AF.Copy, [bp], [b_CL2[z]], scale=-1.0)

        def stT1(k):
            fc, d, q4 = its[k]
            z = k % 2
            dc = d * 16 + fc * 4 + q4
            cT, sT, bt = cos2[z], sin2[z], [b_tab2[z]]
            TS("dve", cT[:], iota[:, 0:Lt_], pc[:, 4, dc:dc + 1], ALU.mult, [b_iota, b_pc], bt)
            TS("dve", sT[:].bitcast(I32), cT[:], 1.0 / (2 * math.pi), ALU.mult, bt, bt)
            CP("act", sT[:], sT[:].bitcast(I32), bt, bt)

        def stT2(k):
            z = k % 2
            cT, sT, bt = cos2[z], sin2[z], [b_tab2[z]]
            STT(cT[:], sT[:], -2 * math.pi, cT[:], ALU.mult, ALU.add, bt, bt)
            TS("dve", cT[:], cT[:], 3.14159, ALU.min, bt, bt, s2=-3.14159, op1=ALU.max)
            ACT(sT[:], cT[:], AF.Sin, bt, bt)
            ACT(cT[:], cT[:], AF.Abs, bt, bt)
            ACT(cT[:], cT[:], AF.Sin, bt, bt, scale=-1.0, bias=math.pi / 2)

        def stX(k):
            fc, d, q4 = its[k]
            z = k % 2
            c = fc * 4 + q4
            dc = d * 16 + c
            cosT_, sinT_, bt = cos2[z], sin2[z], b_tab2[z]
            for h in range(2):
                hs = slice(h * 512, (h + 1) * 512)
                for ri in range(2):
                    MM(xbk[ri][0][:, :], BcL2[z][ri][:], uT[:, fc, hs], True, True, [b_BcL2[z][ri], b_uT], [xbk[ri][1]])
                xre, xim = xbk[0][0][:, :], xbk[1][0][:, :]
                if Lt_ < 512:
                    nr2 = 512 // Lt_
                    cB = cosT_.unsqueeze(1).broadcast_to([128, nr2, Lt_]); sB = sinT_.unsqueeze(1).broadcast_to([128, nr2, Lt_])
                    vv = lambda a, nr2=nr2: a.rearrange("p (s x) -> p s x", s=nr2)
                else:
                    cB, sB = cosT_[:, hs], sinT_[:, hs]
                    vv = lambda a: a
                TT("dve", vv(xr[1][:, hs]), vv(xim), cB, ALU.mult, [xbk[1][1], bt], [b_xr[1]])
                TT("dve", vv(tmpy[:, hs]), vv(xre), sB, ALU.mult, [xbk[0][1], bt], bY)
                TT("pool", xr[1][:, hs], xr[1][:, hs], tmpy[:, hs], ALU.subtract if d == 0 else ALU.add, [b_xr[1]] + bY, [b_xr[1]])
                TT("dve", vv(xr[0][:, hs]), vv(xre), cB, ALU.mult, [xbk[0][1], bt], [b_xr[0]])
                TT("dve", vv(tmpx[:, hs]), vv(xim), sB, ALU.mult, [xbk[1][1], bt], bA)
                TT("pool", xr[0][:, hs], xr[0][:, hs], tmpx[:, hs], ALU.add if d == 0 else ALU.subtract, [b_xr[0]] + bA, [b_xr[0]])
            if is_s:
                sre = s0c[:, (d * 2 + 0) * 16 + c:(d * 2 + 0) * 16 + c + 1]; sim = s0c[:, (d * 2 + 1) * 16 + c:(d * 2 + 1) * 16 + c + 1]
                abre, abim = pc[:, 6, dc:dc + 1], pc[:, 7, dc:dc + 1]
                sm = small
                RS, WS_ = [b_small, b_s0c, b_pc, bt], [b_small]
                TT("dve", sm[:, 20:21], sre, abre, ALU.mult, RS, WS_); TT("dve", sm[:, 21:22], sim, abim, ALU.mult, RS, WS_)
                TT("dve", sm[:, 22:23], sm[:, 20:21], sm[:, 21:22], ALU.subtract, RS, WS_)
                TT("dve", sm[:, 20:21], sre, abim, ALU.mult, RS, WS_); TT("dve", sm[:, 21:22], sim, abre, ALU.mult, RS, WS_)
                TT("dve", sm[:, 23:24], sm[:, 20:21], sm[:, 21:22], ALU.add, RS, WS_)
                if d == 0:
                    TT("dve", xr[0][:, 0:1], xr[0][:, 0:1], sm[:, 22:23], ALU.add, [b_xr[0], b_small], [b_xr[0]])
                    TT("dve", xr[1][:, 0:1], xr[1][:, 0:1], sm[:, 23:24], ALU.add, [b_xr[1], b_small], [b_xr[1]])
                else:
                    cl, sl = cosT_[:, L - 1:L], sinT_[:, L - 1:L]
                    TT("dve", sm[:, 20:21], sm[:, 22:23], cl, ALU.mult, RS, WS_); TT("dve", sm[:, 21:22], sm[:, 23:24], sl, ALU.mult, RS, WS_)
                    TT("dve", sm[:, 24:25], sm[:, 20:21], sm[:, 21:22], ALU.subtract, RS, WS_)
                    TT("dve", sm[:, 20:21], sm[:, 22:23], sl, ALU.mult, RS, WS_); TT("dve", sm[:, 21:22], sm[:, 23:24], cl, ALU.mult, RS, WS_)
                    TT("dve", sm[:, 25:26], sm[:, 20:21], sm[:, 21:22], ALU.add, RS, WS_)
                    TT("dve", xr[0][:, L - 1:L], xr[0][:, L - 1:L], sm[:, 24:25], ALU.add, [b_xr[0], b_small], [b_xr[0]])
                    TT("dve", xr[1][:, L - 1:L], xr[1][:, L - 1:L], sm[:, 25:26], ALU.add, [b_xr[1], b_small], [b_xr[1]])

        def stS(k):
            fc, d, q4 = its[k]
            z = k % 2
            dc = d * 16 + fc * 4 + q4
            cosT_, sinT_, bt = cos2[z], sin2[z], b_tab2[z]
            if is_s:
                rm_ = pc[:, 5, dc:dc + 1].broadcast_to([128, T]); rm_r = rm_; brm = [b_pc]
            else:
                TS("dve", rmt, (rmF if d == 0 else rmB)[:], pc[:, 5, dc:dc + 1], ALU.mult, [b_rmF, b_rmB, b_pc], bB)
                rm_ = rmt; rm_r = rmt[:, ::-1]; brm = bB
            for ri in (1, 0):
                if d == 0:
                    S.op("dve", lambda e, ri=ri, rm_=rm_: e.tensor_tensor_scan(out=xr[ri][:], data0=rm_, data1=xr[ri][:], initial=0.0, op0=ALU.mult, op1=ALU.add),
                         brm + [b_xr[ri]], [b_xr[ri]])
                else:
                    S.op("dve", lambda e, ri=ri, rm_r=rm_r: e.tensor_tensor_scan(out=xr[ri][:, ::-1], data0=rm_r, data1=xr[ri][:, ::-1], initial=0.0, op0=ALU.mult, op1=ALU.add),
                         brm + [b_xr[ri]], [b_xr[ri]])
            if not is_s:
                lpos = L - 1 if d == 0 else 0
                for ri in range(2):
                    w3 = xr[ri].rearrange("p (s x) -> p s x", s=nseq)[:, :, lpos:lpos + 1].rearrange("p s x -> p (s x)")
                    CP("act", wcap[:, ri, dc, :], w3, [b_xr[ri]], [b_wcap])
                CP("act", tcap[:, 0, dc:dc + 1], cosT_[:, lpos:lpos + 1], [bt], [b_wcap])
                CP("act", tcap[:, 1, dc:dc + 1], sinT_[:, lpos:lpos + 1], [bt], [b_wcap])

        def stP(k):
            fc, d, q4 = its[k]
            z = k % 2
            cosT_, sinT_, bt = cos2[z], sin2[z], b_tab2[z]
            cosB = cosT_.unsqueeze(1).broadcast_to([128, nrep, Lt_]) if nrep > 1 else cosT_
            sinB = sinT_.unsqueeze(1).broadcast_to([128, nrep, Lt_]) if nrep > 1 else sinT_
            TT("dve", v3(pr[1]), v3(xr[1][:]), sinB, ALU.mult, [b_xr[1], bt], [b_pr[1]])
            TT("pool", v3(pr[0]), v3(xr[0][:]), cosB, ALU.mult, [b_xr[0], bt], [b_pr[0]])
            TT("dve", v3(pr[3]), v3(xr[1][:]), cosB, ALU.mult, [b_xr[1], bt], [b_pr[3]])
            TT("pool", v3(pr[2]), v3(xr[0][:]), sinB, ALU.mult, [b_xr[0], bt], [b_pr[2]])
            sel = [0, 1, 3, 3] if d == 0 else [0, 0, 2, 3]
            first_y = (d == 0 and q4 == 0)
            last_y = (d == 1 and q4 == 3)
            for h in range(2):
                hs = slice(h * 512, (h + 1) * 512)
                for k4 in range(4):
                    MM(ybk[h][0][:, :], CL2[z][:, sel[k4], :], pr[k4][:, hs], first_y and k4 == 0, last_y and k4 == 3, [b_CL2[z], b_pr[k4]], [ybk[h][1]])
            if last_y:
                for h in range(2):
                    hs = slice(h * 512, (h + 1) * 512)
                    STT(uT[:, fc, hs], uT[:, fc, hs], d5[:, fc:fc + 1], ybk[h][0][:, :], ALU.mult, ALU.add, [b_uT, b_d5, ybk[h][1]], [b_uT])

        Sched.PHASE = _p0 + 'L'
        stB(0); stT1(0); stT2(0)
        for k in range(NI):
            if k + 1 < NI:
                stB(k + 1); stT1(k + 1)
            stX(k)
            if k + 1 < NI:
                stT2(k + 1)
            stS(k)
            stP(k)
        if not is_s:
            cB_ = tcap[:, 0, :].unsqueeze(2).broadcast_to([128, 32, 4]); sB_ = tcap[:, 1, :].unsqueeze(2).broadcast_to([128, 32, 4])
            RW = [b_wcap]
            TT("dve", wtmp[:, 0, :, :], wcap[:, 0, :, :], cB_, ALU.mult, RW, RW)
            TT("dve", wtmp[:, 1, :, :], wcap[:, 1, :, :], sB_, ALU.mult, RW, RW)
            TT("dve", wtmp[:, 2, :, :], wcap[:, 0, :, :], sB_, ALU.mult, RW, RW)
            TT("dve", wtmp[:, 3, :, :], wcap[:, 1, :, :], cB_, ALU.mult, RW, RW)
            f4 = finS.rearrange("p (s d r c) -> p s d r c", s=4, d=2, r=2)
            for d in range(2):
                src = lambda k, d=d: wtmp[:, k, d * 16:(d + 1) * 16, :].rearrange("p c s -> p s c")
                TT("dve", f4[:, :, d, 0, :], src(0), src(1), ALU.subtract if d == 0 else ALU.add, RW, [b_finS])
                TT("dve", f4[:, :, d, 1, :], src(3), src(2), ALU.add if d == 0 else ALU.subtract, RW, [b_finS])
            for half in range(2):
                ps, bp = rr.get()
                transpose_to(ps[:, 0:128], finS[:, half * 128:(half + 1) * 128], [b_finS], [bp], dt=F32)
                CP("act", fT[:], ps[:, 0:128], [bp], [b_fT])
                for s2 in range(2):
                    STO(o_s5[half * 2 + s2, l], fT[s2 * 64:(s2 + 1) * 64, :], b_fT)
        Sched.PHASE = _p0 + 'G'
        for blk in range(2):
            w, bw = wload(w_glu[l][:, blk * 256:(blk + 1) * 256], 4, 256)
            w2, bw2 = wload(w_glu[l][:, (blk + 2) * 256:(blk + 3) * 256], 4, 256)
            for c2 in range(2):
                cc = blk * 2 + c2
                for h in range(2):
                    hs = slice(h * 512, (h + 1) * 512)
                    psg, bpg = rr.get()
                    for k in range(4):
                        MM(psg[:, :], w2[:, k, c2 * 128:(c2 + 1) * 128], y5T[:, k, hs], k == 0, k == 3, [b_y5, bw2], [bpg])
                    ACT(sg[:], psg[:, :], AF.Sigmoid, [bpg, b_d5], [b_sg], bias=bg[:, 4 + cc:5 + cc])
                    psv, bpv = rr.get()
                    for k in range(4):
                        MM(psv[:, :], w[:, k, c2 * 128:(c2 + 1) * 128], y5T[:, k, hs], k == 0, k == 3, [b_y5, bw], [bpv])
                    yt_, by_ = yT(1, cc)
                    STT(yt_[:, hs], psv[:, :], bg[:, cc:cc + 1], sg[:], ALU.add, ALU.mult, [bpv, b_d5, b_sg], [by_])
        S.op("pool", lambda e: e.memset(prow[:, 0:1], 0.0), [], ([b_uT, b_y5, b_prow, b_pc, b_pci, b_Bst, b_Cn, b_Bc, b_Bt, b_CL, b_tab, b_d5, b_finS, b_s0c, b_sg, b_fT]
             + b_Bx + b_BcL + b_Cx + b_xr + b_pr + (bY if not is_s else []) + [b_iota, b_rmF, b_rmB, b_wcap]
             + [b_Bc2[1], b_Bt2[1], b_CL2[1], b_tab2[1]] + b_Bx2[1] + b_BcL2[1] + b_Cx2[1]) + [b_scr_all])

    def rms_groups(ps, bp, ncols, gain_bc, b_gain, qf, b_qf, sq, b_sq, rs, b_rs, t, rope):
        ng = ncols // 64
        CP("act", qf[:, 0:ncols], ps[:, 0:ncols], [bp], [b_qf])
        TT("dve", sq[:, 0:ncols], qf[:, 0:ncols], qf[:, 0:ncols], ALU.mult, [b_qf], [b_sq])
        S.op("dve", lambda e: e.tensor_reduce(out=rs[:, 0:ng], in_=sq[:, 0:ncols].rearrange("p (g x) -> p g x", g=ng), op=ALU.add, axis=AX.X), [b_sq], [b_rs])
        rstd_from_ss(rs[:, 0:ng], 64, rs[:, 0:ng], [b_rs], [b_rs])
        q3 = qf[:, 0:ncols].rearrange("p (g x) -> p g x", g=ng)
        TT("dve", q3, q3, rs[:, 0:ng].unsqueeze(2).broadcast_to([128, ng, 64]), ALU.mult, [b_qf, b_rs], [b_qf])
        TT("dve", q3, q3, gain_bc.unsqueeze(1).broadcast_to([128, ng, 64]), ALU.mult, [b_qf, b_gain], [b_qf])
        if rope:
            s3 = sq[:, 0:ncols].rearrange("p (g a q f) -> p (g a) q f", g=ng, a=2, q=2)
            x4 = qf[:, 0:ncols].rearrange("p (g a q f) -> p (g a) q f", g=ng, a=2, q=2)
            S4 = ropeS[:, t, :].rearrange("p (a q f) -> p a q f", a=2, q=2)
            for pz in range(2):
                TT("dve", s3[:, :, pz, :].rearrange("p (g a) f -> p g a f", g=ng), x4[:, :, 1 - pz, :].rearrange("p (g a) f -> p g a f", g=ng),
                   S4[:, :, pz, :].unsqueeze(1).broadcast_to([128, ng, 2, 16]), ALU.mult, [b_qf, b_ropeS], [b_sq])
            TT("dve", q3, q3, ropeC[:, t, :].unsqueeze(1).broadcast_to([128, ng, 64]), ALU.mult, [b_qf, b_ropeC], [b_qf])
            TT("dve", qf[:, 0:ncols], qf[:, 0:ncols], sq[:, 0:ncols], ALU.add, [b_qf, b_sq], [b_qf])

    def branch_diff(l, path, nseq, L, nt, is_s):
        barrier_begin()
        _p0 = Sched.PHASE
        cv = Carve()
        nk_ctx = 2 if is_s else 0
        NKT = 8 + nk_ctx
        qT = cv.take([4, T], BF16); b_qT = NB("qT")
        kT = cv.take([4, NKT * 128], BF16); b_kT = NB("kT")
        vaug = cv.take([NKT, 4, 130], BF16); b_va = NB("vaug")
        qfL = [cv.take([512]) for _ in range(2)]; b_qfL = [NB(f"qf{i}") for i in range(2)]
        sqL = [cv.take([512]) for _ in range(2)]; b_sqL = [NB(f"sq{i}") for i in range(2)]
        rsL = [cv.take([16]) for _ in range(2)]; b_rsL = [NB(f"rs{i}") for i in range(2)]
        rot_ = [0]

        def nxt():
            i = rot_[0] % 2; rot_[0] += 1
            return qfL[i], b_qfL[i], sqL[i], b_sqL[i], rsL[i], b_rsL[i]
        qb = [cv.take([512], BF16), cv.take([512], BF16)]; b_qb = [NB("qb0"), NB("qb1")]
        gq = cv.take([64]); gk = cv.take([64]); b_g = NB("dg")
        lamt = cv.take([4, 64]); lamc = cv.take([8]); b_lam = NB("lam")
        o1 = cv.take([4, 128]); b_o1 = NB("o1")
        odn = cv.take([8, 512], BF16); b_odn = NB("odn")
        pT = [cv.take([512], BF16) for _ in range(4)]; b_pT = [NB(f"pT{i}") for i in range(4)]
        rd = cv.take([8]); b_rd = NB("rd")
        oh = cv.take([128]); b_oh = NB("oh")
        gsub = cv.take([1]); b_gsub = NB("gsub")
        kstL = [cv.take([512]) for _ in range(2)]; b_kstL = [NB(f"kst{i}") for i in range(2)]
        kst, b_kst = kstL[0], b_kstL[0]
        lam_init = 0.8 - 0.6 * math.exp(-0.3 * l)
        S.op("pool", lambda e: e.memset(gq[:], 0.0), [b_scr_all], [b_g, b_scr_all])
        LD(gq[:], dqg[l:l + 1, :].partition_broadcast(128), b_g); LD(gk[:], dkg[l:l + 1, :].partition_broadcast(128), b_g, group=True)
        TS("dve", gq[:], gq[:], 0.125, ALU.mult, [b_g], [b_g])
        LD(lamt[:].rearrange("p a b -> p (a b)"), dlam[l:l + 1, :].partition_broadcast(128), b_lam)
        LD(gsub[:], dsubc[l], b_gsub)
        TS("dve", gsub[:], gsub[:], 1.0 - lam_init, ALU.mult, [b_gsub], [b_gsub])
        TT("dve", lamt[:, 0, :], lamt[:, 0, :], lamt[:, 1, :], ALU.mult, [b_lam], [b_lam])
        TT("dve", lamt[:, 2, :], lamt[:, 2, :], lamt[:, 3, :], ALU.mult, [b_lam], [b_lam])
        S.op("dve", lambda e: e.tensor_reduce(out=lamc[:, 0:1], in_=lamt[:, 0, :], op=ALU.add, axis=AX.X), [b_lam], [b_lam])
        S.op("dve", lambda e: e.tensor_reduce(out=lamc[:, 1:2], in_=lamt[:, 2, :], op=ALU.add, axis=AX.X), [b_lam], [b_lam])
        ACT(lamc[:, 0:2], lamc[:, 0:2], AF.Exp, [b_lam], [b_lam])
        TT("dve", lamc[:, 2:3], lamc[:, 0:1], lamc[:, 1:2], ALU.subtract, [b_lam], [b_lam])
        TS("dve", lamc[:, 2:3], lamc[:, 2:3], lam_init, ALU.add, [b_lam], [b_lam], s2=-1.0, op1=ALU.mult)
        MSET("pool", vaug[:].rearrange("p a b c -> p (a b c)"), 1.0, [b_va])
        if sub == 1:
            raise _Stop()
        Sched.PHASE = _p0 + 'A'
        stagesA = []
        for which in range(2):
            col0 = C_DQ if which == 0 else C_DK
            wd = {}
            for t in range(8):
                st = {}

                def FA(st=st, which=which, t=t, wd=wd, col0=col0):
                    if t == 0:
                        wd["A"] = wload(w_in[l][:, col0:col0 + 256], 8, 256)
                        wd["B"] = wload(w_in[l][:, col0 + 256:col0 + 512], 8, 256)
                    wA, bwA = wd["A"]; wB, bwB = wd["B"]
                    ps, bp = rr.get()
                    for k in range(8):
                        MM(ps[:, 0:256], hT[:, k, t * 128:(t + 1) * 128], wA[:, k, :], k == 0, k == 7, [b_hT, bwA], [bp])
                    for k in range(8):
                        MM(ps[:, 256:512], hT[:, k, t * 128:(t + 1) * 128], wB[:, k, :], k == 0, k == 7, [b_hT, bwB], [bp])
                    qf, b_qf, sq, b_sq, rs, b_rs = nxt()
                    kst, b_kst = kstL[t % 2], b_kstL[t % 2]
                    if which == 1 and not is_s:
                        rms_groups(ps, bp, 512, gk[:], b_g, kst, b_kst, sq, b_sq, rs, b_rs, t, False)
                        STO(o_dk[t // 2, l, (t % 2) * 128:(t % 2 + 1) * 128, :], kst[:], b_kst)
                        st["src"] = (kst, b_kst)
                    else:
                        rms_groups(ps, bp, 512, (gq if which == 0 else gk)[:], b_g, qf, b_qf, sq, b_sq, rs, b_rs, t, is_s)
                        st["src"] = (qf, b_qf)

                def FB(st=st, which=which, t=t):
                    src, bsrc = st["src"]
                    qb_, bqb_ = qb[t % 2], b_qb[t % 2]
                    CP("pool", qb_[:], src[:], [bsrc], [bqb_])
                    for j4 in range(4):
                        ps2, bp2 = rr.get()
                        pv = ps2[:].bitcast(BF16)[:, 0:128]
                        transpose_to(pv, qb_[:, j4 * 128:(j4 + 1) * 128], [bqb_], [bp2])
                        if which == 0:
                            CP("act", qT[:, j4, t * 128:(t + 1) * 128], pv, [bp2], [b_qT])
                        else:
                            CP("act", kT[:, j4, (nk_ctx + t) * 128:(nk_ctx + t + 1) * 128], pv, [bp2], [b_kT])
                stagesA.append((FA, FB))
        stagesA[0][0]()
        for k in range(len(stagesA)):
            if k + 1 < len(stagesA):
                stagesA[k + 1][0]()
            stagesA[k][1]()
        if is_s:
            for kt in range(2):
                LD(kst[:], cdk[l, kt * 128:(kt + 1) * 128, :], b_kst)
                CP("pool", qb[0][:], kst[:], [b_kst], [b_qb[0]])
                for j4 in range(4):
                    ps2, bp2 = rr.get()
                    pv = ps2[:].bitcast(BF16)[:, 0:128]
                    transpose_to(pv, qb[0][:, j4 * 128:(j4 + 1) * 128], [b_qb[0]], [bp2])
                    CP("act", kT[:, j4, kt * 128:(kt + 1) * 128], pv, [bp2], [b_kT])
                LD(kst[:], cdv[l, kt * 128:(kt + 1) * 128, :], b_kst)
                CP("pool", vaug[:, kt, :, 0:128], kst[:].rearrange("p (h e) -> p h e", h=4), [b_kst], [b_va])
        Sched.PHASE = _p0 + 'B'
        wA, bwA = wload(w_in[l][:, C_DV:C_DV + 256], 8, 256)
        wB, bwB = wload(w_in[l][:, C_DV + 256:C_DV + 512], 8, 256)
        for t in range(8):
            ps, bp = rr.get()
            for k in range(8):
                MM(ps[:, 0:256], hT[:, k, t * 128:(t + 1) * 128], wA[:, k, :], k == 0, k == 7, [b_hT, bwA], [bp])
            for k in range(8):
                MM(ps[:, 256:512], hT[:, k, t * 128:(t + 1) * 128], wB[:, k, :], k == 0, k == 7, [b_hT, bwB], [bp])
            if sub != 31:
                kst, b_kst = kstL[t % 2], b_kstL[t % 2]
            CP("act", vaug[:, nk_ctx + t, :, 0:128], ps[:, :].rearrange("p (h e) -> p h e", h=4), [bp], [b_va])
            if not is_s and sub != 32:
                CP("dve", kst[:], ps[:, :], [bp], [b_kst])
                STO(o_dv[t // 2, l, (t % 2) * 128:(t % 2 + 1) * 128, :], kst[:], b_kst)
        if sub in (3, 31, 32):
            raise _Stop()
        Sched.PHASE = _p0 + 'C'
        obk = [(psum[4 + i], psb[4 + i]) for i in range(4)]
        pc_ = [0]
        stages = []
        for s in range(nseq):
            keyt = list(range(nk_ctx)) + [nk_ctx + s * nt + i for i in range(nt)]
            nq = min(L, 512)
            for qc in range(L // nq):
                q0 = s * L + qc * nq
                nqt = nq // 128
                for h in range(4):
                    for c in range(2):
                        ksl = slice(c * 64, (c + 1) * 64)
                        for ki, kt in enumerate(keyt):
                            st = {}

                            def A(st=st, ksl=ksl, h=h, kt=kt, q0=q0, nq=nq):
                                psS, bpS = rr.get()
                                MM(psS[:, 0:nq], kT[ksl, h, kt * 128:(kt + 1) * 128], qT[ksl, h, q0:q0 + nq], True, True, [b_kT, b_qT], [bpS])
                                z = pc_[0] % 4; pc_[0] += 1
                                st["z"] = z
                                ACT(pT[z][:, 0:nq], psS[:, 0:nq], AF.Exp, [bpS], [b_pT[z]])

                            def B(st=st, h=h, c=c, kt=kt, ki=ki, nk=len(keyt), nqt=nqt, q0=q0):
                                z = st["z"]
                                for qt in range(nqt):
                                    MM(obk[qt][0][:, 0:129], pT[z][:, qt * 128:(qt + 1) * 128], vaug[:, kt, h, 0:129], ki == 0, ki == nk - 1, [b_pT[z], b_va], [obk[qt][1]])
                                if ki != nk - 1:
                                    return
                                for qt in range(nqt):
                                    tq = q0 // 128 + qt
                                    ob, bob = obk[qt]
                                    S.op("dve", lambda e, ob=ob, qt=qt, c=c: e.reciprocal(out=rd[:, qt * 2 + c:qt * 2 + c + 1], in_=ob[:, 128:129]), [bob], [b_rd])
                                    if c == 0:
                                        TS("dve", o1[:, qt, :], ob[:, 0:128], rd[:, qt * 2:qt * 2 + 1], ALU.mult, [bob, b_rd], [b_o1])
                                    else:
                                        TT("dve", rd[:, qt * 2 + 1:qt * 2 + 2], rd[:, qt * 2 + 1:qt * 2 + 2], lamc[:, 2:3], ALU.mult, [b_rd, b_lam], [b_rd])
                                        STT(oh[:], ob[:, 0:128], rd[:, qt * 2 + 1:qt * 2 + 2], o1[:, qt, :], ALU.mult, ALU.add, [bob, b_rd, b_o1], [b_oh])
                                        ACT(junk[:, 0:128], oh[:], AF.Square, [b_oh], [b_junk, b_small], accum=small[:, 40:41])
                                        rstd_from_ss(small[:, 40:41], 128, small[:, 41:42], [b_small], [b_small])
                                        TS("dve", odn[:, tq, h * 128:(h + 1) * 128], oh[:], small[:, 41:42], ALU.mult, [b_oh, b_small], [b_odn])
                            stages.append((A, B))
        LA = 3
        for k in range(min(LA, len(stages))):
            stages[k][0]()
        for k in range(len(stages)):
            if k + LA < len(stages):
                stages[k + LA][0]()
            stages[k][1]()
        if sub == 4:
            raise _Stop()
        Sched.PHASE = _p0 + 'D'
        for t in range(8):
            for h in range(4):
                ps2, bp2 = rr.get()
                pv = ps2[:].bitcast(BF16)[:, 0:128]
                transpose_to(pv, odn[:, t, h * 128:(h + 1) * 128], [b_odn], [bp2])
                yt_, by_ = yT(2, h)
                ACT(yt_[:, t * 128:(t + 1) * 128], pv, AF.Copy, [bp2, b_gsub], [by_], scale=gsub[:, 0:1])
        S.op("pool", lambda e: e.memset(gq[:, 0:1], 0.0), [], ([b_qT, b_kT, b_va, b_g, b_lam, b_o1, b_odn, b_rd, b_oh, b_gsub] + b_kstL + b_qb + b_pT + b_qfL + b_sqL + b_rsL) + [b_scr_all])

    def branch_win(l, path, nseq, L, nt, is_s):
        barrier_begin()
        cv = Carve()
        nk_ctx = 2 if is_s else 0
        NKT = 8 + nk_ctx
        qT = cv.take([4, T], BF16); b_qT = NB("wqT")
        kT = cv.take([2, NKT * 128], BF16); b_kT = NB("wkT")
        vaug = cv.take([NKT, 2, 66], BF16); b_va = NB("wvaug")
        qfL = [cv.take([512]) for _ in range(2)]; b_qfL = [NB(f"wqf{i}") for i in range(2)]
        sqL = [cv.take([512]) for _ in range(2)]; b_sqL = [NB(f"wsq{i}") for i in range(2)]
        rsL = [cv.take([16]) for _ in range(2)]; b_rsL = [NB(f"wrs{i}") for i in range(2)]
        rot_ = [0]

        def nxt():
            i = rot_[0] % 2; rot_[0] += 1
            return qfL[i], b_qfL[i], sqL[i], b_sqL[i], rsL[i], b_rsL[i]
        qb = [cv.take([512], BF16), cv.take([512], BF16)]; b_qb = [NB("wqb0"), NB("wqb1")]
        gq = cv.take([64]); gk = cv.take([64]); b_g = NB("wg")
        snk = cv.take([8]); b_snk = NB("snk")
        on = cv.take([8, 512], BF16); b_on = NB("won")
        pT = [cv.take([512], BF16) for _ in range(4)]; b_pT = [NB(f"wpT{i}") for i in range(4)]
        rd = cv.take([8]); b_rd = NB("wrd")
        kst = cv.take([256]); b_kst = NB("wkst")
        S.op("pool", lambda e: e.memset(gq[:], 0.0), [b_scr_all], [b_g, b_scr_all])
        LD(gq[:], wqg[l:l + 1, :].partition_broadcast(128), b_g); LD(gk[:], wkg[l:l + 1, :].partition_broadcast(128), b_g, group=True)
        TS("dve", gq[:], gq[:], 0.125, ALU.mult, [b_g], [b_g])
        LD(snk[:], wsink[l:l + 1, :].partition_broadcast(128), b_snk)
        ACT(snk[:], snk[:], AF.Exp, [b_snk], [b_snk])
        MSET("pool", vaug[:].rearrange("p a b c -> p (a b c)"), 1.0, [b_va])
        wA, bwA = wload(w_in[l][:, C_WQ:C_WQ + 256], 8, 256)
        wB, bwB = wload(w_in[l][:, C_WQ + 256:C_WQ + 512], 8, 256)
        stq = []
        for t in range(8):
            st = {}

            def QA(st=st, t=t):
                ps, bp = rr.get()
                for k in range(8):
                    MM(ps[:, 0:256], hT[:, k, t * 128:(t + 1) * 128], wA[:, k, :], k == 0, k == 7, [b_hT, bwA], [bp])
                for k in range(8):
                    MM(ps[:, 256:512], hT[:, k, t * 128:(t + 1) * 128], wB[:, k, :], k == 0, k == 7, [b_hT, bwB], [bp])
                qf, b_qf, sq, b_sq, rs, b_rs = nxt()
                rms_groups(ps, bp, 512, gq[:], b_g, qf, b_qf, sq, b_sq, rs, b_rs, t, is_s)
                st["q"] = (qf, b_qf)

            def QB(st=st, t=t):
                qf, b_qf = st["q"]
                qb_, bqb_ = qb[t % 2], b_qb[t % 2]
                CP("pool", qb_[:], qf[:], [b_qf], [bqb_])
                for j4 in range(4):
                    ps2, bp2 = rr.get()
                    pv = ps2[:].bitcast(BF16)[:, 0:128]
                    transpose_to(pv, qb_[:, j4 * 128:(j4 + 1) * 128], [bqb_], [bp2])
                    CP("act", qT[:, j4, t * 128:(t + 1) * 128], pv, [bp2], [b_qT])
            stq.append((QA, QB))
        stq[0][0]()
        for k in range(8):
            if k + 1 < 8:
                stq[k + 1][0]()
            stq[k][1]()
        wK, bwK = wload(w_in[l][:, C_WK:C_WK + 256], 8, 256)

        def put_k(src_f32, bsrc, ktile):
            for n in range(2):
                CP("pool", qb[0][:, 0:64], src_f32[:, n * 64:(n + 1) * 64], [bsrc], [b_qb[0]])
                CP("pool", qb[0][:, 64:128], src_f32[:, n * 64:(n + 1) * 64], [bsrc], [b_qb[0]])
                ps2, bp2 = rr.get()
                pv = ps2[:].bitcast(BF16)[:, 0:128]
                transpose_to(pv, qb[0][:, 0:128], [b_qb[0]], [bp2])
                CP("act", kT[:, n, ktile * 128:(ktile + 1) * 128], pv, [bp2], [b_kT])
        for t in range(8):
            ps, bp = rr.get()
            for k in range(8):
                MM(ps[:, 0:256], hT[:, k, t * 128:(t + 1) * 128], wK[:, k, :], k == 0, k == 7, [b_hT, bwK], [bp])
            CP("act", vaug[:, nk_ctx + t, :, 0:64], ps[:, 128:256].rearrange("p (n e) -> p n e", n=2), [bp], [b_va])
            qf, b_qf, sq, b_sq, rs, b_rs = nxt()
            if not is_s:
                CP("dve", kst[:, 128:256], ps[:, 128:256], [bp], [b_kst])
                STO(o_wv[t // 2, l, (t % 2) * 128:(t % 2 + 1) * 128, :], kst[:, 128:256], b_kst)
                rms_groups(ps, bp, 128, gk[:], b_g, kst, b_kst, sq, b_sq, rs, b_rs, t, False)
                STO(o_wk[t // 2, l, (t % 2) * 128:(t % 2 + 1) * 128, :], kst[:, 0:128], b_kst)
                put_k(kst, b_kst, nk_ctx + t)
            else:
                rms_groups(ps, bp, 128, gk[:], b_g, qf, b_qf, sq, b_sq, rs, b_rs, t, True)
                put_k(qf, b_qf, nk_ctx + t)
        if is_s:
            for kt in range(2):
                LD(kst[:, 0:128], cwk[l, kt * 128:(kt + 1) * 128, :], b_kst)
                put_k(kst, b_kst, kt)
                LD(kst[:, 128:256], cwv[l, kt * 128:(kt + 1) * 128, :], b_kst)
                CP("pool", vaug[:, kt, :, 0:64], kst[:, 128:256].rearrange("p (n e) -> p n e", n=2), [b_kst], [b_va])
        obk = [(psum[4 + i], psb[4 + i]) for i in range(4)]
        pc_ = [0]
        stages = []

        def evac(ob, bob, qt, tq, h):
            TT("dve", rd[:, qt:qt + 1], ob[:, 64:65], snk[:, h:h + 1], ALU.add, [bob, b_snk], [b_rd])
            S.op("dve", lambda e, qt=qt: e.reciprocal(out=rd[:, qt:qt + 1], in_=rd[:, qt:qt + 1]), [b_rd], [b_rd])
            TS("dve", on[:, tq, h * 64:(h + 1) * 64], ob[:, 0:64], rd[:, qt:qt + 1], ALU.mult, [bob, b_rd], [b_on])
        for h in range(8):
            n = h // 4
            j4 = h // 2
            bsl = slice((h % 2) * 64, (h % 2 + 1) * 64)
            if not is_s:
                for s in range(nseq):
                    q0 = s * L
                    keyt = [s * nt + i for i in range(nt)]
                    for ki, kt in enumerate(keyt):
                        st = {}

                        def A(st=st, bsl=bsl, n=n, j4=j4, kt=kt, q0=q0):
                            psS, bpS = rr.get()
                            MM(psS[:, 0:L], kT[bsl, n, kt * 128:(kt + 1) * 128], qT[bsl, j4, q0:q0 + L], True, True, [b_kT, b_qT], [bpS])
                            z = pc_[0] % 4; pc_[0] += 1
                            st["z"] = z
                            ACT(pT[z][:, 0:L], psS[:, 0:L], AF.Exp, [bpS], [b_pT[z]])

                        def B(st=st, n=n, kt=kt, ki=ki, nk=len(keyt), s=s, h=h):
                            z = st["z"]
                            for qt in range(nt):
                                MM(obk[qt][0][:, 0:65], pT[z][:, qt * 128:(qt + 1) * 128], vaug[:, kt, n, 0:65], ki == 0, ki == nk - 1, [b_pT[z], b_va], [obk[qt][1]])
                            if ki == nk - 1:
                                for qt in range(nt):
                                    evac(obk[qt][0], obk[qt][1], qt, s * nt + qt, h)
                        stages.append((A, B))
            else:
                for tq in range(8):
                    qt = tq % 4
                    keys = [(0, None), (1, None)]
                    if tq > 0:
                        keys.append((nk_ctx + tq - 1, "prev"))
                    keys.append((nk_ctx + tq, None))
                    if tq < 7:
                        keys.append((nk_ctx + tq + 1, "next"))
                    for ki, (kt, msk) in enumerate(keys):
                        st = {}

                        def A(st=st, bsl=bsl, n=n, j4=j4, kt=kt, tq=tq, msk=msk):
                            psS, bpS = rr.get()
                            MM(psS[:, 0:128], kT[bsl, n, kt * 128:(kt + 1) * 128], qT[bsl, j4, tq * 128:(tq + 1) * 128], True, True, [b_kT, b_qT], [bpS])
                            z = pc_[0] % 4; pc_[0] += 1
                            st["z"] = z
                            ACT(pT[z][:, 0:128], psS[:, 0:128], AF.Exp, [bpS], [b_pT[z]])
                            if msk is not None:
                                TT("dve", pT[z][:, 0:128], pT[z][:, 0:128], (tril if msk == "prev" else triu)[:], ALU.mult, [b_pT[z], b_tril, b_triu], [b_pT[z]])

                        def B(st=st, n=n, kt=kt, ki=ki, nk=len(keys), qt=qt, tq=tq, h=h):
                            z = st["z"]
                            ob, bob = obk[qt]
                            MM(ob[:, 0:65], pT[z][:, 0:128], vaug[:, kt, n, 0:65], ki == 0, ki == nk - 1, [b_pT[z], b_va], [bob])
                            if ki == nk - 1:
                                evac(ob, bob, qt, tq, h)
                        stages.append((A, B))
        LA = 3
        for k in range(min(LA, len(stages))):
            stages[k][0]()
        for k in range(len(stages)):
            if k + LA < len(stages):
                stages[k + LA][0]()
            stages[k][1]()
        for t in range(8):
            for j4 in range(4):
                ps2, bp2 = rr.get()
                pv = ps2[:].bitcast(BF16)[:, 0:128]
                transpose_to(pv, on[:, t, j4 * 128:(j4 + 1) * 128], [b_on], [bp2])
                yt_, by_ = yT(3, j4)
                CP("act", yt_[:, t * 128:(t + 1) * 128], pv, [bp2], [by_])
        S.op("pool", lambda e: e.memset(gq[:, 0:1], 0.0), [], ([b_qT, b_kT, b_va, b_g, b_snk, b_on, b_rd, b_kst] + b_qb + b_pT + b_qfL + b_sqL + b_rsL) + [b_scr_all])

    def merge(l):
        rr.set(range(8))
        barrier_begin()
        cv = Carve()
        mT = cv.take([8, T], BF16); b_mT = [NB(f"mT{c}") for c in range(8)]
        gs = [cv.take([512]) for _ in range(4)]; b_gs = [NB(f"gs{i}") for i in range(4)]
        tmp = [cv.take([512]) for _ in range(4)]; b_tmp = [NB(f"mt{i}") for i in range(4)]
        bgt = cv.take([32]); b_bgt = NB("bgt")
        S.op("pool", lambda e: e.memset(bgt[:], 0.0), [b_scr_all], [b_bgt, b_scr_all])
        LD(bgt[:], bgatec[l], b_bgt)
        kq = [0]
        for dcp in range(4):
            for br in range(4):
                wg, bwg = wload(w_gate[l][:, br * 1024 + dcp * 256: br * 1024 + (dcp + 1) * 256], 8, 256)
                wb_, bwb_ = wload(w_br[l][br * 512:(br + 1) * 512, dcp * 256:(dcp + 1) * 256], 4, 256)
                for c2 in range(2):
                    dc = dcp * 2 + c2
                    for h in range(2):
                        hs = slice(h * 512, (h + 1) * 512)
                        ti_ = c2 * 2 + h
                        psg, bpg = rr.get()
                        for k in range(8):
                            MM(psg[:, :], wg[:, k, c2 * 128:(c2 + 1) * 128], hT[:, k, hs], k == 0, k == 7, [b_hT, bwg], [bpg])
                        z = kq[0] % 4; kq[0] += 1
                        ACT(gs[z][:], psg[:, :], AF.Sigmoid, [bpg, b_bgt], [b_gs[z]], bias=bgt[:, br * 8 + dc:br * 8 + dc + 1])
                        psb_, bpb_ = rr.get()
                        for k in range(4):
                            yt_, by_ = yT(br, k)
                            MM(psb_[:, :], wb_[:, k, c2 * 128:(c2 + 1) * 128], yt_[:, hs], k == 0, k == 3, [by_, bwb_], [bpb_])
                        if br == 0:
                            TT("dve", tmp[ti_][:], psb_[:, :], gs[z][:], ALU.mult, [bpb_, b_gs[z]], [b_tmp[ti_]])
                        else:
                            TT("dve", gs[z][:], psb_[:, :], gs[z][:], ALU.mult, [bpb_, b_gs[z]], [b_gs[z]])
                            if br < 3:
                                TT("pool", tmp[ti_][:], tmp[ti_][:], gs[z][:], ALU.add, [b_tmp[ti_], b_gs[z]], [b_tmp[ti_]])
                            else:
                                TT("pool", mT[:, dc, hs], tmp[ti_][:], gs[z][:], ALU.add, [b_tmp[ti_], b_gs[z]], [b_mT[dc]])
        if sub == 54:
            raise _Stop()
        for cb4 in range(4):
            w, bw = wload(w_out[l][:, cb4 * 256:(cb4 + 1) * 256], 8, 256)
            for t in range(8):
                ps, bp = rr.get()
                for k in range(8):
                    MM(ps[:, 0:256], mT[:, k, t * 128:(t + 1) * 128], w[:, k, :], k == 0, k == 7, [b_mT[k], bw], [bp])
                cs = slice(cb4 * 256, (cb4 + 1) * 256)
                zz = kq[0] % 4; kq[0] += 1
                TT("dve", gs[zz][:, 0:256], ps[:, 0:256], gbc[:, 0, cs], ALU.mult, [bp, b_gbc[0]], [b_gs[zz]])
                TT("pool", xres[:, t, cs], xres[:, t, cs], gs[zz][:, 0:256], ALU.add, [b_xres[t], b_gs[zz]], [b_xres[t]])
        S.op("pool", lambda e: e.memset(bgt[:, 0:1], 0.0), [], (b_mT + b_gs + b_tmp + [b_bgt]) + [b_scr_all])
        rr.set(range(4))

    def mlp(l):
        barrier_begin()
        cvm = Carve()
        rl = [cvm.take([512]) for _ in range(4)]; b_rl = [NB(f"rl{i}") for i in range(4)]
        rs_ = [cvm.take([256]) for _ in range(4)]; b_rs_ = [NB(f"rsd{i}") for i in range(4)]
        mk = [0, 0]
        for h in range(2):
            hs = slice(h * 512, (h + 1) * 512)

            def aTv(kc):
                return big[:, kc // 2, (kc % 2) * 512:(kc % 2 + 1) * 512], b_big[kc // 2]
            for blk in range(16):
                w, bw = wload(w_fc1[l][:, blk * 256:(blk + 1) * 256], 8, 256)
                for c2 in range(2):
                    kc = blk * 2 + c2
                    ps, bp = rr.get()
                    for k in range(8):
                        MM(ps[:, :], w[:, k, c2 * 128:(c2 + 1) * 128], hT[:, k, hs], k == 0, k == 7, [b_hT, bw], [bp])
                    a_, ba_ = aTv(kc)
                    zr = mk[0] % 4; mk[0] += 1
                    ACT(rl[zr][:], ps[:, :], AF.Relu, [bp], [b_rl[zr]])
                    TT("dve", a_, rl[zr][:], rl[zr][:], ALU.mult, [b_rl[zr]], [ba_])
            for cb4 in range(4):
                cs = slice(cb4 * 256, (cb4 + 1) * 256)
                accb = [(psum[4 + i], psb[4 + i]) for i in range(4)]
                for kg in range(4):
                    w, bw = wload(w_fc2[l][kg * 1024:(kg + 1) * 1024, cs], 8, 256)
                    for tt in range(4):
                        for k in range(8):
                            kc = kg * 8 + k
                            a_, ba_ = aTv(kc)
                            MM(accb[tt][0][:, 0:256], a_[:, tt * 128:(tt + 1) * 128], w[:, k, :], kc == 0, kc == 31, [ba_, bw], [accb[tt][1]])
                for tt in range(4):
                    t = h * 4 + tt
                    zq = mk[1] % 4; mk[1] += 1
                    TT("dve", rs_[zq][:], accb[tt][0][:, 0:256], gbc[:, 1, cs], ALU.mult, [accb[tt][1], b_gbc[1]], [b_rs_[zq]])
                    TT("pool", xres[:, t, cs], xres[:, t, cs], rs_[zq][:], ALU.add, [b_xres[t], b_rs_[zq]], [b_xres[t]])

        S.op("pool", lambda e: e.memset(small[:, 62:63], 0.0), [], b_rl + b_rs_ + [b_scr_all])

    try:
        Sched.PHASE = "prologue"
        adaln_weights(0)
        adaln_weights(1)
        run_pass(0)
        run_pass(1)
    except _Stop:
        pass
    if stop is not None:
        d_hT = nc.dram_tensor("dbg_hT", [128, 8, T], BF16, kind="ExternalOutput").ap()
        d_big = nc.dram_tensor("dbg_big", [128, 16, 1024], BF16, kind="ExternalOutput").ap()
        d_x = nc.dram_tensor("dbg_x", [128, 8, D], F32, kind="ExternalOutput").ap()
        d_modc = nc.dram_tensor("dbg_modc", [128, 48], F32, kind="ExternalOutput").ap()
        S.dma("sp", d_hT[:, :, :], hT[:], reads=[b_hT])
        S.dma("sp", d_big[:, :, :], big[:], reads=b_big, sbuf=b_big[0])
        S.dma("sp", d_x[:, :, :], xres[:], reads=b_xres, sbuf=b_xres[0])
        S.dma("sp", d_modc[:, :], modc[:], reads=[b_modc])
    with nc.Block() as block:
        S.emit(block)
    es.close()
    nc._phases = {e: [o.phase for o in S.ops[e]] for e in ENGS}
    return nc


_NC_CACHE = {}


def _consts():
    bf = ml_dtypes.bfloat16
    c = {}
    c["c_identb"] = np.eye(128, dtype=np.float32).astype(bf)
    c["c_identf"] = np.eye(128, dtype=np.float32)
    c["c_ones"] = np.ones((128, 128), np.float32)
    k = np.arange(128)[:, None]; t = np.arange(128)[None, :]
    c["c_triu"] = (k <= t).astype(np.float32)
    c["c_tril"] = (k >= t).astype(np.float32)
    c["c_mnegF"] = np.where(k <= t, 0.0, -1e30).astype(np.float32)
    c["c_mnegB"] = np.where(k >= t, 0.0, -1e30).astype(np.float32)
    c["c_bprev"] = (t <= k).astype(np.float32)
    c["c_bnext"] = (k <= t).astype(np.float32)
    mB = np.zeros((128, 4, 128), np.float32)
    mC = np.zeros((128, 4, 128), np.float32)
    for q in range(4):
        for gl in range(2):
            g8 = 2 * q + gl
            mB[gl * 64:(gl + 1) * 64, q, g8 * 16:(g8 + 1) * 16] = 1.0
            mC[g8 * 16:(g8 + 1) * 16, q, gl * 64:(gl + 1) * 64] = 1.0
    c["c_maskB"] = mB; c["c_maskC"] = mC
    c["c_iota"] = np.broadcast_to(np.arange(1024, dtype=np.float32)[None, :], (128, 1024)).copy()
    Ls = 1024
    row = np.repeat(np.arange(Ls // 64), 64).astype(np.float32); col = np.tile(np.arange(64), Ls // 64).astype(np.float32)
    nf = 16
    inv = (10000.0 ** (-np.arange(nf, dtype=np.float32) / nf)).astype(np.float32)
    ang = np.concatenate([row[:, None] * inv, col[:, None] * inv], axis=-1).astype(np.float32)
    cs, sn = np.cos(ang).astype(np.float32), np.sin(ang).astype(np.float32)
    C64 = np.zeros((Ls, 2, 2, 16), np.float32); S64 = np.zeros((Ls, 2, 2, 16), np.float32)
    for a in range(2):
        for p in range(2):
            C64[:, a, p, :] = cs[:, a * 16:(a + 1) * 16]
            S64[:, a, p, :] = (-1.0 if p == 0 else 1.0) * sn[:, a * 16:(a + 1) * 16]
    c["c_ropeC"] = C64.reshape(8, 128, 64).transpose(1, 0, 2).copy()
    c["c_ropeS"] = S64.reshape(8, 128, 64).transpose(1, 0, 2).copy()
    lidx = np.arange(1024)
    c["c_rmF"] = np.broadcast_to((lidx % 256 != 0).astype(np.float32)[None, :], (128, 1024)).astype(bf)
    c["c_rmB"] = np.broadcast_to((lidx % 256 != 255).astype(np.float32)[None, :], (128, 1024)).astype(bf)
    return c


def _colmajor(v, nchunk):
    return np.ascontiguousarray(np.swapaxes(v.reshape(v.shape[:-1] + (nchunk, 128)), -1, -2))


def make_in_maps(inp):
    f = lambda a: np.ascontiguousarray(np.asarray(a, dtype=np.float32))
    I = {k: f(v) for k, v in inp.items()}
    shared = dict(_consts())
    shared.update({
        "w_mod": I["w_mod"], "w_in": I["w_in"], "w_gate": I["w_gate"], "w_out": I["w_out"], "w_fc1": I["w_fc1"], "w_fc2": I["w_fc2"],
        "w_glu": I["s5_w_glu"], "w_br": I["w_branch"].reshape(2, 2048, 1024),
        "bmodc": _colmajor(I["b_mod"], 48), "g1c": _colmajor(I["g_norm1"], 8), "g2c": _colmajor(I["g_norm2"], 8),
        "convw": np.ascontiguousarray(I["ssd_conv_w"].transpose(0, 2, 1).reshape(2, 6, 128, 7).transpose(0, 2, 1, 3)),
        "convb": _colmajor(I["ssd_conv_b"], 6),
        "dtb": I["ssd_dt_bias"].reshape(2, 16), "alog": I["ssd_a_log"].reshape(2, 16), "ssdd": I["ssd_d"], "normgc": _colmajor(I["ssd_norm_g"], 4),
        "lamre": I["s5_lam_re"].reshape(2, 32, 128), "lamim": I["s5_lam_im"].reshape(2, 32, 128),
        "lsx": np.ascontiguousarray(np.repeat(I["s5_log_step"].reshape(2, 2, 32, 1), 64, axis=-1).reshape(2, 32, 128)),
        "s5bre": I["s5_b_re"].reshape(2, 2, 2048, 16), "s5bim": I["s5_b_im"].reshape(2, 2, 2048, 16),
        "s5cre": I["s5_c_re"].reshape(2, 2, 512, 64), "s5cim": I["s5_c_im"].reshape(2, 2, 512, 64),
        "s5dc": _colmajor(I["s5_d"], 4), "bgluc": _colmajor(I["s5_b_glu"], 8),
        "dqg": I["diff_qn_g"], "dkg": I["diff_kn_g"], "dlam": I["diff_lambda"].reshape(2, 256), "dsubc": I["diff_subln_g"].reshape(2, 128, 1),
        "wqg": I["win_qn_g"], "wkg": I["win_kn_g"], "wsink": I["win_sink"], "bgatec": _colmajor(I["b_gate"], 32),
    })
    in_maps = []
    for i in range(8):
        b = i // 2
        cv = np.stack([I["c_ctx"], I["c"][b]], axis=0)
        m = dict(shared)
        m.update({
            "xp": I["x_prompt"][4 * i:4 * i + 4].reshape(1024, 1024), "xs": I["x_sample"][b],
            "cvT": np.ascontiguousarray(cv.reshape(2, 8, 128).transpose(2, 1, 0)),
            "st_ssd": I["state_ssd"][b], "st_s5": I["state_s5"][b].reshape(2, 64, 128),
            "cdk": I["cache_diff_k"][b].reshape(2, 256, 512), "cdv": I["cache_diff_v"][b].reshape(2, 256, 512),
            "cwk": I["cache_win_k"][b].reshape(2, 256, 128), "cwv": I["cache_win_v"][b].reshape(2, 256, 128),
        })
        in_maps.append({k: np.ascontiguousarray(v) for k, v in m.items()})
    return in_maps


def kernel(**inp):
    if "nc" not in _NC_CACHE:
        _NC_CACHE["nc"] = build_program()
    nc = _NC_CACHE["nc"]
    in_maps = make_in_maps(inp)
    res = run_bass_kernel_spmd(nc, in_maps, core_ids=list(range(8)))
    R = res.results
    yp = np.concatenate([R[i]["yp"].reshape(4, 256, 1024) for i in range(8)], axis=0)
    ys = np.stack([R[2 * b]["ys"] for b in range(4)], axis=0)
    ssd = np.concatenate([R[i]["o_ssd"] for i in range(8)], axis=0)
    s5 = np.concatenate([R[i]["o_s5"].reshape(4, 2, 2, 2, 32, 64) for i in range(8)], axis=0)
    dk = np.concatenate([R[i]["o_dk"].reshape(4, 2, 256, 4, 2, 64) for i in range(8)], axis=0)
    dv = np.concatenate([R[i]["o_dv"].reshape(4, 2, 256, 4, 128) for i in range(8)], axis=0)
    wk = np.concatenate([R[i]["o_wk"].reshape(4, 2, 256, 2, 64) for i in range(8)], axis=0)
    wv = np.concatenate([R[i]["o_wv"].reshape(4, 2, 256, 2, 64) for i in range(8)], axis=0)
    return tuple(np.ascontiguousarray(a.astype(np.float32)) for a in (yp, ys, ssd, s5, dk, dv, wk, wv))
```

```python
import math
import numpy as np
from contextlib import ExitStack
import ml_dtypes
import concourse.bass as bass
import concourse.mybir as mybir
from concourse.bass_utils import run_bass_kernel_spmd

F32 = mybir.dt.float32
BF16 = mybir.dt.bfloat16
I32 = mybir.dt.int32
ALU = mybir.AluOpType
AF = mybir.ActivationFunctionType
AX = mybir.AxisListType
ENGS = ("pe", "dve", "act", "pool", "sp")
EPS = 1e-6


class Buf:
    __slots__ = ("name", "last_w", "readers", "load_sem", "load_cnt", "store_sem", "store_cnt", "excl")

    def __init__(self, name, excl=False):
        self.name = name
        self.excl = excl
        self.last_w = None
        self.readers = []
        self.load_sem = None
        self.load_cnt = 0
        self.store_sem = None
        self.store_cnt = 0


class Op:
    __slots__ = ("eng", "fn", "deps", "signal", "semval", "is_dma", "dsem", "dval", "phase")

    def __init__(self, eng, fn):
        self.phase = Sched.PHASE
        self.eng = eng
        self.fn = fn
        self.deps = []
        self.signal = False
        self.semval = 0
        self.is_dma = False
        self.dsem = None
        self.dval = 0


class Sched:
    PHASE = ""

    def __init__(self, nc, es):
        self.nc = nc
        self.es = es
        self.ops = {e: [] for e in ENGS}
        self.sems = {e: es.enter_context(nc.semaphore("c_" + e)) for e in ENGS}
        self.store_bufs = []
        self.nsem = 5
        self.pool = {}

    def new_sem(self, name):
        self.nsem += 1
        return self.es.enter_context(self.nc.semaphore(f"{name}_{self.nsem}"))

    def _track(self, op, reads, writes, skip_waw=False):
        deps = op.deps
        for r in reads:
            if r.last_w is not None and r.last_w is not op:
                deps.append(r.last_w)
            if r.excl:
                deps.extend(x for x in r.readers if x is not op and x.eng != op.eng)
            r.readers.append(op)
        for w in writes:
            if w.last_w is not None and w.last_w is not op and not skip_waw:
                deps.append(w.last_w)
            deps.extend(r for r in w.readers if r is not op)
            w.last_w = op
            w.readers = []

    def op(self, eng, fn, reads=(), writes=()):
        o = Op(eng, fn)
        self._track(o, reads, writes)
        self.ops[eng].append(o)
        return o

    def dma(self, q, out, in_, reads=(), writes=(), group=False, sbuf=None, **kw):
        o = Op(q, lambda e: e.dma_start(out=out, in_=in_, **kw))
        o.is_dma = True
        self._track(o, reads, writes, skip_waw=group)
        if sbuf is None:
            sbuf = writes[0] if writes else reads[0]
        key = ("l_" if sbuf in writes else "s_") + sbuf.name
        ent = self.pool.get(key)
        if ent is None:
            ent = [self.new_sem(key), 0]
            self.pool[key] = ent
        ent[1] += 16
        o.dsem, o.dval = ent[0], ent[1]
        self.ops[q].append(o)
        return o

    def emit(self, block):
        for e in ENGS:
            for o in self.ops[e]:
                for d in o.deps:
                    if not d.is_dma and not (d.eng == "pe" and o.eng == "pe"):
                        d.signal = True
        for e in ENGS:
            v = 0
            for o in self.ops[e]:
                if o.signal and not o.is_dma:
                    v += 1
                    o.semval = v
        engmap = {"pe": block.tensor, "dve": block.vector, "act": block.scalar,
                  "pool": block.gpsimd, "sp": block.sync}
        sems = self.sems
        store_bufs = self.store_bufs
        for e in ENGS:
            def body(eng, ops=self.ops[e], e=e):
                known = {}
                for o in ops:
                    need = {}
                    for d in o.deps:
                        if d.is_dma:
                            key, val = d.dsem, d.dval
                        else:
                            if d.eng == "pe" and e == "pe":
                                continue
                            key, val = sems[d.eng], d.semval
                        if need.get(key, 0) < val:
                            need[key] = val
                    for key, val in need.items():
                        if known.get(key, 0) < val:
                            eng.wait_ge(key, val)
                            known[key] = val
                    inst = o.fn(eng)
                    if o.is_dma:
                        inst.then_inc(o.dsem, 16)
                    elif o.signal:
                        inst.then_inc(sems[e], 1)
                if e == "sp":
                    for key, ent in self.pool.items():
                        if key.startswith("s_"):
                            eng.wait_ge(ent[0], ent[1])
            engmap[e](body)


D = 1024
T = 1024
W_IN = 4112
C_Z, C_XBC, C_DT, C_U, C_DQ, C_DK, C_DV, C_WQ, C_WK, C_WV = 0, 512, 1280, 1296, 1808, 2320, 2832, 3344, 3856, 3984


class _Stop(Exception):
    pass


def build_program(stop=None, sub=None):
    nc = bass.Bass("TRN2", target_bir_lowering=False)
    es = ExitStack()
    S = Sched(nc, es)

    def din(name, shape, dt=F32):
        return nc.dram_tensor(name, list(shape), dt, kind="ExternalInput").ap()

    def dout(name, shape):
        return nc.dram_tensor(name, list(shape), F32, kind="ExternalOutput").ap()

    cnt = [0]

    def sb(shape, dt=F32, name=None):
        cnt[0] += 1
        return es.enter_context(nc.sbuf_tensor(name or f"t{cnt[0]}", list(shape), dt))

    xin = [din("xp", [T, D]), din("xs", [T, D])]
    yout = [dout("yp", [T, D]), dout("ys", [T, D])]
    cvT_d = din("cvT", [128, 8, 2])
    st_ssd = din("st_ssd", [2, 2, 8, 64, 64])
    st_s5 = din("st_s5", [2, 64, 128])
    cdk = din("cdk", [2, 256, 512]); cdv = din("cdv", [2, 256, 512])
    cwk = din("cwk", [2, 256, 128]); cwv = din("cwv", [2, 256, 128])
    w_mod = din("w_mod", [2, D, 6 * D]); w_in = din("w_in", [2, D, W_IN]); w_gate = din("w_gate", [2, D, 4 * D])
    w_out = din("w_out", [2, D, D]); w_fc1 = din("w_fc1", [2, D, 4 * D]); w_fc2 = din("w_fc2", [2, 4 * D, D])
    w_glu = din("w_glu", [2, 512, 1024]); w_br = din("w_br", [2, 2048, 1024])
    bmodc = din("bmodc", [2, 128, 48]); g1c = din("g1c", [2, 128, 8]); g2c = din("g2c", [2, 128, 8])
    convw = din("convw", [2, 128, 6, 7]); convb = din("convb", [2, 128, 6])
    dtb = din("dtb", [2, 16]); alog = din("alog", [2, 16]); ssdd = din("ssdd", [2, 8]); normgc = din("normgc", [2, 128, 4])
    lamre = din("lamre", [2, 32, 128]); lamim = din("lamim", [2, 32, 128]); lsx = din("lsx", [2, 32, 128])
    s5bre = din("s5bre", [2, 2, 2048, 16]); s5bim = din("s5bim", [2, 2, 2048, 16])
    s5cre = din("s5cre", [2, 2, 512, 64]); s5cim = din("s5cim", [2, 2, 512, 64])
    s5dc = din("s5dc", [2, 128, 4]); bgluc = din("bgluc", [2, 128, 8])
    dqg = din("dqg", [2, 64]); dkg = din("dkg", [2, 64]); dlam = din("dlam", [2, 256]); dsubc = din("dsubc", [2, 128, 1])
    wqg = din("wqg", [2, 64]); wkg = din("wkg", [2, 64]); wsink = din("wsink", [2, 8]); bgatec = din("bgatec", [2, 128, 32])
    c_identb = din("c_identb", [128, 128], BF16); c_identf = din("c_identf", [128, 128]); c_ones = din("c_ones", [128, 128])
    c_triu = din("c_triu", [128, 128]); c_tril = din("c_tril", [128, 128])
    c_mnegF = din("c_mnegF", [128, 128]); c_mnegB = din("c_mnegB", [128, 128])
    c_bprev = din("c_bprev", [128, 128]); c_bnext = din("c_bnext", [128, 128])
    c_maskB = din("c_maskB", [128, 4, 128]); c_maskC = din("c_maskC", [128, 4, 128])
    c_iota = din("c_iota", [128, 1024]); c_ropeC = din("c_ropeC", [128, 8, 64]); c_ropeS = din("c_ropeS", [128, 8, 64])
    c_rmF = din("c_rmF", [128, 1024], BF16); c_rmB = din("c_rmB", [128, 1024], BF16)
    o_ssd = dout("o_ssd", [4, 2, 2, 8, 64, 64]); o_s5 = dout("o_s5", [4, 2, 64, 128])
    o_dk = dout("o_dk", [4, 2, 256, 512]); o_dv = dout("o_dv", [4, 2, 256, 512])
    o_wk = dout("o_wk", [4, 2, 256, 128]); o_wv = dout("o_wv", [4, 2, 256, 128])

    def TT(eng, out, in0, in1, op, r, w):
        S.op(eng, lambda e: e.tensor_tensor(out=out, in0=in0, in1=in1, op=op), r, w)

    def TS(eng, out, in0, s1, op0, r, w, s2=None, op1=None):
        if op1 is None:
            S.op(eng, lambda e: e.tensor_scalar(out=out, in0=in0, scalar1=s1, scalar2=None, op0=op0), r, w)
        else:
            S.op(eng, lambda e: e.tensor_scalar(out=out, in0=in0, scalar1=s1, scalar2=s2, op0=op0, op1=op1), r, w)

    def STT(out, in0, scalar, in1, op0, op1, r, w):
        S.op("dve", lambda e: e.scalar_tensor_tensor(out=out, in0=in0, scalar=scalar, in1=in1, op0=op0, op1=op1), r, w)

    def ACT(out, in_, func, r, w, scale=1.0, bias=None, accum=None):
        kw = {}
        if bias is not None:
            kw["bias"] = bias
        if accum is not None:
            kw["accum_out"] = accum
        S.op("act", lambda e: e.activation(out=out, in_=in_, func=func, scale=scale, **kw), r, w)

    def CP(eng, out, in_, r, w):
        if eng == "act":
            S.op("act", lambda e: e.copy(out=out, in_=in_), r, w)
        else:
            S.op(eng, lambda e: e.tensor_copy(out=out, in_=in_), r, w)

    def MM(out, lhsT, rhs, start, stop, r, w):
        S.op("pe", lambda e: e.matmul(out, lhsT=lhsT, rhs=rhs, start=start, stop=stop), r, w)

    def MSET(eng, out, val, w):
        S.op(eng, lambda e: e.memset(out, val), (), w)

    def LD(out, in_, b, q="sp", group=False):
        S.dma(q, out, in_, writes=[b], group=group)

    def STO(out, in_, b, q="sp"):
        S.dma(q, out, in_, reads=[b])

    def const(src, shape, dt=F32):
        t = sb(shape, dt)
        b = Buf(f"c{cnt[0]}")
        LD(t[:], src, b)
        return t, b

    identb, b_identb = const(c_identb[:, :], [128, 128], BF16)
    identf, b_identf = const(c_identf[:, :], [128, 128])
    onesf, b_ones = const(c_ones[:, :], [128, 128])
    triu, b_triu = const(c_triu[:, :], [128, 128]); tril, b_tril = const(c_tril[:, :], [128, 128])
    mnegF, b_mnegF = const(c_mnegF[:, :], [128, 128]); mnegB, b_mnegB = const(c_mnegB[:, :], [128, 128])
    maskB, b_maskB = const(c_maskB[:, :, :], [128, 4, 128]); maskC, b_maskC = const(c_maskC[:, :, :], [128, 4, 128])
    ropeC, b_ropeC = const(c_ropeC[:, :, :], [128, 8, 64]); ropeS, b_ropeS = const(c_ropeS[:, :, :], [128, 8, 64])
    CONSTB = [b_identb, b_identf, b_ones]

    psum = [es.enter_context(nc.psum_tensor(f"ps{i}", [128, 512], F32)) for i in range(8)]
    psb = [Buf(f"ps{i}", excl=True) for i in range(8)]

    class RR:
        def __init__(self, ids):
            self.ids = list(ids); self.i = 0

        def get(self):
            k = self.ids[self.i % len(self.ids)]; self.i += 1
            return psum[k], psb[k]

        def set(self, ids):
            self.ids = list(ids)

    rr = RR(range(0, 4))

    xres = sb([128, 8, D]); b_xres = [Buf(f"xres{t}") for t in range(8)]
    hT = sb([128, 8, T], BF16); b_hT = Buf("hT")
    NST, NBF = 3, 3
    wst = [sb([128, 8, 256]) for _ in range(NST)]; b_wst = [Buf(f"wst{i}") for i in range(NST)]
    wbf = [sb([128, 8, 256], BF16) for _ in range(NBF)]; b_wbf = [Buf(f"wbf{i}") for i in range(NBF)]
    wctr = [0, 0]
    big = sb([128, 16, 1024], BF16)
    b_big = [Buf(f"big{i}") for i in range(16)]
    modc = sb([128, 48]); b_modc = Buf("modc")
    scol = sb([128, 8, 2]); b_scol = Buf("scol")
    G1 = sb([128, 8]); SH1 = sb([128, 8]); G2 = sb([128, 8]); SH2 = sb([128, 8]); b_G = Buf("G")
    gbc = sb([128, 2, D]); b_gbc = [Buf("gbc0"), Buf("gbc1")]
    small = sb([128, 64]); b_small = Buf("small")
    junk = sb([128, 768]); b_junk = Buf("junk")
    SCR_BYTES = 60 * 1024
    scr = sb([128, SCR_BYTES // 4])

    def wload(src, nk, ncols, cast=True):
        i = wctr[0] % NST; wctr[0] += 1
        st, bs = wst[i], b_wst[i]
        LD(st[:, 0:nk, 0:ncols], src.rearrange("(k p) c -> p k c", p=128), bs)
        if not cast:
            return st, bs
        j = wctr[1] % NBF; wctr[1] += 1
        wb, bb = wbf[j], b_wbf[j]
        heavy = any(k in Sched.PHASE for k in ("prologue", "merge", "mlp"))
        eng = "act" if (wctr[1] % 2 == 0 or not heavy) else "dve"
        CP(eng, wb[:, 0:nk, 0:ncols], st[:, 0:nk, 0:ncols], [bs], [bb])
        return wb, bb

    def proj_tm(src, bsrc, nk, w, bw, ncols, tiles, evac):
        for t in tiles:
            ps, bp = rr.get()
            for k in range(nk):
                MM(ps[:, 0:ncols], src[:, k, t * 128:(t + 1) * 128], w[:, k, 0:ncols], k == 0, k == nk - 1, [bsrc, bw], [bp])
            evac(t, ps, bp)

    def proj_fm(src, bsrc, nk, w, bw, ncols, evac, halves=(0, 1)):
        for cc in range((ncols + 127) // 128):
            m = min(128, ncols - cc * 128)
            for h in halves:
                ps, bp = rr.get()
                for k in range(nk):
                    MM(ps[0:m, :], w[:, k, cc * 128:cc * 128 + m], src[:, k, h * 512:(h + 1) * 512], k == 0, k == nk - 1, [bsrc, bw], [bp])
                evac(cc, h, ps, bp)

    def transpose_to(ps_out, in_, r, w, dt=BF16, np_=128):
        idn = identb if dt == BF16 else identf
        S.op("pe", lambda e: e.transpose(out=ps_out, in_=in_, identity=idn[0:np_, 0:np_]), list(r) + CONSTB, w)

    def bcast_rows(col_ap, bcol, ps_out, bp):
        dg = sb_diag[dgc[0] % 4]; bd = b_diag[dgc[0] % 4]; dgc[0] += 1
        TS("dve", dg[:], identf[:], col_ap, ALU.mult, [b_identf, bcol], [bd])
        MM(ps_out, onesf[:], dg[:], True, True, [b_ones, bd], [bp])

    sb_diag = [sb([128, 128]) for _ in range(4)]; b_diag = [Buf(f"dg{i}") for i in range(4)]; dgc = [0]

    def rstd_from_ss(ss_ap, n, out_ap, r, w, ncols=1):
        TS("dve", out_ap, ss_ap, 1.0 / n, ALU.mult, r, w, s2=EPS, op1=ALU.add)
        ACT(out_ap, out_ap, AF.Sqrt, w, w)
        S.op("dve", lambda e: e.reciprocal(out=out_ap, in_=out_ap), w, w)

    LD(scol[:], cvT_d[:, :, :], b_scol)
    ACT(scol[:], scol[:], AF.Silu, [b_scol], [b_scol])
    scolb = sb([128, 8, 2], BF16)
    CP("dve", scolb[:], scol[:], [b_scol], [b_scol])

    modall = sb([128, 2, 2, 48]); b_modall = Buf("modall")

    def adaln_weights(l):
        rr.set(range(8))
        bm = sb_bm; LD(bm[:], bmodc[l], b_bm)
        for blk in range(24):
            w, bw = wload(w_mod[l][:, blk * 256:(blk + 1) * 256], 8, 256)
            for cc in range(2):
                ps, bp = rr.get()
                for k in range(8):
                    MM(ps[:, 0:2], w[:, k, cc * 128:(cc + 1) * 128], scolb[:, k, 0:2], k == 0, k == 7, [bw, b_scol], [bp])
                c = blk * 2 + cc
                TT("dve", modall[:, l, :, c], ps[:, 0:2], bm[:, c:c + 1].broadcast_to([128, 2]), ALU.add, [bp, b_bm], [b_modall])
        rr.set(range(4))

    def adaln(l, path):
        rr.set(range(8))
        CP("dve", modc[:], modall[:, l, path, :], [b_modall], [b_modc])
        gt = sb_gt; LD(gt[:, 0:8], g1c[l], b_gt); LD(gt[:, 8:16], g2c[l], b_gt, group=True)
        STT(G1[:], modc[:, 8:16], 1.0, gt[:, 0:8], ALU.add, ALU.mult, [b_modc, b_gt], [b_G])
        STT(G2[:], modc[:, 32:40], 1.0, gt[:, 8:16], ALU.add, ALU.mult, [b_modc, b_gt], [b_G])
        CP("dve", SH1[:], modc[:, 0:8], [b_modc], [b_G])
        CP("dve", SH2[:], modc[:, 24:32], [b_modc], [b_G])
        for gi, base in enumerate((16, 40)):
            for c in range(8):
                ps, bp = rr.get()
                bcast_rows(modc[:, base + c:base + c + 1], b_modc, ps[:, 0:128], bp)
                CP("act", gbc[:, gi, c * 128:(c + 1) * 128], ps[:, 0:128], [bp], [b_gbc[gi]])
        rr.set(range(4))

    sb_bm = sb([128, 48]); b_bm = Buf("bm"); sb_gt = sb([128, 16]); b_gt = Buf("gt")

    xn = [sb([128, D], BF16), sb([128, D], BF16)]; b_xn = [Buf("xn0"), Buf("xn1")]

    def norm_mod(Gc, SHc):
        rr.set(range(8))
        jb = junk[:].bitcast(BF16)[:, 0:D]
        for t in range(8):
            ACT(jb, xres[:, t, :], AF.Square, [b_xres[t]], [b_junk, b_small], accum=small[:, t:t + 1])
        rstd_from_ss(small[:, 0:8], D, small[:, 8:16], [b_small], [b_small])
        stg = []
        for t in range(8):
            def FA(t=t):
                x_, bx_ = xn[t % 2], b_xn[t % 2]
                ACT(x_[:], xres[:, t, :], AF.Copy, [b_xres[t], b_small], [bx_], scale=small[:, 8 + t:9 + t])

            def FB(t=t):
                x_, bx_ = xn[t % 2], b_xn[t % 2]
                for c in range(8):
                    ps, bp = rr.get()
                    pv = ps[:].bitcast(BF16)[:, 0:128]
                    transpose_to(pv, x_[:, c * 128:(c + 1) * 128], [bx_], [bp])
                    if c % 2 == 0:
                        ACT(hT[:, c, t * 128:(t + 1) * 128], pv, AF.Identity, [bp, b_G], [b_hT], scale=Gc[:, c:c + 1], bias=SHc[:, c:c + 1])
                    else:
                        TS("dve", hT[:, c, t * 128:(t + 1) * 128], pv, Gc[:, c:c + 1], ALU.mult, [bp, b_G], [b_hT], s2=SHc[:, c:c + 1], op1=ALU.add)
            stg.append((FA, FB))
        stg[0][0]()
        for k in range(8):
            if k + 1 < 8:
                stg[k + 1][0]()
            stg[k][1]()
        rr.set(range(4))

    def yT(br, fc):
        return big[:, br * 4 + fc, :], b_big[br * 4 + fc]

    def run_pass(path):
        nseq, L = (4, 256) if path == 0 else (1, 1024)
        nt = L // 128
        is_s = path == 1
        for t in range(8):
            LD(xres[:, t, :], xin[path][t * 128:(t + 1) * 128, :], b_xres[t])
        def chk(stage, l):
            if stop is not None and stop == (path, l, stage):
                raise _Stop()
        for l in range(2):
            def ph(n):
                Sched.PHASE = f"{'PS'[path]}{l}_{n}"
            ph("adaln"); adaln(l, path); chk("adaln", l)
            ph("norm1"); norm_mod(G1, SH1); chk("norm1", l)
            ph("ssd"); branch_ssd(l, path, nseq, L, nt, is_s); chk("ssd", l)
            ph("s5"); branch_s5(l, path, nseq, L, nt, is_s); chk("s5", l)
            ph("diff"); branch_diff(l, path, nseq, L, nt, is_s); chk("diff", l)
            ph("win"); branch_win(l, path, nseq, L, nt, is_s); chk("win", l)
            ph("merge"); merge(l); chk("merge", l)
            ph("norm2"); norm_mod(G2, SH2); chk("norm2", l)
            ph("mlp"); mlp(l); chk("mlp", l)
        for t in range(8):
            STO(yout[path][t * 128:(t + 1) * 128, :], xres[:, t, :], b_xres[t])

    class Carve:
        def __init__(self):
            self.off = 0

        def take(self, shape, dt=F32):
            n = int(np.prod(shape))
            nbytes = n * (4 if dt in (F32, I32) else 2)
            nbytes = (nbytes + 31) // 32 * 32
            assert self.off + nbytes <= SCR_BYTES, (self.off, nbytes)
            v = scr[:, self.off // 4:(self.off + nbytes) // 4]
            self.off += nbytes
            if dt != F32:
                v = v.bitcast(dt)
            v = v[:, 0:n]
            if len(shape) == 2:
                return v.rearrange("p (a b) -> p a b", a=shape[0])
            if len(shape) == 3:
                return v.rearrange("p (a b c) -> p a b c", a=shape[0], b=shape[1])
            return v

    b_scr_all = Buf("scrall")
    bar = [None]

    def NB(name):
        x = Buf(name)
        x.last_w = bar[0]
        return x

    def barrier_begin():
        bar[0] = S.op("pool", lambda e: e.memset(small[:, 63:64], 0.0), [b_scr_all], [b_scr_all])


    def branch_ssd(l, path, nseq, L, nt, is_s):
        barrier_begin()
        cv = Carve()
        zs = cv.take([8, 512], BF16); b_zs = NB("zs")
        dt = cv.take([8, 16]); dtA = cv.take([8, 16]); ainc = cv.take([8, 16]); arest = cv.take([8, 16]); edt = cv.take([8, 16]); einc = cv.take([8, 16])
        nainc = cv.take([8, 16])
        b_dt = NB("dt"); b_cum = NB("cum")
        cumP = cv.take([8, 32]); b_cumP = NB("cumP")
        Lp = L + 6
        raw = cv.take([nseq * Lp]); b_raw = NB("raw")
        acc = cv.take([T]); b_acc = NB("acc")
        xrot = [cv.take([T], BF16), cv.take([T], BF16)]; b_xrot = [NB("xrot0"), NB("xrot1")]
        xB = cv.take([T], BF16); xC = cv.take([T], BF16); b_xB = NB("xB"); b_xC = NB("xC")
        xs_tok = cv.take([8, 512], BF16); b_xs = NB("xs_tok")
        B_tok = cv.take([8, 128], BF16); b_Btok = NB("Btok")
        NDP = 12
        GS = 4
        b_seg = []
        Lt = [cv.take([128]) for _ in range(NDP)]; b_Lt = [NB(f"Lt{i}") for i in range(NDP)]
        sc = [cv.take([128], BF16) for _ in range(NDP)]; b_sc = [NB(f"sc{i}") for i in range(NDP)]
        yacc = cv.take([512]); b_yacc = NB("yacc")
        ytmp = cv.take([512]); b_ytmp = NB("ytmp")
        ynb = cv.take([512], BF16); b_ynb = NB("ynb")
        prm = cv.take([64]); b_prm = NB("ssdprm")
        cw = cv.take([6, 7]); cb = cv.take([6]); ngc = cv.take([4]); b_cw = NB("cw")
        Bw = [cv.take([64], BF16), cv.take([64], BF16)]; b_Bw = [NB("Bw0"), NB("Bw1")]
        fin = None; s0T = None; st_ld = None
        b_fin = NB("fin"); b_s0T = NB("s0T"); b_stld = NB("stld")
        if is_s:
            s0T = cv.take([8, 128], BF16); st_ld = cv.take([8, 128])
        else:
            fin = cv.take([16, 64])
        _p0 = Sched.PHASE
        S.op("pool", lambda e: e.memset(prm[:, 0:64], 0.0), [b_scr_all], [b_prm, b_scr_all])
        LD(prm[:, 0:16], dtb[l:l + 1, :].partition_broadcast(128), b_prm)
        LD(prm[:, 16:32], alog[l:l + 1, :].partition_broadcast(128), b_prm, group=True)
        LD(prm[:, 32:40], ssdd[l:l + 1, :].partition_broadcast(128), b_prm, group=True)
        ACT(prm[:, 16:32], prm[:, 16:32], AF.Exp, [b_prm], [b_prm])
        TS("dve", prm[:, 16:32], prm[:, 16:32], -1.0, ALU.mult, [b_prm], [b_prm])
        LD(cw[:], convw[l], b_cw); LD(cb[:], convb[l], b_cw, group=True); LD(ngc[:], normgc[l], b_cw, group=True)
        Sched.PHASE = _p0 + 'A'
        for blk in range(2):
            w, bw = wload(w_in[l][:, C_Z + blk * 256:C_Z + (blk + 1) * 256], 8, 256)
            proj_tm(hT, b_hT, 8, w, bw, 256, range(8),
                    lambda t, ps, bp, blk=blk: ACT(zs[:, t, blk * 256:(blk + 1) * 256], ps[:, 0:256], AF.Silu, [bp], [b_zs]))
        Sched.PHASE = _p0 + 'B'
        w, bw = wload(w_in[l][:, C_DT:C_DT + 16], 8, 16)

        def ev_dt(t, ps, bp):
            TT("dve", dt[:, t, :], ps[:, 0:16], prm[:, 0:16], ALU.add, [bp, b_prm], [b_dt])
            ACT(dt[:, t, :], dt[:, t, :], AF.Exp, [b_dt], [b_dt])
            ACT(dt[:, t, :], dt[:, t, :], AF.Ln, [b_dt], [b_dt], bias=1.0)
            TT("dve", dtA[:, t, :], dt[:, t, :], prm[:, 16:32], ALU.mult, [b_dt, b_prm], [b_dt])
        proj_tm(hT, b_hT, 8, w, bw, 16, range(8), ev_dt)
        Sched.PHASE = _p0 + 'C'
        for blk in range(3):
            w, bw = wload(w_in[l][:, C_XBC + blk * 256:C_XBC + (blk + 1) * 256], 8, 256)
            for c2 in range(2):
                cc = blk * 2 + c2
                if cc < 4:
                    xa, bxa = xrot[cc % 2], b_xrot[cc % 2]
                elif cc == 4:
                    xa, bxa = xB, b_xB
                else:
                    xa, bxa = xC, b_xC
                MSET("pool", raw[:], 0.0, [b_raw])
                rw3 = raw.rearrange("p (s x) -> p s x", s=nseq)
                for h in range(2):
                    ps, bp = rr.get()
                    for k in range(8):
                        MM(ps[:, :], w[:, k, c2 * 128:(c2 + 1) * 128], hT[:, k, h * 512:(h + 1) * 512], k == 0, k == 7, [b_hT, bw], [bp])
                    if is_s:
                        CP("act", raw[:, 3 + h * 512:3 + (h + 1) * 512], ps[:, :], [bp], [b_raw])
                    else:
                        CP("act", rw3[:, 2 * h:2 * h + 2, 3:3 + L], ps[:, :].rearrange("p (s x) -> p s x", s=2), [bp], [b_raw])
                ac3 = acc.rearrange("p (s x) -> p s x", s=nseq)
                TS("dve", ac3, rw3[:, :, 0:L], cw[:, cc, 0:1], ALU.mult, [b_raw, b_cw], [b_acc])
                for k in range(1, 7):
                    STT(ac3, rw3[:, :, k:k + L], cw[:, cc, k:k + 1], ac3, ALU.mult, ALU.add, [b_raw, b_cw, b_acc], [b_acc])
                ACT(xa[:], acc[:], AF.Silu, [b_acc, b_cw], [bxa], bias=cb[:, cc:cc + 1])
                if cc < 5:
                    for t in range(8):
                        ps, bp = rr.get()
                        pv = ps[:].bitcast(BF16)[:, 0:128]
                        transpose_to(pv, xa[:, t * 128:(t + 1) * 128], [bxa], [bp])
                        if cc < 4:
                            CP("act", xs_tok[:, t, cc * 128:(cc + 1) * 128], pv, [bp], [b_xs])
                        else:
                            CP("act", B_tok[:, t, :], pv, [bp], [b_Btok])
        Sched.PHASE = _p0 + 'E'
        for s in range(nseq):
            for j in range(nt):
                tj = s * nt + j
                ps, bp = rr.get()
                for i in range(j + 1):
                    MM(ps[:, 0:16], (triu if i == j else onesf)[:], dtA[:, s * nt + i, :], i == 0, i == j, [b_triu, b_ones, b_dt], [bp])
                for i in range(nt - 1, j - 1, -1):
                    MM(ps[:, 16:32], (tril if i == j else onesf)[:], dtA[:, s * nt + i, :], i == nt - 1, i == j, [b_tril, b_ones, b_dt], [bp])
                CP("act", cumP[:, tj, :], ps[:, 0:32], [bp], [b_cumP])
                CP("dve", ainc[:, tj, 0:8], cumP[:, tj, 0:8], [b_cumP], [b_cum])
                CP("dve", ainc[:, tj, 8:16], cumP[:, tj, 24:32], [b_cumP], [b_cum])
                TT("dve", arest[:, tj, 0:8], cumP[:, tj, 16:24], dtA[:, tj, 0:8], ALU.subtract, [b_cumP, b_dt], [b_cum])
                TT("dve", arest[:, tj, 8:16], cumP[:, tj, 8:16], dtA[:, tj, 8:16], ALU.subtract, [b_cumP, b_dt], [b_cum])
                ACT(edt[:, tj, :], arest[:, tj, :], AF.Exp, [b_cum], [b_cum])
                TT("dve", edt[:, tj, :], edt[:, tj, :], dt[:, tj, :], ALU.mult, [b_cum, b_dt], [b_cum])
                ACT(einc[:, tj, :], ainc[:, tj, :], AF.Exp, [b_cum], [b_cum])
                TS("dve", nainc[:, tj, :], ainc[:, tj, :], -1.0, ALU.mult, [b_cum], [b_cum])
        if is_s:
            stv = st_ssd[l].rearrange("d h p n -> (d h p) n").rearrange("(j q) n -> q j n", q=128)
            LD(st_ld[:, :, 0:64], stv, b_stld); LD(st_ld[:, :, 64:128], stv, b_stld, group=True)
            for j8 in range(8):
                ps, bp = rr.get()
                transpose_to(ps[:, 0:128], st_ld[:, j8, :], [b_stld], [bp], dt=F32)
                CP("act", s0T[:, j8, :], ps[:, 0:128], [bp], [b_s0T])
        Sched.PHASE = _p0 + 'F'
        ybanks = [(psum[4], psb[4]), (psum[7], psb[7])]
        pa_ = [0]
        k_ = [0]
        stages = []
        for s in range(nseq):
            for j in range(nt):
                tj = s * nt + j
                ybank, b_yb = ybanks[tj % 2]
                for h in range(8):
                    g = h // 4
                    gsl = slice(g * 64, (g + 1) * 64)
                    units = [(0, i) for i in range(j + 1)] + [(1, i) for i in range(j, nt)]
                    cur = {"psA": None}
                    for g0 in range(0, len(units), GS):
                        grp = list(enumerate(units))[g0:g0 + GS]
                        st = {}

                        def A(st=st, grp=grp, units=units, s=s, j=j, tj=tj, h=h, gsl=gsl, cur=cur):
                            for ui, (d, i) in grp:
                                ti = s * nt + i
                                dh = d * 8 + h
                                if ui == 0 or units[ui - 1][0] != d:
                                    cur["psA"] = (psum[5 + pa_[0] % 2], psb[5 + pa_[0] % 2]); pa_[0] += 1
                                    bcast_rows(ainc[:, tj, dh:dh + 1], b_cum, cur["psA"][0][:, 0:128], cur["psA"][1])
                                psA_t, b_psA = cur["psA"]
                                q = k_[0] % NDP; k_[0] += 1
                                st[ui] = q
                                if i == j:
                                    STT(Lt[q][:], psA_t[:, 0:128], ainc[:, ti, dh:dh + 1], (mnegF if d == 0 else mnegB)[:], ALU.subtract, ALU.add,
                                        [b_psA, b_cum, b_mnegF, b_mnegB], [b_Lt[q]])
                                    ACT(Lt[q][:], Lt[q][:], AF.Exp, [b_Lt[q]], [b_Lt[q]])
                                elif is_s:
                                    ACT(Lt[q][:], psA_t[:, 0:128], AF.Exp, [b_psA, b_cum], [b_Lt[q]], bias=nainc[:, ti, dh:dh + 1])
                                else:
                                    TS("dve", Lt[q][:], psA_t[:, 0:128], ainc[:, ti, dh:dh + 1], ALU.subtract, [b_psA, b_cum], [b_Lt[q]], s2=0.0, op1=ALU.min)
                                    ACT(Lt[q][:], Lt[q][:], AF.Exp, [b_Lt[q]], [b_Lt[q]])
                            for ui, (d, i) in grp:
                                ti = s * nt + i
                                dh = d * 8 + h
                                q = st[ui]
                                psG, bpG = rr.get()
                                MM(psG[:, 0:128], xB[gsl, ti * 128:(ti + 1) * 128], xC[gsl, tj * 128:(tj + 1) * 128], True, True, [b_xB, b_xC], [bpG])
                                STT(sc[q][:], psG[:, 0:128], dt[:, ti, dh:dh + 1], Lt[q][:], ALU.mult, ALU.mult, [bpG, b_dt, b_Lt[q]], [b_sc[q]])

                        def B(st=st, grp=grp, units=units, s=s, tj=tj, h=h, ybank=ybank, b_yb=b_yb, last_grp=(g0 + GS >= len(units))):
                            for ui, (d, i) in grp:
                                ti = s * nt + i
                                q = st[ui]
                                MM(ybank[:, h * 64:(h + 1) * 64], sc[q][:], xs_tok[:, ti, h * 64:(h + 1) * 64], ui == 0, ui == len(units) - 1, [b_sc[q], b_xs], [b_yb])
                            if h == 7 and last_grp:
                                finalize(tj, ybank, b_yb)
                        stages.append((A, B))

        def finalize(tj, ybank, b_yb):
            if True:
                TT("dve", ytmp.rearrange("p (h x) -> p h x", h=8), xs_tok[:, tj, :].rearrange("p (h x) -> p h x", h=8),
                   prm[:, 32:40].unsqueeze(2).broadcast_to([128, 8, 64]), ALU.mult, [b_xs, b_prm], [b_ytmp])
                TT("dve", yacc[:], ybank[:, :], ytmp[:], ALU.add, [b_yb, b_ytmp], [b_yacc])
                if is_s:
                    for d in range(2):
                        for h in range(8):
                            g = h // 4
                            gsl = slice(g * 64, (g + 1) * 64)
                            j8 = (d * 8 + h) // 2
                            h2 = (d * 8 + h) % 2
                            psO, bpO = rr.get()
                            MM(psO[:, 0:64], xC[gsl, tj * 128:(tj + 1) * 128], s0T[gsl, j8, h2 * 64:(h2 + 1) * 64], True, True, [b_xC, b_s0T], [bpO])
                            STT(yacc[:, h * 64:(h + 1) * 64], psO[:, 0:64], einc[:, tj, d * 8 + h:d * 8 + h + 1], yacc[:, h * 64:(h + 1) * 64], ALU.mult, ALU.add,
                                [bpO, b_cum, b_yacc], [b_yacc])
                TT("dve", yacc[:], yacc[:], zs[:, tj, :], ALU.mult, [b_yacc, b_zs], [b_yacc])
                ACT(ytmp[:], yacc[:], AF.Square, [b_yacc], [b_ytmp, b_small], accum=small[:, 16:17])
                rstd_from_ss(small[:, 16:17], 512, small[:, 17:18], [b_small], [b_small])
                ACT(ynb[:], yacc[:], AF.Copy, [b_yacc, b_small], [b_ynb], scale=small[:, 17:18])
                for c4 in range(4):
                    ps, bp = rr.get()
                    pv = ps[:].bitcast(BF16)[:, 0:128]
                    transpose_to(pv, ynb[:, c4 * 128:(c4 + 1) * 128], [b_ynb], [bp])
                    yt_, by_ = yT(0, c4)
                    ACT(yt_[:, tj * 128:(tj + 1) * 128], pv, AF.Copy, [bp, b_cw], [by_], scale=ngc[:, c4:c4 + 1])
        LA = 2 if is_s else 3
        for k in range(min(LA, len(stages))):
            stages[k][0]()
        for k in range(len(stages)):
            if k + LA < len(stages):
                stages[k + LA][0]()
            stages[k][1]()
        Sched.PHASE = _p0 + 'G'
        if not is_s:
            for s in range(nseq):
                for d in range(2):
                    for h in range(8):
                        g = h // 4
                        psF, bpF = rr.get()
                        for i in range(nt):
                            ti = s * nt + i
                            q = k_[0] % 2; k_[0] += 1
                            TS("dve", Bw[q][:], B_tok[:, ti, g * 64:(g + 1) * 64], edt[:, ti, d * 8 + h:d * 8 + h + 1], ALU.mult, [b_Btok, b_cum], [b_Bw[q]])
                            MM(psF[0:64, 0:64], xs_tok[:, ti, h * 64:(h + 1) * 64], Bw[q][:], i == 0, i == nt - 1, [b_xs, b_Bw[q]], [bpF])
                        CP("act", fin[0:64, d * 8 + h, :], psF[0:64, 0:64], [bpF], [b_fin])
                STO(o_ssd[s, l].rearrange("d h p n -> p (d h) n"), fin[0:64, :, :], b_fin)
        S.op("pool", lambda e: e.memset(prm[:, 0:1], 0.0), [], ([b_fin, b_yacc, b_ynb, b_Btok, b_xs, b_cum, b_dt, b_zs, b_cumP, b_xB, b_xC, b_raw, b_acc, b_ytmp, b_cw, b_s0T, b_stld, b_prm]
             + b_xrot + b_seg + b_Lt + b_sc + b_Bw) + [b_scr_all])

    def branch_s5(l, path, nseq, L, nt, is_s):
        barrier_begin()
        _p0 = Sched.PHASE
        cv = Carve()
        uT = cv.take([4, T], BF16); b_uT = NB("uT")
        y5T = uT; b_y5 = b_uT
        prow = cv.take([128]); b_prow = NB("prow")
        pc = cv.take([12, 32]); b_pc = NB("pc")
        pci = cv.take([32], I32); b_pci = NB("pci")
        Bst = cv.take([4, 4, 16]); b_Bst = NB("Bst")
        Cn = cv.take([4, 64]); b_Cn = NB("Cn")
        Bc = cv.take([2, 16]); b_Bc = NB("Bc"); Bt = cv.take([16]); b_Bt = NB("Bt")
        Bx = [cv.take([128], BF16), cv.take([128], BF16)]; b_Bx = [NB("Bx0"), NB("Bx1")]
        BcL = [cv.take([128], BF16), cv.take([128], BF16)]; b_BcL = [NB("BcL0"), NB("BcL1")]
        Cx = [cv.take([128], BF16), cv.take([128], BF16)]; b_Cx = [NB("Cx0"), NB("Cx1")]
        CL = cv.take([4, 128], BF16); b_CL = NB("CL")
        Lt_ = L
        cosT = cv.take([Lt_]); sinT = cv.take([Lt_]); b_tab = NB("tab")
        xr = [cv.take([T]), cv.take([T])]; b_xr = [NB("xr0"), NB("xr1")]
        prR = cv.take([2 * T])
        prb = prR.bitcast(BF16)
        pr = [prb[:, k * T:(k + 1) * T] for k in range(4)]; b_pr = [NB(f"pr{i}") for i in range(4)]
        argF = prR[:, 0:Lt_]; argI = prR[:, T:T + Lt_].bitcast(I32)
        tmpx = prR[:, 0:T]; rmt = prR[:, T:2 * T]
        bA = [b_pr[0], b_pr[1]]; bB = [b_pr[2], b_pr[3]]
        if not is_s:
            tmpy = cv.take([T]); bY = [NB("tmpy")]
        else:
            tmpy = tmpx; bY = bA
        d5 = cv.take([4]); bg = cv.take([8]); b_d5 = NB("d5")
        finS = cv.take([256]); b_finS = NB("finS")
        b_wcap = NB("wcap")
        if not is_s:
            wcap = cv.take([2, 32, 4]); tcap = cv.take([2, 32]); wtmp = cv.take([4, 32, 4])
        s0c = cv.take([64]); b_s0c = NB("s0c")
        sg = cv.take([512]); b_sg = NB("sg")
        fT = cv.take([128]); b_fT = NB("fT")
        iota = cv.take([Lt_]); b_iota = NB("iota")
        LD(iota[:], c_iota[:, 0:Lt_], b_iota)
        if not is_s:
            rmF = cv.take([T], BF16); rmB = cv.take([T], BF16); b_rmF = NB("rmF"); b_rmB = NB("rmB")
            LD(rmF[:], c_rmF[:, :], b_rmF); LD(rmB[:], c_rmB[:, :], b_rmB)
        else:
            rmF = rmB = None; b_rmF = b_rmB = b_iota
        S.op("pool", lambda e: e.memset(prow[:], 0.0), [b_scr_all], [b_prow, b_scr_all])
        LD(prow[0:32, :], lamre[l], b_prow); LD(prow[32:64, :], lamim[l], b_prow, group=True); LD(prow[64:96, :], lsx[l], b_prow, group=True)
        ps, bp = rr.get()
        transpose_to(ps[:, 0:96], prow[0:96, :], [b_prow], [bp], dt=F32, np_=96)
        CP("act", pc[:, 0:3, :].rearrange("p a b -> p (a b)"), ps[:, 0:96], [bp], [b_pc])
        P_ = lambda i: pc[:, i, :]
        R, W_ = [b_pc], [b_pc]
        ACT(P_(2), P_(2), AF.Exp, R, W_)
        TT("dve", P_(3), P_(0), P_(2), ALU.mult, R, W_)
        TT("dve", P_(4), P_(1), P_(2), ALU.mult, R, W_)
        ACT(P_(5), P_(3), AF.Exp, R, W_)
        TS("dve", pci[:], P_(4), 1.0 / (2 * math.pi), ALU.mult, R, [b_pci])
        CP("dve", P_(10), pci[:], [b_pci], W_)
        STT(P_(11), P_(10), -2 * math.pi, P_(4), ALU.mult, ALU.add, R, W_)
        TS("dve", P_(11), P_(11), 3.14159, ALU.min, R, W_, s2=-3.14159, op1=ALU.max)
        ACT(P_(7), P_(11), AF.Sin, R, W_)
        ACT(P_(10), P_(11), AF.Abs, R, W_)
        ACT(P_(6), P_(10), AF.Sin, R, W_, scale=-1.0, bias=math.pi / 2)
        TT("dve", P_(6), P_(6), P_(5), ALU.mult, R, W_)
        TT("dve", P_(7), P_(7), P_(5), ALU.mult, R, W_)
        TT("dve", P_(10), P_(0), P_(0), ALU.mult, R, W_)
        TT("dve", P_(11), P_(1), P_(1), ALU.mult, R, W_)
        TT("dve", P_(10), P_(10), P_(11), ALU.add, R, W_)
        S.op("dve", lambda e: e.reciprocal(out=P_(10), in_=P_(10)), R, W_)
        TS("dve", P_(11), P_(6), -1.0, ALU.add, R, W_)
        TT("dve", P_(8), P_(11), P_(0), ALU.mult, R, W_)
        TT("dve", P_(9), P_(7), P_(1), ALU.mult, R, W_)
        TT("dve", P_(8), P_(8), P_(9), ALU.add, R, W_)
        TT("dve", P_(8), P_(8), P_(10), ALU.mult, R, W_)
        TT("dve", P_(9), P_(7), P_(0), ALU.mult, R, W_)
        TT("dve", P_(11), P_(11), P_(1), ALU.mult, R, W_)
        TT("dve", P_(9), P_(9), P_(11), ALU.subtract, R, W_)
        TT("dve", P_(9), P_(9), P_(10), ALU.mult, R, W_)
        LD(d5[:], s5dc[l], b_d5); LD(bg[:], bgluc[l], b_d5, group=True)
        if is_s:
            LD(fT[0:64, :], st_s5[l], b_fT)
            ps, bp = rr.get()
            transpose_to(ps[:, 0:64], fT[0:64, :], [b_fT], [bp], dt=F32, np_=64)
            CP("act", s0c[:], ps[:, 0:64], [bp], [b_s0c])
        Sched.PHASE = _p0 + 'u'
        for blk in range(2):
            w, bw = wload(w_in[l][:, C_U + blk * 256:C_U + (blk + 1) * 256], 8, 256)
            proj_fm(hT, b_hT, 8, w, bw, 256,
                    lambda cc, h, ps, bp, blk=blk: CP("act", uT[:, blk * 2 + cc, h * 512:(h + 1) * 512], ps[:, :], [bp], [b_uT]))
        ybk = [(psum[4], psb[4]), (psum[5], psb[5])]
        xbk = [(psum[6], psb[6]), (psum[7], psb[7])]
        nrep = T // Lt_
        v3 = (lambda a: a.rearrange("p (s x) -> p s x", s=nrep)) if nrep > 1 else (lambda a: a)
        Bc2 = [Bc, cv.take([2, 16])]; b_Bc2 = [b_Bc, NB("Bc_1")]; Bt2 = [Bt, cv.take([16])]; b_Bt2 = [b_Bt, NB("Bt_1")]
        Bx2 = [Bx, [cv.take([128], BF16), cv.take([128], BF16)]]; b_Bx2 = [b_Bx, [NB("Bx0_1"), NB("Bx1_1")]]
        BcL2 = [BcL, [cv.take([128], BF16), cv.take([128], BF16)]]; b_BcL2 = [b_BcL, [NB("BcL0_1"), NB("BcL1_1")]]
        Cx2 = [Cx, [cv.take([128], BF16), cv.take([128], BF16)]]; b_Cx2 = [b_Cx, [NB("Cx0_1"), NB("Cx1_1")]]
        CL2 = [CL, cv.take([4, 128], BF16)]; b_CL2 = [b_CL, NB("CL_1")]
        cos2 = [cosT, cv.take([Lt_])]; sin2 = [sinT, cv.take([Lt_])]; b_tab2 = [b_tab, NB("tab_1")]
        its = [(fc, d, q4) for fc in range(4) for d in range(2) for q4 in range(4)]
        NI = len(its)

        def stB(k):
            fc, d, q4 = its[k]
            z = k % 2
            if d == 0 and q4 == 0:
                for dd in range(2):
                    LD(Bst[:, dd * 2 + 0, :, :], s5bre[l, dd][fc * 512:(fc + 1) * 512, :].rearrange("(c p) m -> p c m", p=128), b_Bst, group=(dd > 0))
                    LD(Bst[:, dd * 2 + 1, :, :], s5bim[l, dd][fc * 512:(fc + 1) * 512, :].rearrange("(c p) m -> p c m", p=128), b_Bst, group=True)
                    LD(Cn[:, dd * 2 + 0, :], s5cre[l, dd][fc * 128:(fc + 1) * 128, :], b_Cn, group=(dd > 0))
                    LD(Cn[:, dd * 2 + 1, :], s5cim[l, dd][fc * 128:(fc + 1) * 128, :], b_Cn, group=True)
            c = fc * 4 + q4
            dc = d * 16 + c
            cre, cim = pc[:, 8, dc:dc + 1], pc[:, 9, dc:dc + 1]
            Bre, Bim = Bst[:, d * 2 + 0, q4, :], Bst[:, d * 2 + 1, q4, :]
            Bc_, bBc_, Bt_, bBt_ = Bc2[z], b_Bc2[z], Bt2[z], b_Bt2[z]
            TS("dve", Bt_[:], Bim, cim, ALU.mult, [b_Bst, b_pc], [bBt_])
            STT(Bc_[:, 0, :], Bre, cre, Bt_[:], ALU.mult, ALU.subtract, [b_Bst, b_pc, bBt_], [bBc_])
            TS("dve", Bt_[:], Bre, cim, ALU.mult, [b_Bst, b_pc], [bBt_])
            STT(Bc_[:, 1, :], Bim, cre, Bt_[:], ALU.mult, ALU.add, [b_Bst, b_pc, bBt_], [bBc_])
            for ri in range(2):
                TT("pool", Bx2[z][ri].rearrange("p (g m) -> p g m", g=8), maskB[:, q4, :].rearrange("p (g m) -> p g m", g=8),
                   Bc_[:, ri, :].unsqueeze(1).broadcast_to([128, 8, 16]), ALU.mult, [b_maskB, bBc_], [b_Bx2[z][ri]])
                ps, bp = rr.get()
                pv = ps[:].bitcast(BF16)[:, 0:128]
                transpose_to(pv, Bx2[z][ri][:], [b_Bx2[z][ri]], [bp])
                CP("act", BcL2[z][ri][:], pv, [bp], [b_BcL2[z][ri]])
            for ri in range(2):
                TT("pool", Cx2[z][ri].rearrange("p (g n) -> p g n", g=2), maskC[:, q4, :].rearrange("p (g n) -> p g n", g=2),
                   Cn[:, d * 2 + ri, :].unsqueeze(1).broadcast_to([128, 2, 64]), ALU.mult, [b_maskC, b_Cn], [b_Cx2[z][ri]])
                ps, bp = rr.get()
                pv = ps[:].bitcast(BF16)[:, 0:128]
                transpose_to(pv, Cx2[z][ri][:], [b_Cx2[z][ri]], [bp])
                CP("act", CL2[z][:, 2 * ri, :], pv, [bp], [b_CL2[z]])
                ACT(CL2[z][:, 2 * ri + 1, :], pv, AF.Copy, [bp], [b_CL2[z]], scale=-1.0)

        def stT1(k):
            fc, d, q4 = its[k]
            z = k % 2
            dc = d * 16 + fc * 4 + q4
            cT, sT, bt = cos2[z], sin2[z], [b_tab2[z]]
            TS("dve", cT[:], iota[:, 0:Lt_], pc[:, 4, dc:dc + 1], ALU.mult, [b_iota, b_pc], bt)
            TS("dve", sT[:].bitcast(I32), cT[:], 1.0 / (2 * math.pi), ALU.mult, bt, bt)
            CP("act", sT[:], sT[:].bitcast(I32), bt, bt)

        def stT2(k):
            z = k % 2
            cT, sT, bt = cos2[z], sin2[z], [b_tab2[z]]
            STT(cT[:], sT[:], -2 * math.pi, cT[:], ALU.mult, ALU.add, bt, bt)
            TS("dve", cT[:], cT[:], 3.14159, ALU.min, bt, bt, s2=-3.14159, op1=ALU.max)
            ACT(sT[:], cT[:], AF.Sin, bt, bt)
            ACT(cT[:], cT[:], AF.Abs, bt, bt)
            ACT(cT[:], cT[:], AF.Sin, bt, bt, scale=-1.0, bias=math.pi / 2)

        def stX(k):
            fc, d, q4 = its[k]
            z = k % 2
            c = fc * 4 + q4
            dc = d * 16 + c
            cosT_, sinT_, bt = cos2[z], sin2[z], b_tab2[z]
            for h in range(2):
                hs = slice(h * 512, (h + 1) * 512)
                for ri in range(2):
                    MM(xbk[ri][0][:, :], BcL2[z][ri][:], uT[:, fc, hs], True, True, [b_BcL2[z][ri], b_uT], [xbk[ri][1]])
                xre, xim = xbk[0][0][:, :], xbk[1][0][:, :]
                if Lt_ < 512:
                    nr2 = 512 // Lt_
                    cB = cosT_.unsqueeze(1).broadcast_to([128, nr2, Lt_]); sB = sinT_.unsqueeze(1).broadcast_to([128, nr2, Lt_])
                    vv = lambda a, nr2=nr2: a.rearrange("p (s x) -> p s x", s=nr2)
                else:
                    cB, sB = cosT_[:, hs], sinT_[:, hs]
                    vv = lambda a: a
                TT("dve", vv(xr[1][:, hs]), vv(xim), cB, ALU.mult, [xbk[1][1], bt], [b_xr[1]])
                TT("dve", vv(tmpy[:, hs]), vv(xre), sB, ALU.mult, [xbk[0][1], bt], bY)
                TT("pool", xr[1][:, hs], xr[1][:, hs], tmpy[:, hs], ALU.subtract if d == 0 else ALU.add, [b_xr[1]] + bY, [b_xr[1]])
                TT("dve", vv(xr[0][:, hs]), vv(xre), cB, ALU.mult, [xbk[0][1], bt], [b_xr[0]])
                TT("dve", vv(tmpx[:, hs]), vv(xim), sB, ALU.mult, [xbk[1][1], bt], bA)
                TT("pool", xr[0][:, hs], xr[0][:, hs], tmpx[:, hs], ALU.add if d == 0 else ALU.subtract, [b_xr[0]] + bA, [b_xr[0]])
            if is_s:
                sre = s0c[:, (d * 2 + 0) * 16 + c:(d * 2 + 0) * 16 + c + 1]; sim = s0c[:, (d * 2 + 1) * 16 + c:(d * 2 + 1) * 16 + c + 1]
                abre, abim = pc[:, 6, dc:dc + 1], pc[:, 7, dc:dc + 1]
                sm = small
                RS, WS_ = [b_small, b_s0c, b_pc, bt], [b_small]
                TT("dve", sm[:, 20:21], sre, abre, ALU.mult, RS, WS_); TT("dve", sm[:, 21:22], sim, abim, ALU.mult, RS, WS_)
                TT("dve", sm[:, 22:23], sm[:, 20:21], sm[:, 21:22], ALU.subtract, RS, WS_)
                TT("dve", sm[:, 20:21], sre, abim, ALU.mult, RS, WS_); TT("dve", sm[:, 21:22], sim, abre, ALU.mult, RS, WS_)
                TT("dve", sm[:, 23:24], sm[:, 20:21], sm[:, 21:22], ALU.add, RS, WS_)
                if d == 0:
                    TT("dve", xr[0][:, 0:1], xr[0][:, 0:1], sm[:, 22:23], ALU.add, [b_xr[0], b_small], [b_xr[0]])
                    TT("dve", xr[1][:, 0:1], xr[1][:, 0:1], sm[:, 23:24], ALU.add, [b_xr[1], b_small], [b_xr[1]])
                else:
                    cl, sl = cosT_[:, L - 1:L], sinT_[:, L - 1:L]
                    TT("dve", sm[:, 20:21], sm[:, 22:23], cl, ALU.mult, RS, WS_); TT("dve", sm[:, 21:22], sm[:, 23:24], sl, ALU.mult, RS, WS_)
                    TT("dve", sm[:, 24:25], sm[:, 20:21], sm[:, 21:22], ALU.subtract, RS, WS_)
                    TT("dve", sm[:, 20:21], sm[:, 22:23], sl, ALU.mult, RS, WS_); TT("dve", sm[:, 21:22], sm[:, 23:24], cl, ALU.mult, RS, WS_)
                    TT("dve", sm[:, 25:26], sm[:, 20:21], sm[:, 21:22], ALU.add, RS, WS_)
                    TT("dve", xr[0][:, L - 1:L], xr[0][:, L - 1:L], sm[:, 24:25], ALU.add, [b_xr[0], b_small], [b_xr[0]])
                    TT("dve", xr[1][:, L - 1:L], xr[1][:, L - 1:L], sm[:, 25:26], ALU.add, [b_xr[1], b_small], [b_xr[1]])

        def stS(k):
            fc, d, q4 = its[k]
            z = k % 2
            dc = d * 16 + fc * 4 + q4
            cosT_, sinT_, bt = cos2[z], sin2[z], b_tab2[z]
            if is_s:
                rm_ = pc[:, 5, dc:dc + 1].broadcast_to([128, T]); rm_r = rm_; brm = [b_pc]
            else:
                TS("dve", rmt, (rmF if d == 0 else rmB)[:], pc[:, 5, dc:dc + 1], ALU.mult, [b_rmF, b_rmB, b_pc], bB)
                rm_ = rmt; rm_r = rmt[:, ::-1]; brm = bB
            for ri in (1, 0):
                if d == 0:
                    S.op("dve", lambda e, ri=ri, rm_=rm_: e.tensor_tensor_scan(out=xr[ri][:], data0=rm_, data1=xr[ri][:], initial=0.0, op0=ALU.mult, op1=ALU.add),
                         brm + [b_xr[ri]], [b_xr[ri]])
                else:
                    S.op("dve", lambda e, ri=ri, rm_r=rm_r: e.tensor_tensor_scan(out=xr[ri][:, ::-1], data0=rm_r, data1=xr[ri][:, ::-1], initial=0.0, op0=ALU.mult, op1=ALU.add),
                         brm + [b_xr[ri]], [b_xr[ri]])
            if not is_s:
                lpos = L - 1 if d == 0 else 0
                for ri in range(2):
                    w3 = xr[ri].rearrange("p (s x) -> p s x", s=nseq)[:, :, lpos:lpos + 1].rearrange("p s x -> p (s x)")
                    CP("act", wcap[:, ri, dc, :], w3, [b_xr[ri]], [b_wcap])
                CP("act", tcap[:, 0, dc:dc + 1], cosT_[:, lpos:lpos + 1], [bt], [b_wcap])
                CP("act", tcap[:, 1, dc:dc + 1], sinT_[:, lpos:lpos + 1], [bt], [b_wcap])

        def stP(k):
            fc, d, q4 = its[k]
            z = k % 2
            cosT_, sinT_, bt = cos2[z], sin2[z], b_tab2[z]
            cosB = cosT_.unsqueeze(1).broadcast_to([128, nrep, Lt_]) if nrep > 1 else cosT_
            sinB = sinT_.unsqueeze(1).broadcast_to([128, nrep, Lt_]) if nrep > 1 else sinT_
            TT("dve", v3(pr[1]), v3(xr[1][:]), sinB, ALU.mult, [b_xr[1], bt], [b_pr[1]])
            TT("pool", v3(pr[0]), v3(xr[0][:]), cosB, ALU.mult, [b_xr[0], bt], [b_pr[0]])
            TT("dve", v3(pr[3]), v3(xr[1][:]), cosB, ALU.mult, [b_xr[1], bt], [b_pr[3]])
            TT("pool", v3(pr[2]), v3(xr[0][:]), sinB, ALU.mult, [b_xr[0], bt], [b_pr[2]])
            sel = [0, 1, 3, 3] if d == 0 else [0, 0, 2, 3]
            first_y = (d == 0 and q4 == 0)
            last_y = (d == 1 and q4 == 3)
            for h in range(2):
                hs = slice(h * 512, (h + 1) * 512)
                for k4 in range(4):
                    MM(ybk[h][0][:, :], CL2[z][:, sel[k4], :], pr[k4][:, hs], first_y and k4 == 0, last_y and k4 == 3, [b_CL2[z], b_pr[k4]], [ybk[h][1]])
            if last_y:
                for h in range(2):
                    hs = slice(h * 512, (h + 1) * 512)
                    STT(uT[:, fc, hs], uT[:, fc, hs], d5[:, fc:fc + 1], ybk[h][0][:, :], ALU.mult, ALU.add, [b_uT, b_d5, ybk[h][1]], [b_uT])

        Sched.PHASE = _p0 + 'L'
        stB(0); stT1(0); stT2(0)
        for k in range(NI):
            if k + 1 < NI:
                stB(k + 1); stT1(k + 1)
            stX(k)
            if k + 1 < NI:
                stT2(k + 1)
            stS(k)
            stP(k)
        if not is_s:
            cB_ = tcap[:, 0, :].unsqueeze(2).broadcast_to([128, 32, 4]); sB_ = tcap[:, 1, :].unsqueeze(2).broadcast_to([128, 32, 4])
            RW = [b_wcap]
            TT("dve", wtmp[:, 0, :, :], wcap[:, 0, :, :], cB_, ALU.mult, RW, RW)
            TT("dve", wtmp[:, 1, :, :], wcap[:, 1, :, :], sB_, ALU.mult, RW, RW)
            TT("dve", wtmp[:, 2, :, :], wcap[:, 0, :, :], sB_, ALU.mult, RW, RW)
            TT("dve", wtmp[:, 3, :, :], wcap[:, 1, :, :], cB_, ALU.mult, RW, RW)
            f4 = finS.rearrange("p (s d r c) -> p s d r c", s=4, d=2, r=2)
            for d in range(2):
                src = lambda k, d=d: wtmp[:, k, d * 16:(d + 1) * 16, :].rearrange("p c s -> p s c")
                TT("dve", f4[:, :, d, 0, :], src(0), src(1), ALU.subtract if d == 0 else ALU.add, RW, [b_finS])
                TT("dve", f4[:, :, d, 1, :], src(3), src(2), ALU.add if d == 0 else ALU.subtract, RW, [b_finS])
            for half in range(2):
                ps, bp = rr.get()
                transpose_to(ps[:, 0:128], finS[:, half * 128:(half + 1) * 128], [b_finS], [bp], dt=F32)
                CP("act", fT[:], ps[:, 0:128], [bp], [b_fT])
                for s2 in range(2):
                    STO(o_s5[half * 2 + s2, l], fT[s2 * 64:(s2 + 1) * 64, :], b_fT)
        Sched.PHASE = _p0 + 'G'
        for blk in range(2):
            w, bw = wload(w_glu[l][:, blk * 256:(blk + 1) * 256], 4, 256)
            w2, bw2 = wload(w_glu[l][:, (blk + 2) * 256:(blk + 3) * 256], 4, 256)
            for c2 in range(2):
                cc = blk * 2 + c2
                for h in range(2):
                    hs = slice(h * 512, (h + 1) * 512)
                    psg, bpg = rr.get()
                    for k in range(4):
                        MM(psg[:, :], w2[:, k, c2 * 128:(c2 + 1) * 128], y5T[:, k, hs], k == 0, k == 3, [b_y5, bw2], [bpg])
                    ACT(sg[:], psg[:, :], AF.Sigmoid, [bpg, b_d5], [b_sg], bias=bg[:, 4 + cc:5 + cc])
                    psv, bpv = rr.get()
                    for k in range(4):
                        MM(psv[:, :], w[:, k, c2 * 128:(c2 + 1) * 128], y5T[:, k, hs], k == 0, k == 3, [b_y5, bw], [bpv])
                    yt_, by_ = yT(1, cc)
                    STT(yt_[:, hs], psv[:, :], bg[:, cc:cc + 1], sg[:], ALU.add, ALU.mult, [bpv, b_d5, b_sg], [by_])
        S.op("pool", lambda e: e.memset(prow[:, 0:1], 0.0), [], ([b_uT, b_y5, b_prow, b_pc, b_pci, b_Bst, b_Cn, b_Bc, b_Bt, b_CL, b_tab, b_d5, b_finS, b_s0c, b_sg, b_fT]
             + b_Bx + b_BcL + b_Cx + b_xr + b_pr + (bY if not is_s else []) + [b_iota, b_rmF, b_rmB, b_wcap]
             + [b_Bc2[1], b_Bt2[1], b_CL2[1], b_tab2[1]] + b_Bx2[1] + b_BcL2[1] + b_Cx2[1]) + [b_scr_all])

    def rms_groups(ps, bp, ncols, gain_bc, b_gain, qf, b_qf, sq, b_sq, rs, b_rs, t, rope):
        ng = ncols // 64
        CP("act", qf[:, 0:ncols], ps[:, 0:ncols], [bp], [b_qf])
        TT("dve", sq[:, 0:ncols], qf[:, 0:ncols], qf[:, 0:ncols], ALU.mult, [b_qf], [b_sq])
        S.op("dve", lambda e: e.tensor_reduce(out=rs[:, 0:ng], in_=sq[:, 0:ncols].rearrange("p (g x) -> p g x", g=ng), op=ALU.add, axis=AX.X), [b_sq], [b_rs])
        rstd_from_ss(rs[:, 0:ng], 64, rs[:, 0:ng], [b_rs], [b_rs])
        q3 = qf[:, 0:ncols].rearrange("p (g x) -> p g x", g=ng)
        TT("dve", q3, q3, rs[:, 0:ng].unsqueeze(2).broadcast_to([128, ng, 64]), ALU.mult, [b_qf, b_rs], [b_qf])
        TT("dve", q3, q3, gain_bc.unsqueeze(1).broadcast_to([128, ng, 64]), ALU.mult, [b_qf, b_gain], [b_qf])
        if rope:
            s3 = sq[:, 0:ncols].rearrange("p (g a q f) -> p (g a) q f", g=ng, a=2, q=2)
            x4 = qf[:, 0:ncols].rearrange("p (g a q f) -> p (g a) q f", g=ng, a=2, q=2)
            S4 = ropeS[:, t, :].rearrange("p (a q f) -> p a q f", a=2, q=2)
            for pz in range(2):
                TT("dve", s3[:, :, pz, :].rearrange("p (g a) f -> p g a f", g=ng), x4[:, :, 1 - pz, :].rearrange("p (g a) f -> p g a f", g=ng),
                   S4[:, :, pz, :].unsqueeze(1).broadcast_to([128, ng, 2, 16]), ALU.mult, [b_qf, b_ropeS], [b_sq])
            TT("dve", q3, q3, ropeC[:, t, :].unsqueeze(1).broadcast_to([128, ng, 64]), ALU.mult, [b_qf, b_ropeC], [b_qf])
            TT("dve", qf[:, 0:ncols], qf[:, 0:ncols], sq[:, 0:ncols], ALU.add, [b_qf, b_sq], [b_qf])

    def branch_diff(l, path, nseq, L, nt, is_s):
        barrier_begin()
        _p0 = Sched.PHASE
        cv = Carve()
        nk_ctx = 2 if is_s else 0
        NKT = 8 + nk_ctx
        qT = cv.take([4, T], BF16); b_qT = NB("qT")
        kT = cv.take([4, NKT * 128], BF16); b_kT = NB("kT")
        vaug = cv.take([NKT, 4, 130], BF16); b_va = NB("vaug")
        qfL = [cv.take([512]) for _ in range(2)]; b_qfL = [NB(f"qf{i}") for i in range(2)]
        sqL = [cv.take([512]) for _ in range(2)]; b_sqL = [NB(f"sq{i}") for i in range(2)]
        rsL = [cv.take([16]) for _ in range(2)]; b_rsL = [NB(f"rs{i}") for i in range(2)]
        rot_ = [0]

        def nxt():
            i = rot_[0] % 2; rot_[0] += 1
            return qfL[i], b_qfL[i], sqL[i], b_sqL[i], rsL[i], b_rsL[i]
        qb = [cv.take([512], BF16), cv.take([512], BF16)]; b_qb = [NB("qb0"), NB("qb1")]
        gq = cv.take([64]); gk = cv.take([64]); b_g = NB("dg")
        lamt = cv.take([4, 64]); lamc = cv.take([8]); b_lam = NB("lam")
        o1 = cv.take([4, 128]); b_o1 = NB("o1")
        odn = cv.take([8, 512], BF16); b_odn = NB("odn")
        pT = [cv.take([512], BF16) for _ in range(4)]; b_pT = [NB(f"pT{i}") for i in range(4)]
        rd = cv.take([8]); b_rd = NB("rd")
        oh = cv.take([128]); b_oh = NB("oh")
        gsub = cv.take([1]); b_gsub = NB("gsub")
        kstL = [cv.take([512]) for _ in range(2)]; b_kstL = [NB(f"kst{i}") for i in range(2)]
        kst, b_kst = kstL[0], b_kstL[0]
        lam_init = 0.8 - 0.6 * math.exp(-0.3 * l)
        S.op("pool", lambda e: e.memset(gq[:], 0.0), [b_scr_all], [b_g, b_scr_all])
        LD(gq[:], dqg[l:l + 1, :].partition_broadcast(128), b_g); LD(gk[:], dkg[l:l + 1, :].partition_broadcast(128), b_g, group=True)
        TS("dve", gq[:], gq[:], 0.125, ALU.mult, [b_g], [b_g])
        LD(lamt[:].rearrange("p a b -> p (a b)"), dlam[l:l + 1, :].partition_broadcast(128), b_lam)
        LD(gsub[:], dsubc[l], b_gsub)
        TS("dve", gsub[:], gsub[:], 1.0 - lam_init, ALU.mult, [b_gsub], [b_gsub])
        TT("dve", lamt[:, 0, :], lamt[:, 0, :], lamt[:, 1, :], ALU.mult, [b_lam], [b_lam])
        TT("dve", lamt[:, 2, :], lamt[:, 2, :], lamt[:, 3, :], ALU.mult, [b_lam], [b_lam])
        S.op("dve", lambda e: e.tensor_reduce(out=lamc[:, 0:1], in_=lamt[:, 0, :], op=ALU.add, axis=AX.X), [b_lam], [b_lam])
        S.op("dve", lambda e: e.tensor_reduce(out=lamc[:, 1:2], in_=lamt[:, 2, :], op=ALU.add, axis=AX.X), [b_lam], [b_lam])
        ACT(lamc[:, 0:2], lamc[:, 0:2], AF.Exp, [b_lam], [b_lam])
        TT("dve", lamc[:, 2:3], lamc[:, 0:1], lamc[:, 1:2], ALU.subtract, [b_lam], [b_lam])
        TS("dve", lamc[:, 2:3], lamc[:, 2:3], lam_init, ALU.add, [b_lam], [b_lam], s2=-1.0, op1=ALU.mult)
        MSET("pool", vaug[:].rearrange("p a b c -> p (a b c)"), 1.0, [b_va])
        if sub == 1:
            raise _Stop()
        Sched.PHASE = _p0 + 'A'
        stagesA = []
        for which in range(2):
            col0 = C_DQ if which == 0 else C_DK
            wd = {}
            for t in range(8):
                st = {}

                def FA(st=st, which=which, t=t, wd=wd, col0=col0):
                    if t == 0:
                        wd["A"] = wload(w_in[l][:, col0:col0 + 256], 8, 256)
                        wd["B"] = wload(w_in[l][:, col0 + 256:col0 + 512], 8, 256)
                    wA, bwA = wd["A"]; wB, bwB = wd["B"]
                    ps, bp = rr.get()
                    for k in range(8):
                        MM(ps[:, 0:256], hT[:, k, t * 128:(t + 1) * 128], wA[:, k, :], k == 0, k == 7, [b_hT, bwA], [bp])
                    for k in range(8):
                        MM(ps[:, 256:512], hT[:, k, t * 128:(t + 1) * 128], wB[:, k, :], k == 0, k == 7, [b_hT, bwB], [bp])
                    qf, b_qf, sq, b_sq, rs, b_rs = nxt()
                    kst, b_kst = kstL[t % 2], b_kstL[t % 2]
                    if which == 1 and not is_s:
                        rms_groups(ps, bp, 512, gk[:], b_g, kst, b_kst, sq, b_sq, rs, b_rs, t, False)
                        STO(o_dk[t // 2, l, (t % 2) * 128:(t % 2 + 1) * 128, :], kst[:], b_kst)
                        st["src"] = (kst, b_kst)
                    else:
                        rms_groups(ps, bp, 512, (gq if which == 0 else gk)[:], b_g, qf, b_qf, sq, b_sq, rs, b_rs, t, is_s)
                        st["src"] = (qf, b_qf)

                def FB(st=st, which=which, t=t):
                    src, bsrc = st["src"]
                    qb_, bqb_ = qb[t % 2], b_qb[t % 2]
                    CP("pool", qb_[:], src[:], [bsrc], [bqb_])
                    for j4 in range(4):
                        ps2, bp2 = rr.get()
                        pv = ps2[:].bitcast(BF16)[:, 0:128]
                        transpose_to(pv, qb_[:, j4 * 128:(j4 + 1) * 128], [bqb_], [bp2])
                        if which == 0:
                            CP("act", qT[:, j4, t * 128:(t + 1) * 128], pv, [bp2], [b_qT])
                        else:
                            CP("act", kT[:, j4, (nk_ctx + t) * 128:(nk_ctx + t + 1) * 128], pv, [bp2], [b_kT])
                stagesA.append((FA, FB))
        stagesA[0][0]()
        for k in range(len(stagesA)):
            if k + 1 < len(stagesA):
                stagesA[k + 1][0]()
            stagesA[k][1]()
        if is_s:
            for kt in range(2):
                LD(kst[:], cdk[l, kt * 128:(kt + 1) * 128, :], b_kst)
                CP("pool", qb[0][:], kst[:], [b_kst], [b_qb[0]])
                for j4 in range(4):
                    ps2, bp2 = rr.get()
                    pv = ps2[:].bitcast(BF16)[:, 0:128]
                    transpose_to(pv, qb[0][:, j4 * 128:(j4 + 1) * 128], [b_qb[0]], [bp2])
                    CP("act", kT[:, j4, kt * 128:(kt + 1) * 128], pv, [bp2], [b_kT])
                LD(kst[:], cdv[l, kt * 128:(kt + 1) * 128, :], b_kst)
                CP("pool", vaug[:, kt, :, 0:128], kst[:].rearrange("p (h e) -> p h e", h=4), [b_kst], [b_va])
        Sched.PHASE = _p0 + 'B'
        wA, bwA = wload(w_in[l][:, C_DV:C_DV + 256], 8, 256)
        wB, bwB = wload(w_in[l][:, C_DV + 256:C_DV + 512], 8, 256)
        for t in range(8):
            ps, bp = rr.get()
            for k in range(8):
                MM(ps[:, 0:256], hT[:, k, t * 128:(t + 1) * 128], wA[:, k, :], k == 0, k == 7, [b_hT, bwA], [bp])
            for k in range(8):
                MM(ps[:, 256:512], hT[:, k, t * 128:(t + 1) * 128], wB[:, k, :], k == 0, k == 7, [b_hT, bwB], [bp])
            if sub != 31:
                kst, b_kst = kstL[t % 2], b_kstL[t % 2]
            CP("act", vaug[:, nk_ctx + t, :, 0:128], ps[:, :].rearrange("p (h e) -> p h e", h=4), [bp], [b_va])
            if not is_s and sub != 32:
                CP("dve", kst[:], ps[:, :], [bp], [b_kst])
                STO(o_dv[t // 2, l, (t % 2) * 128:(t % 2 + 1) * 128, :], kst[:], b_kst)
        if sub in (3, 31, 32):
            raise _Stop()
        Sched.PHASE = _p0 + 'C'
        obk = [(psum[4 + i], psb[4 + i]) for i in range(4)]
        pc_ = [0]
        stages = []
        for s in range(nseq):
            keyt = list(range(nk_ctx)) + [nk_ctx + s * nt + i for i in range(nt)]
            nq = min(L, 512)
            for qc in range(L // nq):
                q0 = s * L + qc * nq
                nqt = nq // 128
                for h in range(4):
                    for c in range(2):
                        ksl = slice(c * 64, (c + 1) * 64)
                        for ki, kt in enumerate(keyt):
                            st = {}

                            def A(st=st, ksl=ksl, h=h, kt=kt, q0=q0, nq=nq):
                                psS, bpS = rr.get()
                                MM(psS[:, 0:nq], kT[ksl, h, kt * 128:(kt + 1) * 128], qT[ksl, h, q0:q0 + nq], True, True, [b_kT, b_qT], [bpS])
                                z = pc_[0] % 4; pc_[0] += 1
                                st["z"] = z
                                ACT(pT[z][:, 0:nq], psS[:, 0:nq], AF.Exp, [bpS], [b_pT[z]])

                            def B(st=st, h=h, c=c, kt=kt, ki=ki, nk=len(keyt), nqt=nqt, q0=q0):
                                z = st["z"]
                                for qt in range(nqt):
                                    MM(obk[qt][0][:, 0:129], pT[z][:, qt * 128:(qt + 1) * 128], vaug[:, kt, h, 0:129], ki == 0, ki == nk - 1, [b_pT[z], b_va], [obk[qt][1]])
                                if ki != nk - 1:
                                    return
                                for qt in range(nqt):
                                    tq = q0 // 128 + qt
                                    ob, bob = obk[qt]
                                    S.op("dve", lambda e, ob=ob, qt=qt, c=c: e.reciprocal(out=rd[:, qt * 2 + c:qt * 2 + c + 1], in_=ob[:, 128:129]), [bob], [b_rd])
                                    if c == 0:
                                        TS("dve", o1[:, qt, :], ob[:, 0:128], rd[:, qt * 2:qt * 2 + 1], ALU.mult, [bob, b_rd], [b_o1])
                                    else:
                                        TT("dve", rd[:, qt * 2 + 1:qt * 2 + 2], rd[:, qt * 2 + 1:qt * 2 + 2], lamc[:, 2:3], ALU.mult, [b_rd, b_lam], [b_rd])
                                        STT(oh[:], ob[:, 0:128], rd[:, qt * 2 + 1:qt * 2 + 2], o1[:, qt, :], ALU.mult, ALU.add, [bob, b_rd, b_o1], [b_oh])
                                        ACT(junk[:, 0:128], oh[:], AF.Square, [b_oh], [b_junk, b_small], accum=small[:, 40:41])
                                        rstd_from_ss(small[:, 40:41], 128, small[:, 41:42], [b_small], [b_small])
                                        TS("dve", odn[:, tq, h * 128:(h + 1) * 128], oh[:], small[:, 41:42], ALU.mult, [b_oh, b_small], [b_odn])
                            stages.append((A, B))
        LA = 3
        for k in range(min(LA, len(stages))):
            stages[k][0]()
        for k in range(len(stages)):
            if k + LA < len(stages):
                stages[k + LA][0]()
            stages[k][1]()
        if sub == 4:
            raise _Stop()
        Sched.PHASE = _p0 + 'D'
        for t in range(8):
            for h in range(4):
                ps2, bp2 = rr.get()
                pv = ps2[:].bitcast(BF16)[:, 0:128]
                transpose_to(pv, odn[:, t, h * 128:(h + 1) * 128], [b_odn], [bp2])
                yt_, by_ = yT(2, h)
                ACT(yt_[:, t * 128:(t + 1) * 128], pv, AF.Copy, [bp2, b_gsub], [by_], scale=gsub[:, 0:1])
        S.op("pool", lambda e: e.memset(gq[:, 0:1], 0.0), [], ([b_qT, b_kT, b_va, b_g, b_lam, b_o1, b_odn, b_rd, b_oh, b_gsub] + b_kstL + b_qb + b_pT + b_qfL + b_sqL + b_rsL) + [b_scr_all])

    def branch_win(l, path, nseq, L, nt, is_s):
        barrier_begin()
        cv = Carve()
        nk_ctx = 2 if is_s else 0
        NKT = 8 + nk_ctx
        qT = cv.take([4, T], BF16); b_qT = NB("wqT")
        kT = cv.take([2, NKT * 128], BF16); b_kT = NB("wkT")
        vaug = cv.take([NKT, 2, 66], BF16); b_va = NB("wvaug")
        qfL = [cv.take([512]) for _ in range(2)]; b_qfL = [NB(f"wqf{i}") for i in range(2)]
        sqL = [cv.take([512]) for _ in range(2)]; b_sqL = [NB(f"wsq{i}") for i in range(2)]
        rsL = [cv.take([16]) for _ in range(2)]; b_rsL = [NB(f"wrs{i}") for i in range(2)]
        rot_ = [0]

        def nxt():
            i = rot_[0] % 2; rot_[0] += 1
            return qfL[i], b_qfL[i], sqL[i], b_sqL[i], rsL[i], b_rsL[i]
        qb = [cv.take([512], BF16), cv.take([512], BF16)]; b_qb = [NB("wqb0"), NB("wqb1")]
        gq = cv.take([64]); gk = cv.take([64]); b_g = NB("wg")
        snk = cv.take([8]); b_snk = NB("snk")
        on = cv.take([8, 512], BF16); b_on = NB("won")
        pT = [cv.take([512], BF16) for _ in range(4)]; b_pT = [NB(f"wpT{i}") for i in range(4)]
        rd = cv.take([8]); b_rd = NB("wrd")
        kst = cv.take([256]); b_kst = NB("wkst")
        S.op("pool", lambda e: e.memset(gq[:], 0.0), [b_scr_all], [b_g, b_scr_all])
        LD(gq[:], wqg[l:l + 1, :].partition_broadcast(128), b_g); LD(gk[:], wkg[l:l + 1, :].partition_broadcast(128), b_g, group=True)
        TS("dve", gq[:], gq[:], 0.125, ALU.mult, [b_g], [b_g])
        LD(snk[:], wsink[l:l + 1, :].partition_broadcast(128), b_snk)
        ACT(snk[:], snk[:], AF.Exp, [b_snk], [b_snk])
        MSET("pool", vaug[:].rearrange("p a b c -> p (a b c)"), 1.0, [b_va])
        wA, bwA = wload(w_in[l][:, C_WQ:C_WQ + 256], 8, 256)
        wB, bwB = wload(w_in[l][:, C_WQ + 256:C_WQ + 512], 8, 256)
        stq = []
        for t in range(8):
            st = {}

            def QA(st=st, t=t):
                ps, bp = rr.get()
                for k in range(8):
                    MM(ps[:, 0:256], hT[:, k, t * 128:(t + 1) * 128], wA[:, k, :], k == 0, k == 7, [b_hT, bwA], [bp])
                for k in range(8):
                    MM(ps[:, 256:512], hT[:, k, t * 128:(t + 1) * 128], wB[:, k, :], k == 0, k == 7, [b_hT, bwB], [bp])
                qf, b_qf, sq, b_sq, rs, b_rs = nxt()
                rms_groups(ps, bp, 512, gq[:], b_g, qf, b_qf, sq, b_sq, rs, b_rs, t, is_s)
                st["q"] = (qf, b_qf)

            def QB(st=st, t=t):
                qf, b_qf = st["q"]
                qb_, bqb_ = qb[t % 2], b_qb[t % 2]
                CP("pool", qb_[:], qf[:], [b_qf], [bqb_])
                for j4 in range(4):
                    ps2, bp2 = rr.get()
                    pv = ps2[:].bitcast(BF16)[:, 0:128]
                    transpose_to(pv, qb_[:, j4 * 128:(j4 + 1) * 128], [bqb_], [bp2])
                    CP("act", qT[:, j4, t * 128:(t + 1) * 128], pv, [bp2], [b_qT])
            stq.append((QA, QB))
        stq[0][0]()
        for k in range(8):
            if k + 1 < 8:
                stq[k + 1][0]()
            stq[k][1]()
        wK, bwK = wload(w_in[l][:, C_WK:C_WK + 256], 8, 256)

        def put_k(src_f32, bsrc, ktile):
            for n in range(2):
                CP("pool", qb[0][:, 0:64], src_f32[:, n * 64:(n + 1) * 64], [bsrc], [b_qb[0]])
                CP("pool", qb[0][:, 64:128], src_f32[:, n * 64:(n + 1) * 64], [bsrc], [b_qb[0]])
                ps2, bp2 = rr.get()
                pv = ps2[:].bitcast(BF16)[:, 0:128]
                transpose_to(pv, qb[0][:, 0:128], [b_qb[0]], [bp2])
                CP("act", kT[:, n, ktile * 128:(ktile + 1) * 128], pv, [bp2], [b_kT])
        for t in range(8):
            ps, bp = rr.get()
            for k in range(8):
                MM(ps[:, 0:256], hT[:, k, t * 128:(t + 1) * 128], wK[:, k, :], k == 0, k == 7, [b_hT, bwK], [bp])
            CP("act", vaug[:, nk_ctx + t, :, 0:64], ps[:, 128:256].rearrange("p (n e) -> p n e", n=2), [bp], [b_va])
            qf, b_qf, sq, b_sq, rs, b_rs = nxt()
            if not is_s:
                CP("dve", kst[:, 128:256], ps[:, 128:256], [bp], [b_kst])
                STO(o_wv[t // 2, l, (t % 2) * 128:(t % 2 + 1) * 128, :], kst[:, 128:256], b_kst)
                rms_groups(ps, bp, 128, gk[:], b_g, kst, b_kst, sq, b_sq, rs, b_rs, t, False)
                STO(o_wk[t // 2, l, (t % 2) * 128:(t % 2 + 1) * 128, :], kst[:, 0:128], b_kst)
                put_k(kst, b_kst, nk_ctx + t)
            else:
                rms_groups(ps, bp, 128, gk[:], b_g, qf, b_qf, sq, b_sq, rs, b_rs, t, True)
                put_k(qf, b_qf, nk_ctx + t)
        if is_s:
            for kt in range(2):
                LD(kst[:, 0:128], cwk[l, kt * 128:(kt + 1) * 128, :], b_kst)
                put_k(kst, b_kst, kt)
                LD(kst[:, 128:256], cwv[l, kt * 128:(kt + 1) * 128, :], b_kst)
                CP("pool", vaug[:, kt, :, 0:64], kst[:, 128:256].rearrange("p (n e) -> p n e", n=2), [b_kst], [b_va])
        obk = [(psum[4 + i], psb[4 + i]) for i in range(4)]
        pc_ = [0]
        stages = []

        def evac(ob, bob, qt, tq, h):
            TT("dve", rd[:, qt:qt + 1], ob[:, 64:65], snk[:, h:h + 1], ALU.add, [bob, b_snk], [b_rd])
            S.op("dve", lambda e, qt=qt: e.reciprocal(out=rd[:, qt:qt + 1], in_=rd[:, qt:qt + 1]), [b_rd], [b_rd])
            TS("dve", on[:, tq, h * 64:(h + 1) * 64], ob[:, 0:64], rd[:, qt:qt + 1], ALU.mult, [bob, b_rd], [b_on])
        for h in range(8):
            n = h // 4
            j4 = h // 2
            bsl = slice((h % 2) * 64, (h % 2 + 1) * 64)
            if not is_s:
                for s in range(nseq):
                    q0 = s * L
                    keyt = [s * nt + i for i in range(nt)]
                    for ki, kt in enumerate(keyt):
                        st = {}

                        def A(st=st, bsl=bsl, n=n, j4=j4, kt=kt, q0=q0):
                            psS, bpS = rr.get()
                            MM(psS[:, 0:L], kT[bsl, n, kt * 128:(kt + 1) * 128], qT[bsl, j4, q0:q0 + L], True, True, [b_kT, b_qT], [bpS])
                            z = pc_[0] % 4; pc_[0] += 1
                            st["z"] = z
                            ACT(pT[z][:, 0:L], psS[:, 0:L], AF.Exp, [bpS], [b_pT[z]])

                        def B(st=st, n=n, kt=kt, ki=ki, nk=len(keyt), s=s, h=h):
                            z = st["z"]
                            for qt in range(nt):
                                MM(obk[qt][0][:, 0:65], pT[z][:, qt * 128:(qt + 1) * 128], vaug[:, kt, n, 0:65], ki == 0, ki == nk - 1, [b_pT[z], b_va], [obk[qt][1]])
                            if ki == nk - 1:
                                for qt in range(nt):
                                    evac(obk[qt][0], obk[qt][1], qt, s * nt + qt, h)
                        stages.append((A, B))
            else:
                for tq in range(8):
                    qt = tq % 4
                    keys = [(0, None), (1, None)]
                    if tq > 0:
                        keys.append((nk_ctx + tq - 1, "prev"))
                    keys.append((nk_ctx + tq, None))
                    if tq < 7:
                        keys.append((nk_ctx + tq + 1, "next"))
                    for ki, (kt, msk) in enumerate(keys):
                        st = {}

                        def A(st=st, bsl=bsl, n=n, j4=j4, kt=kt, tq=tq, msk=msk):
                            psS, bpS = rr.get()
                            MM(psS[:, 0:128], kT[bsl, n, kt * 128:(kt + 1) * 128], qT[bsl, j4, tq * 128:(tq + 1) * 128], True, True, [b_kT, b_qT], [bpS])
                            z = pc_[0] % 4; pc_[0] += 1
                            st["z"] = z
                            ACT(pT[z][:, 0:128], psS[:, 0:128], AF.Exp, [bpS], [b_pT[z]])
                            if msk is not None:
                                TT("dve", pT[z][:, 0:128], pT[z][:, 0:128], (tril if msk == "prev" else triu)[:], ALU.mult, [b_pT[z], b_tril, b_triu], [b_pT[z]])

                        def B(st=st, n=n, kt=kt, ki=ki, nk=len(keys), qt=qt, tq=tq, h=h):
                            z = st["z"]
                            ob, bob = obk[qt]
                            MM(ob[:, 0:65], pT[z][:, 0:128], vaug[:, kt, n, 0:65], ki == 0, ki == nk - 1, [b_pT[z], b_va], [bob])
                            if ki == nk - 1:
                                evac(ob, bob, qt, tq, h)
                        stages.append((A, B))
        LA = 3
        for k in range(min(LA, len(stages))):
            stages[k][0]()
        for k in range(len(stages)):
            if k + LA < len(stages):
                stages[k + LA][0]()
            stages[k][1]()
        for t in range(8):
            for j4 in range(4):
                ps2, bp2 = rr.get()
                pv = ps2[:].bitcast(BF16)[:, 0:128]
                transpose_to(pv, on[:, t, j4 * 128:(j4 + 1) * 128], [b_on], [bp2])
                yt_, by_ = yT(3, j4)
                CP("act", yt_[:, t * 128:(t + 1) * 128], pv, [bp2], [by_])
        S.op("pool", lambda e: e.memset(gq[:, 0:1], 0.0), [], ([b_qT, b_kT, b_va, b_g, b_snk, b_on, b_rd, b_kst] + b_qb + b_pT + b_qfL + b_sqL + b_rsL) + [b_scr_all])

    def merge(l):
        rr.set(range(8))
        barrier_begin()
        cv = Carve()
        mT = cv.take([8, T], BF16); b_mT = [NB(f"mT{c}") for c in range(8)]
        gs = [cv.take([512]) for _ in range(4)]; b_gs = [NB(f"gs{i}") for i in range(4)]
        tmp = [cv.take([512]) for _ in range(4)]; b_tmp = [NB(f"mt{i}") for i in range(4)]
        bgt = cv.take([32]); b_bgt = NB("bgt")
        S.op("pool", lambda e: e.memset(bgt[:], 0.0), [b_scr_all], [b_bgt, b_scr_all])
        LD(bgt[:], bgatec[l], b_bgt)
        kq = [0]
        for dcp in range(4):
            for br in range(4):
                wg, bwg = wload(w_gate[l][:, br * 1024 + dcp * 256: br * 1024 + (dcp + 1) * 256], 8, 256)
                wb_, bwb_ = wload(w_br[l][br * 512:(br + 1) * 512, dcp * 256:(dcp + 1) * 256], 4, 256)
                for c2 in range(2):
                    dc = dcp * 2 + c2
                    for h in range(2):
                        hs = slice(h * 512, (h + 1) * 512)
                        ti_ = c2 * 2 + h
                        psg, bpg = rr.get()
                        for k in range(8):
                            MM(psg[:, :], wg[:, k, c2 * 128:(c2 + 1) * 128], hT[:, k, hs], k == 0, k == 7, [b_hT, bwg], [bpg])
                        z = kq[0] % 4; kq[0] += 1
                        ACT(gs[z][:], psg[:, :], AF.Sigmoid, [bpg, b_bgt], [b_gs[z]], bias=bgt[:, br * 8 + dc:br * 8 + dc + 1])
                        psb_, bpb_ = rr.get()
                        for k in range(4):
                            yt_, by_ = yT(br, k)
                            MM(psb_[:, :], wb_[:, k, c2 * 128:(c2 + 1) * 128], yt_[:, hs], k == 0, k == 3, [by_, bwb_], [bpb_])
                        if br == 0:
                            TT("dve", tmp[ti_][:], psb_[:, :], gs[z][:], ALU.mult, [bpb_, b_gs[z]], [b_tmp[ti_]])
                        else:
                            TT("dve", gs[z][:], psb_[:, :], gs[z][:], ALU.mult, [bpb_, b_gs[z]], [b_gs[z]])
                            if br < 3:
                                TT("pool", tmp[ti_][:], tmp[ti_][:], gs[z][:], ALU.add, [b_tmp[ti_], b_gs[z]], [b_tmp[ti_]])
                            else:
                                TT("pool", mT[:, dc, hs], tmp[ti_][:], gs[z][:], ALU.add, [b_tmp[ti_], b_gs[z]], [b_mT[dc]])
        if sub == 54:
            raise _Stop()
        for cb4 in range(4):
            w, bw = wload(w_out[l][:, cb4 * 256:(cb4 + 1) * 256], 8, 256)
            for t in range(8):
                ps, bp = rr.get()
                for k in range(8):
                    MM(ps[:, 0:256], mT[:, k, t * 128:(t + 1) * 128], w[:, k, :], k == 0, k == 7, [b_mT[k], bw], [bp])
                cs = slice(cb4 * 256, (cb4 + 1) * 256)
                zz = kq[0] % 4; kq[0] += 1
                TT("dve", gs[zz][:, 0:256], ps[:, 0:256], gbc[:, 0, cs], ALU.mult, [bp, b_gbc[0]], [b_gs[zz]])
                TT("pool", xres[:, t, cs], xres[:, t, cs], gs[zz][:, 0:256], ALU.add, [b_xres[t], b_gs[zz]], [b_xres[t]])
        S.op("pool", lambda e: e.memset(bgt[:, 0:1], 0.0), [], (b_mT + b_gs + b_tmp + [b_bgt]) + [b_scr_all])
        rr.set(range(4))

    def mlp(l):
        barrier_begin()
        cvm = Carve()
        rl = [cvm.take([512]) for _ in range(4)]; b_rl = [NB(f"rl{i}") for i in range(4)]
        rs_ = [cvm.take([256]) for _ in range(4)]; b_rs_ = [NB(f"rsd{i}") for i in range(4)]
        mk = [0, 0]
        for h in range(2):
            hs = slice(h * 512, (h + 1) * 512)

            def aTv(kc):
                return big[:, kc // 2, (kc % 2) * 512:(kc % 2 + 1) * 512], b_big[kc // 2]
            for blk in range(16):
                w, bw = wload(w_fc1[l][:, blk * 256:(blk + 1) * 256], 8, 256)
                for c2 in range(2):
                    kc = blk * 2 + c2
                    ps, bp = rr.get()
                    for k in range(8):
                        MM(ps[:, :], w[:, k, c2 * 128:(c2 + 1) * 128], hT[:, k, hs], k == 0, k == 7, [b_hT, bw], [bp])
                    a_, ba_ = aTv(kc)
                    zr = mk[0] % 4; mk[0] += 1
                    ACT(rl[zr][:], ps[:, :], AF.Relu, [bp], [b_rl[zr]])
                    TT("dve", a_, rl[zr][:], rl[zr][:], ALU.mult, [b_rl[zr]], [ba_])
            for cb4 in range(4):
                cs = slice(cb4 * 256, (cb4 + 1) * 256)
                accb = [(psum[4 + i], psb[4 + i]) for i in range(4)]
                for kg in range(4):
                    w, bw = wload(w_fc2[l][kg * 1024:(kg + 1) * 1024, cs], 8, 256)
                    for tt in range(4):
                        for k in range(8):
                            kc = kg * 8 + k
                            a_, ba_ = aTv(kc)
                            MM(accb[tt][0][:, 0:256], a_[:, tt * 128:(tt + 1) * 128], w[:, k, :], kc == 0, kc == 31, [ba_, bw], [accb[tt][1]])
                for tt in range(4):
                    t = h * 4 + tt
                    zq = mk[1] % 4; mk[1] += 1
                    TT("dve", rs_[zq][:], accb[tt][0][:, 0:256], gbc[:, 1, cs], ALU.mult, [accb[tt][1], b_gbc[1]], [b_rs_[zq]])
                    TT("pool", xres[:, t, cs], xres[:, t, cs], rs_[zq][:], ALU.add, [b_xres[t], b_rs_[zq]], [b_xres[t]])

        S.op("pool", lambda e: e.memset(small[:, 62:63], 0.0), [], b_rl + b_rs_ + [b_scr_all])

    try:
        Sched.PHASE = "prologue"
        adaln_weights(0)
        adaln_weights(1)
        run_pass(0)
        run_pass(1)
    except _Stop:
        pass
    if stop is not None:
        d_hT = nc.dram_tensor("dbg_hT", [128, 8, T], BF16, kind="ExternalOutput").ap()
        d_big = nc.dram_tensor("dbg_big", [128, 16, 1024], BF16, kind="ExternalOutput").ap()
        d_x = nc.dram_tensor("dbg_x", [128, 8, D], F32, kind="ExternalOutput").ap()
        d_modc = nc.dram_tensor("dbg_modc", [128, 48], F32, kind="ExternalOutput").ap()
        S.dma("sp", d_hT[:, :, :], hT[:], reads=[b_hT])
        S.dma("sp", d_big[:, :, :], big[:], reads=b_big, sbuf=b_big[0])
        S.dma("sp", d_x[:, :, :], xres[:], reads=b_xres, sbuf=b_xres[0])
        S.dma("sp", d_modc[:, :], modc[:], reads=[b_modc])
    with nc.Block() as block:
        S.emit(block)
    es.close()
    nc._phases = {e: [o.phase for o in S.ops[e]] for e in ENGS}
    return nc


_NC_CACHE = {}


def _consts():
    bf = ml_dtypes.bfloat16
    c = {}
    c["c_identb"] = np.eye(128, dtype=np.float32).astype(bf)
    c["c_identf"] = np.eye(128, dtype=np.float32)
    c["c_ones"] = np.ones((128, 128), np.float32)
    k = np.arange(128)[:, None]; t = np.arange(128)[None, :]
    c["c_triu"] = (k <= t).astype(np.float32)
    c["c_tril"] = (k >= t).astype(np.float32)
    c["c_mnegF"] = np.where(k <= t, 0.0, -1e30).astype(np.float32)
    c["c_mnegB"] = np.where(k >= t, 0.0, -1e30).astype(np.float32)
    c["c_bprev"] = (t <= k).astype(np.float32)
    c["c_bnext"] = (k <= t).astype(np.float32)
    mB = np.zeros((128, 4, 128), np.float32)
    mC = np.zeros((128, 4, 128), np.float32)
    for q in range(4):
        for gl in range(2):
            g8 = 2 * q + gl
            mB[gl * 64:(gl + 1) * 64, q, g8 * 16:(g8 + 1) * 16] = 1.0
            mC[g8 * 16:(g8 + 1) * 16, q, gl * 64:(gl + 1) * 64] = 1.0
    c["c_maskB"] = mB; c["c_maskC"] = mC
    c["c_iota"] = np.broadcast_to(np.arange(1024, dtype=np.float32)[None, :], (128, 1024)).copy()
    Ls = 1024
    row = np.repeat(np.arange(Ls // 64), 64).astype(np.float32); col = np.tile(np.arange(64), Ls // 64).astype(np.float32)
    nf = 16
    inv = (10000.0 ** (-np.arange(nf, dtype=np.float32) / nf)).astype(np.float32)
    ang = np.concatenate([row[:, None] * inv, col[:, None] * inv], axis=-1).astype(np.float32)
    cs, sn = np.cos(ang).astype(np.float32), np.sin(ang).astype(np.float32)
    C64 = np.zeros((Ls, 2, 2, 16), np.float32); S64 = np.zeros((Ls, 2, 2, 16), np.float32)
    for a in range(2):
        for p in range(2):
            C64[:, a, p, :] = cs[:, a * 16:(a + 1) * 16]
            S64[:, a, p, :] = (-1.0 if p == 0 else 1.0) * sn[:, a * 16:(a + 1) * 16]
    c["c_ropeC"] = C64.reshape(8, 128, 64).transpose(1, 0, 2).copy()
    c["c_ropeS"] = S64.reshape(8, 128, 64).transpose(1, 0, 2).copy()
    lidx = np.arange(1024)
    c["c_rmF"] = np.broadcast_to((lidx % 256 != 0).astype(np.float32)[None, :], (128, 1024)).astype(bf)
    c["c_rmB"] = np.broadcast_to((lidx % 256 != 255).astype(np.float32)[None, :], (128, 1024)).astype(bf)
    return c


def _colmajor(v, nchunk):
    return np.ascontiguousarray(np.swapaxes(v.reshape(v.shape[:-1] + (nchunk, 128)), -1, -2))


def make_in_maps(inp):
    f = lambda a: np.ascontiguousarray(np.asarray(a, dtype=np.float32))
    I = {k: f(v) for k, v in inp.items()}
    shared = dict(_consts())
    shared.update({
        "w_mod": I["w_mod"], "w_in": I["w_in"], "w_gate": I["w_gate"], "w_out": I["w_out"], "w_fc1": I["w_fc1"], "w_fc2": I["w_fc2"],
        "w_glu": I["s5_w_glu"], "w_br": I["w_branch"].reshape(2, 2048, 1024),
        "bmodc": _colmajor(I["b_mod"], 48), "g1c": _colmajor(I["g_norm1"], 8), "g2c": _colmajor(I["g_norm2"], 8),
        "convw": np.ascontiguousarray(I["ssd_conv_w"].transpose(0, 2, 1).reshape(2, 6, 128, 7).transpose(0, 2, 1, 3)),
        "convb": _colmajor(I["ssd_conv_b"], 6),
        "dtb": I["ssd_dt_bias"].reshape(2, 16), "alog": I["ssd_a_log"].reshape(2, 16), "ssdd": I["ssd_d"], "normgc": _colmajor(I["ssd_norm_g"], 4),
        "lamre": I["s5_lam_re"].reshape(2, 32, 128), "lamim": I["s5_lam_im"].reshape(2, 32, 128),
        "lsx": np.ascontiguousarray(np.repeat(I["s5_log_step"].reshape(2, 2, 32, 1), 64, axis=-1).reshape(2, 32, 128)),
        "s5bre": I["s5_b_re"].reshape(2, 2, 2048, 16), "s5bim": I["s5_b_im"].reshape(2, 2, 2048, 16),
        "s5cre": I["s5_c_re"].reshape(2, 2, 512, 64), "s5cim": I["s5_c_im"].reshape(2, 2, 512, 64),
        "s5dc": _colmajor(I["s5_d"], 4), "bgluc": _colmajor(I["s5_b_glu"], 8),
        "dqg": I["diff_qn_g"], "dkg": I["diff_kn_g"], "dlam": I["diff_lambda"].reshape(2, 256), "dsubc": I["diff_subln_g"].reshape(2, 128, 1),
        "wqg": I["win_qn_g"], "wkg": I["win_kn_g"], "wsink": I["win_sink"], "bgatec": _colmajor(I["b_gate"], 32),
    })
    in_maps = []
    for i in range(8):
        b = i // 2
        cv = np.stack([I["c_ctx"], I["c"][b]], axis=0)
        m = dict(shared)
        m.update({
            "xp": I["x_prompt"][4 * i:4 * i + 4].reshape(1024, 1024), "xs": I["x_sample"][b],
            "cvT": np.ascontiguousarray(cv.reshape(2, 8, 128).transpose(2, 1, 0)),
            "st_ssd": I["state_ssd"][b], "st_s5": I["state_s5"][b].reshape(2, 64, 128),
            "cdk": I["cache_diff_k"][b].reshape(2, 256, 512), "cdv": I["cache_diff_v"][b].reshape(2, 256, 512),
            "cwk": I["cache_win_k"][b].reshape(2, 256, 128), "cwv": I["cache_win_v"][b].reshape(2, 256, 128),
        })
        in_maps.append({k: np.ascontiguousarray(v) for k, v in m.items()})
    return in_maps


def kernel(**inp):
    if "nc" not in _NC_CACHE:
        _NC_CACHE["nc"] = build_program()
    nc = _NC_CACHE["nc"]
    in_maps = make_in_maps(inp)
    res = run_bass_kernel_spmd(nc, in_maps, core_ids=list(range(8)))
    R = res.results
    yp = np.concatenate([R[i]["yp"].reshape(4, 256, 1024) for i in range(8)], axis=0)
    ys = np.stack([R[2 * b]["ys"] for b in range(4)], axis=0)
    ssd = np.concatenate([R[i]["o_ssd"] for i in range(8)], axis=0)
    s5 = np.concatenate([R[i]["o_s5"].reshape(4, 2, 2, 2, 32, 64) for i in range(8)], axis=0)
    dk = np.concatenate([R[i]["o_dk"].reshape(4, 2, 256, 4, 2, 64) for i in range(8)], axis=0)
    dv = np.concatenate([R[i]["o_dv"].reshape(4, 2, 256, 4, 128) for i in range(8)], axis=0)
    wk = np.concatenate([R[i]["o_wk"].reshape(4, 2, 256, 2, 64) for i in range(8)], axis=0)
    wv = np.concatenate([R[i]["o_wv"].reshape(4, 2, 256, 2, 64) for i in range(8)], axis=0)
    return tuple(np.ascontiguousarray(a.astype(np.float32)) for a in (yp, ys, ssd, s5, dk, dv, wk, wv))
```

```python
import math
import numpy as np
from contextlib import ExitStack
import ml_dtypes
import concourse.bass as bass
import concourse.mybir as mybir
from concourse.bass_utils import run_bass_kernel_spmd

F32 = mybir.dt.float32
BF16 = mybir.dt.bfloat16
I32 = mybir.dt.int32
ALU = mybir.AluOpType
AF = mybir.ActivationFunctionType
AX = mybir.AxisListType
ENGS = ("pe", "dve", "act", "pool", "sp")
EPS = 1e-6


class Buf:
    __slots__ = ("name", "last_w", "readers", "load_sem", "load_cnt", "store_sem", "store_cnt", "excl")

    def __init__(self, name, excl=False):
        self.name = name
        self.excl = excl
        self.last_w = None
        self.readers = []
        self.load_sem = None
        self.load_cnt = 0
        self.store_sem = None
        self.store_cnt = 0


class Op:
    __slots__ = ("eng", "fn", "deps", "signal", "semval", "is_dma", "dsem", "dval", "phase")

    def __init__(self, eng, fn):
        self.phase = Sched.PHASE
        self.eng = eng
        self.fn = fn
        self.deps = []
        self.signal = False
        self.semval = 0
        self.is_dma = False
        self.dsem = None
        self.dval = 0


class Sched:
    PHASE = ""

    def __init__(self, nc, es):
        self.nc = nc
        self.es = es
        self.ops = {e: [] for e in ENGS}
        self.sems = {e: es.enter_context(nc.semaphore("c_" + e)) for e in ENGS}
        self.store_bufs = []
        self.nsem = 5
        self.pool = {}

    def new_sem(self, name):
        self.nsem += 1
        return self.es.enter_context(self.nc.semaphore(f"{name}_{self.nsem}"))

    def _track(self, op, reads, writes, skip_waw=False):
        deps = op.deps
        for r in reads:
            if r.last_w is not None and r.last_w is not op:
                deps.append(r.last_w)
            if r.excl:
                deps.extend(x for x in r.readers if x is not op and x.eng != op.eng)
            r.readers.append(op)
        for w in writes:
            if w.last_w is not None and w.last_w is not op and not skip_waw:
                deps.append(w.last_w)
            deps.extend(r for r in w.readers if r is not op)
            w.last_w = op
            w.readers = []

    def op(self, eng, fn, reads=(), writes=()):
        o = Op(eng, fn)
        self._track(o, reads, writes)
        self.ops[eng].append(o)
        return o

    def dma(self, q, out, in_, reads=(), writes=(), group=False, sbuf=None, **kw):
        o = Op(q, lambda e: e.dma_start(out=out, in_=in_, **kw))
        o.is_dma = True
        self._track(o, reads, writes, skip_waw=group)
        if sbuf is None:
            sbuf = writes[0] if writes else reads[0]
        key = ("l_" if sbuf in writes else "s_") + sbuf.name
        ent = self.pool.get(key)
        if ent is None:
            ent = [self.new_sem(key), 0]
            self.pool[key] = ent
        ent[1] += 16
        o.dsem, o.dval = ent[0], ent[1]
        self.ops[q].append(o)
        return o

    def emit(self, block):
        for e in ENGS:
            for o in self.ops[e]:
                for d in o.deps:
                    if not d.is_dma and not (d.eng == "pe" and o.eng == "pe"):
                        d.signal = True
        for e in ENGS:
            v = 0
            for o in self.ops[e]:
                if o.signal and not o.is_dma:
                    v += 1
                    o.semval = v
        engmap = {"pe": block.tensor, "dve": block.vector, "act": block.scalar,
                  "pool": block.gpsimd, "sp": block.sync}
        sems = self.sems
        store_bufs = self.store_bufs
        for e in ENGS:
            def body(eng, ops=self.ops[e], e=e):
                known = {}
                for o in ops:
                    need = {}
                    for d in o.deps:
                        if d.is_dma:
                            key, val = d.dsem, d.dval
                        else:
                            if d.eng == "pe" and e == "pe":
                                continue
                            key, val = sems[d.eng], d.semval
                        if need.get(key, 0) < val:
                            need[key] = val
                    for key, val in need.items():
                        if known.get(key, 0) < val:
                            eng.wait_ge(key, val)
                            known[key] = val
                    inst = o.fn(eng)
                    if o.is_dma:
                        inst.then_inc(o.dsem, 16)
                    elif o.signal:
                        inst.then_inc(sems[e], 1)
                if e == "sp":
                    for key, ent in self.pool.items():
                        if key.startswith("s_"):
                            eng.wait_ge(ent[0], ent[1])
            engmap[e](body)


D = 1024
T = 1024
W_IN = 4112
C_Z, C_XBC, C_DT, C_U, C_DQ, C_DK, C_DV, C_WQ, C_WK, C_WV = 0, 512, 1280, 1296, 1808, 2320, 2832, 3344, 3856, 3984


class _Stop(Exception):
    pass


def build_program(stop=None, sub=None):
    nc = bass.Bass("TRN2", target_bir_lowering=False)
    es = ExitStack()
    S = Sched(nc, es)

    def din(name, shape, dt=F32):
        return nc.dram_tensor(name, list(shape), dt, kind="ExternalInput").ap()

    def dout(name, shape):
        return nc.dram_tensor(name, list(shape), F32, kind="ExternalOutput").ap()

    cnt = [0]

    def sb(shape, dt=F32, name=None):
        cnt[0] += 1
        return es.enter_context(nc.sbuf_tensor(name or f"t{cnt[0]}", list(shape), dt))

    xin = [din("xp", [T, D]), din("xs", [T, D])]
    yout = [dout("yp", [T, D]), dout("ys", [T, D])]
    cvT_d = din("cvT", [128, 8, 2])
    st_ssd = din("st_ssd", [2, 2, 8, 64, 64])
    st_s5 = din("st_s5", [2, 64, 128])
    cdk = din("cdk", [2, 256, 512]); cdv = din("cdv", [2, 256, 512])
    cwk = din("cwk", [2, 256, 128]); cwv = din("cwv", [2, 256, 128])
    w_mod = din("w_mod", [2, D, 6 * D]); w_in = din("w_in", [2, D, W_IN]); w_gate = din("w_gate", [2, D, 4 * D])
    w_out = din("w_out", [2, D, D]); w_fc1 = din("w_fc1", [2, D, 4 * D]); w_fc2 = din("w_fc2", [2, 4 * D, D])
    w_glu = din("w_glu", [2, 512, 1024]); w_br = din("w_br", [2, 2048, 1024])
    bmodc = din("bmodc", [2, 128, 48]); g1c = din("g1c", [2, 128, 8]); g2c = din("g2c", [2, 128, 8])
    convw = din("convw", [2, 128, 6, 7]); convb = din("convb", [2, 128, 6])
    dtb = din("dtb", [2, 16]); alog = din("alog", [2, 16]); ssdd = din("ssdd", [2, 8]); normgc = din("normgc", [2, 128, 4])
    lamre = din("lamre", [2, 32, 128]); lamim = din("lamim", [2, 32, 128]); lsx = din("lsx", [2, 32, 128])
    s5bre = din("s5bre", [2, 2, 2048, 16]); s5bim = din("s5bim", [2, 2, 2048, 16])
    s5cre = din("s5cre", [2, 2, 512, 64]); s5cim = din("s5cim", [2, 2, 512, 64])
    s5dc = din("s5dc", [2, 128, 4]); bgluc = din("bgluc", [2, 128, 8])
    dqg = din("dqg", [2, 64]); dkg = din("dkg", [2, 64]); dlam = din("dlam", [2, 256]); dsubc = din("dsubc", [2, 128, 1])
    wqg = din("wqg", [2, 64]); wkg = din("wkg", [2, 64]); wsink = din("wsink", [2, 8]); bgatec = din("bgatec", [2, 128, 32])
    c_identb = din("c_identb", [128, 128], BF16); c_identf = din("c_identf", [128, 128]); c_ones = din("c_ones", [128, 128])
    c_triu = din("c_triu", [128, 128]); c_tril = din("c_tril", [128, 128])
    c_mnegF = din("c_mnegF", [128, 128]); c_mnegB = din("c_mnegB", [128, 128])
    c_bprev = din("c_bprev", [128, 128]); c_bnext = din("c_bnext", [128, 128])
    c_maskB = din("c_maskB", [128, 4, 128]); c_maskC = din("c_maskC", [128, 4, 128])
    c_iota = din("c_iota", [128, 1024]); c_ropeC = din("c_ropeC", [128, 8, 64]); c_ropeS = din("c_ropeS", [128, 8, 64])
    c_rmF = din("c_rmF", [128, 1024], BF16); c_rmB = din("c_rmB", [128, 1024], BF16)
    o_ssd = dout("o_ssd", [4, 2, 2, 8, 64, 64]); o_s5 = dout("o_s5", [4, 2, 64, 128])
    o_dk = dout("o_dk", [4, 2, 256, 512]); o_dv = dout("o_dv", [4, 2, 256, 512])
    o_wk = dout("o_wk", [4, 2, 256, 128]); o_wv = dout("o_wv", [4, 2, 256, 128])

    def TT(eng, out, in0, in1, op, r, w):
        S.op(eng, lambda e: e.tensor_tensor(out=out, in0=in0, in1=in1, op=op), r, w)

    def TS(eng, out, in0, s1, op0, r, w, s2=None, op1=None):
        if op1 is None:
            S.op(eng, lambda e: e.tensor_scalar(out=out, in0=in0, scalar1=s1, scalar2=None, op0=op0), r, w)
        else:
            S.op(eng, lambda e: e.tensor_scalar(out=out, in0=in0, scalar1=s1, scalar2=s2, op0=op0, op1=op1), r, w)

    def STT(out, in0, scalar, in1, op0, op1, r, w):
        S.op("dve", lambda e: e.scalar_tensor_tensor(out=out, in0=in0, scalar=scalar, in1=in1, op0=op0, op1=op1), r, w)

    def ACT(out, in_, func, r, w, scale=1.0, bias=None, accum=None):
        kw = {}
        if bias is not None:
            kw["bias"] = bias
        if accum is not None:
            kw["accum_out"] = accum
        S.op("act", lambda e: e.activation(out=out, in_=in_, func=func, scale=scale, **kw), r, w)

    def CP(eng, out, in_, r, w):
        if eng == "act":
            S.op("act", lambda e: e.copy(out=out, in_=in_), r, w)
        else:
            S.op(eng, lambda e: e.tensor_copy(out=out, in_=in_), r, w)

    def MM(out, lhsT, rhs, start, stop, r, w):
        S.op("pe", lambda e: e.matmul(out, lhsT=lhsT, rhs=rhs, start=start, stop=stop), r, w)

    def MSET(eng, out, val, w):
        S.op(eng, lambda e: e.memset(out, val), (), w)

    def LD(out, in_, b, q="sp", group=False):
        S.dma(q, out, in_, writes=[b], group=group)

    def STO(out, in_, b, q="sp"):
        S.dma(q, out, in_, reads=[b])

    def const(src, shape, dt=F32):
        t = sb(shape, dt)
        b = Buf(f"c{cnt[0]}")
        LD(t[:], src, b)
        return t, b

    identb, b_identb = const(c_identb[:, :], [128, 128], BF16)
    identf, b_identf = const(c_identf[:, :], [128, 128])
    onesf, b_ones = const(c_ones[:, :], [128, 128])
    triu, b_triu = const(c_triu[:, :], [128, 128]); tril, b_tril = const(c_tril[:, :], [128, 128])
    mnegF, b_mnegF = const(c_mnegF[:, :], [128, 128]); mnegB, b_mnegB = const(c_mnegB[:, :], [128, 128])
    maskB, b_maskB = const(c_maskB[:, :, :], [128, 4, 128]); maskC, b_maskC = const(c_maskC[:, :, :], [128, 4, 128])
    ropeC, b_ropeC = const(c_ropeC[:, :, :], [128, 8, 64]); ropeS, b_ropeS = const(c_ropeS[:, :, :], [128, 8, 64])
    CONSTB = [b_identb, b_identf, b_ones]

    psum = [es.enter_context(nc.psum_tensor(f"ps{i}", [128, 512], F32)) for i in range(8)]
    psb = [Buf(f"ps{i}", excl=True) for i in range(8)]

    class RR:
        def __init__(self, ids):
            self.ids = list(ids); self.i = 0

        def get(self):
            k = self.ids[self.i % len(self.ids)]; self.i += 1
            return psum[k], psb[k]

        def set(self, ids):
            self.ids = list(ids)

    rr = RR(range(0, 4))

    xres = sb([128, 8, D]); b_xres = [Buf(f"xres{t}") for t in range(8)]
    hT = sb([128, 8, T], BF16); b_hT = Buf("hT")
    NST, NBF = 3, 3
    wst = [sb([128, 8, 256]) for _ in range(NST)]; b_wst = [Buf(f"wst{i}") for i in range(NST)]
    wbf = [sb([128, 8, 256], BF16) for _ in range(NBF)]; b_wbf = [Buf(f"wbf{i}") for i in range(NBF)]
    wctr = [0, 0]
    big = sb([128, 16, 1024], BF16)
    b_big = [Buf(f"big{i}") for i in range(16)]
    modc = sb([128, 48]); b_modc = Buf("modc")
    scol = sb([128, 8, 2]); b_scol = Buf("scol")
    G1 = sb([128, 8]); SH1 = sb([128, 8]); G2 = sb([128, 8]); SH2 = sb([128, 8]); b_G = Buf("G")
    gbc = sb([128, 2, D]); b_gbc = [Buf("gbc0"), Buf("gbc1")]
    small = sb([128, 64]); b_small = Buf("small")
    junk = sb([128, 768]); b_junk = Buf("junk")
    SCR_BYTES = 60 * 1024
    scr = sb([128, SCR_BYTES // 4])

    def wload(src, nk, ncols, cast=True):
        i = wctr[0] % NST; wctr[0] += 1
        st, bs = wst[i], b_wst[i]
        LD(st[:, 0:nk, 0:ncols], src.rearrange("(k p) c -> p k c", p=128), bs)
        if not cast:
            return st, bs
        j = wctr[1] % NBF; wctr[1] += 1
        wb, bb = wbf[j], b_wbf[j]
        heavy = any(k in Sched.PHASE for k in ("prologue", "merge", "mlp"))
        eng = "act" if (wctr[1] % 2 == 0 or not heavy) else "dve"
        CP(eng, wb[:, 0:nk, 0:ncols], st[:, 0:nk, 0:ncols], [bs], [bb])
        return wb, bb

    def proj_tm(src, bsrc, nk, w, bw, ncols, tiles, evac):
        for t in tiles:
            ps, bp = rr.get()
            for k in range(nk):
                MM(ps[:, 0:ncols], src[:, k, t * 128:(t + 1) * 128], w[:, k, 0:ncols], k == 0, k == nk - 1, [bsrc, bw], [bp])
            evac(t, ps, bp)

    def proj_fm(src, bsrc, nk, w, bw, ncols, evac, halves=(0, 1)):
        for cc in range((ncols + 127) // 128):
            m = min(128, ncols - cc * 128)
            for h in halves:
                ps, bp = rr.get()
                for k in range(nk):
                    MM(ps[0:m, :], w[:, k, cc * 128:cc * 128 + m], src[:, k, h * 512:(h + 1) * 512], k == 0, k == nk - 1, [bsrc, bw], [bp])
                evac(cc, h, ps, bp)

    def transpose_to(ps_out, in_, r, w, dt=BF16, np_=128):
        idn = identb if dt == BF16 else identf
        S.op("pe", lambda e: e.transpose(out=ps_out, in_=in_, identity=idn[0:np_, 0:np_]), list(r) + CONSTB, w)

    def bcast_rows(col_ap, bcol, ps_out, bp):
        dg = sb_diag[dgc[0] % 4]; bd = b_diag[dgc[0] % 4]; dgc[0] += 1
        TS("dve", dg[:], identf[:], col_ap, ALU.mult, [b_identf, bcol], [bd])
        MM(ps_out, onesf[:], dg[:], True, True, [b_ones, bd], [bp])

    sb_diag = [sb([128, 128]) for _ in range(4)]; b_diag = [Buf(f"dg{i}") for i in range(4)]; dgc = [0]

    def rstd_from_ss(ss_ap, n, out_ap, r, w, ncols=1):
        TS("dve", out_ap, ss_ap, 1.0 / n, ALU.mult, r, w, s2=EPS, op1=ALU.add)
        ACT(out_ap, out_ap, AF.Sqrt, w, w)
        S.op("dve", lambda e: e.reciprocal(out=out_ap, in_=out_ap), w, w)

    LD(scol[:], cvT_d[:, :, :], b_scol)
    ACT(scol[:], scol[:], AF.Silu, [b_scol], [b_scol])
    scolb = sb([128, 8, 2], BF16)
    CP("dve", scolb[:], scol[:], [b_scol], [b_scol])

    modall = sb([128, 2, 2, 48]); b_modall = Buf("modall")

    def adaln_weights(l):
        rr.set(range(8))
        bm = sb_bm; LD(bm[:], bmodc[l], b_bm)
        for blk in range(24):
            w, bw = wload(w_mod[l][:, blk * 256:(blk + 1) * 256], 8, 256)
            for cc in range(2):
                ps, bp = rr.get()
                for k in range(8):
                    MM(ps[:, 0:2], w[:, k, cc * 128:(cc + 1) * 128], scolb[:, k, 0:2], k == 0, k == 7, [bw, b_scol], [bp])
                c = blk * 2 + cc
                TT("dve", modall[:, l, :, c], ps[:, 0:2], bm[:, c:c + 1].broadcast_to([128, 2]), ALU.add, [bp, b_bm], [b_modall])
        rr.set(range(4))

    def adaln(l, path):
        rr.set(range(8))
        CP("dve", modc[:], modall[:, l, path, :], [b_modall], [b_modc])
        gt = sb_gt; LD(gt[:, 0:8], g1c[l], b_gt); LD(gt[:, 8:16], g2c[l], b_gt, group=True)
        STT(G1[:], modc[:, 8:16], 1.0, gt[:, 0:8], ALU.add, ALU.mult, [b_modc, b_gt], [b_G])
        STT(G2[:], modc[:, 32:40], 1.0, gt[:, 8:16], ALU.add, ALU.mult, [b_modc, b_gt], [b_G])
        CP("dve", SH1[:], modc[:, 0:8], [b_modc], [b_G])
        CP("dve", SH2[:], modc[:, 24:32], [b_modc], [b_G])
        for gi, base in enumerate((16, 40)):
            for c in range(8):
                ps, bp = rr.get()
                bcast_rows(modc[:, base + c:base + c + 1], b_modc, ps[:, 0:128], bp)
                CP("act", gbc[:, gi, c * 128:(c + 1) * 128], ps[:, 0:128], [bp], [b_gbc[gi]])
        rr.set(range(4))

    sb_bm = sb([128, 48]); b_bm = Buf("bm"); sb_gt = sb([128, 16]); b_gt = Buf("gt")

    xn = [sb([128, D], BF16), sb([128, D], BF16)]; b_xn = [Buf("xn0"), Buf("xn1")]

    def norm_mod(Gc, SHc):
        rr.set(range(8))
        jb = junk[:].bitcast(BF16)[:, 0:D]
        for t in range(8):
            ACT(jb, xres[:, t, :], AF.Square, [b_xres[t]], [b_junk, b_small], accum=small[:, t:t + 1])
        rstd_from_ss(small[:, 0:8], D, small[:, 8:16], [b_small], [b_small])
        stg = []
        for t in range(8):
            def FA(t=t):
                x_, bx_ = xn[t % 2], b_xn[t % 2]
                ACT(x_[:], xres[:, t, :], AF.Copy, [b_xres[t], b_small], [bx_], scale=small[:, 8 + t:9 + t])

            def FB(t=t):
                x_, bx_ = xn[t % 2], b_xn[t % 2]
                for c in range(8):
                    ps, bp = rr.get()
                    pv = ps[:].bitcast(BF16)[:, 0:128]
                    transpose_to(pv, x_[:, c * 128:(c + 1) * 128], [bx_], [bp])
                    if c % 2 == 0:
                        ACT(hT[:, c, t * 128:(t + 1) * 128], pv, AF.Identity, [bp, b_G], [b_hT], scale=Gc[:, c:c + 1], bias=SHc[:, c:c + 1])
                    else:
                        TS("dve", hT[:, c, t * 128:(t + 1) * 128], pv, Gc[:, c:c + 1], ALU.mult, [bp, b_G], [b_hT], s2=SHc[:, c:c + 1], op1=ALU.add)
            stg.append((FA, FB))
        stg[0][0]()
        for k in range(8):
            if k + 1 < 8:
                stg[k + 1][0]()
            stg[k][1]()
        rr.set(range(4))

    def yT(br, fc):
        return big[:, br * 4 + fc, :], b_big[br * 4 + fc]

    def run_pass(path):
        nseq, L = (4, 256) if path == 0 else (1, 1024)
        nt = L // 128
        is_s = path == 1
        for t in range(8):
            LD(xres[:, t, :], xin[path][t * 128:(t + 1) * 128, :], b_xres[t])
        def chk(stage, l):
            if stop is not None and stop == (path, l, stage):
                raise _Stop()
        for l in range(2):
            def ph(n):
                Sched.PHASE = f"{'PS'[path]}{l}_{n}"
            ph("adaln"); adaln(l, path); chk("adaln", l)
            ph("norm1"); norm_mod(G1, SH1); chk("norm1", l)
            ph("ssd"); branch_ssd(l, path, nseq, L, nt, is_s); chk("ssd", l)
            ph("s5"); branch_s5(l, path, nseq, L, nt, is_s); chk("s5", l)
            ph("diff"); branch_diff(l, path, nseq, L, nt, is_s); chk("diff", l)
            ph("win"); branch_win(l, path, nseq, L, nt, is_s); chk("win", l)
            ph("merge"); merge(l); chk("merge", l)
            ph("norm2"); norm_mod(G2, SH2); chk("norm2", l)
            ph("mlp"); mlp(l); chk("mlp", l)
        for t in range(8):
            STO(yout[path][t * 128:(t + 1) * 128, :], xres[:, t, :], b_xres[t])

    class Carve:
        def __init__(self):
            self.off = 0

        def take(self, shape, dt=F32):
            n = int(np.prod(shape))
            nbytes = n * (4 if dt in (F32, I32) else 2)
            nbytes = (nbytes + 31) // 32 * 32
            assert self.off + nbytes <= SCR_BYTES, (self.off, nbytes)
            v = scr[:, self.off // 4:(self.off + nbytes) // 4]
            self.off += nbytes
            if dt != F32:
                v = v.bitcast(dt)
            v = v[:, 0:n]
            if len(shape) == 2:
                return v.rearrange("p (a b) -> p a b", a=shape[0])
            if len(shape) == 3:
                return v.rearrange("p (a b c) -> p a b c", a=shape[0], b=shape[1])
            return v

    b_scr_all = Buf("scrall")
    bar = [None]

    def NB(name):
        x = Buf(name)
        x.last_w = bar[0]
        return x

    def barrier_begin():
        bar[0] = S.op("pool", lambda e: e.memset(small[:, 63:64], 0.0), [b_scr_all], [b_scr_all])


    def branch_ssd(l, path, nseq, L, nt, is_s):
        barrier_begin()
        cv = Carve()
        zs = cv.take([8, 512], BF16); b_zs = NB("zs")
        dt = cv.take([8, 16]); dtA = cv.take([8, 16]); ainc = cv.take([8, 16]); arest = cv.take([8, 16]); edt = cv.take([8, 16]); einc = cv.take([8, 16])
        nainc = cv.take([8, 16])
        b_dt = NB("dt"); b_cum = NB("cum")
        cumP = cv.take([8, 32]); b_cumP = NB("cumP")
        Lp = L + 6
        raw = cv.take([nseq * Lp]); b_raw = NB("raw")
        acc = cv.take([T]); b_acc = NB("acc")
        xrot = [cv.take([T], BF16), cv.take([T], BF16)]; b_xrot = [NB("xrot0"), NB("xrot1")]
        xB = cv.take([T], BF16); xC = cv.take([T], BF16); b_xB = NB("xB"); b_xC = NB("xC")
        xs_tok = cv.take([8, 512], BF16); b_xs = NB("xs_tok")
        B_tok = cv.take([8, 128], BF16); b_Btok = NB("Btok")
        NDP = 12
        GS = 4
        b_seg = []
        Lt = [cv.take([128]) for _ in range(NDP)]; b_Lt = [NB(f"Lt{i}") for i in range(NDP)]
        sc = [cv.take([128], BF16) for _ in range(NDP)]; b_sc = [NB(f"sc{i}") for i in range(NDP)]
        yacc = cv.take([512]); b_yacc = NB("yacc")
        ytmp = cv.take([512]); b_ytmp = NB("ytmp")
        ynb = cv.take([512], BF16); b_ynb = NB("ynb")
        prm = cv.take([64]); b_prm = NB("ssdprm")
        cw = cv.take([6, 7]); cb = cv.take([6]); ngc = cv.take([4]); b_cw = NB("cw")
        Bw = [cv.take([64], BF16), cv.take([64], BF16)]; b_Bw = [NB("Bw0"), NB("Bw1")]
        fin = None; s0T = None; st_ld = None
        b_fin = NB("fin"); b_s0T = NB("s0T"); b_stld = NB("stld")
        if is_s:
            s0T = cv.take([8, 128], BF16); st_ld = cv.take([8, 128])
        else:
            fin = cv.take([16, 64])
        _p0 = Sched.PHASE
        S.op("pool", lambda e: e.memset(prm[:, 0:64], 0.0), [b_scr_all], [b_prm, b_scr_all])
        LD(prm[:, 0:16], dtb[l:l + 1, :].partition_broadcast(128), b_prm)
        LD(prm[:, 16:32], alog[l:l + 1, :].partition_broadcast(128), b_prm, group=True)
        LD(prm[:, 32:40], ssdd[l:l + 1, :].partition_broadcast(128), b_prm, group=True)
        ACT(prm[:, 16:32], prm[:, 16:32], AF.Exp, [b_prm], [b_prm])
        TS("dve", prm[:, 16:32], prm[:, 16:32], -1.0, ALU.mult, [b_prm], [b_prm])
        LD(cw[:], convw[l], b_cw); LD(cb[:], convb[l], b_cw, group=True); LD(ngc[:], normgc[l], b_cw, group=True)
        Sched.PHASE = _p0 + 'A'
        for blk in range(2):
            w, bw = wload(w_in[l][:, C_Z + blk * 256:C_Z + (blk + 1) * 256], 8, 256)
            proj_tm(hT, b_hT, 8, w, bw, 256, range(8),
                    lambda t, ps, bp, blk=blk: ACT(zs[:, t, blk * 256:(blk + 1) * 256], ps[:, 0:256], AF.Silu, [bp], [b_zs]))
        Sched.PHASE = _p0 + 'B'
        w, bw = wload(w_in[l][:, C_DT:C_DT + 16], 8, 16)

        def ev_dt(t, ps, bp):
            TT("dve", dt[:, t, :], ps[:, 0:16], prm[:, 0:16], ALU.add, [bp, b_prm], [b_dt])
            ACT(dt[:, t, :], dt[:, t, :], AF.Exp, [b_dt], [b_dt])
            ACT(dt[:, t, :], dt[:, t, :], AF.Ln, [b_dt], [b_dt], bias=1.0)
            TT("dve", dtA[:, t, :], dt[:, t, :], prm[:, 16:32], ALU.mult, [b_dt, b_prm], [b_dt])
        proj_tm(hT, b_hT, 8, w, bw, 16, range(8), ev_dt)
        Sched.PHASE = _p0 + 'C'
        for blk in range(3):
            w, bw = wload(w_in[l][:, C_XBC + blk * 256:C_XBC + (blk + 1) * 256], 8, 256)
            for c2 in range(2):
                cc = blk * 2 + c2
                if cc < 4:
                    xa, bxa = xrot[cc % 2], b_xrot[cc % 2]
                elif cc == 4:
                    xa, bxa = xB, b_xB
                else:
                    xa, bxa = xC, b_xC
                MSET("pool", raw[:], 0.0, [b_raw])
                rw3 = raw.rearrange("p (s x) -> p s x", s=nseq)
                for h in range(2):
                    ps, bp = rr.get()
                    for k in range(8):
                        MM(ps[:, :], w[:, k, c2 * 128:(c2 + 1) * 128], hT[:, k, h * 512:(h + 1) * 512], k == 0, k == 7, [b_hT, bw], [bp])
                    if is_s:
                        CP("act", raw[:, 3 + h * 512:3 + (h + 1) * 512], ps[:, :], [bp], [b_raw])
                    else:
                        CP("act", rw3[:, 2 * h:2 * h + 2, 3:3 + L], ps[:, :].rearrange("p (s x) -> p s x", s=2), [bp], [b_raw])
                ac3 = acc.rearrange("p (s x) -> p s x", s=nseq)
                TS("dve", ac3, rw3[:, :, 0:L], cw[:, cc, 0:1], ALU.mult, [b_raw, b_cw], [b_acc])
                for k in range(1, 7):
                    STT(ac3, rw3[:, :, k:k + L], cw[:, cc, k:k + 1], ac3, ALU.mult, ALU.add, [b_raw, b_cw, b_acc], [b_acc])
                ACT(xa[:], acc[:], AF.Silu, [b_acc, b_cw], [bxa], bias=cb[:, cc:cc + 1])
                if cc < 5:
                    for t in range(8):
                        ps, bp = rr.get()
                        pv = ps[:].bitcast(BF16)[:, 0:128]
                        transpose_to(pv, xa[:, t * 128:(t + 1) * 128], [bxa], [bp])
                        if cc < 4:
                            CP("act", xs_tok[:, t, cc * 128:(cc + 1) * 128], pv, [bp], [b_xs])
                        else:
                            CP("act", B_tok[:, t, :], pv, [bp], [b_Btok])
        Sched.PHASE = _p0 + 'E'
        for s in range(nseq):
            for j in range(nt):
                tj = s * nt + j
                ps, bp = rr.get()
                for i in range(j + 1):
                    MM(ps[:, 0:16], (triu if i == j else onesf)[:], dtA[:, s * nt + i, :], i == 0, i == j, [b_triu, b_ones, b_dt], [bp])
                for i in range(nt - 1, j - 1, -1):
                    MM(ps[:, 16:32], (tril if i == j else onesf)[:], dtA[:, s * nt + i, :], i == nt - 1, i == j, [b_tril, b_ones, b_dt], [bp])
                CP("act", cumP[:, tj, :], ps[:, 0:32], [bp], [b_cumP])
                CP("dve", ainc[:, tj, 0:8], cumP[:, tj, 0:8], [b_cumP], [b_cum])
                CP("dve", ainc[:, tj, 8:16], cumP[:, tj, 24:32], [b_cumP], [b_cum])
                TT("dve", arest[:, tj, 0:8], cumP[:, tj, 16:24], dtA[:, tj, 0:8], ALU.subtract, [b_cumP, b_dt], [b_cum])
                TT("dve", arest[:, tj, 8:16], cumP[:, tj, 8:16], dtA[:, tj, 8:16], ALU.subtract, [b_cumP, b_dt], [b_cum])
                ACT(edt[:, tj, :], arest[:, tj, :], AF.Exp, [b_cum], [b_cum])
                TT("dve", edt[:, tj, :], edt[:, tj, :], dt[:, tj, :], ALU.mult, [b_cum, b_dt], [b_cum])
                ACT(einc[:, tj, :], ainc[:, tj, :], AF.Exp, [b_cum], [b_cum])
                TS("dve", nainc[:, tj, :], ainc[:, tj, :], -1.0, ALU.mult, [b_cum], [b_cum])
        if is_s:
            stv = st_ssd[l].rearrange("d h p n -> (d h p) n").rearrange("(j q) n -> q j n", q=128)
            LD(st_ld[:, :, 0:64], stv, b_stld); LD(st_ld[:, :, 64:128], stv, b_stld, group=True)
            for j8 in range(8):
                ps, bp = rr.get()
                transpose_to(ps[:, 0:128], st_ld[:, j8, :], [b_stld], [bp], dt=F32)
                CP("act", s0T[:, j8, :], ps[:, 0:128], [bp], [b_s0T])
        Sched.PHASE = _p0 + 'F'
        ybanks = [(psum[4], psb[4]), (psum[7], psb[7])]
        pa_ = [0]
        k_ = [0]
        stages = []
        for s in range(nseq):
            for j in range(nt):
                tj = s * nt + j
                ybank, b_yb = ybanks[tj % 2]
                for h in range(8):
                    g = h // 4
                    gsl = slice(g * 64, (g + 1) * 64)
                    units = [(0, i) for i in range(j + 1)] + [(1, i) for i in range(j, nt)]
                    cur = {"psA": None}
                    for g0 in range(0, len(units), GS):
                        grp = list(enumerate(units))[g0:g0 + GS]
                        st = {}

                        def A(st=st, grp=grp, units=units, s=s, j=j, tj=tj, h=h, gsl=gsl, cur=cur):
                            for ui, (d, i) in grp:
                                ti = s * nt + i
                                dh = d * 8 + h
                                if ui == 0 or units[ui - 1][0] != d:
                                    cur["psA"] = (psum[5 + pa_[0] % 2], psb[5 + pa_[0] % 2]); pa_[0] += 1
                                    bcast_rows(ainc[:, tj, dh:dh + 1], b_cum, cur["psA"][0][:, 0:128], cur["psA"][1])
                                psA_t, b_psA = cur["psA"]
                                q = k_[0] % NDP; k_[0] += 1
                                st[ui] = q
                                if i == j:
                                    STT(Lt[q][:], psA_t[:, 0:128], ainc[:, ti, dh:dh + 1], (mnegF if d == 0 else mnegB)[:], ALU.subtract, ALU.add,
                                        [b_psA, b_cum, b_mnegF, b_mnegB], [b_Lt[q]])
                                    ACT(Lt[q][:], Lt[q][:], AF.Exp, [b_Lt[q]], [b_Lt[q]])
                                elif is_s:
                                    ACT(Lt[q][:], psA_t[:, 0:128], AF.Exp, [b_psA, b_cum], [b_Lt[q]], bias=nainc[:, ti, dh:dh + 1])
                                else:
                                    TS("dve", Lt[q][:], psA_t[:, 0:128], ainc[:, ti, dh:dh + 1], ALU.subtract, [b_psA, b_cum], [b_Lt[q]], s2=0.0, op1=ALU.min)
                                    ACT(Lt[q][:], Lt[q][:], AF.Exp, [b_Lt[q]], [b_Lt[q]])
                            for ui, (d, i) in grp:
                                ti = s * nt + i
                                dh = d * 8 + h
                                q = st[ui]
                                psG, bpG = rr.get()
                                MM(psG[:, 0:128], xB[gsl, ti * 128:(ti + 1) * 128], xC[gsl, tj * 128:(tj + 1) * 128], True, True, [b_xB, b_xC], [bpG])
                                STT(sc[q][:], psG[:, 0:128], dt[:, ti, dh:dh + 1], Lt[q][:], ALU.mult, ALU.mult, [bpG, b_dt, b_Lt[q]], [b_sc[q]])

                        def B(st=st, grp=grp, units=units, s=s, tj=tj, h=h, ybank=ybank, b_yb=b_yb, last_grp=(g0 + GS >= len(units))):
                            for ui, (d, i) in grp:
                                ti = s * nt + i
                                q = st[ui]
                                MM(ybank[:, h * 64:(h + 1) * 64], sc[q][:], xs_tok[:, ti, h * 64:(h + 1) * 64], ui == 0, ui == len(units) - 1, [b_sc[q], b_xs], [b_yb])
                            if h == 7 and last_grp:
                                finalize(tj, ybank, b_yb)
                        stages.append((A, B))

        def finalize(tj, ybank, b_yb):
            if True:
                TT("dve", ytmp.rearrange("p (h x) -> p h x", h=8), xs_tok[:, tj, :].rearrange("p (h x) -> p h x", h=8),
                   prm[:, 32:40].unsqueeze(2).broadcast_to([128, 8, 64]), ALU.mult, [b_xs, b_prm], [b_ytmp])
                TT("dve", yacc[:], ybank[:, :], ytmp[:], ALU.add, [b_yb, b_ytmp], [b_yacc])
                if is_s:
                    for d in range(2):
                        for h in range(8):
                            g = h // 4
                            gsl = slice(g * 64, (g + 1) * 64)
                            j8 = (d * 8 + h) // 2
                            h2 = (d * 8 + h) % 2
                            psO, bpO = rr.get()
                            MM(psO[:, 0:64], xC[gsl, tj * 128:(tj + 1) * 128], s0T[gsl, j8, h2 * 64:(h2 + 1) * 64], True, True, [b_xC, b_s0T], [bpO])
                            STT(yacc[:, h * 64:(h + 1) * 64], psO[:, 0:64], einc[:, tj, d * 8 + h:d * 8 + h + 1], yacc[:, h * 64:(h + 1) * 64], ALU.mult, ALU.add,
                                [bpO, b_cum, b_yacc], [b_yacc])
                TT("dve", yacc[:], yacc[:], zs[:, tj, :], ALU.mult, [b_yacc, b_zs], [b_yacc])
                ACT(ytmp[:], yacc[:], AF.Square, [b_yacc], [b_ytmp, b_small], accum=small[:, 16:17])
                rstd_from_ss(small[:, 16:17], 512, small[:, 17:18], [b_small], [b_small])
                ACT(ynb[:], yacc[:], AF.Copy, [b_yacc, b_small], [b_ynb], scale=small[:, 17:18])
                for c4 in range(4):
                    ps, bp = rr.get()
                    pv = ps[:].bitcast(BF16)[:, 0:128]
                    transpose_to(pv, ynb[:, c4 * 128:(c4 + 1) * 128], [b_ynb], [bp])
                    yt_, by_ = yT(0, c4)
                    ACT(yt_[:, tj * 128:(tj + 1) * 128], pv, AF.Copy, [bp, b_cw], [by_], scale=ngc[:, c4:c4 + 1])
        LA = 2 if is_s else 3
        for k in range(min(LA, len(stages))):
            stages[k][0]()
        for k in range(len(stages)):
            if k + LA < len(stages):
                stages[k + LA][0]()
            stages[k][1]()
        Sched.PHASE = _p0 + 'G'
        if not is_s:
            for s in range(nseq):
                for d in range(2):
                    for h in range(8):
                        g = h // 4
                        psF, bpF = rr.get()
                        for i in range(nt):
                            ti = s * nt + i
                            q = k_[0] % 2; k_[0] += 1
                            TS("dve", Bw[q][:], B_tok[:, ti, g * 64:(g + 1) * 64], edt[:, ti, d * 8 + h:d * 8 + h + 1], ALU.mult, [b_Btok, b_cum], [b_Bw[q]])
                            MM(psF[0:64, 0:64], xs_tok[:, ti, h * 64:(h + 1) * 64], Bw[q][:], i == 0, i == nt - 1, [b_xs, b_Bw[q]], [bpF])
                        CP("act", fin[0:64, d * 8 + h, :], psF[0:64, 0:64], [bpF], [b_fin])
                STO(o_ssd[s, l].rearrange("d h p n -> p (d h) n"), fin[0:64, :, :], b_fin)
        S.op("pool", lambda e: e.memset(prm[:, 0:1], 0.0), [], ([b_fin, b_yacc, b_ynb, b_Btok, b_xs, b_cum, b_dt, b_zs, b_cumP, b_xB, b_xC, b_raw, b_acc, b_ytmp, b_cw, b_s0T, b_stld, b_prm]
             + b_xrot + b_seg + b_Lt + b_sc + b_Bw) + [b_scr_all])

    def branch_s5(l, path, nseq, L, nt, is_s):
        barrier_begin()
        _p0 = Sched.PHASE
        cv = Carve()
        uT = cv.take([4, T], BF16); b_uT = NB("uT")
        y5T = uT; b_y5 = b_uT
        prow = cv.take([128]); b_prow = NB("prow")
        pc = cv.take([12, 32]); b_pc = NB("pc")
        pci = cv.take([32], I32); b_pci = NB("pci")
        Bst = cv.take([4, 4, 16]); b_Bst = NB("Bst")
        Cn = cv.take([4, 64]); b_Cn = NB("Cn")
        Bc = cv.take([2, 16]); b_Bc = NB("Bc"); Bt = cv.take([16]); b_Bt = NB("Bt")
        Bx = [cv.take([128], BF16), cv.take([128], BF16)]; b_Bx = [NB("Bx0"), NB("Bx1")]
        BcL = [cv.take([128], BF16), cv.take([128], BF16)]; b_BcL = [NB("BcL0"), NB("BcL1")]
        Cx = [cv.take([128], BF16), cv.take([128], BF16)]; b_Cx = [NB("Cx0"), NB("Cx1")]
        CL = cv.take([4, 128], BF16); b_CL = NB("CL")
        Lt_ = L
        cosT = cv.take([Lt_]); sinT = cv.take([Lt_]); b_tab = NB("tab")
        xr = [cv.take([T]), cv.take([T])]; b_xr = [NB("xr0"), NB("xr1")]
        prR = cv.take([2 * T])
        prb = prR.bitcast(BF16)
        pr = [prb[:, k * T:(k + 1) * T] for k in range(4)]; b_pr = [NB(f"pr{i}") for i in range(4)]
        argF = prR[:, 0:Lt_]; argI = prR[:, T:T + Lt_].bitcast(I32)
        tmpx = prR[:, 0:T]; rmt = prR[:, T:2 * T]
        bA = [b_pr[0], b_pr[1]]; bB = [b_pr[2], b_pr[3]]
        if not is_s:
            tmpy = cv.take([T]); bY = [NB("tmpy")]
        else:
            tmpy = tmpx; bY = bA
        d5 = cv.take([4]); bg = cv.take([8]); b_d5 = NB("d5")
        finS = cv.take([256]); b_finS = NB("finS")
        b_wcap = NB("wcap")
        if not is_s:
            wcap = cv.take([2, 32, 4]); tcap = cv.take([2, 32]); wtmp = cv.take([4, 32, 4])
        s0c = cv.take([64]); b_s0c = NB("s0c")
        sg = cv.take([512]); b_sg = NB("sg")
        fT = cv.take([128]); b_fT = NB("fT")
        iota = cv.take([Lt_]); b_iota = NB("iota")
        LD(iota[:], c_iota[:, 0:Lt_], b_iota)
        if not is_s:
            rmF = cv.take([T], BF16); rmB = cv.take([T], BF16); b_rmF = NB("rmF"); b_rmB = NB("rmB")
            LD(rmF[:], c_rmF[:, :], b_rmF); LD(rmB[:], c_rmB[:, :], b_rmB)
        else:
            rmF = rmB = None; b_rmF = b_rmB = b_iota
        S.op("pool", lambda e: e.memset(prow[:], 0.0), [b_scr_all], [b_prow, b_scr_all])
        LD(prow[0:32, :], lamre[l], b_prow); LD(prow[32:64, :], lamim[l], b_prow, group=True); LD(prow[64:96, :], lsx[l], b_prow, group=True)
        ps, bp = rr.get()
        transpose_to(ps[:, 0:96], prow[0:96, :], [b_prow], [bp], dt=F32, np_=96)
        CP("act", pc[:, 0:3, :].rearrange("p a b -> p (a b)"), ps[:, 0:96], [bp], [b_pc])
        P_ = lambda i: pc[:, i, :]
        R, W_ = [b_pc], [b_pc]
        ACT(P_(2), P_(2), AF.Exp, R, W_)
        TT("dve", P_(3), P_(0), P_(2), ALU.mult, R, W_)
        TT("dve", P_(4), P_(1), P_(2), ALU.mult, R, W_)
        ACT(P_(5), P_(3), AF.Exp, R, W_)
        TS("dve", pci[:], P_(4), 1.0 / (2 * math.pi), ALU.mult, R, [b_pci])
        CP("dve", P_(10), pci[:], [b_pci], W_)
        STT(P_(11), P_(10), -2 * math.pi, P_(4), ALU.mult, ALU.add, R, W_)
        TS("dve", P_(11), P_(11), 3.14159, ALU.min, R, W_, s2=-3.14159, op1=ALU.max)
        ACT(P_(7), P_(11), AF.Sin, R, W_)
        ACT(P_(10), P_(11), AF.Abs, R, W_)
        ACT(P_(6), P_(10), AF.Sin, R, W_, scale=-1.0, bias=math.pi / 2)
        TT("dve", P_(6), P_(6), P_(5), ALU.mult, R, W_)
        TT("dve", P_(7), P_(7), P_(5), ALU.mult, R, W_)
        TT("dve", P_(10), P_(0), P_(0), ALU.mult, R, W_)
        TT("dve", P_(11), P_(1), P_(1), ALU.mult, R, W_)
        TT("dve", P_(10), P_(10), P_(11), ALU.add, R, W_)
        S.op("dve", lambda e: e.reciprocal(out=P_(10), in_=P_(10)), R, W_)
        TS("dve", P_(11), P_(6), -1.0, ALU.add, R, W_)
        TT("dve", P_(8), P_(11), P_(0), ALU.mult, R, W_)
        TT("dve", P_(9), P_(7), P_(1), ALU.mult, R, W_)
        TT("dve", P_(8), P_(8), P_(9), ALU.add, R, W_)
        TT("dve", P_(8), P_(8), P_(10), ALU.mult, R, W_)
        TT("dve", P_(9), P_(7), P_(0), ALU.mult, R, W_)
        TT("dve", P_(11), P_(11), P_(1), ALU.mult, R, W_)
        TT("dve", P_(9), P_(9), P_(11), ALU.subtract, R, W_)
        TT("dve", P_(9), P_(9), P_(10), ALU.mult, R, W_)
        LD(d5[:], s5dc[l], b_d5); LD(bg[:], bgluc[l], b_d5, group=True)
        if is_s:
            LD(fT[0:64, :], st_s5[l], b_fT)
            ps, bp = rr.get()
            transpose_to(ps[:, 0:64], fT[0:64, :], [b_fT], [bp], dt=F32, np_=64)
            CP("act", s0c[:], ps[:, 0:64], [bp], [b_s0c])
        Sched.PHASE = _p0 + 'u'
        for blk in range(2):
            w, bw = wload(w_in[l][:, C_U + blk * 256:C_U + (blk + 1) * 256], 8, 256)
            proj_fm(hT, b_hT, 8, w, bw, 256,
                    lambda cc, h, ps, bp, blk=blk: CP("act", uT[:, blk * 2 + cc, h * 512:(h + 1) * 512], ps[:, :], [bp], [b_uT]))
        ybk = [(psum[4], psb[4]), (psum[5], psb[5])]
        xbk = [(psum[6], psb[6]), (psum[7], psb[7])]
        nrep = T // Lt_
        v3 = (lambda a: a.rearrange("p (s x) -> p s x", s=nrep)) if nrep > 1 else (lambda a: a)
        Bc2 = [Bc, cv.take([2, 16])]; b_Bc2 = [b_Bc, NB("Bc_1")]; Bt2 = [Bt, cv.take([16])]; b_Bt2 = [b_Bt, NB("Bt_1")]
        Bx2 = [Bx, [cv.take([128], BF16), cv.take([128], BF16)]]; b_Bx2 = [b_Bx, [NB("Bx0_1"), NB("Bx1_1")]]
        BcL2 = [BcL, [cv.take([128], BF16), cv.take([128], BF16)]]; b_BcL2 = [b_BcL, [NB("BcL0_1"), NB("BcL1_1")]]
        Cx2 = [Cx, [cv.take([128], BF16), cv.take([128], BF16)]]; b_Cx2 = [b_Cx, [NB("Cx0_1"), NB("Cx1_1")]]
        CL2 = [CL, cv.take([4, 128], BF16)]; b_CL2 = [b_CL, NB("CL_1")]
        cos2 = [cosT, cv.take([Lt_])]; sin2 = [sinT, cv.take([Lt_])]; b_tab2 = [b_tab, NB("tab_1")]
        its = [(fc, d, q4) for fc in range(4) for d in range(2) for q4 in range(4)]
        NI = len(its)

        def stB(k):
            fc, d, q4 = its[k]
            z = k % 2
            if d == 0 and q4 == 0:
                for dd in range(2):
                    LD(Bst[:, dd * 2 + 0, :, :], s5bre[l, dd][fc * 512:(fc + 1) * 512, :].rearrange("(c p) m -> p c m", p=128), b_Bst, group=(dd > 0))
                    LD(Bst[:, dd * 2 + 1, :, :], s5bim[l, dd][fc * 512:(fc + 1) * 512, :].rearrange("(c p) m -> p c m", p=128), b_Bst, group=True)
                    LD(Cn[:, dd * 2 + 0, :], s5cre[l, dd][fc * 128:(fc + 1) * 128, :], b_Cn, group=(dd > 0))
                    LD(Cn[:, dd * 2 + 1, :], s5cim[l, dd][fc * 128:(fc + 1) * 128, :], b_Cn, group=True)
            c = fc * 4 + q4
            dc = d * 16 + c
            cre, cim = pc[:, 8, dc:dc + 1], pc[:, 9, dc:dc + 1]
            Bre, Bim = Bst[:, d * 2 + 0, q4, :], Bst[:, d * 2 + 1, q4, :]
            Bc_, bBc_, Bt_, bBt_ = Bc2[z], b_Bc2[z], Bt2[z], b_Bt2[z]
            TS("dve", Bt_[:], Bim, cim, ALU.mult, [b_Bst, b_pc], [bBt_])
            STT(Bc_[:, 0, :], Bre, cre, Bt_[:], ALU.mult, ALU.subtract, [b_Bst, b_pc, bBt_], [bBc_])
            TS("dve", Bt_[:], Bre, cim, ALU.mult, [b_Bst, b_pc], [bBt_])
            STT(Bc_[:, 1, :], Bim, cre, Bt_[:], ALU.mult, ALU.add, [b_Bst, b_pc, bBt_], [bBc_])
            for ri in range(2):
                TT("pool", Bx2[z][ri].rearrange("p (g m) -> p g m", g=8), maskB[:, q4, :].rearrange("p (g m) -> p g m", g=8),
                   Bc_[:, ri, :].unsqueeze(1).broadcast_to([128, 8, 16]), ALU.mult, [b_maskB, bBc_], [b_Bx2[z][ri]])
                ps, bp = rr.get()
                pv = ps[:].bitcast(BF16)[:, 0:128]
                transpose_to(pv, Bx2[z][ri][:], [b_Bx2[z][ri]], [bp])
                CP("act", BcL2[z][ri][:], pv, [bp], [b_BcL2[z][ri]])
            for ri in range(2):
                TT("pool", Cx2[z][ri].rearrange("p (g n) -> p g n", g=2), maskC[:, q4, :].rearrange("p (g n) -> p g n", g=2),
                   Cn[:, d * 2 + ri, :].unsqueeze(1).broadcast_to([128, 2, 64]), ALU.mult, [b_maskC, b_Cn], [b_Cx2[z][ri]])
                ps, bp = rr.get()
                pv = ps[:].bitcast(BF16)[:, 0:128]
                transpose_to(pv, Cx2[z][ri][:], [b_Cx2[z][ri]], [bp])
                CP("act", CL2[z][:, 2 * ri, :], pv, [bp], [b_CL2[z]])
                ACT(CL2[z][:, 2 * ri + 1, :], pv, AF.Copy, [bp], [b_CL2[z]], scale=-1.0)

        def stT1(k):
            fc, d, q4 = its[k]
            z = k % 2
            dc = d * 16 + fc * 4 + q4
            cT, sT, bt = cos2[z], sin2[z], [b_tab2[z]]
            TS("dve", cT[:], iota[:, 0:Lt_], pc[:, 4, dc:dc + 1], ALU.mult, [b_iota, b_pc], bt)
            TS("dve", sT[:].bitcast(I32), cT[:], 1.0 / (2 * math.pi), ALU.mult, bt, bt)
            CP("act", sT[:], sT[:].bitcast(I32), bt, bt)

        def stT2(k):
            z = k % 2
            cT, sT, bt = cos2[z], sin2[z], [b_tab2[z]]
            STT(cT[:], sT[:], -2 * math.pi, cT[:], ALU.mult, ALU.add, bt, bt)
            TS("dve", cT[:], cT[:], 3.14159, ALU.min, bt, bt, s2=-3.14159, op1=ALU.max)
            ACT(sT[:], cT[:], AF.Sin, bt, bt)
            ACT(cT[:], cT[:], AF.Abs, bt, bt)
            ACT(cT[:], cT[:], AF.Sin, bt, bt, scale=-1.0, bias=math.pi / 2)

        def stX(k):
            fc, d, q4 = its[k]
            z = k % 2
            c = fc * 4 + q4
            dc = d * 16 + c
            cosT_, sinT_, bt = cos2[z], sin2[z], b_tab2[z]
            for h in range(2):
                hs = slice(h * 512, (h + 1) * 512)
                for ri in range(2):
                    MM(xbk[ri][0][:, :], BcL2[z][ri][:], uT[:, fc, hs], True, True, [b_BcL2[z][ri], b_uT], [xbk[ri][1]])
                xre, xim = xbk[0][0][:, :], xbk[1][0][:, :]
                if Lt_ < 512:
                    nr2 = 512 // Lt_
                    cB = cosT_.unsqueeze(1).broadcast_to([128, nr2, Lt_]); sB = sinT_.unsqueeze(1).broadcast_to([128, nr2, Lt_])
                    vv = lambda a, nr2=nr2: a.rearrange("p (s x) -> p s x", s=nr2)
                else:
                    cB, sB = cosT_[:, hs], sinT_[:, hs]
                    vv = lambda a: a
                TT("dve", vv(xr[1][:, hs]), vv(xim), cB, ALU.mult, [xbk[1][1], bt], [b_xr[1]])
                TT("dve", vv(tmpy[:, hs]), vv(xre), sB, ALU.mult, [xbk[0][1], bt], bY)
                TT("pool", xr[1][:, hs], xr[1][:, hs], tmpy[:, hs], ALU.subtract if d == 0 else ALU.add, [b_xr[1]] + bY, [b_xr[1]])
                TT("dve", vv(xr[0][:, hs]), vv(xre), cB, ALU.mult, [xbk[0][1], bt], [b_xr[0]])
                TT("dve", vv(tmpx[:, hs]), vv(xim), sB, ALU.mult, [xbk[1][1], bt], bA)
                TT("pool", xr[0][:, hs], xr[0][:, hs], tmpx[:, hs], ALU.add if d == 0 else ALU.subtract, [b_xr[0]] + bA, [b_xr[0]])
            if is_s:
                sre = s0c[:, (d * 2 + 0) * 16 + c:(d * 2 + 0) * 16 + c + 1]; sim = s0c[:, (d * 2 + 1) * 16 + c:(d * 2 + 1) * 16 + c + 1]
                abre, abim = pc[:, 6, dc:dc + 1], pc[:, 7, dc:dc + 1]
                sm = small
                RS, WS_ = [b_small, b_s0c, b_pc, bt], [b_small]
                TT("dve", sm[:, 20:21], sre, abre, ALU.mult, RS, WS_); TT("dve", sm[:, 21:22], sim, abim, ALU.mult, RS, WS_)
                TT("dve", sm[:, 22:23], sm[:, 20:21], sm[:, 21:22], ALU.subtract, RS, WS_)
                TT("dve", sm[:, 20:21], sre, abim, ALU.mult, RS, WS_); TT("dve", sm[:, 21:22], sim, abre, ALU.mult, RS, WS_)
                TT("dve", sm[:, 23:24], sm[:, 20:21], sm[:, 21:22], ALU.add, RS, WS_)
                if d == 0:
                    TT("dve", xr[0][:, 0:1], xr[0][:, 0:1], sm[:, 22:23], ALU.add, [b_xr[0], b_small], [b_xr[0]])
                    TT("dve", xr[1][:, 0:1], xr[1][:, 0:1], sm[:, 23:24], ALU.add, [b_xr[1], b_small], [b_xr[1]])
                else:
                    cl, sl = cosT_[:, L - 1:L], sinT_[:, L - 1:L]
                    TT("dve", sm[:, 20:21], sm[:, 22:23], cl, ALU.mult, RS, WS_); TT("dve", sm[:, 21:22], sm[:, 23:24], sl, ALU.mult, RS, WS_)
                    TT("dve", sm[:, 24:25], sm[:, 20:21], sm[:, 21:22], ALU.subtract, RS, WS_)
                    TT("dve", sm[:, 20:21], sm[:, 22:23], sl, ALU.mult, RS, WS_); TT("dve", sm[:, 21:22], sm[:, 23:24], cl, ALU.mult, RS, WS_)
                    TT("dve", sm[:, 25:26], sm[:, 20:21], sm[:, 21:22], ALU.add, RS, WS_)
                    TT("dve", xr[0][:, L - 1:L], xr[0][:, L - 1:L], sm[:, 24:25], ALU.add, [b_xr[0], b_small], [b_xr[0]])
                    TT("dve", xr[1][:, L - 1:L], xr[1][:, L - 1:L], sm[:, 25:26], ALU.add, [b_xr[1], b_small], [b_xr[1]])

        def stS(k):
            fc, d, q4 = its[k]
            z = k % 2
            dc = d * 16 + fc * 4 + q4
            cosT_, sinT_, bt = cos2[z], sin2[z], b_tab2[z]
            if is_s:
                rm_ = pc[:, 5, dc:dc + 1].broadcast_to([128, T]); rm_r = rm_; brm = [b_pc]
            else:
                TS("dve", rmt, (rmF if d == 0 else rmB)[:], pc[:, 5, dc:dc + 1], ALU.mult, [b_rmF, b_rmB, b_pc], bB)
                rm_ = rmt; rm_r = rmt[:, ::-1]; brm = bB
            for ri in (1, 0):
                if d == 0:
                    S.op("dve", lambda e, ri=ri, rm_=rm_: e.tensor_tensor_scan(out=xr[ri][:], data0=rm_, data1=xr[ri][:], initial=0.0, op0=ALU.mult, op1=ALU.add),
                         brm + [b_xr[ri]], [b_xr[ri]])
                else:
                    S.op("dve", lambda e, ri=ri, rm_r=rm_r: e.tensor_tensor_scan(out=xr[ri][:, ::-1], data0=rm_r, data1=xr[ri][:, ::-1], initial=0.0, op0=ALU.mult, op1=ALU.add),
                         brm + [b_xr[ri]], [b_xr[ri]])
            if not is_s:
                lpos = L - 1 if d == 0 else 0
                for ri in range(2):
                    w3 = xr[ri].rearrange("p (s x) -> p s x", s=nseq)[:, :, lpos:lpos + 1].rearrange("p s x -> p (s x)")
                    CP("act", wcap[:, ri, dc, :], w3, [b_xr[ri]], [b_wcap])
                CP("act", tcap[:, 0, dc:dc + 1], cosT_[:, lpos:lpos + 1], [bt], [b_wcap])
                CP("act", tcap[:, 1, dc:dc + 1], sinT_[:, lpos:lpos + 1], [bt], [b_wcap])

        def stP(k):
            fc, d, q4 = its[k]
            z = k % 2
            cosT_, sinT_, bt = cos2[z], sin2[z], b_tab2[z]
            cosB = cosT_.unsqueeze(1).broadcast_to([128, nrep, Lt_]) if nrep > 1 else cosT_
            sinB = sinT_.unsqueeze(1).broadcast_to([128, nrep, Lt_]) if nrep > 1 else sinT_
            TT("dve", v3(pr[1]), v3(xr[1][:]), sinB, ALU.mult, [b_xr[1], bt], [b_pr[1]])
            TT("pool", v3(pr[0]), v3(xr[0][:]), cosB, ALU.mult, [b_xr[0], bt], [b_pr[0]])
            TT("dve", v3(pr[3]), v3(xr[1][:]), cosB, ALU.mult, [b_xr[1], bt], [b_pr[3]])
            TT("pool", v3(pr[2]), v3(xr[0][:]), sinB, ALU.mult, [b_xr[0], bt], [b_pr[2]])
            sel = [0, 1, 3, 3] if d == 0 else [0, 0, 2, 3]
            first_y = (d == 0 and q4 == 0)
            last_y = (d == 1 and q4 == 3)
            for h in range(2):
                hs = slice(h * 512, (h + 1) * 512)
                for k4 in range(4):
                    MM(ybk[h][0][:, :], CL2[z][:, sel[k4], :], pr[k4][:, hs], first_y and k4 == 0, last_y and k4 == 3, [b_CL2[z], b_pr[k4]], [ybk[h][1]])
            if last_y:
                for h in range(2):
                    hs = slice(h * 512, (h + 1) * 512)
                    STT(uT[:, fc, hs], uT[:, fc, hs], d5[:, fc:fc + 1], ybk[h][0][:, :], ALU.mult, ALU.add, [b_uT, b_d5, ybk[h][1]], [b_uT])

        Sched.PHASE = _p0 + 'L'
        stB(0); stT1(0); stT2(0)
        for k in range(NI):
            if k + 1 < NI:
                stB(k + 1); stT1(k + 1)
            stX(k)
            if k + 1 < NI:
                stT2(k + 1)
            stS(k)
            stP(k)
        if not is_s:
            cB_ = tcap[:, 0, :].unsqueeze(2).broadcast_to([128, 32, 4]); sB_ = tcap[:, 1, :].unsqueeze(2).broadcast_to([128, 32, 4])
            RW = [b_wcap]
            TT("dve", wtmp[:, 0, :, :], wcap[:, 0, :, :], cB_, ALU.mult, RW, RW)
            TT("dve", wtmp[:, 1, :, :], wcap[:, 1, :, :], sB_, ALU.mult, RW, RW)
            TT("dve", wtmp[:, 2, :, :], wcap[:, 0, :, :], sB_, ALU.mult, RW, RW)
            TT("dve", wtmp[:, 3, :, :], wcap[:, 1, :, :], cB_, ALU.mult, RW, RW)
            f4 = finS.rearrange("p (s d r c) -> p s d r c", s=4, d=2, r=2)
            for d in range(2):
                src = lambda k, d=d: wtmp[:, k, d * 16:(d + 1) * 16, :].rearrange("p c s -> p s c")
                TT("dve", f4[:, :, d, 0, :], src(0), src(1), ALU.subtract if d == 0 else ALU.add, RW, [b_finS])
                TT("dve", f4[:, :, d, 1, :], src(3), src(2), ALU.add if d == 0 else ALU.subtract, RW, [b_finS])
            for half in range(2):
                ps, bp = rr.get()
                transpose_to(ps[:, 0:128], finS[:, half * 128:(half + 1) * 128], [b_finS], [bp], dt=F32)
                CP("act", fT[:], ps[:, 0:128], [bp], [b_fT])
                for s2 in range(2):
                    STO(o_s5[half * 2 + s2, l], fT[s2 * 64:(s2 + 1) * 64, :], b_fT)
        Sched.PHASE = _p0 + 'G'
        for blk in range(2):
            w, bw = wload(w_glu[l][:, blk * 256:(blk + 1) * 256], 4, 256)
            w2, bw2 = wload(w_glu[l][:, (blk + 2) * 256:(blk + 3) * 256], 4, 256)
            for c2 in range(2):
                cc = blk * 2 + c2
                for h in range(2):
                    hs = slice(h * 512, (h + 1) * 512)
                    psg, bpg = rr.get()
                    for k in range(4):
                        MM(psg[:, :], w2[:, k, c2 * 128:(c2 + 1) * 128], y5T[:, k, hs], k == 0, k == 3, [b_y5, bw2], [bpg])
                    ACT(sg[:], psg[:, :], AF.Sigmoid, [bpg, b_d5], [b_sg], bias=bg[:, 4 + cc:5 + cc])
                    psv, bpv = rr.get()
                    for k in range(4):
                        MM(psv[:, :], w[:, k, c2 * 128:(c2 + 1) * 128], y5T[:, k, hs], k == 0, k == 3, [b_y5, bw], [bpv])
                    yt_, by_ = yT(1, cc)
                    STT(yt_[:, hs], psv[:, :], bg[:, cc:cc + 1], sg[:], ALU.add, ALU.mult, [bpv, b_d5, b_sg], [by_])
        S.op("pool", lambda e: e.memset(prow[:, 0:1], 0.0), [], ([b_uT, b_y5, b_prow, b_pc, b_pci, b_Bst, b_Cn, b_Bc, b_Bt, b_CL, b_tab, b_d5, b_finS, b_s0c, b_sg, b_fT]
             + b_Bx + b_BcL + b_Cx + b_xr + b_pr + (bY if not is_s else []) + [b_iota, b_rmF, b_rmB, b_wcap]
             + [b_Bc2[1], b_Bt2[1], b_CL2[1], b_tab2[1]] + b_Bx2[1] + b_BcL2[1] + b_Cx2[1]) + [b_scr_all])

    def rms_groups(ps, bp, ncols, gain_bc, b_gain, qf, b_qf, sq, b_sq, rs, b_rs, t, rope):
        ng = ncols // 64
        CP("act", qf[:, 0:ncols], ps[:, 0:ncols], [bp], [b_qf])
        TT("dve", sq[:, 0:ncols], qf[:, 0:ncols], qf[:, 0:ncols], ALU.mult, [b_qf], [b_sq])
        S.op("dve", lambda e: e.tensor_reduce(out=rs[:, 0:ng], in_=sq[:, 0:ncols].rearrange("p (g x) -> p g x", g=ng), op=ALU.add, axis=AX.X), [b_sq], [b_rs])
        rstd_from_ss(rs[:, 0:ng], 64, rs[:, 0:ng], [b_rs], [b_rs])
        q3 = qf[:, 0:ncols].rearrange("p (g x) -> p g x", g=ng)
        TT("dve", q3, q3, rs[:, 0:ng].unsqueeze(2).broadcast_to([128, ng, 64]), ALU.mult, [b_qf, b_rs], [b_qf])
        TT("dve", q3, q3, gain_bc.unsqueeze(1).broadcast_to([128, ng, 64]), ALU.mult, [b_qf, b_gain], [b_qf])
        if rope:
            s3 = sq[:, 0:ncols].rearrange("p (g a q f) -> p (g a) q f", g=ng, a=2, q=2)
            x4 = qf[:, 0:ncols].rearrange("p (g a q f) -> p (g a) q f", g=ng, a=2, q=2)
            S4 = ropeS[:, t, :].rearrange("p (a q f) -> p a q f", a=2, q=2)
            for pz in range(2):
                TT("dve", s3[:, :, pz, :].rearrange("p (g a) f -> p g a f", g=ng), x4[:, :, 1 - pz, :].rearrange("p (g a) f -> p g a f", g=ng),
                   S4[:, :, pz, :].unsqueeze(1).broadcast_to([128, ng, 2, 16]), ALU.mult, [b_qf, b_ropeS], [b_sq])
            TT("dve", q3, q3, ropeC[:, t, :].unsqueeze(1).broadcast_to([128, ng, 64]), ALU.mult, [b_qf, b_ropeC], [b_qf])
            TT("dve", qf[:, 0:ncols], qf[:, 0:ncols], sq[:, 0:ncols], ALU.add, [b_qf, b_sq], [b_qf])

    def branch_diff(l, path, nseq, L, nt, is_s):
        barrier_begin()
        _p0 = Sched.PHASE
        cv = Carve()
        nk_ctx = 2 if is_s else 0
        NKT = 8 + nk_ctx
        qT = cv.take([4, T], BF16); b_qT = NB("qT")
        kT = cv.take([4, NKT * 128], BF16); b_kT = NB("kT")
        vaug = cv.take([NKT, 4, 130], BF16); b_va = NB("vaug")
        qfL = [cv.take([512]) for _ in range(2)]; b_qfL = [NB(f"qf{i}") for i in range(2)]
        sqL = [cv.take([512]) for _ in range(2)]; b_sqL = [NB(f"sq{i}") for i in range(2)]
        rsL = [cv.take([16]) for _ in range(2)]; b_rsL = [NB(f"rs{i}") for i in range(2)]
        rot_ = [0]

        def nxt():
            i = rot_[0] % 2; rot_[0] += 1
            return qfL[i], b_qfL[i], sqL[i], b_sqL[i], rsL[i], b_rsL[i]
        qb = [cv.take([512], BF16), cv.take([512], BF16)]; b_qb = [NB("qb0"), NB("qb1")]
        gq = cv.take([64]); gk = cv.take([64]); b_g = NB("dg")
        lamt = cv.take([4, 64]); lamc = cv.take([8]); b_lam = NB("lam")
        o1 = cv.take([4, 128]); b_o1 = NB("o1")
        odn = cv.take([8, 512], BF16); b_odn = NB("odn")
        pT = [cv.take([512], BF16) for _ in range(4)]; b_pT = [NB(f"pT{i}") for i in range(4)]
        rd = cv.take([8]); b_rd = NB("rd")
        oh = cv.take([128]); b_oh = NB("oh")
        gsub = cv.take([1]); b_gsub = NB("gsub")
        kstL = [cv.take([512]) for _ in range(2)]; b_kstL = [NB(f"kst{i}") for i in range(2)]
        kst, b_kst = kstL[0], b_kstL[0]
        lam_init = 0.8 - 0.6 * math.exp(-0.3 * l)
        S.op("pool", lambda e: e.memset(gq[:], 0.0), [b_scr_all], [b_g, b_scr_all])
        LD(gq[:], dqg[l:l + 1, :].partition_broadcast(128), b_g); LD(gk[:], dkg[l:l + 1, :].partition_broadcast(128), b_g, group=True)
        TS("dve", gq[:], gq[:], 0.125, ALU.mult, [b_g], [b_g])
        LD(lamt[:].rearrange("p a b -> p (a b)"), dlam[l:l + 1, :].partition_broadcast(128), b_lam)
        LD(gsub[:], dsubc[l], b_gsub)
        TS("dve", gsub[:], gsub[:], 1.0 - lam_init, ALU.mult, [b_gsub], [b_gsub])
        TT("dve", lamt[:, 0, :], lamt[:, 0, :], lamt[:, 1, :], ALU.mult, [b_lam], [b_lam])
        TT("dve", lamt[:, 2, :], lamt[:, 2, :], lamt[:, 3, :], ALU.mult, [b_lam], [b_lam])
        S.op("dve", lambda e: e.tensor_reduce(out=lamc[:, 0:1], in_=lamt[:, 0, :], op=ALU.add, axis=AX.X), [b_lam], [b_lam])
        S.op("dve", lambda e: e.tensor_reduce(out=lamc[:, 1:2], in_=lamt[:, 2, :], op=ALU.add, axis=AX.X), [b_lam], [b_lam])
        ACT(lamc[:, 0:2], lamc[:, 0:2], AF.Exp, [b_lam], [b_lam])
        TT("dve", lamc[:, 2:3], lamc[:, 0:1], lamc[:, 1:2], ALU.subtract, [b_lam], [b_lam])
        TS("dve", lamc[:, 2:3], lamc[:, 2:3], lam_init, ALU.add, [b_lam], [b_lam], s2=-1.0, op1=ALU.mult)
        MSET("pool", vaug[:].rearrange("p a b c -> p (a b c)"), 1.0, [b_va])
        if sub == 1:
            raise _Stop()
        Sched.PHASE = _p0 + 'A'
        stagesA = []
        for which in range(2):
            col0 = C_DQ if which == 0 else C_DK
            wd = {}
            for t in range(8):
                st = {}

                def FA(st=st, which=which, t=t, wd=wd, col0=col0):
                    if t == 0:
                        wd["A"] = wload(w_in[l][:, col0:col0 + 256], 8, 256)
                        wd["B"] = wload(w_in[l][:, col0 + 256:col0 + 512], 8, 256)
                    wA, bwA = wd["A"]; wB, bwB = wd["B"]
                    ps, bp = rr.get()
                    for k in range(8):
                        MM(ps[:, 0:256], hT[:, k, t * 128:(t + 1) * 128], wA[:, k, :], k == 0, k == 7, [b_hT, bwA], [bp])
                    for k in range(8):
                        MM(ps[:, 256:512], hT[:, k, t * 128:(t + 1) * 128], wB[:, k, :], k == 0, k == 7, [b_hT, bwB], [bp])
                    qf, b_qf, sq, b_sq, rs, b_rs = nxt()
                    kst, b_kst = kstL[t % 2], b_kstL[t % 2]
                    if which == 1 and not is_s:
                        rms_groups(ps, bp, 512, gk[:], b_g, kst, b_kst, sq, b_sq, rs, b_rs, t, False)
                        STO(o_dk[t // 2, l, (t % 2) * 128:(t % 2 + 1) * 128, :], kst[:], b_kst)
                        st["src"] = (kst, b_kst)
                    else:
                        rms_groups(ps, bp, 512, (gq if which == 0 else gk)[:], b_g, qf, b_qf, sq, b_sq, rs, b_rs, t, is_s)
                        st["src"] = (qf, b_qf)

                def FB(st=st, which=which, t=t):
                    src, bsrc = st["src"]
                    qb_, bqb_ = qb[t % 2], b_qb[t % 2]
                    CP("pool", qb_[:], src[:], [bsrc], [bqb_])
                    for j4 in range(4):
                        ps2, bp2 = rr.get()
                        pv = ps2[:].bitcast(BF16)[:, 0:128]
                        transpose_to(pv, qb_[:, j4 * 128:(j4 + 1) * 128], [bqb_], [bp2])
                        if which == 0:
                            CP("act", qT[:, j4, t * 128:(t + 1) * 128], pv, [bp2], [b_qT])
                        else:
                            CP("act", kT[:, j4, (nk_ctx + t) * 128:(nk_ctx + t + 1) * 128], pv, [bp2], [b_kT])
                stagesA.append((FA, FB))
        stagesA[0][0]()
        for k in range(len(stagesA)):
            if k + 1 < len(stagesA):
                stagesA[k + 1][0]()
            stagesA[k][1]()
        if is_s:
            for kt in range(2):
                LD(kst[:], cdk[l, kt * 128:(kt + 1) * 128, :], b_kst)
                CP("pool", qb[0][:], kst[:], [b_kst], [b_qb[0]])
                for j4 in range(4):
                    ps2, bp2 = rr.get()
                    pv = ps2[:].bitcast(BF16)[:, 0:128]
                    transpose_to(pv, qb[0][:, j4 * 128:(j4 + 1) * 128], [b_qb[0]], [bp2])
                    CP("act", kT[:, j4, kt * 128:(kt + 1) * 128], pv, [bp2], [b_kT])
                LD(kst[:], cdv[l, kt * 128:(kt + 1) * 128, :], b_kst)
                CP("pool", vaug[:, kt, :, 0:128], kst[:].rearrange("p (h e) -> p h e", h=4), [b_kst], [b_va])
        Sched.PHASE = _p0 + 'B'
        wA, bwA = wload(w_in[l][:, C_DV:C_DV + 256], 8, 256)
        wB, bwB = wload(w_in[l][:, C_DV + 256:C_DV + 512], 8, 256)
        for t in range(8):
            ps, bp = rr.get()
            for k in range(8):
                MM(ps[:, 0:256], hT[:, k, t * 128:(t + 1) * 128], wA[:, k, :], k == 0, k == 7, [b_hT, bwA], [bp])
            for k in range(8):
                MM(ps[:, 256:512], hT[:, k, t * 128:(t + 1) * 128], wB[:, k, :], k == 0, k == 7, [b_hT, bwB], [bp])
            if sub != 31:
                kst, b_kst = kstL[t % 2], b_kstL[t % 2]
            CP("act", vaug[:, nk_ctx + t, :, 0:128], ps[:, :].rearrange("p (h e) -> p h e", h=4), [bp], [b_va])
            if not is_s and sub != 32:
                CP("dve", kst[:], ps[:, :], [bp], [b_kst])
                STO(o_dv[t // 2, l, (t % 2) * 128:(t % 2 + 1) * 128, :], kst[:], b_kst)
        if sub in (3, 31, 32):
            raise _Stop()
        Sched.PHASE = _p0 + 'C'
        obk = [(psum[4 + i], psb[4 + i]) for i in range(4)]
        pc_ = [0]
        stages = []
        for s in range(nseq):
            keyt = list(range(nk_ctx)) + [nk_ctx + s * nt + i for i in range(nt)]
            nq = min(L, 512)
            for qc in range(L // nq):
                q0 = s * L + qc * nq
                nqt = nq // 128
                for h in range(4):
                    for c in range(2):
                        ksl = slice(c * 64, (c + 1) * 64)
                        for ki, kt in enumerate(keyt):
                            st = {}

                            def A(st=st, ksl=ksl, h=h, kt=kt, q0=q0, nq=nq):
                                psS, bpS = rr.get()
                                MM(psS[:, 0:nq], kT[ksl, h, kt * 128:(kt + 1) * 128], qT[ksl, h, q0:q0 + nq], True, True, [b_kT, b_qT], [bpS])
                                z = pc_[0] % 4; pc_[0] += 1
                                st["z"] = z
                                ACT(pT[z][:, 0:nq], psS[:, 0:nq], AF.Exp, [bpS], [b_pT[z]])

                            def B(st=st, h=h, c=c, kt=kt, ki=ki, nk=len(keyt), nqt=nqt, q0=q0):
                                z = st["z"]
                                for qt in range(nqt):
                                    MM(obk[qt][0][:, 0:129], pT[z][:, qt * 128:(qt + 1) * 128], vaug[:, kt, h, 0:129], ki == 0, ki == nk - 1, [b_pT[z], b_va], [obk[qt][1]])
                                if ki != nk - 1:
                                    return
                                for qt in range(nqt):
                                    tq = q0 // 128 + qt
                                    ob, bob = obk[qt]
                                    S.op("dve", lambda e, ob=ob, qt=qt, c=c: e.reciprocal(out=rd[:, qt * 2 + c:qt * 2 + c + 1], in_=ob[:, 128:129]), [bob], [b_rd])
                                    if c == 0:
                                        TS("dve", o1[:, qt, :], ob[:, 0:128], rd[:, qt * 2:qt * 2 + 1], ALU.mult, [bob, b_rd], [b_o1])
                                    else:
                                        TT("dve", rd[:, qt * 2 + 1:qt * 2 + 2], rd[:, qt * 2 + 1:qt * 2 + 2], lamc[:, 2:3], ALU.mult, [b_rd, b_lam], [b_rd])
                                        STT(oh[:], ob[:, 0:128], rd[:, qt * 2 + 1:qt * 2 + 2], o1[:, qt, :], ALU.mult, ALU.add, [bob, b_rd, b_o1], [b_oh])
                                        ACT(junk[:, 0:128], oh[:], AF.Square, [b_oh], [b_junk, b_small], accum=small[:, 40:41])
                                        rstd_from_ss(small[:, 40:41], 128, small[:, 41:42], [b_small], [b_small])
                                        TS("dve", odn[:, tq, h * 128:(h + 1) * 128], oh[:], small[:, 41:42], ALU.mult, [b_oh, b_small], [b_odn])
                            stages.append((A, B))
        LA = 3
        for k in range(min(LA, len(stages))):
            stages[k][0]()
        for k in range(len(stages)):
            if k + LA < len(stages):
                stages[k + LA][0]()
            stages[k][1]()
        if sub == 4:
            raise _Stop()
        Sched.PHASE = _p0 + 'D'
        for t in range(8):
            for h in range(4):
                ps2, bp2 = rr.get()
                pv = ps2[:].bitcast(BF16)[:, 0:128]
                transpose_to(pv, odn[:, t, h * 128:(h + 1) * 128], [b_odn], [bp2])
                yt_, by_ = yT(2, h)
                ACT(yt_[:, t * 128:(t + 1) * 128], pv, AF.Copy, [bp2, b_gsub], [by_], scale=gsub[:, 0:1])
        S.op("pool", lambda e: e.memset(gq[:, 0:1], 0.0), [], ([b_qT, b_kT, b_va, b_g, b_lam, b_o1, b_odn, b_rd, b_oh, b_gsub] + b_kstL + b_qb + b_pT + b_qfL + b_sqL + b_rsL) + [b_scr_all])

    def branch_win(l, path, nseq, L, nt, is_s):
        barrier_begin()
        cv = Carve()
        nk_ctx = 2 if is_s else 0
        NKT = 8 + nk_ctx
        qT = cv.take([4, T], BF16); b_qT = NB("wqT")
        kT = cv.take([2, NKT * 128], BF16); b_kT = NB("wkT")
        vaug = cv.take([NKT, 2, 66], BF16); b_va = NB("wvaug")
        qfL = [cv.take([512]) for _ in range(2)]; b_qfL = [NB(f"wqf{i}") for i in range(2)]
        sqL = [cv.take([512]) for _ in range(2)]; b_sqL = [NB(f"wsq{i}") for i in range(2)]
        rsL = [cv.take([16]) for _ in range(2)]; b_rsL = [NB(f"wrs{i}") for i in range(2)]
        rot_ = [0]

        def nxt():
            i = rot_[0] % 2; rot_[0] += 1
            return qfL[i], b_qfL[i], sqL[i], b_sqL[i], rsL[i], b_rsL[i]
        qb = [cv.take([512], BF16), cv.take([512], BF16)]; b_qb = [NB("wqb0"), NB("wqb1")]
        gq = cv.take([64]); gk = cv.take([64]); b_g = NB("wg")
        snk = cv.take([8]); b_snk = NB("snk")
        on = cv.take([8, 512], BF16); b_on = NB("won")
        pT = [cv.take([512], BF16) for _ in range(4)]; b_pT = [NB(f"wpT{i}") for i in range(4)]
        rd = cv.take([8]); b_rd = NB("wrd")
        kst = cv.take([256]); b_kst = NB("wkst")
        S.op("pool", lambda e: e.memset(gq[:], 0.0), [b_scr_all], [b_g, b_scr_all])
        LD(gq[:], wqg[l:l + 1, :].partition_broadcast(128), b_g); LD(gk[:], wkg[l:l + 1, :].partition_broadcast(128), b_g, group=True)
        TS("dve", gq[:], gq[:], 0.125, ALU.mult, [b_g], [b_g])
        LD(snk[:], wsink[l:l + 1, :].partition_broadcast(128), b_snk)
        ACT(snk[:], snk[:], AF.Exp, [b_snk], [b_snk])
        MSET("pool", vaug[:].rearrange("p a b c -> p (a b c)"), 1.0, [b_va])
        wA, bwA = wload(w_in[l][:, C_WQ:C_WQ + 256], 8, 256)
        wB, bwB = wload(w_in[l][:, C_WQ + 256:C_WQ + 512], 8, 256)
        stq = []
        for t in range(8):
            st = {}

            def QA(st=st, t=t):
                ps, bp = rr.get()
                for k in range(8):
                    MM(ps[:, 0:256], hT[:, k, t * 128:(t + 1) * 128], wA[:, k, :], k == 0, k == 7, [b_hT, bwA], [bp])
                for k in range(8):
                    MM(ps[:, 256:512], hT[:, k, t * 128:(t + 1) * 128], wB[:, k, :], k == 0, k == 7, [b_hT, bwB], [bp])
                qf, b_qf, sq, b_sq, rs, b_rs = nxt()
                rms_groups(ps, bp, 512, gq[:], b_g, qf, b_qf, sq, b_sq, rs, b_rs, t, is_s)
                st["q"] = (qf, b_qf)

            def QB(st=st, t=t):
                qf, b_qf = st["q"]
                qb_, bqb_ = qb[t % 2], b_qb[t % 2]
                CP("pool", qb_[:], qf[:], [b_qf], [bqb_])
                for j4 in range(4):
                    ps2, bp2 = rr.get()
                    pv = ps2[:].bitcast(BF16)[:, 0:128]
                    transpose_to(pv, qb_[:, j4 * 128:(j4 + 1) * 128], [bqb_], [bp2])
                    CP("act", qT[:, j4, t * 128:(t + 1) * 128], pv, [bp2], [b_qT])
            stq.append((QA, QB))
        stq[0][0]()
        for k in range(8):
            if k + 1 < 8:
                stq[k + 1][0]()
            stq[k][1]()
        wK, bwK = wload(w_in[l][:, C_WK:C_WK + 256], 8, 256)

        def put_k(src_f32, bsrc, ktile):
            for n in range(2):
                CP("dve", qb[n][:, 0:128].rearrange("p (r d) -> p r d", r=2), src_f32[:, n * 64:(n + 1) * 64].unsqueeze(1).broadcast_to([128, 2, 64]), [bsrc], [b_qb[n]])
                ps2, bp2 = rr.get()
                pv = ps2[:].bitcast(BF16)[:, 0:128]
                transpose_to(pv, qb[n][:, 0:128], [b_qb[n]], [bp2])
                CP("act", kT[:, n, ktile * 128:(ktile + 1) * 128], pv, [bp2], [b_kT])
        for t in range(8):
            ps, bp = rr.get()
            for k in range(8):
                MM(ps[:, 0:256], hT[:, k, t * 128:(t + 1) * 128], wK[:, k, :], k == 0, k == 7, [b_hT, bwK], [bp])
            CP("act", vaug[:, nk_ctx + t, :, 0:64], ps[:, 128:256].rearrange("p (n e) -> p n e", n=2), [bp], [b_va])
            qf, b_qf, sq, b_sq, rs, b_rs = nxt()
            if not is_s:
                CP("dve", kst[:, 128:256], ps[:, 128:256], [bp], [b_kst])
                STO(o_wv[t // 2, l, (t % 2) * 128:(t % 2 + 1) * 128, :], kst[:, 128:256], b_kst)
                rms_groups(ps, bp, 128, gk[:], b_g, kst, b_kst, sq, b_sq, rs, b_rs, t, False)
                STO(o_wk[t // 2, l, (t % 2) * 128:(t % 2 + 1) * 128, :], kst[:, 0:128], b_kst)
                put_k(kst, b_kst, nk_ctx + t)
            else:
                rms_groups(ps, bp, 128, gk[:], b_g, qf, b_qf, sq, b_sq, rs, b_rs, t, True)
                put_k(qf, b_qf, nk_ctx + t)
        if is_s:
            for kt in range(2):
                LD(kst[:, 0:128], cwk[l, kt * 128:(kt + 1) * 128, :], b_kst)
                put_k(kst, b_kst, kt)
                LD(kst[:, 128:256], cwv[l, kt * 128:(kt + 1) * 128, :], b_kst)
                CP("pool", vaug[:, kt, :, 0:64], kst[:, 128:256].rearrange("p (n e) -> p n e", n=2), [b_kst], [b_va])
        obk = [(psum[4 + i], psb[4 + i]) for i in range(4)]
        pc_ = [0]
        stages = []

        def evac(ob, bob, qt, tq, h):
            TT("dve", rd[:, qt:qt + 1], ob[:, 64:65], snk[:, h:h + 1], ALU.add, [bob, b_snk], [b_rd])
            S.op("dve", lambda e, qt=qt: e.reciprocal(out=rd[:, qt:qt + 1], in_=rd[:, qt:qt + 1]), [b_rd], [b_rd])
            TS("dve", on[:, tq, h * 64:(h + 1) * 64], ob[:, 0:64], rd[:, qt:qt + 1], ALU.mult, [bob, b_rd], [b_on])
        for h in range(8):
            n = h // 4
            j4 = h // 2
            bsl = slice((h % 2) * 64, (h % 2 + 1) * 64)
            if not is_s:
                for s in range(nseq):
                    q0 = s * L
                    keyt = [s * nt + i for i in range(nt)]
                    for ki, kt in enumerate(keyt):
                        st = {}

                        def A(st=st, bsl=bsl, n=n, j4=j4, kt=kt, q0=q0):
                            psS, bpS = rr.get()
                            MM(psS[:, 0:L], kT[bsl, n, kt * 128:(kt + 1) * 128], qT[bsl, j4, q0:q0 + L], True, True, [b_kT, b_qT], [bpS])
                            z = pc_[0] % 4; pc_[0] += 1
                            st["z"] = z
                            ACT(pT[z][:, 0:L], psS[:, 0:L], AF.Exp, [bpS], [b_pT[z]])

                        def B(st=st, n=n, kt=kt, ki=ki, nk=len(keyt), s=s, h=h):
                            z = st["z"]
                            for qt in range(nt):
                                MM(obk[qt][0][:, 0:65], pT[z][:, qt * 128:(qt + 1) * 128], vaug[:, kt, n, 0:65], ki == 0, ki == nk - 1, [b_pT[z], b_va], [obk[qt][1]])
                            if ki == nk - 1:
                                for qt in range(nt):
                                    evac(obk[qt][0], obk[qt][1], qt, s * nt + qt, h)
                        stages.append((A, B))
            else:
                for tq in range(8):
                    qt = tq % 4
                    keys = [(0, None), (1, None)]
                    if tq > 0:
                        keys.append((nk_ctx + tq - 1, "prev"))
                    keys.append((nk_ctx + tq, None))
                    if tq < 7:
                        keys.append((nk_ctx + tq + 1, "next"))
                    for ki, (kt, msk) in enumerate(keys):
                        st = {}

                        def A(st=st, bsl=bsl, n=n, j4=j4, kt=kt, tq=tq, msk=msk):
                            psS, bpS = rr.get()
                            MM(psS[:, 0:128], kT[bsl, n, kt * 128:(kt + 1) * 128], qT[bsl, j4, tq * 128:(tq + 1) * 128], True, True, [b_kT, b_qT], [bpS])
                            z = pc_[0] % 4; pc_[0] += 1
                            st["z"] = z
                            ACT(pT[z][:, 0:128], psS[:, 0:128], AF.Exp, [bpS], [b_pT[z]])
                            if msk is not None:
                                TT("dve", pT[z][:, 0:128], pT[z][:, 0:128], (tril if msk == "prev" else triu)[:], ALU.mult, [b_pT[z], b_tril, b_triu], [b_pT[z]])

                        def B(st=st, n=n, kt=kt, ki=ki, nk=len(keys), qt=qt, tq=tq, h=h):
                            z = st["z"]
                            ob, bob = obk[qt]
                            MM(ob[:, 0:65], pT[z][:, 0:128], vaug[:, kt, n, 0:65], ki == 0, ki == nk - 1, [b_pT[z], b_va], [bob])
                            if ki == nk - 1:
                                evac(ob, bob, qt, tq, h)
                        stages.append((A, B))
        LA = 3
        for k in range(min(LA, len(stages))):
            stages[k][0]()
        for k in range(len(stages)):
            if k + LA < len(stages):
                stages[k + LA][0]()
            stages[k][1]()
        for t in range(8):
            for j4 in range(4):
                ps2, bp2 = rr.get()
                pv = ps2[:].bitcast(BF16)[:, 0:128]
                transpose_to(pv, on[:, t, j4 * 128:(j4 + 1) * 128], [b_on], [bp2])
                yt_, by_ = yT(3, j4)
                CP("act", yt_[:, t * 128:(t + 1) * 128], pv, [bp2], [by_])
        S.op("pool", lambda e: e.memset(gq[:, 0:1], 0.0), [], ([b_qT, b_kT, b_va, b_g, b_snk, b_on, b_rd, b_kst] + b_qb + b_pT + b_qfL + b_sqL + b_rsL) + [b_scr_all])

    def merge(l):
        rr.set(range(8))
        barrier_begin()
        cv = Carve()
        mT = cv.take([8, T], BF16); b_mT = [NB(f"mT{c}") for c in range(8)]
        gs = [cv.take([512]) for _ in range(4)]; b_gs = [NB(f"gs{i}") for i in range(4)]
        tmp = [cv.take([512]) for _ in range(4)]; b_tmp = [NB(f"mt{i}") for i in range(4)]
        bgt = cv.take([32]); b_bgt = NB("bgt")
        S.op("pool", lambda e: e.memset(bgt[:], 0.0), [b_scr_all], [b_bgt, b_scr_all])
        LD(bgt[:], bgatec[l], b_bgt)
        kq = [0]
        for dcp in range(4):
            for br in range(4):
                wg, bwg = wload(w_gate[l][:, br * 1024 + dcp * 256: br * 1024 + (dcp + 1) * 256], 8, 256)
                wb_, bwb_ = wload(w_br[l][br * 512:(br + 1) * 512, dcp * 256:(dcp + 1) * 256], 4, 256)
                for c2 in range(2):
                    dc = dcp * 2 + c2
                    for h in range(2):
                        hs = slice(h * 512, (h + 1) * 512)
                        ti_ = c2 * 2 + h
                        psg, bpg = rr.get()
                        for k in range(8):
                            MM(psg[:, :], wg[:, k, c2 * 128:(c2 + 1) * 128], hT[:, k, hs], k == 0, k == 7, [b_hT, bwg], [bpg])
                        z = kq[0] % 4; kq[0] += 1
                        ACT(gs[z][:], psg[:, :], AF.Sigmoid, [bpg, b_bgt], [b_gs[z]], bias=bgt[:, br * 8 + dc:br * 8 + dc + 1])
                        psb_, bpb_ = rr.get()
                        for k in range(4):
                            yt_, by_ = yT(br, k)
                            MM(psb_[:, :], wb_[:, k, c2 * 128:(c2 + 1) * 128], yt_[:, hs], k == 0, k == 3, [by_, bwb_], [bpb_])
                        if br == 0:
                            TT("dve", tmp[ti_][:], psb_[:, :], gs[z][:], ALU.mult, [bpb_, b_gs[z]], [b_tmp[ti_]])
                        else:
                            TT("dve", gs[z][:], psb_[:, :], gs[z][:], ALU.mult, [bpb_, b_gs[z]], [b_gs[z]])
                            if br < 3:
                                TT("pool", tmp[ti_][:], tmp[ti_][:], gs[z][:], ALU.add, [b_tmp[ti_], b_gs[z]], [b_tmp[ti_]])
                            else:
                                TT("pool", mT[:, dc, hs], tmp[ti_][:], gs[z][:], ALU.add, [b_tmp[ti_], b_gs[z]], [b_mT[dc]])
        if sub == 54:
            raise _Stop()
        for cb4 in range(4):
            w, bw = wload(w_out[l][:, cb4 * 256:(cb4 + 1) * 256], 8, 256)
            for t in range(8):
                ps, bp = rr.get()
                for k in range(8):
                    MM(ps[:, 0:256], mT[:, k, t * 128:(t + 1) * 128], w[:, k, :], k == 0, k == 7, [b_mT[k], bw], [bp])
                cs = slice(cb4 * 256, (cb4 + 1) * 256)
                zz = kq[0] % 4; kq[0] += 1
                TT("dve", gs[zz][:, 0:256], ps[:, 0:256], gbc[:, 0, cs], ALU.mult, [bp, b_gbc[0]], [b_gs[zz]])
                TT("pool", xres[:, t, cs], xres[:, t, cs], gs[zz][:, 0:256], ALU.add, [b_xres[t], b_gs[zz]], [b_xres[t]])
        S.op("pool", lambda e: e.memset(bgt[:, 0:1], 0.0), [], (b_mT + b_gs + b_tmp + [b_bgt]) + [b_scr_all])
        rr.set(range(4))

    def mlp(l):
        barrier_begin()
        cvm = Carve()
        rl = [cvm.take([512]) for _ in range(4)]; b_rl = [NB(f"rl{i}") for i in range(4)]
        rs_ = [cvm.take([256]) for _ in range(4)]; b_rs_ = [NB(f"rsd{i}") for i in range(4)]
        mk = [0, 0]
        for h in range(2):
            hs = slice(h * 512, (h + 1) * 512)

            def aTv(kc):
                return big[:, kc // 2, (kc % 2) * 512:(kc % 2 + 1) * 512], b_big[kc // 2]
            for blk in range(16):
                w, bw = wload(w_fc1[l][:, blk * 256:(blk + 1) * 256], 8, 256)
                for c2 in range(2):
                    kc = blk * 2 + c2
                    ps, bp = rr.get()
                    for k in range(8):
                        MM(ps[:, :], w[:, k, c2 * 128:(c2 + 1) * 128], hT[:, k, hs], k == 0, k == 7, [b_hT, bw], [bp])
                    a_, ba_ = aTv(kc)
                    zr = mk[0] % 4; mk[0] += 1
                    ACT(rl[zr][:], ps[:, :], AF.Relu, [bp], [b_rl[zr]])
                    TT("dve", a_, rl[zr][:], rl[zr][:], ALU.mult, [b_rl[zr]], [ba_])
            for cb4 in range(4):
                cs = slice(cb4 * 256, (cb4 + 1) * 256)
                accb = [(psum[4 + i], psb[4 + i]) for i in range(4)]
                for kg in range(4):
                    w, bw = wload(w_fc2[l][kg * 1024:(kg + 1) * 1024, cs], 8, 256)
                    for tt in range(4):
                        for k in range(8):
                            kc = kg * 8 + k
                            a_, ba_ = aTv(kc)
                            MM(accb[tt][0][:, 0:256], a_[:, tt * 128:(tt + 1) * 128], w[:, k, :], kc == 0, kc == 31, [ba_, bw], [accb[tt][1]])
                for tt in range(4):
                    t = h * 4 + tt
                    zq = mk[1] % 4; mk[1] += 1
                    TT("dve", rs_[zq][:], accb[tt][0][:, 0:256], gbc[:, 1, cs], ALU.mult, [accb[tt][1], b_gbc[1]], [b_rs_[zq]])
                    TT("pool", xres[:, t, cs], xres[:, t, cs], rs_[zq][:], ALU.add, [b_xres[t], b_rs_[zq]], [b_xres[t]])

        S.op("pool", lambda e: e.memset(small[:, 62:63], 0.0), [], b_rl + b_rs_ + [b_scr_all])

    try:
        Sched.PHASE = "prologue"
        adaln_weights(0)
        adaln_weights(1)
        run_pass(0)
        run_pass(1)
    except _Stop:
        pass
    if stop is not None:
        d_hT = nc.dram_tensor("dbg_hT", [128, 8, T], BF16, kind="ExternalOutput").ap()
        d_big = nc.dram_tensor("dbg_big", [128, 16, 1024], BF16, kind="ExternalOutput").ap()
        d_x = nc.dram_tensor("dbg_x", [128, 8, D], F32, kind="ExternalOutput").ap()
        d_modc = nc.dram_tensor("dbg_modc", [128, 48], F32, kind="ExternalOutput").ap()
        S.dma("sp", d_hT[:, :, :], hT[:], reads=[b_hT])
        S.dma("sp", d_big[:, :, :], big[:], reads=b_big, sbuf=b_big[0])
        S.dma("sp", d_x[:, :, :], xres[:], reads=b_xres, sbuf=b_xres[0])
        S.dma("sp", d_modc[:, :], modc[:], reads=[b_modc])
    with nc.Block() as block:
        S.emit(block)
    es.close()
    nc._phases = {e: [o.phase for o in S.ops[e]] for e in ENGS}
    return nc


_NC_CACHE = {}


def _consts():
    bf = ml_dtypes.bfloat16
    c = {}
    c["c_identb"] = np.eye(128, dtype=np.float32).astype(bf)
    c["c_identf"] = np.eye(128, dtype=np.float32)
    c["c_ones"] = np.ones((128, 128), np.float32)
    k = np.arange(128)[:, None]; t = np.arange(128)[None, :]
    c["c_triu"] = (k <= t).astype(np.float32)
    c["c_tril"] = (k >= t).astype(np.float32)
    c["c_mnegF"] = np.where(k <= t, 0.0, -1e30).astype(np.float32)
    c["c_mnegB"] = np.where(k >= t, 0.0, -1e30).astype(np.float32)
    c["c_bprev"] = (t <= k).astype(np.float32)
    c["c_bnext"] = (k <= t).astype(np.float32)
    mB = np.zeros((128, 4, 128), np.float32)
    mC = np.zeros((128, 4, 128), np.float32)
    for q in range(4):
        for gl in range(2):
            g8 = 2 * q + gl
            mB[gl * 64:(gl + 1) * 64, q, g8 * 16:(g8 + 1) * 16] = 1.0
            mC[g8 * 16:(g8 + 1) * 16, q, gl * 64:(gl + 1) * 64] = 1.0
    c["c_maskB"] = mB; c["c_maskC"] = mC
    c["c_iota"] = np.broadcast_to(np.arange(1024, dtype=np.float32)[None, :], (128, 1024)).copy()
    Ls = 1024
    row = np.repeat(np.arange(Ls // 64), 64).astype(np.float32); col = np.tile(np.arange(64), Ls // 64).astype(np.float32)
    nf = 16
    inv = (10000.0 ** (-np.arange(nf, dtype=np.float32) / nf)).astype(np.float32)
    ang = np.concatenate([row[:, None] * inv, col[:, None] * inv], axis=-1).astype(np.float32)
    cs, sn = np.cos(ang).astype(np.float32), np.sin(ang).astype(np.float32)
    C64 = np.zeros((Ls, 2, 2, 16), np.float32); S64 = np.zeros((Ls, 2, 2, 16), np.float32)
    for a in range(2):
        for p in range(2):
            C64[:, a, p, :] = cs[:, a * 16:(a + 1) * 16]
            S64[:, a, p, :] = (-1.0 if p == 0 else 1.0) * sn[:, a * 16:(a + 1) * 16]
    c["c_ropeC"] = C64.reshape(8, 128, 64).transpose(1, 0, 2).copy()
    c["c_ropeS"] = S64.reshape(8, 128, 64).transpose(1, 0, 2).copy()
    lidx = np.arange(1024)
    c["c_rmF"] = np.broadcast_to((lidx % 256 != 0).astype(np.float32)[None, :], (128, 1024)).astype(bf)
    c["c_rmB"] = np.broadcast_to((lidx % 256 != 255).astype(np.float32)[None, :], (128, 1024)).astype(bf)
    return c


def _colmajor(v, nchunk):
    return np.ascontiguousarray(np.swapaxes(v.reshape(v.shape[:-1] + (nchunk, 128)), -1, -2))


def make_in_maps(inp):
    f = lambda a: np.ascontiguousarray(np.asarray(a, dtype=np.float32))
    I = {k: f(v) for k, v in inp.items()}
    shared = dict(_consts())
    shared.update({
        "w_mod": I["w_mod"], "w_in": I["w_in"], "w_gate": I["w_gate"], "w_out": I["w_out"], "w_fc1": I["w_fc1"], "w_fc2": I["w_fc2"],
        "w_glu": I["s5_w_glu"], "w_br": I["w_branch"].reshape(2, 2048, 1024),
        "bmodc": _colmajor(I["b_mod"], 48), "g1c": _colmajor(I["g_norm1"], 8), "g2c": _colmajor(I["g_norm2"], 8),
        "convw": np.ascontiguousarray(I["ssd_conv_w"].transpose(0, 2, 1).reshape(2, 6, 128, 7).transpose(0, 2, 1, 3)),
        "convb": _colmajor(I["ssd_conv_b"], 6),
        "dtb": I["ssd_dt_bias"].reshape(2, 16), "alog": I["ssd_a_log"].reshape(2, 16), "ssdd": I["ssd_d"], "normgc": _colmajor(I["ssd_norm_g"], 4),
        "lamre": I["s5_lam_re"].reshape(2, 32, 128), "lamim": I["s5_lam_im"].reshape(2, 32, 128),
        "lsx": np.ascontiguousarray(np.repeat(I["s5_log_step"].reshape(2, 2, 32, 1), 64, axis=-1).reshape(2, 32, 128)),
        "s5bre": I["s5_b_re"].reshape(2, 2, 2048, 16), "s5bim": I["s5_b_im"].reshape(2, 2, 2048, 16),
        "s5cre": I["s5_c_re"].reshape(2, 2, 512, 64), "s5cim": I["s5_c_im"].reshape(2, 2, 512, 64),
        "s5dc": _colmajor(I["s5_d"], 4), "bgluc": _colmajor(I["s5_b_glu"], 8),
        "dqg": I["diff_qn_g"], "dkg": I["diff_kn_g"], "dlam": I["diff_lambda"].reshape(2, 256), "dsubc": I["diff_subln_g"].reshape(2, 128, 1),
        "wqg": I["win_qn_g"], "wkg": I["win_kn_g"], "wsink": I["win_sink"], "bgatec": _colmajor(I["b_gate"], 32),
    })
    in_maps = []
    for i in range(8):
        b = i // 2
        cv = np.stack([I["c_ctx"], I["c"][b]], axis=0)
        m = dict(shared)
        m.update({
            "xp": I["x_prompt"][4 * i:4 * i + 4].reshape(1024, 1024), "xs": I["x_sample"][b],
            "cvT": np.ascontiguousarray(cv.reshape(2, 8, 128).transpose(2, 1, 0)),
            "st_ssd": I["state_ssd"][b], "st_s5": I["state_s5"][b].reshape(2, 64, 128),
            "cdk": I["cache_diff_k"][b].reshape(2, 256, 512), "cdv": I["cache_diff_v"][b].reshape(2, 256, 512),
            "cwk": I["cache_win_k"][b].reshape(2, 256, 128), "cwv": I["cache_win_v"][b].reshape(2, 256, 128),
        })
        in_maps.append({k: np.ascontiguousarray(v) for k, v in m.items()})
    return in_maps


def kernel(**inp):
    if "nc" not in _NC_CACHE:
        _NC_CACHE["nc"] = build_program()
    nc = _NC_CACHE["nc"]
    in_maps = make_in_maps(inp)
    res = run_bass_kernel_spmd(nc, in_maps, core_ids=list(range(8)))
    R = res.results
    yp = np.concatenate([R[i]["yp"].reshape(4, 256, 1024) for i in range(8)], axis=0)
    ys = np.stack([R[2 * b]["ys"] for b in range(4)], axis=0)
    ssd = np.concatenate([R[i]["o_ssd"] for i in range(8)], axis=0)
    s5 = np.concatenate([R[i]["o_s5"].reshape(4, 2, 2, 2, 32, 64) for i in range(8)], axis=0)
    dk = np.concatenate([R[i]["o_dk"].reshape(4, 2, 256, 4, 2, 64) for i in range(8)], axis=0)
    dv = np.concatenate([R[i]["o_dv"].reshape(4, 2, 256, 4, 128) for i in range(8)], axis=0)
    wk = np.concatenate([R[i]["o_wk"].reshape(4, 2, 256, 2, 64) for i in range(8)], axis=0)
    wv = np.concatenate([R[i]["o_wv"].reshape(4, 2, 256, 2, 64) for i in range(8)], axis=0)
    return tuple(np.ascontiguousarray(a.astype(np.float32)) for a in (yp, ys, ssd, s5, dk, dv, wk, wv))
```

```python
import math
import numpy as np
from contextlib import ExitStack
import ml_dtypes
import concourse.bass as bass
import concourse.mybir as mybir
from concourse.bass_utils import run_bass_kernel_spmd

F32 = mybir.dt.float32
BF16 = mybir.dt.bfloat16
I32 = mybir.dt.int32
ALU = mybir.AluOpType
AF = mybir.ActivationFunctionType
AX = mybir.AxisListType
ENGS = ("pe", "dve", "act", "pool", "sp")
EPS = 1e-6


class Buf:
    __slots__ = ("name", "last_w", "readers", "load_sem", "load_cnt", "store_sem", "store_cnt", "excl")

    def __init__(self, name, excl=False):
        self.name = name
        self.excl = excl
        self.last_w = None
        self.readers = []
        self.load_sem = None
        self.load_cnt = 0
        self.store_sem = None
        self.store_cnt = 0


class Op:
    __slots__ = ("eng", "fn", "deps", "signal", "semval", "is_dma", "dsem", "dval", "phase")

    def __init__(self, eng, fn):
        self.phase = Sched.PHASE
        self.eng = eng
        self.fn = fn
        self.deps = []
        self.signal = False
        self.semval = 0
        self.is_dma = False
        self.dsem = None
        self.dval = 0


class Sched:
    PHASE = ""

    def __init__(self, nc, es):
        self.nc = nc
        self.es = es
        self.ops = {e: [] for e in ENGS}
        self.sems = {e: es.enter_context(nc.semaphore("c_" + e)) for e in ENGS}
        self.store_bufs = []
        self.nsem = 5
        self.pool = {}

    def new_sem(self, name):
        self.nsem += 1
        return self.es.enter_context(self.nc.semaphore(f"{name}_{self.nsem}"))

    def _track(self, op, reads, writes, skip_waw=False):
        deps = op.deps
        for r in reads:
            if r.last_w is not None and r.last_w is not op:
                deps.append(r.last_w)
            if r.excl:
                deps.extend(x for x in r.readers if x is not op and x.eng != op.eng)
            r.readers.append(op)
        for w in writes:
            if w.last_w is not None and w.last_w is not op and not skip_waw:
                deps.append(w.last_w)
            deps.extend(r for r in w.readers if r is not op)
            w.last_w = op
            w.readers = []

    def op(self, eng, fn, reads=(), writes=()):
        o = Op(eng, fn)
        self._track(o, reads, writes)
        self.ops[eng].append(o)
        return o

    def dma(self, q, out, in_, reads=(), writes=(), group=False, sbuf=None, **kw):
        o = Op(q, lambda e: e.dma_start(out=out, in_=in_, **kw))
        o.is_dma = True
        self._track(o, reads, writes, skip_waw=group)
        if sbuf is None:
            sbuf = writes[0] if writes else reads[0]
        key = ("l_" if sbuf in writes else "s_") + sbuf.name
        ent = self.pool.get(key)
        if ent is None:
            ent = [self.new_sem(key), 0]
            self.pool[key] = ent
        ent[1] += 16
        o.dsem, o.dval = ent[0], ent[1]
        self.ops[q].append(o)
        return o

    def emit(self, block):
        for e in ENGS:
            for o in self.ops[e]:
                for d in o.deps:
                    if not d.is_dma and not (d.eng == "pe" and o.eng == "pe"):
                        d.signal = True
        for e in ENGS:
            v = 0
            for o in self.ops[e]:
                if o.signal and not o.is_dma:
                    v += 1
                    o.semval = v
        engmap = {"pe": block.tensor, "dve": block.vector, "act": block.scalar,
                  "pool": block.gpsimd, "sp": block.sync}
        sems = self.sems
        store_bufs = self.store_bufs
        for e in ENGS:
            def body(eng, ops=self.ops[e], e=e):
                known = {}
                for o in ops:
                    need = {}
                    for d in o.deps:
                        if d.is_dma:
                            key, val = d.dsem, d.dval
                        else:
                            if d.eng == "pe" and e == "pe":
                                continue
                            key, val = sems[d.eng], d.semval
                        if need.get(key, 0) < val:
                            need[key] = val
                    for key, val in need.items():
                        if known.get(key, 0) < val:
                            eng.wait_ge(key, val)
                            known[key] = val
                    inst = o.fn(eng)
                    if o.is_dma:
                        inst.then_inc(o.dsem, 16)
                    elif o.signal:
                        inst.then_inc(sems[e], 1)
                if e == "sp":
                    for key, ent in self.pool.items():
                        if key.startswith("s_"):
                            eng.wait_ge(ent[0], ent[1])
            engmap[e](body)


D = 1024
T = 1024
W_IN = 4112
C_Z, C_XBC, C_DT, C_U, C_DQ, C_DK, C_DV, C_WQ, C_WK, C_WV = 0, 512, 1280, 1296, 1808, 2320, 2832, 3344, 3856, 3984


class _Stop(Exception):
    pass


def build_program(stop=None, sub=None):
    nc = bass.Bass("TRN2", target_bir_lowering=False)
    es = ExitStack()
    S = Sched(nc, es)

    def din(name, shape, dt=F32):
        return nc.dram_tensor(name, list(shape), dt, kind="ExternalInput").ap()

    def dout(name, shape):
        return nc.dram_tensor(name, list(shape), F32, kind="ExternalOutput").ap()

    cnt = [0]

    def sb(shape, dt=F32, name=None):
        cnt[0] += 1
        return es.enter_context(nc.sbuf_tensor(name or f"t{cnt[0]}", list(shape), dt))

    xin = [din("xp", [T, D]), din("xs", [T, D])]
    yout = [dout("yp", [T, D]), dout("ys", [T, D])]
    cvT_d = din("cvT", [128, 8, 2])
    st_ssd = din("st_ssd", [2, 2, 8, 64, 64])
    st_s5 = din("st_s5", [2, 64, 128])
    cdk = din("cdk", [2, 256, 512]); cdv = din("cdv", [2, 256, 512])
    cwk = din("cwk", [2, 256, 128]); cwv = din("cwv", [2, 256, 128])
    w_mod = din("w_mod", [2, D, 6 * D]); w_in = din("w_in", [2, D, W_IN]); w_gate = din("w_gate", [2, D, 4 * D])
    w_out = din("w_out", [2, D, D]); w_fc1 = din("w_fc1", [2, D, 4 * D]); w_fc2 = din("w_fc2", [2, 4 * D, D])
    w_glu = din("w_glu", [2, 512, 1024]); w_br = din("w_br", [2, 2048, 1024])
    bmodc = din("bmodc", [2, 128, 48]); g1c = din("g1c", [2, 128, 8]); g2c = din("g2c", [2, 128, 8])
    convw = din("convw", [2, 128, 6, 7]); convb = din("convb", [2, 128, 6])
    dtb = din("dtb", [2, 16]); alog = din("alog", [2, 16]); ssdd = din("ssdd", [2, 8]); normgc = din("normgc", [2, 128, 4])
    lamre = din("lamre", [2, 32, 128]); lamim = din("lamim", [2, 32, 128]); lsx = din("lsx", [2, 32, 128])
    s5bre = din("s5bre", [2, 2, 2048, 16]); s5bim = din("s5bim", [2, 2, 2048, 16])
    s5cre = din("s5cre", [2, 2, 512, 64]); s5cim = din("s5cim", [2, 2, 512, 64])
    s5dc = din("s5dc", [2, 128, 4]); bgluc = din("bgluc", [2, 128, 8])
    dqg = din("dqg", [2, 64]); dkg = din("dkg", [2, 64]); dlam = din("dlam", [2, 256]); dsubc = din("dsubc", [2, 128, 1])
    wqg = din("wqg", [2, 64]); wkg = din("wkg", [2, 64]); wsink = din("wsink", [2, 8]); bgatec = din("bgatec", [2, 128, 32])
    c_identb = din("c_identb", [128, 128], BF16); c_identf = din("c_identf", [128, 128]); c_ones = din("c_ones", [128, 128])
    c_triu = din("c_triu", [128, 128]); c_tril = din("c_tril", [128, 128])
    c_mnegF = din("c_mnegF", [128, 128]); c_mnegB = din("c_mnegB", [128, 128])
    c_bprev = din("c_bprev", [128, 128]); c_bnext = din("c_bnext", [128, 128])
    c_maskB = din("c_maskB", [128, 4, 128]); c_maskC = din("c_maskC", [128, 4, 128])
    c_iota = din("c_iota", [128, 1024]); c_ropeC = din("c_ropeC", [128, 8, 64]); c_ropeS = din("c_ropeS", [128, 8, 64])
    c_rmF = din("c_rmF", [128, 1024], BF16); c_rmB = din("c_rmB", [128, 1024], BF16)
    o_ssd = dout("o_ssd", [4, 2, 2, 8, 64, 64]); o_s5 = dout("o_s5", [4, 2, 64, 128])
    o_dk = dout("o_dk", [4, 2, 256, 512]); o_dv = dout("o_dv", [4, 2, 256, 512])
    o_wk = dout("o_wk", [4, 2, 256, 128]); o_wv = dout("o_wv", [4, 2, 256, 128])

    def TT(eng, out, in0, in1, op, r, w):
        S.op(eng, lambda e: e.tensor_tensor(out=out, in0=in0, in1=in1, op=op), r, w)

    def TS(eng, out, in0, s1, op0, r, w, s2=None, op1=None):
        if op1 is None:
            S.op(eng, lambda e: e.tensor_scalar(out=out, in0=in0, scalar1=s1, scalar2=None, op0=op0), r, w)
        else:
            S.op(eng, lambda e: e.tensor_scalar(out=out, in0=in0, scalar1=s1, scalar2=s2, op0=op0, op1=op1), r, w)

    def STT(out, in0, scalar, in1, op0, op1, r, w):
        S.op("dve", lambda e: e.scalar_tensor_tensor(out=out, in0=in0, scalar=scalar, in1=in1, op0=op0, op1=op1), r, w)

    def ACT(out, in_, func, r, w, scale=1.0, bias=None, accum=None):
        kw = {}
        if bias is not None:
            kw["bias"] = bias
        if accum is not None:
            kw["accum_out"] = accum
        S.op("act", lambda e: e.activation(out=out, in_=in_, func=func, scale=scale, **kw), r, w)

    def CP(eng, out, in_, r, w):
        if eng == "act":
            S.op("act", lambda e: e.copy(out=out, in_=in_), r, w)
        else:
            S.op(eng, lambda e: e.tensor_copy(out=out, in_=in_), r, w)

    def MM(out, lhsT, rhs, start, stop, r, w):
        S.op("pe", lambda e: e.matmul(out, lhsT=lhsT, rhs=rhs, start=start, stop=stop), r, w)

    def MSET(eng, out, val, w):
        S.op(eng, lambda e: e.memset(out, val), (), w)

    def LD(out, in_, b, q="sp", group=False):
        S.dma(q, out, in_, writes=[b], group=group)

    def STO(out, in_, b, q="sp"):
        S.dma(q, out, in_, reads=[b])

    def const(src, shape, dt=F32):
        t = sb(shape, dt)
        b = Buf(f"c{cnt[0]}")
        LD(t[:], src, b)
        return t, b

    identb, b_identb = const(c_identb[:, :], [128, 128], BF16)
    identf, b_identf = const(c_identf[:, :], [128, 128])
    onesf, b_ones = const(c_ones[:, :], [128, 128])
    triu, b_triu = const(c_triu[:, :], [128, 128]); tril, b_tril = const(c_tril[:, :], [128, 128])
    mnegF, b_mnegF = const(c_mnegF[:, :], [128, 128]); mnegB, b_mnegB = const(c_mnegB[:, :], [128, 128])
    maskB, b_maskB = const(c_maskB[:, :, :], [128, 4, 128]); maskC, b_maskC = const(c_maskC[:, :, :], [128, 4, 128])
    ropeC, b_ropeC = const(c_ropeC[:, :, :], [128, 8, 64]); ropeS, b_ropeS = const(c_ropeS[:, :, :], [128, 8, 64])
    CONSTB = [b_identb, b_identf, b_ones]

    psum = [es.enter_context(nc.psum_tensor(f"ps{i}", [128, 512], F32)) for i in range(8)]
    psb = [Buf(f"ps{i}", excl=True) for i in range(8)]

    class RR:
        def __init__(self, ids):
            self.ids = list(ids); self.i = 0

        def get(self):
            k = self.ids[self.i % len(self.ids)]; self.i += 1
            return psum[k], psb[k]

        def set(self, ids):
            self.ids = list(ids)

    rr = RR(range(0, 4))

    xres = sb([128, 8, D]); b_xres = [Buf(f"xres{t}") for t in range(8)]
    hT = sb([128, 8, T], BF16); b_hT = Buf("hT")
    NST, NBF = 3, 3
    wst = [sb([128, 8, 256]) for _ in range(NST)]; b_wst = [Buf(f"wst{i}") for i in range(NST)]
    wbf = [sb([128, 8, 256], BF16) for _ in range(NBF)]; b_wbf = [Buf(f"wbf{i}") for i in range(NBF)]
    wctr = [0, 0]
    big = sb([128, 16, 1024], BF16)
    b_big = [Buf(f"big{i}") for i in range(16)]
    modc = sb([128, 48]); b_modc = Buf("modc")
    scol = sb([128, 8, 2]); b_scol = Buf("scol")
    G1 = sb([128, 8]); SH1 = sb([128, 8]); G2 = sb([128, 8]); SH2 = sb([128, 8]); b_G = Buf("G")
    gbc = sb([128, 2, D]); b_gbc = [Buf("gbc0"), Buf("gbc1")]
    small = sb([128, 64]); b_small = Buf("small")
    junk = sb([128, 768]); b_junk = Buf("junk")
    SCR_BYTES = 60 * 1024
    scr = sb([128, SCR_BYTES // 4])

    def wload(src, nk, ncols, cast=True):
        i = wctr[0] % NST; wctr[0] += 1
        st, bs = wst[i], b_wst[i]
        LD(st[:, 0:nk, 0:ncols], src.rearrange("(k p) c -> p k c", p=128), bs)
        if not cast:
            return st, bs
        j = wctr[1] % NBF; wctr[1] += 1
        wb, bb = wbf[j], b_wbf[j]
        heavy = any(k in Sched.PHASE for k in ("prologue", "merge", "mlp"))
        eng = "act" if (wctr[1] % 2 == 0 or not heavy) else "dve"
        CP(eng, wb[:, 0:nk, 0:ncols], st[:, 0:nk, 0:ncols], [bs], [bb])
        return wb, bb

    def proj_tm(src, bsrc, nk, w, bw, ncols, tiles, evac):
        for t in tiles:
            ps, bp = rr.get()
            for k in range(nk):
                MM(ps[:, 0:ncols], src[:, k, t * 128:(t + 1) * 128], w[:, k, 0:ncols], k == 0, k == nk - 1, [bsrc, bw], [bp])
            evac(t, ps, bp)

    def proj_fm(src, bsrc, nk, w, bw, ncols, evac, halves=(0, 1)):
        for cc in range((ncols + 127) // 128):
            m = min(128, ncols - cc * 128)
            for h in halves:
                ps, bp = rr.get()
                for k in range(nk):
                    MM(ps[0:m, :], w[:, k, cc * 128:cc * 128 + m], src[:, k, h * 512:(h + 1) * 512], k == 0, k == nk - 1, [bsrc, bw], [bp])
                evac(cc, h, ps, bp)

    def transpose_to(ps_out, in_, r, w, dt=BF16, np_=128):
        idn = identb if dt == BF16 else identf
        S.op("pe", lambda e: e.transpose(out=ps_out, in_=in_, identity=idn[0:np_, 0:np_]), list(r) + CONSTB, w)

    def bcast_rows(col_ap, bcol, ps_out, bp):
        dg = sb_diag[dgc[0] % 4]; bd = b_diag[dgc[0] % 4]; dgc[0] += 1
        TS("dve", dg[:], identf[:], col_ap, ALU.mult, [b_identf, bcol], [bd])
        MM(ps_out, onesf[:], dg[:], True, True, [b_ones, bd], [bp])

    sb_diag = [sb([128, 128]) for _ in range(4)]; b_diag = [Buf(f"dg{i}") for i in range(4)]; dgc = [0]

    def rstd_from_ss(ss_ap, n, out_ap, r, w, ncols=1):
        TS("dve", out_ap, ss_ap, 1.0 / n, ALU.mult, r, w, s2=EPS, op1=ALU.add)
        ACT(out_ap, out_ap, AF.Sqrt, w, w)
        S.op("dve", lambda e: e.reciprocal(out=out_ap, in_=out_ap), w, w)

    LD(scol[:], cvT_d[:, :, :], b_scol)
    ACT(scol[:], scol[:], AF.Silu, [b_scol], [b_scol])
    scolb = sb([128, 8, 2], BF16)
    CP("dve", scolb[:], scol[:], [b_scol], [b_scol])

    modall = sb([128, 2, 2, 48]); b_modall = Buf("modall")

    def adaln_weights(l):
        rr.set(range(8))
        bm = sb_bm; LD(bm[:], bmodc[l], b_bm)
        for blk in range(24):
            w, bw = wload(w_mod[l][:, blk * 256:(blk + 1) * 256], 8, 256)
            for cc in range(2):
                ps, bp = rr.get()
                for k in range(8):
                    MM(ps[:, 0:2], w[:, k, cc * 128:(cc + 1) * 128], scolb[:, k, 0:2], k == 0, k == 7, [bw, b_scol], [bp])
                c = blk * 2 + cc
                TT("dve", modall[:, l, :, c], ps[:, 0:2], bm[:, c:c + 1].broadcast_to([128, 2]), ALU.add, [bp, b_bm], [b_modall])
        rr.set(range(4))

    def adaln(l, path):
        rr.set(range(8))
        CP("dve", modc[:], modall[:, l, path, :], [b_modall], [b_modc])
        gt = sb_gt; LD(gt[:, 0:8], g1c[l], b_gt); LD(gt[:, 8:16], g2c[l], b_gt, group=True)
        STT(G1[:], modc[:, 8:16], 1.0, gt[:, 0:8], ALU.add, ALU.mult, [b_modc, b_gt], [b_G])
        STT(G2[:], modc[:, 32:40], 1.0, gt[:, 8:16], ALU.add, ALU.mult, [b_modc, b_gt], [b_G])
        CP("dve", SH1[:], modc[:, 0:8], [b_modc], [b_G])
        CP("dve", SH2[:], modc[:, 24:32], [b_modc], [b_G])
        for gi, base in enumerate((16, 40)):
            for c in range(8):
                ps, bp = rr.get()
                bcast_rows(modc[:, base + c:base + c + 1], b_modc, ps[:, 0:128], bp)
                CP("act", gbc[:, gi, c * 128:(c + 1) * 128], ps[:, 0:128], [bp], [b_gbc[gi]])
        rr.set(range(4))

    sb_bm = sb([128, 48]); b_bm = Buf("bm"); sb_gt = sb([128, 16]); b_gt = Buf("gt")

    xn = [sb([128, D], BF16), sb([128, D], BF16)]; b_xn = [Buf("xn0"), Buf("xn1")]

    def norm_mod(Gc, SHc):
        rr.set(range(8))
        jb = junk[:].bitcast(BF16)[:, 0:D]
        for t in range(8):
            ACT(jb, xres[:, t, :], AF.Square, [b_xres[t]], [b_junk, b_small], accum=small[:, t:t + 1])
        rstd_from_ss(small[:, 0:8], D, small[:, 8:16], [b_small], [b_small])
        stg = []
        for t in range(8):
            def FA(t=t):
                x_, bx_ = xn[t % 2], b_xn[t % 2]
                ACT(x_[:], xres[:, t, :], AF.Copy, [b_xres[t], b_small], [bx_], scale=small[:, 8 + t:9 + t])

            def FB(t=t):
                x_, bx_ = xn[t % 2], b_xn[t % 2]
                for c in range(8):
                    ps, bp = rr.get()
                    pv = ps[:].bitcast(BF16)[:, 0:128]
                    transpose_to(pv, x_[:, c * 128:(c + 1) * 128], [bx_], [bp])
                    if c % 2 == 0:
                        ACT(hT[:, c, t * 128:(t + 1) * 128], pv, AF.Identity, [bp, b_G], [b_hT], scale=Gc[:, c:c + 1], bias=SHc[:, c:c + 1])
                    else:
                        TS("dve", hT[:, c, t * 128:(t + 1) * 128], pv, Gc[:, c:c + 1], ALU.mult, [bp, b_G], [b_hT], s2=SHc[:, c:c + 1], op1=ALU.add)
            stg.append((FA, FB))
        stg[0][0]()
        for k in range(8):
            if k + 1 < 8:
                stg[k + 1][0]()
            stg[k][1]()
        rr.set(range(4))

    def yT(br, fc):
        return big[:, br * 4 + fc, :], b_big[br * 4 + fc]

    def run_pass(path):
        nseq, L = (4, 256) if path == 0 else (1, 1024)
        nt = L // 128
        is_s = path == 1
        for t in range(8):
            LD(xres[:, t, :], xin[path][t * 128:(t + 1) * 128, :], b_xres[t])
        def chk(stage, l):
            if stop is not None and stop == (path, l, stage):
                raise _Stop()
        for l in range(2):
            def ph(n):
                Sched.PHASE = f"{'PS'[path]}{l}_{n}"
            ph("adaln"); adaln(l, path); chk("adaln", l)
            ph("norm1"); norm_mod(G1, SH1); chk("norm1", l)
            ph("ssd"); branch_ssd(l, path, nseq, L, nt, is_s); chk("ssd", l)
            ph("s5"); branch_s5(l, path, nseq, L, nt, is_s); chk("s5", l)
            ph("diff"); branch_diff(l, path, nseq, L, nt, is_s); chk("diff", l)
            ph("win"); branch_win(l, path, nseq, L, nt, is_s); chk("win", l)
            ph("merge"); merge(l); chk("merge", l)
            ph("norm2"); norm_mod(G2, SH2); chk("norm2", l)
            ph("mlp"); mlp(l); chk("mlp", l)
        for t in range(8):
            STO(yout[path][t * 128:(t + 1) * 128, :], xres[:, t, :], b_xres[t])

    class Carve:
        def __init__(self):
            self.off = 0

        def take(self, shape, dt=F32):
            n = int(np.prod(shape))
            nbytes = n * (4 if dt in (F32, I32) else 2)
            nbytes = (nbytes + 31) // 32 * 32
            assert self.off + nbytes <= SCR_BYTES, (self.off, nbytes)
            v = scr[:, self.off // 4:(self.off + nbytes) // 4]
            self.off += nbytes
            if dt != F32:
                v = v.bitcast(dt)
            v = v[:, 0:n]
            if len(shape) == 2:
                return v.rearrange("p (a b) -> p a b", a=shape[0])
            if len(shape) == 3:
                return v.rearrange("p (a b c) -> p a b c", a=shape[0], b=shape[1])
            return v

    b_scr_all = Buf("scrall")
    bar = [None]

    def NB(name):
        x = Buf(name)
        x.last_w = bar[0]
        return x

    def barrier_begin():
        bar[0] = S.op("pool", lambda e: e.memset(small[:, 63:64], 0.0), [b_scr_all], [b_scr_all])


    def branch_ssd(l, path, nseq, L, nt, is_s):
        barrier_begin()
        cv = Carve()
        zs = cv.take([8, 512], BF16); b_zs = NB("zs")
        dt = cv.take([8, 16]); dtA = cv.take([8, 16]); ainc = cv.take([8, 16]); arest = cv.take([8, 16]); edt = cv.take([8, 16]); einc = cv.take([8, 16])
        nainc = cv.take([8, 16])
        b_dt = NB("dt"); b_cum = NB("cum")
        cumP = cv.take([8, 32]); b_cumP = NB("cumP")
        Lp = L + 6
        raw = cv.take([nseq * Lp]); b_raw = NB("raw")
        acc = cv.take([T]); b_acc = NB("acc")
        xrot = [cv.take([T], BF16), cv.take([T], BF16)]; b_xrot = [NB("xrot0"), NB("xrot1")]
        xB = cv.take([T], BF16); xC = cv.take([T], BF16); b_xB = NB("xB"); b_xC = NB("xC")
        xs_tok = cv.take([8, 512], BF16); b_xs = NB("xs_tok")
        B_tok = cv.take([8, 128], BF16); b_Btok = NB("Btok")
        NDP = 12
        GS = 4
        b_seg = []
        Lt = [cv.take([128]) for _ in range(NDP)]; b_Lt = [NB(f"Lt{i}") for i in range(NDP)]
        sc = [cv.take([128], BF16) for _ in range(NDP)]; b_sc = [NB(f"sc{i}") for i in range(NDP)]
        yacc = cv.take([512]); b_yacc = NB("yacc")
        ytmp = cv.take([512]); b_ytmp = NB("ytmp")
        ynb = cv.take([512], BF16); b_ynb = NB("ynb")
        prm = cv.take([64]); b_prm = NB("ssdprm")
        cw = cv.take([6, 7]); cb = cv.take([6]); ngc = cv.take([4]); b_cw = NB("cw")
        Bw = [cv.take([64], BF16), cv.take([64], BF16)]; b_Bw = [NB("Bw0"), NB("Bw1")]
        fin = None; s0T = None; st_ld = None
        b_fin = NB("fin"); b_s0T = NB("s0T"); b_stld = NB("stld")
        if is_s:
            s0T = cv.take([8, 128], BF16); st_ld = cv.take([8, 128])
        else:
            fin = cv.take([16, 64])
        _p0 = Sched.PHASE
        S.op("pool", lambda e: e.memset(prm[:, 0:64], 0.0), [b_scr_all], [b_prm, b_scr_all])
        LD(prm[:, 0:16], dtb[l:l + 1, :].partition_broadcast(128), b_prm)
        LD(prm[:, 16:32], alog[l:l + 1, :].partition_broadcast(128), b_prm, group=True)
        LD(prm[:, 32:40], ssdd[l:l + 1, :].partition_broadcast(128), b_prm, group=True)
        ACT(prm[:, 16:32], prm[:, 16:32], AF.Exp, [b_prm], [b_prm])
        TS("dve", prm[:, 16:32], prm[:, 16:32], -1.0, ALU.mult, [b_prm], [b_prm])
        LD(cw[:], convw[l], b_cw); LD(cb[:], convb[l], b_cw, group=True); LD(ngc[:], normgc[l], b_cw, group=True)
        Sched.PHASE = _p0 + 'A'
        for blk in range(2):
            w, bw = wload(w_in[l][:, C_Z + blk * 256:C_Z + (blk + 1) * 256], 8, 256)
            proj_tm(hT, b_hT, 8, w, bw, 256, range(8),
                    lambda t, ps, bp, blk=blk: ACT(zs[:, t, blk * 256:(blk + 1) * 256], ps[:, 0:256], AF.Silu, [bp], [b_zs]))
        Sched.PHASE = _p0 + 'B'
        w, bw = wload(w_in[l][:, C_DT:C_DT + 16], 8, 16)

        def ev_dt(t, ps, bp):
            TT("dve", dt[:, t, :], ps[:, 0:16], prm[:, 0:16], ALU.add, [bp, b_prm], [b_dt])
            ACT(dt[:, t, :], dt[:, t, :], AF.Exp, [b_dt], [b_dt])
            ACT(dt[:, t, :], dt[:, t, :], AF.Ln, [b_dt], [b_dt], bias=1.0)
            TT("dve", dtA[:, t, :], dt[:, t, :], prm[:, 16:32], ALU.mult, [b_dt, b_prm], [b_dt])
        proj_tm(hT, b_hT, 8, w, bw, 16, range(8), ev_dt)
        Sched.PHASE = _p0 + 'C'
        for blk in range(3):
            w, bw = wload(w_in[l][:, C_XBC + blk * 256:C_XBC + (blk + 1) * 256], 8, 256)
            for c2 in range(2):
                cc = blk * 2 + c2
                if cc < 4:
                    xa, bxa = xrot[cc % 2], b_xrot[cc % 2]
                elif cc == 4:
                    xa, bxa = xB, b_xB
                else:
                    xa, bxa = xC, b_xC
                MSET("pool", raw[:], 0.0, [b_raw])
                rw3 = raw.rearrange("p (s x) -> p s x", s=nseq)
                for h in range(2):
                    ps, bp = rr.get()
                    for k in range(8):
                        MM(ps[:, :], w[:, k, c2 * 128:(c2 + 1) * 128], hT[:, k, h * 512:(h + 1) * 512], k == 0, k == 7, [b_hT, bw], [bp])
                    if is_s:
                        CP("act", raw[:, 3 + h * 512:3 + (h + 1) * 512], ps[:, :], [bp], [b_raw])
                    else:
                        CP("act", rw3[:, 2 * h:2 * h + 2, 3:3 + L], ps[:, :].rearrange("p (s x) -> p s x", s=2), [bp], [b_raw])
                ac3 = acc.rearrange("p (s x) -> p s x", s=nseq)
                TS("dve", ac3, rw3[:, :, 0:L], cw[:, cc, 0:1], ALU.mult, [b_raw, b_cw], [b_acc])
                for k in range(1, 7):
                    STT(ac3, rw3[:, :, k:k + L], cw[:, cc, k:k + 1], ac3, ALU.mult, ALU.add, [b_raw, b_cw, b_acc], [b_acc])
                ACT(xa[:], acc[:], AF.Silu, [b_acc, b_cw], [bxa], bias=cb[:, cc:cc + 1])
                if cc < 5:
                    for t in range(8):
                        ps, bp = rr.get()
                        pv = ps[:].bitcast(BF16)[:, 0:128]
                        transpose_to(pv, xa[:, t * 128:(t + 1) * 128], [bxa], [bp])
                        if cc < 4:
                            CP("act", xs_tok[:, t, cc * 128:(cc + 1) * 128], pv, [bp], [b_xs])
                        else:
                            CP("act", B_tok[:, t, :], pv, [bp], [b_Btok])
        Sched.PHASE = _p0 + 'E'
        for s in range(nseq):
            for j in range(nt):
                tj = s * nt + j
                ps, bp = rr.get()
                for i in range(j + 1):
                    MM(ps[:, 0:16], (triu if i == j else onesf)[:], dtA[:, s * nt + i, :], i == 0, i == j, [b_triu, b_ones, b_dt], [bp])
                for i in range(nt - 1, j - 1, -1):
                    MM(ps[:, 16:32], (tril if i == j else onesf)[:], dtA[:, s * nt + i, :], i == nt - 1, i == j, [b_tril, b_ones, b_dt], [bp])
                CP("act", cumP[:, tj, :], ps[:, 0:32], [bp], [b_cumP])
                CP("dve", ainc[:, tj, 0:8], cumP[:, tj, 0:8], [b_cumP], [b_cum])
                CP("dve", ainc[:, tj, 8:16], cumP[:, tj, 24:32], [b_cumP], [b_cum])
                TT("dve", arest[:, tj, 0:8], cumP[:, tj, 16:24], dtA[:, tj, 0:8], ALU.subtract, [b_cumP, b_dt], [b_cum])
                TT("dve", arest[:, tj, 8:16], cumP[:, tj, 8:16], dtA[:, tj, 8:16], ALU.subtract, [b_cumP, b_dt], [b_cum])
                ACT(edt[:, tj, :], arest[:, tj, :], AF.Exp, [b_cum], [b_cum])
                TT("dve", edt[:, tj, :], edt[:, tj, :], dt[:, tj, :], ALU.mult, [b_cum, b_dt], [b_cum])
                ACT(einc[:, tj, :], ainc[:, tj, :], AF.Exp, [b_cum], [b_cum])
                TS("dve", nainc[:, tj, :], ainc[:, tj, :], -1.0, ALU.mult, [b_cum], [b_cum])
        if is_s:
            stv = st_ssd[l].rearrange("d h p n -> (d h p) n").rearrange("(j q) n -> q j n", q=128)
            LD(st_ld[:, :, 0:64], stv, b_stld); LD(st_ld[:, :, 64:128], stv, b_stld, group=True)
            for j8 in range(8):
                ps, bp = rr.get()
                transpose_to(ps[:, 0:128], st_ld[:, j8, :], [b_stld], [bp], dt=F32)
                CP("act", s0T[:, j8, :], ps[:, 0:128], [bp], [b_s0T])
        Sched.PHASE = _p0 + 'F'
        ybanks = [(psum[4], psb[4]), (psum[7], psb[7])]
        pa_ = [0]
        k_ = [0]
        stages = []
        for s in range(nseq):
            for j in range(nt):
                tj = s * nt + j
                ybank, b_yb = ybanks[tj % 2]
                for h in range(8):
                    g = h // 4
                    gsl = slice(g * 64, (g + 1) * 64)
                    units = [(0, i) for i in range(j + 1)] + [(1, i) for i in range(j, nt)]
                    cur = {"psA": None}
                    for g0 in range(0, len(units), GS):
                        grp = list(enumerate(units))[g0:g0 + GS]
                        st = {}

                        def A(st=st, grp=grp, units=units, s=s, j=j, tj=tj, h=h, gsl=gsl, cur=cur):
                            for ui, (d, i) in grp:
                                ti = s * nt + i
                                dh = d * 8 + h
                                if ui == 0 or units[ui - 1][0] != d:
                                    if h % 4 == 0:
                                        c0 = d * 8 + h
                                        TT("dve", junk[:, 0:512].rearrange("p (a t) -> p a t", a=4), identf[:].unsqueeze(1).broadcast_to([128, 4, 128]),
                                           ainc[:, tj, c0:c0 + 4].unsqueeze(2).broadcast_to([128, 4, 128]), ALU.mult, [b_identf, b_cum], [b_junk])
                                        MM(psum[5 + d][:, 0:512], onesf[:], junk[:, 0:512], True, True, [b_ones, b_junk], [psb[5 + d]])
                                    cur["psA"] = (psum[5 + d][:, (h % 4) * 128:(h % 4 + 1) * 128], psb[5 + d])
                                psA_t, b_psA = cur["psA"]
                                q = k_[0] % NDP; k_[0] += 1
                                st[ui] = q
                                if i == j:
                                    STT(Lt[q][:], psA_t[:, 0:128], ainc[:, ti, dh:dh + 1], (mnegF if d == 0 else mnegB)[:], ALU.subtract, ALU.add,
                                        [b_psA, b_cum, b_mnegF, b_mnegB], [b_Lt[q]])
                                    ACT(Lt[q][:], Lt[q][:], AF.Exp, [b_Lt[q]], [b_Lt[q]])
                                elif is_s:
                                    ACT(Lt[q][:], psA_t[:, 0:128], AF.Exp, [b_psA, b_cum], [b_Lt[q]], bias=nainc[:, ti, dh:dh + 1])
                                else:
                                    TS("dve", Lt[q][:], psA_t[:, 0:128], ainc[:, ti, dh:dh + 1], ALU.subtract, [b_psA, b_cum], [b_Lt[q]], s2=0.0, op1=ALU.min)
                                    ACT(Lt[q][:], Lt[q][:], AF.Exp, [b_Lt[q]], [b_Lt[q]])
                            for ui, (d, i) in grp:
                                ti = s * nt + i
                                dh = d * 8 + h
                                q = st[ui]
                                psG, bpG = rr.get()
                                MM(psG[:, 0:128], xB[gsl, ti * 128:(ti + 1) * 128], xC[gsl, tj * 128:(tj + 1) * 128], True, True, [b_xB, b_xC], [bpG])
                                STT(sc[q][:], psG[:, 0:128], dt[:, ti, dh:dh + 1], Lt[q][:], ALU.mult, ALU.mult, [bpG, b_dt, b_Lt[q]], [b_sc[q]])

                        def B(st=st, grp=grp, units=units, s=s, tj=tj, h=h, ybank=ybank, b_yb=b_yb, last_grp=(g0 + GS >= len(units))):
                            for ui, (d, i) in grp:
                                ti = s * nt + i
                                q = st[ui]
                                MM(ybank[:, h * 64:(h + 1) * 64], sc[q][:], xs_tok[:, ti, h * 64:(h + 1) * 64], ui == 0, ui == len(units) - 1, [b_sc[q], b_xs], [b_yb])
                            if h == 7 and last_grp:
                                finalize(tj, ybank, b_yb)
                        stages.append((A, B))

        def finalize(tj, ybank, b_yb):
            if True:
                TT("dve", ytmp.rearrange("p (h x) -> p h x", h=8), xs_tok[:, tj, :].rearrange("p (h x) -> p h x", h=8),
                   prm[:, 32:40].unsqueeze(2).broadcast_to([128, 8, 64]), ALU.mult, [b_xs, b_prm], [b_ytmp])
                TT("dve", yacc[:], ybank[:, :], ytmp[:], ALU.add, [b_yb, b_ytmp], [b_yacc])
                if is_s:
                    for d in range(2):
                        for h in range(8):
                            g = h // 4
                            gsl = slice(g * 64, (g + 1) * 64)
                            j8 = (d * 8 + h) // 2
                            h2 = (d * 8 + h) % 2
                            psO, bpO = rr.get()
                            MM(psO[:, 0:64], xC[gsl, tj * 128:(tj + 1) * 128], s0T[gsl, j8, h2 * 64:(h2 + 1) * 64], True, True, [b_xC, b_s0T], [bpO])
                            STT(yacc[:, h * 64:(h + 1) * 64], psO[:, 0:64], einc[:, tj, d * 8 + h:d * 8 + h + 1], yacc[:, h * 64:(h + 1) * 64], ALU.mult, ALU.add,
                                [bpO, b_cum, b_yacc], [b_yacc])
                TT("dve", yacc[:], yacc[:], zs[:, tj, :], ALU.mult, [b_yacc, b_zs], [b_yacc])
                ACT(ytmp[:], yacc[:], AF.Square, [b_yacc], [b_ytmp, b_small], accum=small[:, 16:17])
                rstd_from_ss(small[:, 16:17], 512, small[:, 17:18], [b_small], [b_small])
                ACT(ynb[:], yacc[:], AF.Copy, [b_yacc, b_small], [b_ynb], scale=small[:, 17:18])
                for c4 in range(4):
                    ps, bp = rr.get()
                    pv = ps[:].bitcast(BF16)[:, 0:128]
                    transpose_to(pv, ynb[:, c4 * 128:(c4 + 1) * 128], [b_ynb], [bp])
                    yt_, by_ = yT(0, c4)
                    ACT(yt_[:, tj * 128:(tj + 1) * 128], pv, AF.Copy, [bp, b_cw], [by_], scale=ngc[:, c4:c4 + 1])
        LA = 2 if is_s else 3
        for k in range(min(LA, len(stages))):
            stages[k][0]()
        for k in range(len(stages)):
            if k + LA < len(stages):
                stages[k + LA][0]()
            stages[k][1]()
        Sched.PHASE = _p0 + 'G'
        if not is_s:
            for s in range(nseq):
                for d in range(2):
                    for h in range(8):
                        g = h // 4
                        psF, bpF = rr.get()
                        for i in range(nt):
                            ti = s * nt + i
                            q = k_[0] % 2; k_[0] += 1
                            TS("dve", Bw[q][:], B_tok[:, ti, g * 64:(g + 1) * 64], edt[:, ti, d * 8 + h:d * 8 + h + 1], ALU.mult, [b_Btok, b_cum], [b_Bw[q]])
                            MM(psF[0:64, 0:64], xs_tok[:, ti, h * 64:(h + 1) * 64], Bw[q][:], i == 0, i == nt - 1, [b_xs, b_Bw[q]], [bpF])
                        CP("act", fin[0:64, d * 8 + h, :], psF[0:64, 0:64], [bpF], [b_fin])
                STO(o_ssd[s, l].rearrange("d h p n -> p (d h) n"), fin[0:64, :, :], b_fin)
        S.op("pool", lambda e: e.memset(prm[:, 0:1], 0.0), [], ([b_fin, b_yacc, b_ynb, b_Btok, b_xs, b_cum, b_dt, b_zs, b_cumP, b_xB, b_xC, b_raw, b_acc, b_ytmp, b_cw, b_s0T, b_stld, b_prm]
             + b_xrot + b_seg + b_Lt + b_sc + b_Bw) + [b_scr_all])

    def branch_s5(l, path, nseq, L, nt, is_s):
        barrier_begin()
        _p0 = Sched.PHASE
        cv = Carve()
        uT = cv.take([4, T], BF16); b_uT = NB("uT")
        y5T = uT; b_y5 = b_uT
        prow = cv.take([128]); b_prow = NB("prow")
        pc = cv.take([12, 32]); b_pc = NB("pc")
        pci = cv.take([32], I32); b_pci = NB("pci")
        Bst = cv.take([4, 4, 16]); b_Bst = NB("Bst")
        Cn = cv.take([4, 64]); b_Cn = NB("Cn")
        Bc = cv.take([2, 16]); b_Bc = NB("Bc"); Bt = cv.take([16]); b_Bt = NB("Bt")
        Bx = [cv.take([128], BF16), cv.take([128], BF16)]; b_Bx = [NB("Bx0"), NB("Bx1")]
        BcL = [cv.take([128], BF16), cv.take([128], BF16)]; b_BcL = [NB("BcL0"), NB("BcL1")]
        Cx = [cv.take([128], BF16), cv.take([128], BF16)]; b_Cx = [NB("Cx0"), NB("Cx1")]
        CL = cv.take([4, 128], BF16); b_CL = NB("CL")
        Lt_ = L
        cosT = cv.take([Lt_]); sinT = cv.take([Lt_]); b_tab = NB("tab")
        xr = [cv.take([T]), cv.take([T])]; b_xr = [NB("xr0"), NB("xr1")]
        prR = cv.take([2 * T])
        prb = prR.bitcast(BF16)
        pr = [prb[:, k * T:(k + 1) * T] for k in range(4)]; b_pr = [NB(f"pr{i}") for i in range(4)]
        argF = prR[:, 0:Lt_]; argI = prR[:, T:T + Lt_].bitcast(I32)
        tmpx = prR[:, 0:T]; rmt = prR[:, T:2 * T]
        bA = [b_pr[0], b_pr[1]]; bB = [b_pr[2], b_pr[3]]
        if not is_s:
            tmpy = cv.take([T]); bY = [NB("tmpy")]
        else:
            tmpy = tmpx; bY = bA
        d5 = cv.take([4]); bg = cv.take([8]); b_d5 = NB("d5")
        finS = cv.take([256]); b_finS = NB("finS")
        b_wcap = NB("wcap")
        if not is_s:
            wcap = cv.take([2, 32, 4]); tcap = cv.take([2, 32]); wtmp = cv.take([4, 32, 4])
        s0c = cv.take([64]); b_s0c = NB("s0c")
        sg = cv.take([512]); b_sg = NB("sg")
        fT = cv.take([128]); b_fT = NB("fT")
        iota = cv.take([Lt_]); b_iota = NB("iota")
        LD(iota[:], c_iota[:, 0:Lt_], b_iota)
        if not is_s:
            rmF = cv.take([T], BF16); rmB = cv.take([T], BF16); b_rmF = NB("rmF"); b_rmB = NB("rmB")
            LD(rmF[:], c_rmF[:, :], b_rmF); LD(rmB[:], c_rmB[:, :], b_rmB)
        else:
            rmF = rmB = None; b_rmF = b_rmB = b_iota
        S.op("pool", lambda e: e.memset(prow[:], 0.0), [b_scr_all], [b_prow, b_scr_all])
        LD(prow[0:32, :], lamre[l], b_prow); LD(prow[32:64, :], lamim[l], b_prow, group=True); LD(prow[64:96, :], lsx[l], b_prow, group=True)
        ps, bp = rr.get()
        transpose_to(ps[:, 0:96], prow[0:96, :], [b_prow], [bp], dt=F32, np_=96)
        CP("act", pc[:, 0:3, :].rearrange("p a b -> p (a b)"), ps[:, 0:96], [bp], [b_pc])
        P_ = lambda i: pc[:, i, :]
        R, W_ = [b_pc], [b_pc]
        ACT(P_(2), P_(2), AF.Exp, R, W_)
        TT("dve", P_(3), P_(0), P_(2), ALU.mult, R, W_)
        TT("dve", P_(4), P_(1), P_(2), ALU.mult, R, W_)
        ACT(P_(5), P_(3), AF.Exp, R, W_)
        TS("dve", pci[:], P_(4), 1.0 / (2 * math.pi), ALU.mult, R, [b_pci])
        CP("dve", P_(10), pci[:], [b_pci], W_)
        STT(P_(11), P_(10), -2 * math.pi, P_(4), ALU.mult, ALU.add, R, W_)
        TS("dve", P_(11), P_(11), 3.14159, ALU.min, R, W_, s2=-3.14159, op1=ALU.max)
        ACT(P_(7), P_(11), AF.Sin, R, W_)
        ACT(P_(10), P_(11), AF.Abs, R, W_)
        ACT(P_(6), P_(10), AF.Sin, R, W_, scale=-1.0, bias=math.pi / 2)
        TT("dve", P_(6), P_(6), P_(5), ALU.mult, R, W_)
        TT("dve", P_(7), P_(7), P_(5), ALU.mult, R, W_)
        TT("dve", P_(10), P_(0), P_(0), ALU.mult, R, W_)
        TT("dve", P_(11), P_(1), P_(1), ALU.mult, R, W_)
        TT("dve", P_(10), P_(10), P_(11), ALU.add, R, W_)
        S.op("dve", lambda e: e.reciprocal(out=P_(10), in_=P_(10)), R, W_)
        TS("dve", P_(11), P_(6), -1.0, ALU.add, R, W_)
        TT("dve", P_(8), P_(11), P_(0), ALU.mult, R, W_)
        TT("dve", P_(9), P_(7), P_(1), ALU.mult, R, W_)
        TT("dve", P_(8), P_(8), P_(9), ALU.add, R, W_)
        TT("dve", P_(8), P_(8), P_(10), ALU.mult, R, W_)
        TT("dve", P_(9), P_(7), P_(0), ALU.mult, R, W_)
        TT("dve", P_(11), P_(11), P_(1), ALU.mult, R, W_)
        TT("dve", P_(9), P_(9), P_(11), ALU.subtract, R, W_)
        TT("dve", P_(9), P_(9), P_(10), ALU.mult, R, W_)
        LD(d5[:], s5dc[l], b_d5); LD(bg[:], bgluc[l], b_d5, group=True)
        if is_s:
            LD(fT[0:64, :], st_s5[l], b_fT)
            ps, bp = rr.get()
            transpose_to(ps[:, 0:64], fT[0:64, :], [b_fT], [bp], dt=F32, np_=64)
            CP("act", s0c[:], ps[:, 0:64], [bp], [b_s0c])
        Sched.PHASE = _p0 + 'u'
        for blk in range(2):
            w, bw = wload(w_in[l][:, C_U + blk * 256:C_U + (blk + 1) * 256], 8, 256)
            proj_fm(hT, b_hT, 8, w, bw, 256,
                    lambda cc, h, ps, bp, blk=blk: CP("act", uT[:, blk * 2 + cc, h * 512:(h + 1) * 512], ps[:, :], [bp], [b_uT]))
        ybk = [(psum[4], psb[4]), (psum[5], psb[5])]
        xbk = [(psum[6], psb[6]), (psum[7], psb[7])]
        nrep = T // Lt_
        v3 = (lambda a: a.rearrange("p (s x) -> p s x", s=nrep)) if nrep > 1 else (lambda a: a)
        Bc2 = [Bc, cv.take([2, 16])]; b_Bc2 = [b_Bc, NB("Bc_1")]; Bt2 = [Bt, cv.take([16])]; b_Bt2 = [b_Bt, NB("Bt_1")]
        Bx2 = [Bx, [cv.take([128], BF16), cv.take([128], BF16)]]; b_Bx2 = [b_Bx, [NB("Bx0_1"), NB("Bx1_1")]]
        BcL2 = [BcL, [cv.take([128], BF16), cv.take([128], BF16)]]; b_BcL2 = [b_BcL, [NB("BcL0_1"), NB("BcL1_1")]]
        Cx2 = [Cx, [cv.take([128], BF16), cv.take([128], BF16)]]; b_Cx2 = [b_Cx, [NB("Cx0_1"), NB("Cx1_1")]]
        CL2 = [CL, cv.take([4, 128], BF16)]; b_CL2 = [b_CL, NB("CL_1")]
        cos2 = [cosT, cv.take([Lt_])]; sin2 = [sinT, cv.take([Lt_])]; b_tab2 = [b_tab, NB("tab_1")]
        its = [(fc, d, q4) for fc in range(4) for d in range(2) for q4 in range(4)]
        NI = len(its)

        def stB(k):
            fc, d, q4 = its[k]
            z = k % 2
            if d == 0 and q4 == 0:
                for dd in range(2):
                    LD(Bst[:, dd * 2 + 0, :, :], s5bre[l, dd][fc * 512:(fc + 1) * 512, :].rearrange("(c p) m -> p c m", p=128), b_Bst, group=(dd > 0))
                    LD(Bst[:, dd * 2 + 1, :, :], s5bim[l, dd][fc * 512:(fc + 1) * 512, :].rearrange("(c p) m -> p c m", p=128), b_Bst, group=True)
                    LD(Cn[:, dd * 2 + 0, :], s5cre[l, dd][fc * 128:(fc + 1) * 128, :], b_Cn, group=(dd > 0))
                    LD(Cn[:, dd * 2 + 1, :], s5cim[l, dd][fc * 128:(fc + 1) * 128, :], b_Cn, group=True)
            c = fc * 4 + q4
            dc = d * 16 + c
            cre, cim = pc[:, 8, dc:dc + 1], pc[:, 9, dc:dc + 1]
            Bre, Bim = Bst[:, d * 2 + 0, q4, :], Bst[:, d * 2 + 1, q4, :]
            Bc_, bBc_, Bt_, bBt_ = Bc2[z], b_Bc2[z], Bt2[z], b_Bt2[z]
            TS("dve", Bt_[:], Bim, cim, ALU.mult, [b_Bst, b_pc], [bBt_])
            STT(Bc_[:, 0, :], Bre, cre, Bt_[:], ALU.mult, ALU.subtract, [b_Bst, b_pc, bBt_], [bBc_])
            TS("dve", Bt_[:], Bre, cim, ALU.mult, [b_Bst, b_pc], [bBt_])
            STT(Bc_[:, 1, :], Bim, cre, Bt_[:], ALU.mult, ALU.add, [b_Bst, b_pc, bBt_], [bBc_])
            for ri in range(2):
                TT("pool", Bx2[z][ri].rearrange("p (g m) -> p g m", g=8), maskB[:, q4, :].rearrange("p (g m) -> p g m", g=8),
                   Bc_[:, ri, :].unsqueeze(1).broadcast_to([128, 8, 16]), ALU.mult, [b_maskB, bBc_], [b_Bx2[z][ri]])
                ps, bp = rr.get()
                pv = ps[:].bitcast(BF16)[:, 0:128]
                transpose_to(pv, Bx2[z][ri][:], [b_Bx2[z][ri]], [bp])
                CP("act", BcL2[z][ri][:], pv, [bp], [b_BcL2[z][ri]])
            for ri in range(2):
                TT("pool", Cx2[z][ri].rearrange("p (g n) -> p g n", g=2), maskC[:, q4, :].rearrange("p (g n) -> p g n", g=2),
                   Cn[:, d * 2 + ri, :].unsqueeze(1).broadcast_to([128, 2, 64]), ALU.mult, [b_maskC, b_Cn], [b_Cx2[z][ri]])
                ps, bp = rr.get()
                pv = ps[:].bitcast(BF16)[:, 0:128]
                transpose_to(pv, Cx2[z][ri][:], [b_Cx2[z][ri]], [bp])
                CP("act", CL2[z][:, 2 * ri, :], pv, [bp], [b_CL2[z]])
                ACT(CL2[z][:, 2 * ri + 1, :], pv, AF.Copy, [bp], [b_CL2[z]], scale=-1.0)

        def stT1(k):
            fc, d, q4 = its[k]
            z = k % 2
            dc = d * 16 + fc * 4 + q4
            cT, sT, bt = cos2[z], sin2[z], [b_tab2[z]]
            TS("dve", cT[:], iota[:, 0:Lt_], pc[:, 4, dc:dc + 1], ALU.mult, [b_iota, b_pc], bt)
            TS("dve", sT[:].bitcast(I32), cT[:], 1.0 / (2 * math.pi), ALU.mult, bt, bt)
            CP("act", sT[:], sT[:].bitcast(I32), bt, bt)

        def stT2(k):
            z = k % 2
            cT, sT, bt = cos2[z], sin2[z], [b_tab2[z]]
            STT(cT[:], sT[:], -2 * math.pi, cT[:], ALU.mult, ALU.add, bt, bt)
            TS("dve", cT[:], cT[:], 3.14159, ALU.min, bt, bt, s2=-3.14159, op1=ALU.max)
            ACT(sT[:], cT[:], AF.Sin, bt, bt)
            ACT(cT[:], cT[:], AF.Abs, bt, bt)
            ACT(cT[:], cT[:], AF.Sin, bt, bt, scale=-1.0, bias=math.pi / 2)

        def stX(k):
            fc, d, q4 = its[k]
            z = k % 2
            c = fc * 4 + q4
            dc = d * 16 + c
            cosT_, sinT_, bt = cos2[z], sin2[z], b_tab2[z]
            for h in range(2):
                hs = slice(h * 512, (h + 1) * 512)
                for ri in range(2):
                    MM(xbk[ri][0][:, :], BcL2[z][ri][:], uT[:, fc, hs], True, True, [b_BcL2[z][ri], b_uT], [xbk[ri][1]])
                xre, xim = xbk[0][0][:, :], xbk[1][0][:, :]
                if Lt_ < 512:
                    nr2 = 512 // Lt_
                    cB = cosT_.unsqueeze(1).broadcast_to([128, nr2, Lt_]); sB = sinT_.unsqueeze(1).broadcast_to([128, nr2, Lt_])
                    vv = lambda a, nr2=nr2: a.rearrange("p (s x) -> p s x", s=nr2)
                else:
                    cB, sB = cosT_[:, hs], sinT_[:, hs]
                    vv = lambda a: a
                TT("dve", vv(xr[1][:, hs]), vv(xim), cB, ALU.mult, [xbk[1][1], bt], [b_xr[1]])
                TT("dve", vv(tmpy[:, hs]), vv(xre), sB, ALU.mult, [xbk[0][1], bt], bY)
                TT("pool", xr[1][:, hs], xr[1][:, hs], tmpy[:, hs], ALU.subtract if d == 0 else ALU.add, [b_xr[1]] + bY, [b_xr[1]])
                TT("dve", vv(xr[0][:, hs]), vv(xre), cB, ALU.mult, [xbk[0][1], bt], [b_xr[0]])
                TT("dve", vv(tmpx[:, hs]), vv(xim), sB, ALU.mult, [xbk[1][1], bt], bA)
                TT("pool", xr[0][:, hs], xr[0][:, hs], tmpx[:, hs], ALU.add if d == 0 else ALU.subtract, [b_xr[0]] + bA, [b_xr[0]])
            if is_s:
                sre = s0c[:, (d * 2 + 0) * 16 + c:(d * 2 + 0) * 16 + c + 1]; sim = s0c[:, (d * 2 + 1) * 16 + c:(d * 2 + 1) * 16 + c + 1]
                abre, abim = pc[:, 6, dc:dc + 1], pc[:, 7, dc:dc + 1]
                sm = small
                RS, WS_ = [b_small, b_s0c, b_pc, bt], [b_small]
                TT("dve", sm[:, 20:21], sre, abre, ALU.mult, RS, WS_); TT("dve", sm[:, 21:22], sim, abim, ALU.mult, RS, WS_)
                TT("dve", sm[:, 22:23], sm[:, 20:21], sm[:, 21:22], ALU.subtract, RS, WS_)
                TT("dve", sm[:, 20:21], sre, abim, ALU.mult, RS, WS_); TT("dve", sm[:, 21:22], sim, abre, ALU.mult, RS, WS_)
                TT("dve", sm[:, 23:24], sm[:, 20:21], sm[:, 21:22], ALU.add, RS, WS_)
                if d == 0:
                    TT("dve", xr[0][:, 0:1], xr[0][:, 0:1], sm[:, 22:23], ALU.add, [b_xr[0], b_small], [b_xr[0]])
                    TT("dve", xr[1][:, 0:1], xr[1][:, 0:1], sm[:, 23:24], ALU.add, [b_xr[1], b_small], [b_xr[1]])
                else:
                    cl, sl = cosT_[:, L - 1:L], sinT_[:, L - 1:L]
                    TT("dve", sm[:, 20:21], sm[:, 22:23], cl, ALU.mult, RS, WS_); TT("dve", sm[:, 21:22], sm[:, 23:24], sl, ALU.mult, RS, WS_)
                    TT("dve", sm[:, 24:25], sm[:, 20:21], sm[:, 21:22], ALU.subtract, RS, WS_)
                    TT("dve", sm[:, 20:21], sm[:, 22:23], sl, ALU.mult, RS, WS_); TT("dve", sm[:, 21:22], sm[:, 23:24], cl, ALU.mult, RS, WS_)
                    TT("dve", sm[:, 25:26], sm[:, 20:21], sm[:, 21:22], ALU.add, RS, WS_)
                    TT("dve", xr[0][:, L - 1:L], xr[0][:, L - 1:L], sm[:, 24:25], ALU.add, [b_xr[0], b_small], [b_xr[0]])
                    TT("dve", xr[1][:, L - 1:L], xr[1][:, L - 1:L], sm[:, 25:26], ALU.add, [b_xr[1], b_small], [b_xr[1]])

        def stS(k):
            fc, d, q4 = its[k]
            z = k % 2
            dc = d * 16 + fc * 4 + q4
            cosT_, sinT_, bt = cos2[z], sin2[z], b_tab2[z]
            if is_s:
                rm_ = pc[:, 5, dc:dc + 1].broadcast_to([128, T]); rm_r = rm_; brm = [b_pc]
            else:
                TS("dve", rmt, (rmF if d == 0 else rmB)[:], pc[:, 5, dc:dc + 1], ALU.mult, [b_rmF, b_rmB, b_pc], bB)
                rm_ = rmt; rm_r = rmt[:, ::-1]; brm = bB
            for ri in (1, 0):
                if d == 0:
                    S.op("dve", lambda e, ri=ri, rm_=rm_: e.tensor_tensor_scan(out=xr[ri][:], data0=rm_, data1=xr[ri][:], initial=0.0, op0=ALU.mult, op1=ALU.add),
                         brm + [b_xr[ri]], [b_xr[ri]])
                else:
                    S.op("dve", lambda e, ri=ri, rm_r=rm_r: e.tensor_tensor_scan(out=xr[ri][:, ::-1], data0=rm_r, data1=xr[ri][:, ::-1], initial=0.0, op0=ALU.mult, op1=ALU.add),
                         brm + [b_xr[ri]], [b_xr[ri]])
            if not is_s:
                lpos = L - 1 if d == 0 else 0
                for ri in range(2):
                    w3 = xr[ri].rearrange("p (s x) -> p s x", s=nseq)[:, :, lpos:lpos + 1].rearrange("p s x -> p (s x)")
                    CP("act", wcap[:, ri, dc, :], w3, [b_xr[ri]], [b_wcap])
                CP("act", tcap[:, 0, dc:dc + 1], cosT_[:, lpos:lpos + 1], [bt], [b_wcap])
                CP("act", tcap[:, 1, dc:dc + 1], sinT_[:, lpos:lpos + 1], [bt], [b_wcap])

        def stP(k):
            fc, d, q4 = its[k]
            z = k % 2
            cosT_, sinT_, bt = cos2[z], sin2[z], b_tab2[z]
            cosB = cosT_.unsqueeze(1).broadcast_to([128, nrep, Lt_]) if nrep > 1 else cosT_
            sinB = sinT_.unsqueeze(1).broadcast_to([128, nrep, Lt_]) if nrep > 1 else sinT_
            TT("dve", v3(pr[1]), v3(xr[1][:]), sinB, ALU.mult, [b_xr[1], bt], [b_pr[1]])
            TT("pool", v3(pr[0]), v3(xr[0][:]), cosB, ALU.mult, [b_xr[0], bt], [b_pr[0]])
            TT("dve", v3(pr[3]), v3(xr[1][:]), cosB, ALU.mult, [b_xr[1], bt], [b_pr[3]])
            TT("pool", v3(pr[2]), v3(xr[0][:]), sinB, ALU.mult, [b_xr[0], bt], [b_pr[2]])
            sel = [0, 1, 3, 3] if d == 0 else [0, 0, 2, 3]
            first_y = (d == 0 and q4 == 0)
            last_y = (d == 1 and q4 == 3)
            for h in range(2):
                hs = slice(h * 512, (h + 1) * 512)
                for k4 in range(4):
                    MM(ybk[h][0][:, :], CL2[z][:, sel[k4], :], pr[k4][:, hs], first_y and k4 == 0, last_y and k4 == 3, [b_CL2[z], b_pr[k4]], [ybk[h][1]])
            if last_y:
                for h in range(2):
                    hs = slice(h * 512, (h + 1) * 512)
                    STT(uT[:, fc, hs], uT[:, fc, hs], d5[:, fc:fc + 1], ybk[h][0][:, :], ALU.mult, ALU.add, [b_uT, b_d5, ybk[h][1]], [b_uT])

        Sched.PHASE = _p0 + 'L'
        stB(0); stT1(0); stT2(0)
        for k in range(NI):
            if k + 1 < NI:
                stB(k + 1); stT1(k + 1)
            stX(k)
            if k + 1 < NI:
                stT2(k + 1)
            stS(k)
            stP(k)
        if not is_s:
            cB_ = tcap[:, 0, :].unsqueeze(2).broadcast_to([128, 32, 4]); sB_ = tcap[:, 1, :].unsqueeze(2).broadcast_to([128, 32, 4])
            RW = [b_wcap]
            TT("dve", wtmp[:, 0, :, :], wcap[:, 0, :, :], cB_, ALU.mult, RW, RW)
            TT("dve", wtmp[:, 1, :, :], wcap[:, 1, :, :], sB_, ALU.mult, RW, RW)
            TT("dve", wtmp[:, 2, :, :], wcap[:, 0, :, :], sB_, ALU.mult, RW, RW)
            TT("dve", wtmp[:, 3, :, :], wcap[:, 1, :, :], cB_, ALU.mult, RW, RW)
            f4 = finS.rearrange("p (s d r c) -> p s d r c", s=4, d=2, r=2)
            for d in range(2):
                src = lambda k, d=d: wtmp[:, k, d * 16:(d + 1) * 16, :].rearrange("p c s -> p s c")
                TT("dve", f4[:, :, d, 0, :], src(0), src(1), ALU.subtract if d == 0 else ALU.add, RW, [b_finS])
                TT("dve", f4[:, :, d, 1, :], src(3), src(2), ALU.add if d == 0 else ALU.subtract, RW, [b_finS])
            for half in range(2):
                ps, bp = rr.get()
                transpose_to(ps[:, 0:128], finS[:, half * 128:(half + 1) * 128], [b_finS], [bp], dt=F32)
                CP("act", fT[:], ps[:, 0:128], [bp], [b_fT])
                for s2 in range(2):
                    STO(o_s5[half * 2 + s2, l], fT[s2 * 64:(s2 + 1) * 64, :], b_fT)
        Sched.PHASE = _p0 + 'G'
        for blk in range(2):
            w, bw = wload(w_glu[l][:, blk * 256:(blk + 1) * 256], 4, 256)
            w2, bw2 = wload(w_glu[l][:, (blk + 2) * 256:(blk + 3) * 256], 4, 256)
            for c2 in range(2):
                cc = blk * 2 + c2
                for h in range(2):
                    hs = slice(h * 512, (h + 1) * 512)
                    psg, bpg = rr.get()
                    for k in range(4):
                        MM(psg[:, :], w2[:, k, c2 * 128:(c2 + 1) * 128], y5T[:, k, hs], k == 0, k == 3, [b_y5, bw2], [bpg])
                    ACT(sg[:], psg[:, :], AF.Sigmoid, [bpg, b_d5], [b_sg], bias=bg[:, 4 + cc:5 + cc])
                    psv, bpv = rr.get()
                    for k in range(4):
                        MM(psv[:, :], w[:, k, c2 * 128:(c2 + 1) * 128], y5T[:, k, hs], k == 0, k == 3, [b_y5, bw], [bpv])
                    yt_, by_ = yT(1, cc)
                    STT(yt_[:, hs], psv[:, :], bg[:, cc:cc + 1], sg[:], ALU.add, ALU.mult, [bpv, b_d5, b_sg], [by_])
        S.op("pool", lambda e: e.memset(prow[:, 0:1], 0.0), [], ([b_uT, b_y5, b_prow, b_pc, b_pci, b_Bst, b_Cn, b_Bc, b_Bt, b_CL, b_tab, b_d5, b_finS, b_s0c, b_sg, b_fT]
             + b_Bx + b_BcL + b_Cx + b_xr + b_pr + (bY if not is_s else []) + [b_iota, b_rmF, b_rmB, b_wcap]
             + [b_Bc2[1], b_Bt2[1], b_CL2[1], b_tab2[1]] + b_Bx2[1] + b_BcL2[1] + b_Cx2[1]) + [b_scr_all])

    def rms_groups(ps, bp, ncols, gain_bc, b_gain, qf, b_qf, sq, b_sq, rs, b_rs, t, rope):
        ng = ncols // 64
        CP("act", qf[:, 0:ncols], ps[:, 0:ncols], [bp], [b_qf])
        TT("dve", sq[:, 0:ncols], qf[:, 0:ncols], qf[:, 0:ncols], ALU.mult, [b_qf], [b_sq])
        S.op("dve", lambda e: e.tensor_reduce(out=rs[:, 0:ng], in_=sq[:, 0:ncols].rearrange("p (g x) -> p g x", g=ng), op=ALU.add, axis=AX.X), [b_sq], [b_rs])
        rstd_from_ss(rs[:, 0:ng], 64, rs[:, 0:ng], [b_rs], [b_rs])
        q3 = qf[:, 0:ncols].rearrange("p (g x) -> p g x", g=ng)
        TT("dve", q3, q3, rs[:, 0:ng].unsqueeze(2).broadcast_to([128, ng, 64]), ALU.mult, [b_qf, b_rs], [b_qf])
        TT("dve", q3, q3, gain_bc.unsqueeze(1).broadcast_to([128, ng, 64]), ALU.mult, [b_qf, b_gain], [b_qf])
        if rope:
            s3 = sq[:, 0:ncols].rearrange("p (g a q f) -> p (g a) q f", g=ng, a=2, q=2)
            x4 = qf[:, 0:ncols].rearrange("p (g a q f) -> p (g a) q f", g=ng, a=2, q=2)
            S4 = ropeS[:, t, :].rearrange("p (a q f) -> p a q f", a=2, q=2)
            for pz in range(2):
                TT("dve", s3[:, :, pz, :].rearrange("p (g a) f -> p g a f", g=ng), x4[:, :, 1 - pz, :].rearrange("p (g a) f -> p g a f", g=ng),
                   S4[:, :, pz, :].unsqueeze(1).broadcast_to([128, ng, 2, 16]), ALU.mult, [b_qf, b_ropeS], [b_sq])
            TT("dve", q3, q3, ropeC[:, t, :].unsqueeze(1).broadcast_to([128, ng, 64]), ALU.mult, [b_qf, b_ropeC], [b_qf])
            TT("dve", qf[:, 0:ncols], qf[:, 0:ncols], sq[:, 0:ncols], ALU.add, [b_qf, b_sq], [b_qf])

    def branch_diff(l, path, nseq, L, nt, is_s):
        barrier_begin()
        _p0 = Sched.PHASE
        cv = Carve()
        nk_ctx = 2 if is_s else 0
        NKT = 8 + nk_ctx
        qT = cv.take([4, T], BF16); b_qT = NB("qT")
        kT = cv.take([4, NKT * 128], BF16); b_kT = NB("kT")
        vaug = cv.take([NKT, 4, 130], BF16); b_va = NB("vaug")
        qfL = [cv.take([512]) for _ in range(2)]; b_qfL = [NB(f"qf{i}") for i in range(2)]
        sqL = [cv.take([512]) for _ in range(2)]; b_sqL = [NB(f"sq{i}") for i in range(2)]
        rsL = [cv.take([16]) for _ in range(2)]; b_rsL = [NB(f"rs{i}") for i in range(2)]
        rot_ = [0]

        def nxt():
            i = rot_[0] % 2; rot_[0] += 1
            return qfL[i], b_qfL[i], sqL[i], b_sqL[i], rsL[i], b_rsL[i]
        qb = [cv.take([512], BF16), cv.take([512], BF16)]; b_qb = [NB("qb0"), NB("qb1")]
        gq = cv.take([64]); gk = cv.take([64]); b_g = NB("dg")
        lamt = cv.take([4, 64]); lamc = cv.take([8]); b_lam = NB("lam")
        o1 = cv.take([4, 128]); b_o1 = NB("o1")
        odn = cv.take([8, 512], BF16); b_odn = NB("odn")
        pT = [cv.take([512], BF16) for _ in range(4)]; b_pT = [NB(f"pT{i}") for i in range(4)]
        rd = cv.take([8]); b_rd = NB("rd")
        oh = cv.take([128]); b_oh = NB("oh")
        gsub = cv.take([1]); b_gsub = NB("gsub")
        kstL = [cv.take([512]) for _ in range(2)]; b_kstL = [NB(f"kst{i}") for i in range(2)]
        kst, b_kst = kstL[0], b_kstL[0]
        lam_init = 0.8 - 0.6 * math.exp(-0.3 * l)
        S.op("pool", lambda e: e.memset(gq[:], 0.0), [b_scr_all], [b_g, b_scr_all])
        LD(gq[:], dqg[l:l + 1, :].partition_broadcast(128), b_g); LD(gk[:], dkg[l:l + 1, :].partition_broadcast(128), b_g, group=True)
        TS("dve", gq[:], gq[:], 0.125, ALU.mult, [b_g], [b_g])
        LD(lamt[:].rearrange("p a b -> p (a b)"), dlam[l:l + 1, :].partition_broadcast(128), b_lam)
        LD(gsub[:], dsubc[l], b_gsub)
        TS("dve", gsub[:], gsub[:], 1.0 - lam_init, ALU.mult, [b_gsub], [b_gsub])
        TT("dve", lamt[:, 0, :], lamt[:, 0, :], lamt[:, 1, :], ALU.mult, [b_lam], [b_lam])
        TT("dve", lamt[:, 2, :], lamt[:, 2, :], lamt[:, 3, :], ALU.mult, [b_lam], [b_lam])
        S.op("dve", lambda e: e.tensor_reduce(out=lamc[:, 0:1], in_=lamt[:, 0, :], op=ALU.add, axis=AX.X), [b_lam], [b_lam])
        S.op("dve", lambda e: e.tensor_reduce(out=lamc[:, 1:2], in_=lamt[:, 2, :], op=ALU.add, axis=AX.X), [b_lam], [b_lam])
        ACT(lamc[:, 0:2], lamc[:, 0:2], AF.Exp, [b_lam], [b_lam])
        TT("dve", lamc[:, 2:3], lamc[:, 0:1], lamc[:, 1:2], ALU.subtract, [b_lam], [b_lam])
        TS("dve", lamc[:, 2:3], lamc[:, 2:3], lam_init, ALU.add, [b_lam], [b_lam], s2=-1.0, op1=ALU.mult)
        MSET("pool", vaug[:].rearrange("p a b c -> p (a b c)"), 1.0, [b_va])
        if sub == 1:
            raise _Stop()
        Sched.PHASE = _p0 + 'A'
        stagesA = []
        for which in range(2):
            col0 = C_DQ if which == 0 else C_DK
            wd = {}
            for t in range(8):
                st = {}

                def FA(st=st, which=which, t=t, wd=wd, col0=col0):
                    if t == 0:
                        wd["A"] = wload(w_in[l][:, col0:col0 + 256], 8, 256)
                        wd["B"] = wload(w_in[l][:, col0 + 256:col0 + 512], 8, 256)
                    wA, bwA = wd["A"]; wB, bwB = wd["B"]
                    ps, bp = rr.get()
                    for k in range(8):
                        MM(ps[:, 0:256], hT[:, k, t * 128:(t + 1) * 128], wA[:, k, :], k == 0, k == 7, [b_hT, bwA], [bp])
                    for k in range(8):
                        MM(ps[:, 256:512], hT[:, k, t * 128:(t + 1) * 128], wB[:, k, :], k == 0, k == 7, [b_hT, bwB], [bp])
                    qf, b_qf, sq, b_sq, rs, b_rs = nxt()
                    kst, b_kst = kstL[t % 2], b_kstL[t % 2]
                    if which == 1 and not is_s:
                        rms_groups(ps, bp, 512, gk[:], b_g, kst, b_kst, sq, b_sq, rs, b_rs, t, False)
                        STO(o_dk[t // 2, l, (t % 2) * 128:(t % 2 + 1) * 128, :], kst[:], b_kst)
                        st["src"] = (kst, b_kst)
                    else:
                        rms_groups(ps, bp, 512, (gq if which == 0 else gk)[:], b_g, qf, b_qf, sq, b_sq, rs, b_rs, t, is_s)
                        st["src"] = (qf, b_qf)

                def FB(st=st, which=which, t=t):
                    src, bsrc = st["src"]
                    qb_, bqb_ = qb[t % 2], b_qb[t % 2]
                    CP("pool", qb_[:], src[:], [bsrc], [bqb_])
                    for j4 in range(4):
                        ps2, bp2 = rr.get()
                        pv = ps2[:].bitcast(BF16)[:, 0:128]
                        transpose_to(pv, qb_[:, j4 * 128:(j4 + 1) * 128], [bqb_], [bp2])
                        if which == 0:
                            CP("act", qT[:, j4, t * 128:(t + 1) * 128], pv, [bp2], [b_qT])
                        else:
                            CP("act", kT[:, j4, (nk_ctx + t) * 128:(nk_ctx + t + 1) * 128], pv, [bp2], [b_kT])
                stagesA.append((FA, FB))
        stagesA[0][0]()
        for k in range(len(stagesA)):
            if k + 1 < len(stagesA):
                stagesA[k + 1][0]()
            stagesA[k][1]()
        if is_s:
            for kt in range(2):
                LD(kst[:], cdk[l, kt * 128:(kt + 1) * 128, :], b_kst)
                CP("pool", qb[0][:], kst[:], [b_kst], [b_qb[0]])
                for j4 in range(4):
                    ps2, bp2 = rr.get()
                    pv = ps2[:].bitcast(BF16)[:, 0:128]
                    transpose_to(pv, qb[0][:, j4 * 128:(j4 + 1) * 128], [b_qb[0]], [bp2])
                    CP("act", kT[:, j4, kt * 128:(kt + 1) * 128], pv, [bp2], [b_kT])
                LD(kst[:], cdv[l, kt * 128:(kt + 1) * 128, :], b_kst)
                CP("pool", vaug[:, kt, :, 0:128], kst[:].rearrange("p (h e) -> p h e", h=4), [b_kst], [b_va])
        Sched.PHASE = _p0 + 'B'
        wA, bwA = wload(w_in[l][:, C_DV:C_DV + 256], 8, 256)
        wB, bwB = wload(w_in[l][:, C_DV + 256:C_DV + 512], 8, 256)
        for t in range(8):
            ps, bp = rr.get()
            for k in range(8):
                MM(ps[:, 0:256], hT[:, k, t * 128:(t + 1) * 128], wA[:, k, :], k == 0, k == 7, [b_hT, bwA], [bp])
            for k in range(8):
                MM(ps[:, 256:512], hT[:, k, t * 128:(t + 1) * 128], wB[:, k, :], k == 0, k == 7, [b_hT, bwB], [bp])
            if sub != 31:
                kst, b_kst = kstL[t % 2], b_kstL[t % 2]
            CP("act", vaug[:, nk_ctx + t, :, 0:128], ps[:, :].rearrange("p (h e) -> p h e", h=4), [bp], [b_va])
            if not is_s and sub != 32:
                CP("dve", kst[:], ps[:, :], [bp], [b_kst])
                STO(o_dv[t // 2, l, (t % 2) * 128:(t % 2 + 1) * 128, :], kst[:], b_kst)
        if sub in (3, 31, 32):
            raise _Stop()
        Sched.PHASE = _p0 + 'C'
        obk = [(psum[4 + i], psb[4 + i]) for i in range(4)]
        pc_ = [0]
        stages = []
        for s in range(nseq):
            keyt = list(range(nk_ctx)) + [nk_ctx + s * nt + i for i in range(nt)]
            nq = min(L, 512)
            for qc in range(L // nq):
                q0 = s * L + qc * nq
                nqt = nq // 128
                for h in range(4):
                    for c in range(2):
                        ksl = slice(c * 64, (c + 1) * 64)
                        for ki, kt in enumerate(keyt):
                            st = {}

                            def A(st=st, ksl=ksl, h=h, kt=kt, q0=q0, nq=nq):
                                psS, bpS = rr.get()
                                MM(psS[:, 0:nq], kT[ksl, h, kt * 128:(kt + 1) * 128], qT[ksl, h, q0:q0 + nq], True, True, [b_kT, b_qT], [bpS])
                                z = pc_[0] % 4; pc_[0] += 1
                                st["z"] = z
                                ACT(pT[z][:, 0:nq], psS[:, 0:nq], AF.Exp, [bpS], [b_pT[z]])

                            def B(st=st, h=h, c=c, kt=kt, ki=ki, nk=len(keyt), nqt=nqt, q0=q0):
                                z = st["z"]
                                for qt in range(nqt):
                                    MM(obk[qt][0][:, 0:129], pT[z][:, qt * 128:(qt + 1) * 128], vaug[:, kt, h, 0:129], ki == 0, ki == nk - 1, [b_pT[z], b_va], [obk[qt][1]])
                                if ki != nk - 1:
                                    return
                                for qt in range(nqt):
                                    tq = q0 // 128 + qt
                                    ob, bob = obk[qt]
                                    S.op("dve", lambda e, ob=ob, qt=qt, c=c: e.reciprocal(out=rd[:, qt * 2 + c:qt * 2 + c + 1], in_=ob[:, 128:129]), [bob], [b_rd])
                                    if c == 0:
                                        TS("dve", o1[:, qt, :], ob[:, 0:128], rd[:, qt * 2:qt * 2 + 1], ALU.mult, [bob, b_rd], [b_o1])
                                    else:
                                        TT("dve", rd[:, qt * 2 + 1:qt * 2 + 2], rd[:, qt * 2 + 1:qt * 2 + 2], lamc[:, 2:3], ALU.mult, [b_rd, b_lam], [b_rd])
                                        STT(oh[:], ob[:, 0:128], rd[:, qt * 2 + 1:qt * 2 + 2], o1[:, qt, :], ALU.mult, ALU.add, [bob, b_rd, b_o1], [b_oh])
                                        ACT(junk[:, 0:128], oh[:], AF.Square, [b_oh], [b_junk, b_small], accum=small[:, 40:41])
                                        rstd_from_ss(small[:, 40:41], 128, small[:, 41:42], [b_small], [b_small])
                                        TS("dve", odn[:, tq, h * 128:(h + 1) * 128], oh[:], small[:, 41:42], ALU.mult, [b_oh, b_small], [b_odn])
                            stages.append((A, B))
        LA = 3
        for k in range(min(LA, len(stages))):
            stages[k][0]()
        for k in range(len(stages)):
            if k + LA < len(stages):
                stages[k + LA][0]()
            stages[k][1]()
        if sub == 4:
            raise _Stop()
        Sched.PHASE = _p0 + 'D'
        for t in range(8):
            for h in range(4):
                ps2, bp2 = rr.get()
                pv = ps2[:].bitcast(BF16)[:, 0:128]
                transpose_to(pv, odn[:, t, h * 128:(h + 1) * 128], [b_odn], [bp2])
                yt_, by_ = yT(2, h)
                ACT(yt_[:, t * 128:(t + 1) * 128], pv, AF.Copy, [bp2, b_gsub], [by_], scale=gsub[:, 0:1])
        S.op("pool", lambda e: e.memset(gq[:, 0:1], 0.0), [], ([b_qT, b_kT, b_va, b_g, b_lam, b_o1, b_odn, b_rd, b_oh, b_gsub] + b_kstL + b_qb + b_pT + b_qfL + b_sqL + b_rsL) + [b_scr_all])

    def branch_win(l, path, nseq, L, nt, is_s):
        barrier_begin()
        cv = Carve()
        nk_ctx = 2 if is_s else 0
        NKT = 8 + nk_ctx
        qT = cv.take([4, T], BF16); b_qT = NB("wqT")
        kT = cv.take([2, NKT * 128], BF16); b_kT = NB("wkT")
        vaug = cv.take([NKT, 2, 66], BF16); b_va = NB("wvaug")
        qfL = [cv.take([512]) for _ in range(2)]; b_qfL = [NB(f"wqf{i}") for i in range(2)]
        sqL = [cv.take([512]) for _ in range(2)]; b_sqL = [NB(f"wsq{i}") for i in range(2)]
        rsL = [cv.take([16]) for _ in range(2)]; b_rsL = [NB(f"wrs{i}") for i in range(2)]
        rot_ = [0]

        def nxt():
            i = rot_[0] % 2; rot_[0] += 1
            return qfL[i], b_qfL[i], sqL[i], b_sqL[i], rsL[i], b_rsL[i]
        qb = [cv.take([512], BF16), cv.take([512], BF16)]; b_qb = [NB("wqb0"), NB("wqb1")]
        gq = cv.take([64]); gk = cv.take([64]); b_g = NB("wg")
        snk = cv.take([8]); b_snk = NB("snk")
        on = cv.take([8, 512], BF16); b_on = NB("won")
        pT = [cv.take([512], BF16) for _ in range(4)]; b_pT = [NB(f"wpT{i}") for i in range(4)]
        rd = cv.take([8]); b_rd = NB("wrd")
        kst = cv.take([256]); b_kst = NB("wkst")
        S.op("pool", lambda e: e.memset(gq[:], 0.0), [b_scr_all], [b_g, b_scr_all])
        LD(gq[:], wqg[l:l + 1, :].partition_broadcast(128), b_g); LD(gk[:], wkg[l:l + 1, :].partition_broadcast(128), b_g, group=True)
        TS("dve", gq[:], gq[:], 0.125, ALU.mult, [b_g], [b_g])
        LD(snk[:], wsink[l:l + 1, :].partition_broadcast(128), b_snk)
        ACT(snk[:], snk[:], AF.Exp, [b_snk], [b_snk])
        MSET("pool", vaug[:].rearrange("p a b c -> p (a b c)"), 1.0, [b_va])
        wA, bwA = wload(w_in[l][:, C_WQ:C_WQ + 256], 8, 256)
        wB, bwB = wload(w_in[l][:, C_WQ + 256:C_WQ + 512], 8, 256)
        stq = []
        for t in range(8):
            st = {}

            def QA(st=st, t=t):
                ps, bp = rr.get()
                for k in range(8):
                    MM(ps[:, 0:256], hT[:, k, t * 128:(t + 1) * 128], wA[:, k, :], k == 0, k == 7, [b_hT, bwA], [bp])
                for k in range(8):
                    MM(ps[:, 256:512], hT[:, k, t * 128:(t + 1) * 128], wB[:, k, :], k == 0, k == 7, [b_hT, bwB], [bp])
                qf, b_qf, sq, b_sq, rs, b_rs = nxt()
                rms_groups(ps, bp, 512, gq[:], b_g, qf, b_qf, sq, b_sq, rs, b_rs, t, is_s)
                st["q"] = (qf, b_qf)

            def QB(st=st, t=t):
                qf, b_qf = st["q"]
                qb_, bqb_ = qb[t % 2], b_qb[t % 2]
                CP("pool", qb_[:], qf[:], [b_qf], [bqb_])
                for j4 in range(4):
                    ps2, bp2 = rr.get()
                    pv = ps2[:].bitcast(BF16)[:, 0:128]
                    transpose_to(pv, qb_[:, j4 * 128:(j4 + 1) * 128], [bqb_], [bp2])
                    CP("act", qT[:, j4, t * 128:(t + 1) * 128], pv, [bp2], [b_qT])
            stq.append((QA, QB))
        stq[0][0]()
        for k in range(8):
            if k + 1 < 8:
                stq[k + 1][0]()
            stq[k][1]()
        wK, bwK = wload(w_in[l][:, C_WK:C_WK + 256], 8, 256)

        def put_k(src_f32, bsrc, ktile):
            for n in range(2):
                CP("dve", qb[n][:, 0:128].rearrange("p (r d) -> p r d", r=2), src_f32[:, n * 64:(n + 1) * 64].unsqueeze(1).broadcast_to([128, 2, 64]), [bsrc], [b_qb[n]])
                ps2, bp2 = rr.get()
                pv = ps2[:].bitcast(BF16)[:, 0:128]
                transpose_to(pv, qb[n][:, 0:128], [b_qb[n]], [bp2])
                CP("act", kT[:, n, ktile * 128:(ktile + 1) * 128], pv, [bp2], [b_kT])
        for t in range(8):
            ps, bp = rr.get()
            for k in range(8):
                MM(ps[:, 0:256], hT[:, k, t * 128:(t + 1) * 128], wK[:, k, :], k == 0, k == 7, [b_hT, bwK], [bp])
            CP("act", vaug[:, nk_ctx + t, :, 0:64], ps[:, 128:256].rearrange("p (n e) -> p n e", n=2), [bp], [b_va])
            qf, b_qf, sq, b_sq, rs, b_rs = nxt()
            if not is_s:
                CP("dve", kst[:, 128:256], ps[:, 128:256], [bp], [b_kst])
                STO(o_wv[t // 2, l, (t % 2) * 128:(t % 2 + 1) * 128, :], kst[:, 128:256], b_kst)
                rms_groups(ps, bp, 128, gk[:], b_g, kst, b_kst, sq, b_sq, rs, b_rs, t, False)
                STO(o_wk[t // 2, l, (t % 2) * 128:(t % 2 + 1) * 128, :], kst[:, 0:128], b_kst)
                put_k(kst, b_kst, nk_ctx + t)
            else:
                rms_groups(ps, bp, 128, gk[:], b_g, qf, b_qf, sq, b_sq, rs, b_rs, t, True)
                put_k(qf, b_qf, nk_ctx + t)
        if is_s:
            for kt in range(2):
                LD(kst[:, 0:128], cwk[l, kt * 128:(kt + 1) * 128, :], b_kst)
                put_k(kst, b_kst, kt)
                LD(kst[:, 128:256], cwv[l, kt * 128:(kt + 1) * 128, :], b_kst)
                CP("pool", vaug[:, kt, :, 0:64], kst[:, 128:256].rearrange("p (n e) -> p n e", n=2), [b_kst], [b_va])
        obk = [(psum[4 + i], psb[4 + i]) for i in range(4)]
        pc_ = [0]
        stages = []

        def evac(ob, bob, qt, tq, h):
            TT("dve", rd[:, qt:qt + 1], ob[:, 64:65], snk[:, h:h + 1], ALU.add, [bob, b_snk], [b_rd])
            S.op("dve", lambda e, qt=qt: e.reciprocal(out=rd[:, qt:qt + 1], in_=rd[:, qt:qt + 1]), [b_rd], [b_rd])
            TS("dve", on[:, tq, h * 64:(h + 1) * 64], ob[:, 0:64], rd[:, qt:qt + 1], ALU.mult, [bob, b_rd], [b_on])
        for h in range(8):
            n = h // 4
            j4 = h // 2
            bsl = slice((h % 2) * 64, (h % 2 + 1) * 64)
            if not is_s:
                for s in range(nseq):
                    q0 = s * L
                    keyt = [s * nt + i for i in range(nt)]
                    for ki, kt in enumerate(keyt):
                        st = {}

                        def A(st=st, bsl=bsl, n=n, j4=j4, kt=kt, q0=q0):
                            psS, bpS = rr.get()
                            MM(psS[:, 0:L], kT[bsl, n, kt * 128:(kt + 1) * 128], qT[bsl, j4, q0:q0 + L], True, True, [b_kT, b_qT], [bpS])
                            z = pc_[0] % 4; pc_[0] += 1
                            st["z"] = z
                            ACT(pT[z][:, 0:L], psS[:, 0:L], AF.Exp, [bpS], [b_pT[z]])

                        def B(st=st, n=n, kt=kt, ki=ki, nk=len(keyt), s=s, h=h):
                            z = st["z"]
                            for qt in range(nt):
                                MM(obk[qt][0][:, 0:65], pT[z][:, qt * 128:(qt + 1) * 128], vaug[:, kt, n, 0:65], ki == 0, ki == nk - 1, [b_pT[z], b_va], [obk[qt][1]])
                            if ki == nk - 1:
                                for qt in range(nt):
                                    evac(obk[qt][0], obk[qt][1], qt, s * nt + qt, h)
                        stages.append((A, B))
            else:
                for tq in range(8):
                    qt = tq % 4
                    keys = [(0, None), (1, None)]
                    if tq > 0:
                        keys.append((nk_ctx + tq - 1, "prev"))
                    keys.append((nk_ctx + tq, None))
                    if tq < 7:
                        keys.append((nk_ctx + tq + 1, "next"))
                    for ki, (kt, msk) in enumerate(keys):
                        st = {}

                        def A(st=st, bsl=bsl, n=n, j4=j4, kt=kt, tq=tq, msk=msk):
                            psS, bpS = rr.get()
                            MM(psS[:, 0:128], kT[bsl, n, kt * 128:(kt + 1) * 128], qT[bsl, j4, tq * 128:(tq + 1) * 128], True, True, [b_kT, b_qT], [bpS])
                            z = pc_[0] % 4; pc_[0] += 1
                            st["z"] = z
                            ACT(pT[z][:, 0:128], psS[:, 0:128], AF.Exp, [bpS], [b_pT[z]])
                            if msk is not None:
                                TT("dve", pT[z][:, 0:128], pT[z][:, 0:128], (tril if msk == "prev" else triu)[:], ALU.mult, [b_pT[z], b_tril, b_triu], [b_pT[z]])

                        def B(st=st, n=n, kt=kt, ki=ki, nk=len(keys), qt=qt, tq=tq, h=h):
                            z = st["z"]
                            ob, bob = obk[qt]
                            MM(ob[:, 0:65], pT[z][:, 0:128], vaug[:, kt, n, 0:65], ki == 0, ki == nk - 1, [b_pT[z], b_va], [bob])
                            if ki == nk - 1:
                                evac(ob, bob, qt, tq, h)
                        stages.append((A, B))
        LA = 3
        for k in range(min(LA, len(stages))):
            stages[k][0]()
        for k in range(len(stages)):
            if k + LA < len(stages):
                stages[k + LA][0]()
            stages[k][1]()
        for t in range(8):
            for j4 in range(4):
                ps2, bp2 = rr.get()
                pv = ps2[:].bitcast(BF16)[:, 0:128]
                transpose_to(pv, on[:, t, j4 * 128:(j4 + 1) * 128], [b_on], [bp2])
                yt_, by_ = yT(3, j4)
                CP("act", yt_[:, t * 128:(t + 1) * 128], pv, [bp2], [by_])
        S.op("pool", lambda e: e.memset(gq[:, 0:1], 0.0), [], ([b_qT, b_kT, b_va, b_g, b_snk, b_on, b_rd, b_kst] + b_qb + b_pT + b_qfL + b_sqL + b_rsL) + [b_scr_all])

    def merge(l):
        rr.set(range(8))
        barrier_begin()
        cv = Carve()
        mT = cv.take([8, T], BF16); b_mT = [NB(f"mT{c}") for c in range(8)]
        gs = [cv.take([512]) for _ in range(4)]; b_gs = [NB(f"gs{i}") for i in range(4)]
        tmp = [cv.take([512]) for _ in range(4)]; b_tmp = [NB(f"mt{i}") for i in range(4)]
        bgt = cv.take([32]); b_bgt = NB("bgt")
        S.op("pool", lambda e: e.memset(bgt[:], 0.0), [b_scr_all], [b_bgt, b_scr_all])
        LD(bgt[:], bgatec[l], b_bgt)
        kq = [0]
        for dcp in range(4):
            for br in range(4):
                wg, bwg = wload(w_gate[l][:, br * 1024 + dcp * 256: br * 1024 + (dcp + 1) * 256], 8, 256)
                wb_, bwb_ = wload(w_br[l][br * 512:(br + 1) * 512, dcp * 256:(dcp + 1) * 256], 4, 256)
                for c2 in range(2):
                    dc = dcp * 2 + c2
                    for h in range(2):
                        hs = slice(h * 512, (h + 1) * 512)
                        ti_ = c2 * 2 + h
                        psg, bpg = rr.get()
                        for k in range(8):
                            MM(psg[:, :], wg[:, k, c2 * 128:(c2 + 1) * 128], hT[:, k, hs], k == 0, k == 7, [b_hT, bwg], [bpg])
                        z = kq[0] % 4; kq[0] += 1
                        ACT(gs[z][:], psg[:, :], AF.Sigmoid, [bpg, b_bgt], [b_gs[z]], bias=bgt[:, br * 8 + dc:br * 8 + dc + 1])
                        psb_, bpb_ = rr.get()
                        for k in range(4):
                            yt_, by_ = yT(br, k)
                            MM(psb_[:, :], wb_[:, k, c2 * 128:(c2 + 1) * 128], yt_[:, hs], k == 0, k == 3, [by_, bwb_], [bpb_])
                        if br == 0:
                            TT("dve", tmp[ti_][:], psb_[:, :], gs[z][:], ALU.mult, [bpb_, b_gs[z]], [b_tmp[ti_]])
                        else:
                            TT("dve", gs[z][:], psb_[:, :], gs[z][:], ALU.mult, [bpb_, b_gs[z]], [b_gs[z]])
                            if br < 3:
                                TT("pool", tmp[ti_][:], tmp[ti_][:], gs[z][:], ALU.add, [b_tmp[ti_], b_gs[z]], [b_tmp[ti_]])
                            else:
                                TT("pool", mT[:, dc, hs], tmp[ti_][:], gs[z][:], ALU.add, [b_tmp[ti_], b_gs[z]], [b_mT[dc]])
        if sub == 54:
            raise _Stop()
        for cb4 in range(4):
            w, bw = wload(w_out[l][:, cb4 * 256:(cb4 + 1) * 256], 8, 256)
            for t in range(8):
                ps, bp = rr.get()
                for k in range(8):
                    MM(ps[:, 0:256], mT[:, k, t * 128:(t + 1) * 128], w[:, k, :], k == 0, k == 7, [b_mT[k], bw], [bp])
                cs = slice(cb4 * 256, (cb4 + 1) * 256)
                zz = kq[0] % 4; kq[0] += 1
                TT("dve", gs[zz][:, 0:256], ps[:, 0:256], gbc[:, 0, cs], ALU.mult, [bp, b_gbc[0]], [b_gs[zz]])
                TT("pool", xres[:, t, cs], xres[:, t, cs], gs[zz][:, 0:256], ALU.add, [b_xres[t], b_gs[zz]], [b_xres[t]])
        S.op("pool", lambda e: e.memset(bgt[:, 0:1], 0.0), [], (b_mT + b_gs + b_tmp + [b_bgt]) + [b_scr_all])
        rr.set(range(4))

    def mlp(l):
        barrier_begin()
        cvm = Carve()
        rl = [cvm.take([512]) for _ in range(4)]; b_rl = [NB(f"rl{i}") for i in range(4)]
        rs_ = [cvm.take([256]) for _ in range(4)]; b_rs_ = [NB(f"rsd{i}") for i in range(4)]
        mk = [0, 0]
        for h in range(2):
            hs = slice(h * 512, (h + 1) * 512)

            def aTv(kc):
                return big[:, kc // 2, (kc % 2) * 512:(kc % 2 + 1) * 512], b_big[kc // 2]
            for blk in range(16):
                w, bw = wload(w_fc1[l][:, blk * 256:(blk + 1) * 256], 8, 256)
                for c2 in range(2):
                    kc = blk * 2 + c2
                    ps, bp = rr.get()
                    for k in range(8):
                        MM(ps[:, :], w[:, k, c2 * 128:(c2 + 1) * 128], hT[:, k, hs], k == 0, k == 7, [b_hT, bw], [bp])
                    a_, ba_ = aTv(kc)
                    zr = mk[0] % 4; mk[0] += 1
                    ACT(rl[zr][:], ps[:, :], AF.Relu, [bp], [b_rl[zr]])
                    TT("dve", a_, rl[zr][:], rl[zr][:], ALU.mult, [b_rl[zr]], [ba_])
            for cb4 in range(4):
                cs = slice(cb4 * 256, (cb4 + 1) * 256)
                accb = [(psum[4 + i], psb[4 + i]) for i in range(4)]
                for kg in range(4):
                    w, bw = wload(w_fc2[l][kg * 1024:(kg + 1) * 1024, cs], 8, 256)
                    for tt in range(4):
                        for k in range(8):
                            kc = kg * 8 + k
                            a_, ba_ = aTv(kc)
                            MM(accb[tt][0][:, 0:256], a_[:, tt * 128:(tt + 1) * 128], w[:, k, :], kc == 0, kc == 31, [ba_, bw], [accb[tt][1]])
                for tt in range(4):
                    t = h * 4 + tt
                    zq = mk[1] % 4; mk[1] += 1
                    TT("dve", rs_[zq][:], accb[tt][0][:, 0:256], gbc[:, 1, cs], ALU.mult, [accb[tt][1], b_gbc[1]], [b_rs_[zq]])
                    TT("pool", xres[:, t, cs], xres[:, t, cs], rs_[zq][:], ALU.add, [b_xres[t], b_rs_[zq]], [b_xres[t]])

        S.op("pool", lambda e: e.memset(small[:, 62:63], 0.0), [], b_rl + b_rs_ + [b_scr_all])

    try:
        Sched.PHASE = "prologue"
        adaln_weights(0)
        adaln_weights(1)
        run_pass(0)
        run_pass(1)
    except _Stop:
        pass
    if stop is not None:
        d_hT = nc.dram_tensor("dbg_hT", [128, 8, T], BF16, kind="ExternalOutput").ap()
        d_big = nc.dram_tensor("dbg_big", [128, 16, 1024], BF16, kind="ExternalOutput").ap()
        d_x = nc.dram_tensor("dbg_x", [128, 8, D], F32, kind="ExternalOutput").ap()
        d_modc = nc.dram_tensor("dbg_modc", [128, 48], F32, kind="ExternalOutput").ap()
        S.dma("sp", d_hT[:, :, :], hT[:], reads=[b_hT])
        S.dma("sp", d_big[:, :, :], big[:], reads=b_big, sbuf=b_big[0])
        S.dma("sp", d_x[:, :, :], xres[:], reads=b_xres, sbuf=b_xres[0])
        S.dma("sp", d_modc[:, :], modc[:], reads=[b_modc])
    with nc.Block() as block:
        S.emit(block)
    es.close()
    nc._phases = {e: [o.phase for o in S.ops[e]] for e in ENGS}
    return nc


_NC_CACHE = {}


def _consts():
    bf = ml_dtypes.bfloat16
    c = {}
    c["c_identb"] = np.eye(128, dtype=np.float32).astype(bf)
    c["c_identf"] = np.eye(128, dtype=np.float32)
    c["c_ones"] = np.ones((128, 128), np.float32)
    k = np.arange(128)[:, None]; t = np.arange(128)[None, :]
    c["c_triu"] = (k <= t).astype(np.float32)
    c["c_tril"] = (k >= t).astype(np.float32)
    c["c_mnegF"] = np.where(k <= t, 0.0, -1e30).astype(np.float32)
    c["c_mnegB"] = np.where(k >= t, 0.0, -1e30).astype(np.float32)
    c["c_bprev"] = (t <= k).astype(np.float32)
    c["c_bnext"] = (k <= t).astype(np.float32)
    mB = np.zeros((128, 4, 128), np.float32)
    mC = np.zeros((128, 4, 128), np.float32)
    for q in range(4):
        for gl in range(2):
            g8 = 2 * q + gl
            mB[gl * 64:(gl + 1) * 64, q, g8 * 16:(g8 + 1) * 16] = 1.0
            mC[g8 * 16:(g8 + 1) * 16, q, gl * 64:(gl + 1) * 64] = 1.0
    c["c_maskB"] = mB; c["c_maskC"] = mC
    c["c_iota"] = np.broadcast_to(np.arange(1024, dtype=np.float32)[None, :], (128, 1024)).copy()
    Ls = 1024
    row = np.repeat(np.arange(Ls // 64), 64).astype(np.float32); col = np.tile(np.arange(64), Ls // 64).astype(np.float32)
    nf = 16
    inv = (10000.0 ** (-np.arange(nf, dtype=np.float32) / nf)).astype(np.float32)
    ang = np.concatenate([row[:, None] * inv, col[:, None] * inv], axis=-1).astype(np.float32)
    cs, sn = np.cos(ang).astype(np.float32), np.sin(ang).astype(np.float32)
    C64 = np.zeros((Ls, 2, 2, 16), np.float32); S64 = np.zeros((Ls, 2, 2, 16), np.float32)
    for a in range(2):
        for p in range(2):
            C64[:, a, p, :] = cs[:, a * 16:(a + 1) * 16]
            S64[:, a, p, :] = (-1.0 if p == 0 else 1.0) * sn[:, a * 16:(a + 1) * 16]
    c["c_ropeC"] = C64.reshape(8, 128, 64).transpose(1, 0, 2).copy()
    c["c_ropeS"] = S64.reshape(8, 128, 64).transpose(1, 0, 2).copy()
    lidx = np.arange(1024)
    c["c_rmF"] = np.broadcast_to((lidx % 256 != 0).astype(np.float32)[None, :], (128, 1024)).astype(bf)
    c["c_rmB"] = np.broadcast_to((lidx % 256 != 255).astype(np.float32)[None, :], (128, 1024)).astype(bf)
    return c


def _colmajor(v, nchunk):
    return np.ascontiguousarray(np.swapaxes(v.reshape(v.shape[:-1] + (nchunk, 128)), -1, -2))


def make_in_maps(inp):
    f = lambda a: np.ascontiguousarray(np.asarray(a, dtype=np.float32))
    I = {k: f(v) for k, v in inp.items()}
    shared = dict(_consts())
    shared.update({
        "w_mod": I["w_mod"], "w_in": I["w_in"], "w_gate": I["w_gate"], "w_out": I["w_out"], "w_fc1": I["w_fc1"], "w_fc2": I["w_fc2"],
        "w_glu": I["s5_w_glu"], "w_br": I["w_branch"].reshape(2, 2048, 1024),
        "bmodc": _colmajor(I["b_mod"], 48), "g1c": _colmajor(I["g_norm1"], 8), "g2c": _colmajor(I["g_norm2"], 8),
        "convw": np.ascontiguousarray(I["ssd_conv_w"].transpose(0, 2, 1).reshape(2, 6, 128, 7).transpose(0, 2, 1, 3)),
        "convb": _colmajor(I["ssd_conv_b"], 6),
        "dtb": I["ssd_dt_bias"].reshape(2, 16), "alog": I["ssd_a_log"].reshape(2, 16), "ssdd": I["ssd_d"], "normgc": _colmajor(I["ssd_norm_g"], 4),
        "lamre": I["s5_lam_re"].reshape(2, 32, 128), "lamim": I["s5_lam_im"].reshape(2, 32, 128),
        "lsx": np.ascontiguousarray(np.repeat(I["s5_log_step"].reshape(2, 2, 32, 1), 64, axis=-1).reshape(2, 32, 128)),
        "s5bre": I["s5_b_re"].reshape(2, 2, 2048, 16), "s5bim": I["s5_b_im"].reshape(2, 2, 2048, 16),
        "s5cre": I["s5_c_re"].reshape(2, 2, 512, 64), "s5cim": I["s5_c_im"].reshape(2, 2, 512, 64),
        "s5dc": _colmajor(I["s5_d"], 4), "bgluc": _colmajor(I["s5_b_glu"], 8),
        "dqg": I["diff_qn_g"], "dkg": I["diff_kn_g"], "dlam": I["diff_lambda"].reshape(2, 256), "dsubc": I["diff_subln_g"].reshape(2, 128, 1),
        "wqg": I["win_qn_g"], "wkg": I["win_kn_g"], "wsink": I["win_sink"], "bgatec": _colmajor(I["b_gate"], 32),
    })
    in_maps = []
    for i in range(8):
        b = i // 2
        cv = np.stack([I["c_ctx"], I["c"][b]], axis=0)
        m = dict(shared)
        m.update({
            "xp": I["x_prompt"][4 * i:4 * i + 4].reshape(1024, 1024), "xs": I["x_sample"][b],
            "cvT": np.ascontiguousarray(cv.reshape(2, 8, 128).transpose(2, 1, 0)),
            "st_ssd": I["state_ssd"][b], "st_s5": I["state_s5"][b].reshape(2, 64, 128),
            "cdk": I["cache_diff_k"][b].reshape(2, 256, 512), "cdv": I["cache_diff_v"][b].reshape(2, 256, 512),
            "cwk": I["cache_win_k"][b].reshape(2, 256, 128), "cwv": I["cache_win_v"][b].reshape(2, 256, 128),
        })
        in_maps.append({k: np.ascontiguousarray(v) for k, v in m.items()})
    return in_maps


def kernel(**inp):
    if "nc" not in _NC_CACHE:
        _NC_CACHE["nc"] = build_program()
    nc = _NC_CACHE["nc"]
    in_maps = make_in_maps(inp)
    res = run_bass_kernel_spmd(nc, in_maps, core_ids=list(range(8)))
    R = res.results
    yp = np.concatenate([R[i]["yp"].reshape(4, 256, 1024) for i in range(8)], axis=0)
    ys = np.stack([R[2 * b]["ys"] for b in range(4)], axis=0)
    ssd = np.concatenate([R[i]["o_ssd"] for i in range(8)], axis=0)
    s5 = np.concatenate([R[i]["o_s5"].reshape(4, 2, 2, 2, 32, 64) for i in range(8)], axis=0)
    dk = np.concatenate([R[i]["o_dk"].reshape(4, 2, 256, 4, 2, 64) for i in range(8)], axis=0)
    dv = np.concatenate([R[i]["o_dv"].reshape(4, 2, 256, 4, 128) for i in range(8)], axis=0)
    wk = np.concatenate([R[i]["o_wk"].reshape(4, 2, 256, 2, 64) for i in range(8)], axis=0)
    wv = np.concatenate([R[i]["o_wv"].reshape(4, 2, 256, 2, 64) for i in range(8)], axis=0)
    return tuple(np.ascontiguousarray(a.astype(np.float32)) for a in (yp, ys, ssd, s5, dk, dv, wk, wv))
```

```python
import math
import numpy as np
from contextlib import ExitStack
import ml_dtypes
import concourse.bass as bass
import concourse.mybir as mybir
from concourse.bass_utils import run_bass_kernel_spmd

F32 = mybir.dt.float32
BF16 = mybir.dt.bfloat16
I32 = mybir.dt.int32
ALU = mybir.AluOpType
AF = mybir.ActivationFunctionType
AX = mybir.AxisListType
ENGS = ("pe", "dve", "act", "pool", "sp")
EPS = 1e-6


class Buf:
    __slots__ = ("name", "last_w", "readers", "load_sem", "load_cnt", "store_sem", "store_cnt", "excl")

    def __init__(self, name, excl=False):
        self.name = name
        self.excl = excl
        self.last_w = None
        self.readers = []
        self.load_sem = None
        self.load_cnt = 0
        self.store_sem = None
        self.store_cnt = 0


class Op:
    __slots__ = ("eng", "fn", "deps", "signal", "semval", "is_dma", "dsem", "dval", "phase")

    def __init__(self, eng, fn):
        self.phase = Sched.PHASE
        self.eng = eng
        self.fn = fn
        self.deps = []
        self.signal = False
        self.semval = 0
        self.is_dma = False
        self.dsem = None
        self.dval = 0


class Sched:
    PHASE = ""

    def __init__(self, nc, es):
        self.nc = nc
        self.es = es
        self.ops = {e: [] for e in ENGS}
        self.sems = {e: es.enter_context(nc.semaphore("c_" + e)) for e in ENGS}
        self.store_bufs = []
        self.nsem = 5
        self.pool = {}

    def new_sem(self, name):
        self.nsem += 1
        return self.es.enter_context(self.nc.semaphore(f"{name}_{self.nsem}"))

    def _track(self, op, reads, writes, skip_waw=False):
        deps = op.deps
        for r in reads:
            if r.last_w is not None and r.last_w is not op:
                deps.append(r.last_w)
            if r.excl:
                deps.extend(x for x in r.readers if x is not op and x.eng != op.eng)
            r.readers.append(op)
        for w in writes:
            if w.last_w is not None and w.last_w is not op and not skip_waw:
                deps.append(w.last_w)
            deps.extend(r for r in w.readers if r is not op)
            w.last_w = op
            w.readers = []

    def op(self, eng, fn, reads=(), writes=()):
        o = Op(eng, fn)
        self._track(o, reads, writes)
        self.ops[eng].append(o)
        return o

    def dma(self, q, out, in_, reads=(), writes=(), group=False, sbuf=None, **kw):
        o = Op(q, lambda e: e.dma_start(out=out, in_=in_, **kw))
        o.is_dma = True
        self._track(o, reads, writes, skip_waw=group)
        if sbuf is None:
            sbuf = writes[0] if writes else reads[0]
        key = ("l_" if sbuf in writes else "s_") + sbuf.name
        ent = self.pool.get(key)
        if ent is None:
            ent = [self.new_sem(key), 0]
            self.pool[key] = ent
        ent[1] += 16
        o.dsem, o.dval = ent[0], ent[1]
        self.ops[q].append(o)
        return o

    def emit(self, block):
        for e in ENGS:
            for o in self.ops[e]:
                for d in o.deps:
                    if not d.is_dma and not (d.eng == "pe" and o.eng == "pe"):
                        d.signal = True
        for e in ENGS:
            v = 0
            for o in self.ops[e]:
                if o.signal and not o.is_dma:
                    v += 1
                    o.semval = v
        engmap = {"pe": block.tensor, "dve": block.vector, "act": block.scalar,
                  "pool": block.gpsimd, "sp": block.sync}
        sems = self.sems
        store_bufs = self.store_bufs
        for e in ENGS:
            def body(eng, ops=self.ops[e], e=e):
                known = {}
                for o in ops:
                    need = {}
                    for d in o.deps:
                        if d.is_dma:
                            key, val = d.dsem, d.dval
                        else:
                            if d.eng == "pe" and e == "pe":
                                continue
                            key, val = sems[d.eng], d.semval
                        if need.get(key, 0) < val:
                            need[key] = val
                    for key, val in need.items():
                        if known.get(key, 0) < val:
                            eng.wait_ge(key, val)
                            known[key] = val
                    inst = o.fn(eng)
                    if o.is_dma:
                        inst.then_inc(o.dsem, 16)
                    elif o.signal:
                        inst.then_inc(sems[e], 1)
                if e == "sp":
                    for key, ent in self.pool.items():
                        if key.startswith("s_"):
                            eng.wait_ge(ent[0], ent[1])
            engmap[e](body)


D = 1024
T = 1024
W_IN = 4112
C_Z, C_XBC, C_DT, C_U, C_DQ, C_DK, C_DV, C_WQ, C_WK, C_WV = 0, 512, 1280, 1296, 1808, 2320, 2832, 3344, 3856, 3984


class _Stop(Exception):
    pass


def build_program(stop=None, sub=None):
    nc = bass.Bass("TRN2", target_bir_lowering=False)
    es = ExitStack()
    S = Sched(nc, es)

    def din(name, shape, dt=F32):
        return nc.dram_tensor(name, list(shape), dt, kind="ExternalInput").ap()

    def dout(name, shape):
        return nc.dram_tensor(name, list(shape), F32, kind="ExternalOutput").ap()

    cnt = [0]

    def sb(shape, dt=F32, name=None):
        cnt[0] += 1
        return es.enter_context(nc.sbuf_tensor(name or f"t{cnt[0]}", list(shape), dt))

    xin = [din("xp", [T, D]), din("xs", [T, D])]
    yout = [dout("yp", [T, D]), dout("ys", [T, D])]
    cvT_d = din("cvT", [128, 8, 2])
    st_ssd = din("st_ssd", [2, 2, 8, 64, 64])
    st_s5 = din("st_s5", [2, 64, 128])
    cdk = din("cdk", [2, 256, 512]); cdv = din("cdv", [2, 256, 512])
    cwk = din("cwk", [2, 256, 128]); cwv = din("cwv", [2, 256, 128])
    w_mod = din("w_mod", [2, D, 6 * D]); w_in = din("w_in", [2, D, W_IN]); w_gate = din("w_gate", [2, D, 4 * D])
    w_out = din("w_out", [2, D, D]); w_fc1 = din("w_fc1", [2, D, 4 * D]); w_fc2 = din("w_fc2", [2, 4 * D, D])
    w_glu = din("w_glu", [2, 512, 1024]); w_br = din("w_br", [2, 2048, 1024])
    bmodc = din("bmodc", [2, 128, 48]); g1c = din("g1c", [2, 128, 8]); g2c = din("g2c", [2, 128, 8])
    convw = din("convw", [2, 128, 6, 7]); convb = din("convb", [2, 128, 6])
    dtb = din("dtb", [2, 16]); alog = din("alog", [2, 16]); ssdd = din("ssdd", [2, 8]); normgc = din("normgc", [2, 128, 4])
    lamre = din("lamre", [2, 32, 128]); lamim = din("lamim", [2, 32, 128]); lsx = din("lsx", [2, 32, 128])
    s5bre = din("s5bre", [2, 2, 2048, 16]); s5bim = din("s5bim", [2, 2, 2048, 16])
    s5cre = din("s5cre", [2, 2, 512, 64]); s5cim = din("s5cim", [2, 2, 512, 64])
    s5dc = din("s5dc", [2, 128, 4]); bgluc = din("bgluc", [2, 128, 8])
    dqg = din("dqg", [2, 64]); dkg = din("dkg", [2, 64]); dlam = din("dlam", [2, 256]); dsubc = din("dsubc", [2, 128, 1])
    wqg = din("wqg", [2, 64]); wkg = din("wkg", [2, 64]); wsink = din("wsink", [2, 8]); bgatec = din("bgatec", [2, 128, 32])
    c_identb = din("c_identb", [128, 128], BF16); c_identf = din("c_identf", [128, 128]); c_ones = din("c_ones", [128, 128])
    c_triu = din("c_triu", [128, 128]); c_tril = din("c_tril", [128, 128])
    c_mnegF = din("c_mnegF", [128, 128]); c_mnegB = din("c_mnegB", [128, 128])
    c_bprev = din("c_bprev", [128, 128]); c_bnext = din("c_bnext", [128, 128])
    c_maskB = din("c_maskB", [128, 4, 128]); c_maskC = din("c_maskC", [128, 4, 128])
    c_iota = din("c_iota", [128, 1024]); c_ropeC = din("c_ropeC", [128, 8, 64]); c_ropeS = din("c_ropeS", [128, 8, 64])
    c_rmF = din("c_rmF", [128, 1024], BF16); c_rmB = din("c_rmB", [128, 1024], BF16)
    o_ssd = dout("o_ssd", [4, 2, 2, 8, 64, 64]); o_s5 = dout("o_s5", [4, 2, 64, 128])
    o_dk = dout("o_dk", [4, 2, 256, 512]); o_dv = dout("o_dv", [4, 2, 256, 512])
    o_wk = dout("o_wk", [4, 2, 256, 128]); o_wv = dout("o_wv", [4, 2, 256, 128])

    def TT(eng, out, in0, in1, op, r, w):
        S.op(eng, lambda e: e.tensor_tensor(out=out, in0=in0, in1=in1, op=op), r, w)

    def TS(eng, out, in0, s1, op0, r, w, s2=None, op1=None):
        if op1 is None:
            S.op(eng, lambda e: e.tensor_scalar(out=out, in0=in0, scalar1=s1, scalar2=None, op0=op0), r, w)
        else:
            S.op(eng, lambda e: e.tensor_scalar(out=out, in0=in0, scalar1=s1, scalar2=s2, op0=op0, op1=op1), r, w)

    def STT(out, in0, scalar, in1, op0, op1, r, w):
        S.op("dve", lambda e: e.scalar_tensor_tensor(out=out, in0=in0, scalar=scalar, in1=in1, op0=op0, op1=op1), r, w)

    def ACT(out, in_, func, r, w, scale=1.0, bias=None, accum=None):
        kw = {}
        if bias is not None:
            kw["bias"] = bias
        if accum is not None:
            kw["accum_out"] = accum
        S.op("act", lambda e: e.activation(out=out, in_=in_, func=func, scale=scale, **kw), r, w)

    def CP(eng, out, in_, r, w):
        if eng == "act":
            S.op("act", lambda e: e.copy(out=out, in_=in_), r, w)
        else:
            S.op(eng, lambda e: e.tensor_copy(out=out, in_=in_), r, w)

    def MM(out, lhsT, rhs, start, stop, r, w):
        S.op("pe", lambda e: e.matmul(out, lhsT=lhsT, rhs=rhs, start=start, stop=stop), r, w)

    def MSET(eng, out, val, w):
        S.op(eng, lambda e: e.memset(out, val), (), w)

    def LD(out, in_, b, q="sp", group=False):
        S.dma(q, out, in_, writes=[b], group=group)

    def STO(out, in_, b, q="sp"):
        S.dma(q, out, in_, reads=[b])

    def const(src, shape, dt=F32):
        t = sb(shape, dt)
        b = Buf(f"c{cnt[0]}")
        LD(t[:], src, b)
        return t, b

    identb, b_identb = const(c_identb[:, :], [128, 128], BF16)
    identf, b_identf = const(c_identf[:, :], [128, 128])
    onesf, b_ones = const(c_ones[:, :], [128, 128])
    triu, b_triu = const(c_triu[:, :], [128, 128]); tril, b_tril = const(c_tril[:, :], [128, 128])
    mnegF, b_mnegF = const(c_mnegF[:, :], [128, 128]); mnegB, b_mnegB = const(c_mnegB[:, :], [128, 128])
    maskB, b_maskB = const(c_maskB[:, :, :], [128, 4, 128]); maskC, b_maskC = const(c_maskC[:, :, :], [128, 4, 128])
    ropeC, b_ropeC = const(c_ropeC[:, :, :], [128, 8, 64]); ropeS, b_ropeS = const(c_ropeS[:, :, :], [128, 8, 64])
    CONSTB = [b_identb, b_identf, b_ones]

    psum = [es.enter_context(nc.psum_tensor(f"ps{i}", [128, 512], F32)) for i in range(8)]
    psb = [Buf(f"ps{i}", excl=True) for i in range(8)]

    class RR:
        def __init__(self, ids):
            self.ids = list(ids); self.i = 0

        def get(self):
            k = self.ids[self.i % len(self.ids)]; self.i += 1
            return psum[k], psb[k]

        def set(self, ids):
            self.ids = list(ids)

    rr = RR(range(0, 4))

    xres = sb([128, 8, D]); b_xres = [Buf(f"xres{t}") for t in range(8)]
    hT = sb([128, 8, T], BF16); b_hT = Buf("hT")
    NST, NBF = 3, 3
    wst = [sb([128, 8, 256]) for _ in range(NST)]; b_wst = [Buf(f"wst{i}") for i in range(NST)]
    wbf = [sb([128, 8, 256], BF16) for _ in range(NBF)]; b_wbf = [Buf(f"wbf{i}") for i in range(NBF)]
    wctr = [0, 0]
    big = sb([128, 16, 1024], BF16)
    b_big = [Buf(f"big{i}") for i in range(16)]
    modc = sb([128, 48]); b_modc = Buf("modc")
    scol = sb([128, 8, 2]); b_scol = Buf("scol")
    G1 = sb([128, 8]); SH1 = sb([128, 8]); G2 = sb([128, 8]); SH2 = sb([128, 8]); b_G = Buf("G")
    gbc = sb([128, 2, D]); b_gbc = [Buf("gbc0"), Buf("gbc1")]
    small = sb([128, 64]); b_small = Buf("small")
    junk = sb([128, 768]); b_junk = Buf("junk")
    SCR_BYTES = 60 * 1024
    scr = sb([128, SCR_BYTES // 4])

    def wload(src, nk, ncols, cast=True):
        i = wctr[0] % NST; wctr[0] += 1
        st, bs = wst[i], b_wst[i]
        LD(st[:, 0:nk, 0:ncols], src.rearrange("(k p) c -> p k c", p=128), bs)
        if not cast:
            return st, bs
        j = wctr[1] % NBF; wctr[1] += 1
        wb, bb = wbf[j], b_wbf[j]
        heavy = any(k in Sched.PHASE for k in ("prologue", "merge", "mlp"))
        eng = "act" if (wctr[1] % 2 == 0 or not heavy) else "dve"
        CP(eng, wb[:, 0:nk, 0:ncols], st[:, 0:nk, 0:ncols], [bs], [bb])
        return wb, bb

    def proj_tm(src, bsrc, nk, w, bw, ncols, tiles, evac):
        for t in tiles:
            ps, bp = rr.get()
            for k in range(nk):
                MM(ps[:, 0:ncols], src[:, k, t * 128:(t + 1) * 128], w[:, k, 0:ncols], k == 0, k == nk - 1, [bsrc, bw], [bp])
            evac(t, ps, bp)

    def proj_fm(src, bsrc, nk, w, bw, ncols, evac, halves=(0, 1)):
        for cc in range((ncols + 127) // 128):
            m = min(128, ncols - cc * 128)
            for h in halves:
                ps, bp = rr.get()
                for k in range(nk):
                    MM(ps[0:m, :], w[:, k, cc * 128:cc * 128 + m], src[:, k, h * 512:(h + 1) * 512], k == 0, k == nk - 1, [bsrc, bw], [bp])
                evac(cc, h, ps, bp)

    def transpose_to(ps_out, in_, r, w, dt=BF16, np_=128):
        idn = identb if dt == BF16 else identf
        S.op("pe", lambda e: e.transpose(out=ps_out, in_=in_, identity=idn[0:np_, 0:np_]), list(r) + CONSTB, w)

    def bcast_rows(col_ap, bcol, ps_out, bp):
        dg = sb_diag[dgc[0] % 4]; bd = b_diag[dgc[0] % 4]; dgc[0] += 1
        TS("dve", dg[:], identf[:], col_ap, ALU.mult, [b_identf, bcol], [bd])
        MM(ps_out, onesf[:], dg[:], True, True, [b_ones, bd], [bp])

    sb_diag = [sb([128, 128]) for _ in range(4)]; b_diag = [Buf(f"dg{i}") for i in range(4)]; dgc = [0]

    def rstd_from_ss(ss_ap, n, out_ap, r, w, ncols=1):
        TS("dve", out_ap, ss_ap, 1.0 / n, ALU.mult, r, w, s2=EPS, op1=ALU.add)
        ACT(out_ap, out_ap, AF.Sqrt, w, w)
        S.op("dve", lambda e: e.reciprocal(out=out_ap, in_=out_ap), w, w)

    LD(scol[:], cvT_d[:, :, :], b_scol)
    ACT(scol[:], scol[:], AF.Silu, [b_scol], [b_scol])
    scolb = sb([128, 8, 2], BF16)
    CP("dve", scolb[:], scol[:], [b_scol], [b_scol])

    modall = sb([128, 2, 2, 48]); b_modall = Buf("modall")

    def adaln_weights(l):
        rr.set(range(8))
        bm = sb_bm; LD(bm[:], bmodc[l], b_bm)
        for blk in range(24):
            w, bw = wload(w_mod[l][:, blk * 256:(blk + 1) * 256], 8, 256)
            for cc in range(2):
                ps, bp = rr.get()
                for k in range(8):
                    MM(ps[:, 0:2], w[:, k, cc * 128:(cc + 1) * 128], scolb[:, k, 0:2], k == 0, k == 7, [bw, b_scol], [bp])
                c = blk * 2 + cc
                TT("dve", modall[:, l, :, c], ps[:, 0:2], bm[:, c:c + 1].broadcast_to([128, 2]), ALU.add, [bp, b_bm], [b_modall])
        rr.set(range(4))

    def adaln(l, path):
        rr.set(range(8))
        CP("dve", modc[:], modall[:, l, path, :], [b_modall], [b_modc])
        gt = sb_gt; LD(gt[:, 0:8], g1c[l], b_gt); LD(gt[:, 8:16], g2c[l], b_gt, group=True)
        STT(G1[:], modc[:, 8:16], 1.0, gt[:, 0:8], ALU.add, ALU.mult, [b_modc, b_gt], [b_G])
        STT(G2[:], modc[:, 32:40], 1.0, gt[:, 8:16], ALU.add, ALU.mult, [b_modc, b_gt], [b_G])
        CP("dve", SH1[:], modc[:, 0:8], [b_modc], [b_G])
        CP("dve", SH2[:], modc[:, 24:32], [b_modc], [b_G])
        for gi, base in enumerate((16, 40)):
            for c in range(8):
                ps, bp = rr.get()
                bcast_rows(modc[:, base + c:base + c + 1], b_modc, ps[:, 0:128], bp)
                CP("act", gbc[:, gi, c * 128:(c + 1) * 128], ps[:, 0:128], [bp], [b_gbc[gi]])
        rr.set(range(4))

    sb_bm = sb([128, 48]); b_bm = Buf("bm"); sb_gt = sb([128, 16]); b_gt = Buf("gt")

    xn = [sb([128, D], BF16), sb([128, D], BF16)]; b_xn = [Buf("xn0"), Buf("xn1")]

    def norm_mod(Gc, SHc):
        rr.set(range(8))
        jb = junk[:].bitcast(BF16)[:, 0:D]
        for t in range(8):
            ACT(jb, xres[:, t, :], AF.Square, [b_xres[t]], [b_junk, b_small], accum=small[:, t:t + 1])
        rstd_from_ss(small[:, 0:8], D, small[:, 8:16], [b_small], [b_small])
        stg = []
        for t in range(8):
            def FA(t=t):
                x_, bx_ = xn[t % 2], b_xn[t % 2]
                ACT(x_[:], xres[:, t, :], AF.Copy, [b_xres[t], b_small], [bx_], scale=small[:, 8 + t:9 + t])

            def FB(t=t):
                x_, bx_ = xn[t % 2], b_xn[t % 2]
                for c in range(8):
                    ps, bp = rr.get()
                    pv = ps[:].bitcast(BF16)[:, 0:128]
                    transpose_to(pv, x_[:, c * 128:(c + 1) * 128], [bx_], [bp])
                    if c % 2 == 0:
                        ACT(hT[:, c, t * 128:(t + 1) * 128], pv, AF.Identity, [bp, b_G], [b_hT], scale=Gc[:, c:c + 1], bias=SHc[:, c:c + 1])
                    else:
                        TS("dve", hT[:, c, t * 128:(t + 1) * 128], pv, Gc[:, c:c + 1], ALU.mult, [bp, b_G], [b_hT], s2=SHc[:, c:c + 1], op1=ALU.add)
            stg.append((FA, FB))
        stg[0][0]()
        for k in range(8):
            if k + 1 < 8:
                stg[k + 1][0]()
            stg[k][1]()
        rr.set(range(4))

    def yT(br, fc):
        return big[:, br * 4 + fc, :], b_big[br * 4 + fc]

    def run_pass(path):
        nseq, L = (4, 256) if path == 0 else (1, 1024)
        nt = L // 128
        is_s = path == 1
        for t in range(8):
            LD(xres[:, t, :], xin[path][t * 128:(t + 1) * 128, :], b_xres[t])
        def chk(stage, l):
            if stop is not None and stop == (path, l, stage):
                raise _Stop()
        for l in range(2):
            def ph(n):
                Sched.PHASE = f"{'PS'[path]}{l}_{n}"
            ph("adaln"); adaln(l, path); chk("adaln", l)
            ph("norm1"); norm_mod(G1, SH1); chk("norm1", l)
            ph("ssd"); branch_ssd(l, path, nseq, L, nt, is_s); chk("ssd", l)
            ph("s5"); branch_s5(l, path, nseq, L, nt, is_s); chk("s5", l)
            ph("diff"); branch_diff(l, path, nseq, L, nt, is_s); chk("diff", l)
            ph("win"); branch_win(l, path, nseq, L, nt, is_s); chk("win", l)
            ph("merge"); merge(l); chk("merge", l)
            ph("norm2"); norm_mod(G2, SH2); chk("norm2", l)
            ph("mlp"); mlp(l); chk("mlp", l)
        for t in range(8):
            STO(yout[path][t * 128:(t + 1) * 128, :], xres[:, t, :], b_xres[t])

    class Carve:
        def __init__(self):
            self.off = 0

        def take(self, shape, dt=F32):
            n = int(np.prod(shape))
            nbytes = n * (4 if dt in (F32, I32) else 2)
            nbytes = (nbytes + 31) // 32 * 32
            assert self.off + nbytes <= SCR_BYTES, (self.off, nbytes)
            v = scr[:, self.off // 4:(self.off + nbytes) // 4]
            self.off += nbytes
            if dt != F32:
                v = v.bitcast(dt)
            v = v[:, 0:n]
            if len(shape) == 2:
                return v.rearrange("p (a b) -> p a b", a=shape[0])
            if len(shape) == 3:
                return v.rearrange("p (a b c) -> p a b c", a=shape[0], b=shape[1])
            return v

    b_scr_all = Buf("scrall")
    bar = [None]

    def NB(name):
        x = Buf(name)
        x.last_w = bar[0]
        return x

    def barrier_begin():
        bar[0] = S.op("pool", lambda e: e.memset(small[:, 63:64], 0.0), [b_scr_all], [b_scr_all])


    def branch_ssd(l, path, nseq, L, nt, is_s):
        barrier_begin()
        cv = Carve()
        zs = cv.take([8, 512], BF16); b_zs = NB("zs")
        dt = cv.take([8, 16]); dtA = cv.take([8, 16]); ainc = cv.take([8, 16]); arest = cv.take([8, 16]); edt = cv.take([8, 16]); einc = cv.take([8, 16])
        nainc = cv.take([8, 16])
        b_dt = NB("dt"); b_cum = NB("cum")
        cumP = cv.take([8, 32]); b_cumP = NB("cumP")
        Lp = L + 6
        raw = cv.take([nseq * Lp]); b_raw = NB("raw")
        acc = cv.take([T]); b_acc = NB("acc")
        xrot = [cv.take([T], BF16), cv.take([T], BF16)]; b_xrot = [NB("xrot0"), NB("xrot1")]
        xB = cv.take([T], BF16); xC = cv.take([T], BF16); b_xB = NB("xB"); b_xC = NB("xC")
        xs_tok = cv.take([8, 512], BF16); b_xs = NB("xs_tok")
        B_tok = cv.take([8, 128], BF16); b_Btok = NB("Btok")
        NDP = 12
        GS = 4
        b_seg = []
        Lt = [cv.take([128]) for _ in range(NDP)]; b_Lt = [NB(f"Lt{i}") for i in range(NDP)]
        sc = [cv.take([128], BF16) for _ in range(NDP)]; b_sc = [NB(f"sc{i}") for i in range(NDP)]
        yacc = cv.take([512]); b_yacc = NB("yacc")
        ytmp = cv.take([512]); b_ytmp = NB("ytmp")
        ynb = cv.take([512], BF16); b_ynb = NB("ynb")
        prm = cv.take([64]); b_prm = NB("ssdprm")
        cw = cv.take([6, 7]); cb = cv.take([6]); ngc = cv.take([4]); b_cw = NB("cw")
        Bw = [cv.take([64], BF16), cv.take([64], BF16)]; b_Bw = [NB("Bw0"), NB("Bw1")]
        fin = None; s0T = None; st_ld = None
        b_fin = NB("fin"); b_s0T = NB("s0T"); b_stld = NB("stld")
        if is_s:
            s0T = cv.take([8, 128], BF16); st_ld = cv.take([8, 128])
        else:
            fin = cv.take([16, 64])
        _p0 = Sched.PHASE
        S.op("pool", lambda e: e.memset(prm[:, 0:64], 0.0), [b_scr_all], [b_prm, b_scr_all])
        LD(prm[:, 0:16], dtb[l:l + 1, :].partition_broadcast(128), b_prm)
        LD(prm[:, 16:32], alog[l:l + 1, :].partition_broadcast(128), b_prm, group=True)
        LD(prm[:, 32:40], ssdd[l:l + 1, :].partition_broadcast(128), b_prm, group=True)
        ACT(prm[:, 16:32], prm[:, 16:32], AF.Exp, [b_prm], [b_prm])
        TS("dve", prm[:, 16:32], prm[:, 16:32], -1.0, ALU.mult, [b_prm], [b_prm])
        LD(cw[:], convw[l], b_cw); LD(cb[:], convb[l], b_cw, group=True); LD(ngc[:], normgc[l], b_cw, group=True)
        Sched.PHASE = _p0 + 'A'
        for blk in range(2):
            w, bw = wload(w_in[l][:, C_Z + blk * 256:C_Z + (blk + 1) * 256], 8, 256)
            proj_tm(hT, b_hT, 8, w, bw, 256, range(8),
                    lambda t, ps, bp, blk=blk: ACT(zs[:, t, blk * 256:(blk + 1) * 256], ps[:, 0:256], AF.Silu, [bp], [b_zs]))
        Sched.PHASE = _p0 + 'B'
        w, bw = wload(w_in[l][:, C_DT:C_DT + 16], 8, 16)

        def ev_dt(t, ps, bp):
            TT("dve", dt[:, t, :], ps[:, 0:16], prm[:, 0:16], ALU.add, [bp, b_prm], [b_dt])
            ACT(dt[:, t, :], dt[:, t, :], AF.Exp, [b_dt], [b_dt])
            ACT(dt[:, t, :], dt[:, t, :], AF.Ln, [b_dt], [b_dt], bias=1.0)
            TT("dve", dtA[:, t, :], dt[:, t, :], prm[:, 16:32], ALU.mult, [b_dt, b_prm], [b_dt])
        proj_tm(hT, b_hT, 8, w, bw, 16, range(8), ev_dt)
        Sched.PHASE = _p0 + 'C'
        for blk in range(3):
            w, bw = wload(w_in[l][:, C_XBC + blk * 256:C_XBC + (blk + 1) * 256], 8, 256)
            for c2 in range(2):
                cc = blk * 2 + c2
                if cc < 4:
                    xa, bxa = xrot[cc % 2], b_xrot[cc % 2]
                elif cc == 4:
                    xa, bxa = xB, b_xB
                else:
                    xa, bxa = xC, b_xC
                MSET("pool", raw[:], 0.0, [b_raw])
                rw3 = raw.rearrange("p (s x) -> p s x", s=nseq)
                for h in range(2):
                    ps, bp = rr.get()
                    for k in range(8):
                        MM(ps[:, :], w[:, k, c2 * 128:(c2 + 1) * 128], hT[:, k, h * 512:(h + 1) * 512], k == 0, k == 7, [b_hT, bw], [bp])
                    if is_s:
                        CP("act", raw[:, 3 + h * 512:3 + (h + 1) * 512], ps[:, :], [bp], [b_raw])
                    else:
                        CP("act", rw3[:, 2 * h:2 * h + 2, 3:3 + L], ps[:, :].rearrange("p (s x) -> p s x", s=2), [bp], [b_raw])
                ac3 = acc.rearrange("p (s x) -> p s x", s=nseq)
                TS("dve", ac3, rw3[:, :, 0:L], cw[:, cc, 0:1], ALU.mult, [b_raw, b_cw], [b_acc])
                for k in range(1, 7):
                    STT(ac3, rw3[:, :, k:k + L], cw[:, cc, k:k + 1], ac3, ALU.mult, ALU.add, [b_raw, b_cw, b_acc], [b_acc])
                ACT(xa[:], acc[:], AF.Silu, [b_acc, b_cw], [bxa], bias=cb[:, cc:cc + 1])
                if cc < 5:
                    for t in range(8):
                        ps, bp = rr.get()
                        pv = ps[:].bitcast(BF16)[:, 0:128]
                        transpose_to(pv, xa[:, t * 128:(t + 1) * 128], [bxa], [bp])
                        if cc < 4:
                            CP("act", xs_tok[:, t, cc * 128:(cc + 1) * 128], pv, [bp], [b_xs])
                        else:
                            CP("act", B_tok[:, t, :], pv, [bp], [b_Btok])
        Sched.PHASE = _p0 + 'E'
        for s in range(nseq):
            for j in range(nt):
                tj = s * nt + j
                ps, bp = rr.get()
                for i in range(j + 1):
                    MM(ps[:, 0:16], (triu if i == j else onesf)[:], dtA[:, s * nt + i, :], i == 0, i == j, [b_triu, b_ones, b_dt], [bp])
                for i in range(nt - 1, j - 1, -1):
                    MM(ps[:, 16:32], (tril if i == j else onesf)[:], dtA[:, s * nt + i, :], i == nt - 1, i == j, [b_tril, b_ones, b_dt], [bp])
                CP("act", cumP[:, tj, :], ps[:, 0:32], [bp], [b_cumP])
                CP("dve", ainc[:, tj, 0:8], cumP[:, tj, 0:8], [b_cumP], [b_cum])
                CP("dve", ainc[:, tj, 8:16], cumP[:, tj, 24:32], [b_cumP], [b_cum])
                TT("dve", arest[:, tj, 0:8], cumP[:, tj, 16:24], dtA[:, tj, 0:8], ALU.subtract, [b_cumP, b_dt], [b_cum])
                TT("dve", arest[:, tj, 8:16], cumP[:, tj, 8:16], dtA[:, tj, 8:16], ALU.subtract, [b_cumP, b_dt], [b_cum])
                ACT(edt[:, tj, :], arest[:, tj, :], AF.Exp, [b_cum], [b_cum])
                TT("dve", edt[:, tj, :], edt[:, tj, :], dt[:, tj, :], ALU.mult, [b_cum, b_dt], [b_cum])
                ACT(einc[:, tj, :], ainc[:, tj, :], AF.Exp, [b_cum], [b_cum])
                TS("dve", nainc[:, tj, :], ainc[:, tj, :], -1.0, ALU.mult, [b_cum], [b_cum])
        if is_s:
            stv = st_ssd[l].rearrange("d h p n -> (d h p) n").rearrange("(j q) n -> q j n", q=128)
            LD(st_ld[:, :, 0:64], stv, b_stld); LD(st_ld[:, :, 64:128], stv, b_stld, group=True)
            for j8 in range(8):
                ps, bp = rr.get()
                transpose_to(ps[:, 0:128], st_ld[:, j8, :], [b_stld], [bp], dt=F32)
                CP("act", s0T[:, j8, :], ps[:, 0:128], [bp], [b_s0T])
        Sched.PHASE = _p0 + 'F'
        ybanks = [(psum[4], psb[4]), (psum[7], psb[7])]
        pa_ = [0]
        k_ = [0]
        stages = []
        for s in range(nseq):
            for j in range(nt):
                tj = s * nt + j
                ybank, b_yb = ybanks[tj % 2]
                for h in range(8):
                    g = h // 4
                    gsl = slice(g * 64, (g + 1) * 64)
                    units = [(0, i) for i in range(j + 1)] + [(1, i) for i in range(j, nt)]
                    cur = {"psA": None}
                    for g0 in range(0, len(units), GS):
                        grp = list(enumerate(units))[g0:g0 + GS]
                        st = {}

                        def A(st=st, grp=grp, units=units, s=s, j=j, tj=tj, h=h, gsl=gsl, cur=cur):
                            for ui, (d, i) in grp:
                                ti = s * nt + i
                                dh = d * 8 + h
                                if ui == 0 or units[ui - 1][0] != d:
                                    if h % 4 == 0:
                                        c0 = d * 8 + h
                                        TT("dve", junk[:, 0:512].rearrange("p (a t) -> p a t", a=4), identf[:].unsqueeze(1).broadcast_to([128, 4, 128]),
                                           ainc[:, tj, c0:c0 + 4].unsqueeze(2).broadcast_to([128, 4, 128]), ALU.mult, [b_identf, b_cum], [b_junk])
                                        MM(psum[5 + d][:, 0:512], onesf[:], junk[:, 0:512], True, True, [b_ones, b_junk], [psb[5 + d]])
                                    cur["psA"] = (psum[5 + d][:, (h % 4) * 128:(h % 4 + 1) * 128], psb[5 + d])
                                psA_t, b_psA = cur["psA"]
                                q = k_[0] % NDP; k_[0] += 1
                                st[ui] = q
                                if i == j:
                                    STT(Lt[q][:], psA_t[:, 0:128], ainc[:, ti, dh:dh + 1], (mnegF if d == 0 else mnegB)[:], ALU.subtract, ALU.add,
                                        [b_psA, b_cum, b_mnegF, b_mnegB], [b_Lt[q]])
                                    ACT(Lt[q][:], Lt[q][:], AF.Exp, [b_Lt[q]], [b_Lt[q]])
                                elif True:
                                    ACT(Lt[q][:], psA_t[:, 0:128], AF.Exp, [b_psA, b_cum], [b_Lt[q]], bias=nainc[:, ti, dh:dh + 1])
                                else:
                                    TS("dve", Lt[q][:], psA_t[:, 0:128], ainc[:, ti, dh:dh + 1], ALU.subtract, [b_psA, b_cum], [b_Lt[q]], s2=0.0, op1=ALU.min)
                                    ACT(Lt[q][:], Lt[q][:], AF.Exp, [b_Lt[q]], [b_Lt[q]])
                            for ui, (d, i) in grp:
                                ti = s * nt + i
                                dh = d * 8 + h
                                q = st[ui]
                                psG, bpG = rr.get()
                                MM(psG[:, 0:128], xB[gsl, ti * 128:(ti + 1) * 128], xC[gsl, tj * 128:(tj + 1) * 128], True, True, [b_xB, b_xC], [bpG])
                                STT(sc[q][:], psG[:, 0:128], dt[:, ti, dh:dh + 1], Lt[q][:], ALU.mult, ALU.mult, [bpG, b_dt, b_Lt[q]], [b_sc[q]])

                        def B(st=st, grp=grp, units=units, s=s, tj=tj, h=h, ybank=ybank, b_yb=b_yb, last_grp=(g0 + GS >= len(units))):
                            for ui, (d, i) in grp:
                                ti = s * nt + i
                                q = st[ui]
                                MM(ybank[:, h * 64:(h + 1) * 64], sc[q][:], xs_tok[:, ti, h * 64:(h + 1) * 64], ui == 0, ui == len(units) - 1, [b_sc[q], b_xs], [b_yb])
                            if h == 7 and last_grp:
                                finalize(tj, ybank, b_yb)
                        stages.append((A, B))

        def finalize(tj, ybank, b_yb):
            if True:
                TT("dve", ytmp.rearrange("p (h x) -> p h x", h=8), xs_tok[:, tj, :].rearrange("p (h x) -> p h x", h=8),
                   prm[:, 32:40].unsqueeze(2).broadcast_to([128, 8, 64]), ALU.mult, [b_xs, b_prm], [b_ytmp])
                TT("dve", yacc[:], ybank[:, :], ytmp[:], ALU.add, [b_yb, b_ytmp], [b_yacc])
                if is_s:
                    for d in range(2):
                        for h in range(8):
                            g = h // 4
                            gsl = slice(g * 64, (g + 1) * 64)
                            j8 = (d * 8 + h) // 2
                            h2 = (d * 8 + h) % 2
                            psO, bpO = rr.get()
                            MM(psO[:, 0:64], xC[gsl, tj * 128:(tj + 1) * 128], s0T[gsl, j8, h2 * 64:(h2 + 1) * 64], True, True, [b_xC, b_s0T], [bpO])
                            STT(yacc[:, h * 64:(h + 1) * 64], psO[:, 0:64], einc[:, tj, d * 8 + h:d * 8 + h + 1], yacc[:, h * 64:(h + 1) * 64], ALU.mult, ALU.add,
                                [bpO, b_cum, b_yacc], [b_yacc])
                TT("dve", yacc[:], yacc[:], zs[:, tj, :], ALU.mult, [b_yacc, b_zs], [b_yacc])
                ACT(ytmp[:], yacc[:], AF.Square, [b_yacc], [b_ytmp, b_small], accum=small[:, 16:17])
                rstd_from_ss(small[:, 16:17], 512, small[:, 17:18], [b_small], [b_small])
                ACT(ynb[:], yacc[:], AF.Copy, [b_yacc, b_small], [b_ynb], scale=small[:, 17:18])
                for c4 in range(4):
                    ps, bp = rr.get()
                    pv = ps[:].bitcast(BF16)[:, 0:128]
                    transpose_to(pv, ynb[:, c4 * 128:(c4 + 1) * 128], [b_ynb], [bp])
                    yt_, by_ = yT(0, c4)
                    ACT(yt_[:, tj * 128:(tj + 1) * 128], pv, AF.Copy, [bp, b_cw], [by_], scale=ngc[:, c4:c4 + 1])
        LA = 2 if is_s else 3
        for k in range(min(LA, len(stages))):
            stages[k][0]()
        for k in range(len(stages)):
            if k + LA < len(stages):
                stages[k + LA][0]()
            stages[k][1]()
        Sched.PHASE = _p0 + 'G'
        if not is_s:
            for s in range(nseq):
                for d in range(2):
                    for h in range(8):
                        g = h // 4
                        psF, bpF = rr.get()
                        for i in range(nt):
                            ti = s * nt + i
                            q = k_[0] % 2; k_[0] += 1
                            TS("dve", Bw[q][:], B_tok[:, ti, g * 64:(g + 1) * 64], edt[:, ti, d * 8 + h:d * 8 + h + 1], ALU.mult, [b_Btok, b_cum], [b_Bw[q]])
                            MM(psF[0:64, 0:64], xs_tok[:, ti, h * 64:(h + 1) * 64], Bw[q][:], i == 0, i == nt - 1, [b_xs, b_Bw[q]], [bpF])
                        CP("act", fin[0:64, d * 8 + h, :], psF[0:64, 0:64], [bpF], [b_fin])
                STO(o_ssd[s, l].rearrange("d h p n -> p (d h) n"), fin[0:64, :, :], b_fin)
        S.op("pool", lambda e: e.memset(prm[:, 0:1], 0.0), [], ([b_fin, b_yacc, b_ynb, b_Btok, b_xs, b_cum, b_dt, b_zs, b_cumP, b_xB, b_xC, b_raw, b_acc, b_ytmp, b_cw, b_s0T, b_stld, b_prm]
             + b_xrot + b_seg + b_Lt + b_sc + b_Bw) + [b_scr_all])

    def branch_s5(l, path, nseq, L, nt, is_s):
        barrier_begin()
        _p0 = Sched.PHASE
        cv = Carve()
        uT = cv.take([4, T], BF16); b_uT = NB("uT")
        y5T = uT; b_y5 = b_uT
        prow = cv.take([128]); b_prow = NB("prow")
        pc = cv.take([12, 32]); b_pc = NB("pc")
        pci = cv.take([32], I32); b_pci = NB("pci")
        Bst = cv.take([4, 4, 16]); b_Bst = NB("Bst")
        Cn = cv.take([4, 64]); b_Cn = NB("Cn")
        Bc = cv.take([2, 16]); b_Bc = NB("Bc"); Bt = cv.take([16]); b_Bt = NB("Bt")
        Bx = [cv.take([128], BF16), cv.take([128], BF16)]; b_Bx = [NB("Bx0"), NB("Bx1")]
        BcL = [cv.take([128], BF16), cv.take([128], BF16)]; b_BcL = [NB("BcL0"), NB("BcL1")]
        Cx = [cv.take([128], BF16), cv.take([128], BF16)]; b_Cx = [NB("Cx0"), NB("Cx1")]
        CL = cv.take([4, 128], BF16); b_CL = NB("CL")
        Lt_ = L
        cosT = cv.take([Lt_]); sinT = cv.take([Lt_]); b_tab = NB("tab")
        xr = [cv.take([T]), cv.take([T])]; b_xr = [NB("xr0"), NB("xr1")]
        prR = cv.take([2 * T])
        prb = prR.bitcast(BF16)
        pr = [prb[:, k * T:(k + 1) * T] for k in range(4)]; b_pr = [NB(f"pr{i}") for i in range(4)]
        argF = prR[:, 0:Lt_]; argI = prR[:, T:T + Lt_].bitcast(I32)
        tmpx = prR[:, 0:T]; rmt = prR[:, T:2 * T]
        bA = [b_pr[0], b_pr[1]]; bB = [b_pr[2], b_pr[3]]
        if not is_s:
            tmpy = cv.take([T]); bY = [NB("tmpy")]
        else:
            tmpy = tmpx; bY = bA
        d5 = cv.take([4]); bg = cv.take([8]); b_d5 = NB("d5")
        finS = cv.take([256]); b_finS = NB("finS")
        b_wcap = NB("wcap")
        if not is_s:
            wcap = cv.take([2, 32, 4]); tcap = cv.take([2, 32]); wtmp = cv.take([4, 32, 4])
        s0c = cv.take([64]); b_s0c = NB("s0c")
        sg = cv.take([512]); b_sg = NB("sg")
        fT = cv.take([128]); b_fT = NB("fT")
        iota = cv.take([Lt_]); b_iota = NB("iota")
        LD(iota[:], c_iota[:, 0:Lt_], b_iota)
        if not is_s:
            rmF = cv.take([T], BF16); rmB = cv.take([T], BF16); b_rmF = NB("rmF"); b_rmB = NB("rmB")
            LD(rmF[:], c_rmF[:, :], b_rmF); LD(rmB[:], c_rmB[:, :], b_rmB)
        else:
            rmF = rmB = None; b_rmF = b_rmB = b_iota
        S.op("pool", lambda e: e.memset(prow[:], 0.0), [b_scr_all], [b_prow, b_scr_all])
        LD(prow[0:32, :], lamre[l], b_prow); LD(prow[32:64, :], lamim[l], b_prow, group=True); LD(prow[64:96, :], lsx[l], b_prow, group=True)
        ps, bp = rr.get()
        transpose_to(ps[:, 0:96], prow[0:96, :], [b_prow], [bp], dt=F32, np_=96)
        CP("act", pc[:, 0:3, :].rearrange("p a b -> p (a b)"), ps[:, 0:96], [bp], [b_pc])
        P_ = lambda i: pc[:, i, :]
        R, W_ = [b_pc], [b_pc]
        ACT(P_(2), P_(2), AF.Exp, R, W_)
        TT("dve", P_(3), P_(0), P_(2), ALU.mult, R, W_)
        TT("dve", P_(4), P_(1), P_(2), ALU.mult, R, W_)
        ACT(P_(5), P_(3), AF.Exp, R, W_)
        TS("dve", pci[:], P_(4), 1.0 / (2 * math.pi), ALU.mult, R, [b_pci])
        CP("dve", P_(10), pci[:], [b_pci], W_)
        STT(P_(11), P_(10), -2 * math.pi, P_(4), ALU.mult, ALU.add, R, W_)
        TS("dve", P_(11), P_(11), 3.14159, ALU.min, R, W_, s2=-3.14159, op1=ALU.max)
        ACT(P_(7), P_(11), AF.Sin, R, W_)
        ACT(P_(10), P_(11), AF.Abs, R, W_)
        ACT(P_(6), P_(10), AF.Sin, R, W_, scale=-1.0, bias=math.pi / 2)
        TT("dve", P_(6), P_(6), P_(5), ALU.mult, R, W_)
        TT("dve", P_(7), P_(7), P_(5), ALU.mult, R, W_)
        TT("dve", P_(10), P_(0), P_(0), ALU.mult, R, W_)
        TT("dve", P_(11), P_(1), P_(1), ALU.mult, R, W_)
        TT("dve", P_(10), P_(10), P_(11), ALU.add, R, W_)
        S.op("dve", lambda e: e.reciprocal(out=P_(10), in_=P_(10)), R, W_)
        TS("dve", P_(11), P_(6), -1.0, ALU.add, R, W_)
        TT("dve", P_(8), P_(11), P_(0), ALU.mult, R, W_)
        TT("dve", P_(9), P_(7), P_(1), ALU.mult, R, W_)
        TT("dve", P_(8), P_(8), P_(9), ALU.add, R, W_)
        TT("dve", P_(8), P_(8), P_(10), ALU.mult, R, W_)
        TT("dve", P_(9), P_(7), P_(0), ALU.mult, R, W_)
        TT("dve", P_(11), P_(11), P_(1), ALU.mult, R, W_)
        TT("dve", P_(9), P_(9), P_(11), ALU.subtract, R, W_)
        TT("dve", P_(9), P_(9), P_(10), ALU.mult, R, W_)
        LD(d5[:], s5dc[l], b_d5); LD(bg[:], bgluc[l], b_d5, group=True)
        if is_s:
            LD(fT[0:64, :], st_s5[l], b_fT)
            ps, bp = rr.get()
            transpose_to(ps[:, 0:64], fT[0:64, :], [b_fT], [bp], dt=F32, np_=64)
            CP("act", s0c[:], ps[:, 0:64], [bp], [b_s0c])
        Sched.PHASE = _p0 + 'u'
        for blk in range(2):
            w, bw = wload(w_in[l][:, C_U + blk * 256:C_U + (blk + 1) * 256], 8, 256)
            proj_fm(hT, b_hT, 8, w, bw, 256,
                    lambda cc, h, ps, bp, blk=blk: CP("act", uT[:, blk * 2 + cc, h * 512:(h + 1) * 512], ps[:, :], [bp], [b_uT]))
        ybk = [(psum[4], psb[4]), (psum[5], psb[5])]
        xbk = [(psum[6], psb[6]), (psum[7], psb[7])]
        nrep = T // Lt_
        v3 = (lambda a: a.rearrange("p (s x) -> p s x", s=nrep)) if nrep > 1 else (lambda a: a)
        Bc2 = [Bc, cv.take([2, 16])]; b_Bc2 = [b_Bc, NB("Bc_1")]; Bt2 = [Bt, cv.take([16])]; b_Bt2 = [b_Bt, NB("Bt_1")]
        Bx2 = [Bx, [cv.take([128], BF16), cv.take([128], BF16)]]; b_Bx2 = [b_Bx, [NB("Bx0_1"), NB("Bx1_1")]]
        BcL2 = [BcL, [cv.take([128], BF16), cv.take([128], BF16)]]; b_BcL2 = [b_BcL, [NB("BcL0_1"), NB("BcL1_1")]]
        Cx2 = [Cx, [cv.take([128], BF16), cv.take([128], BF16)]]; b_Cx2 = [b_Cx, [NB("Cx0_1"), NB("Cx1_1")]]
        CL2 = [CL, cv.take([4, 128], BF16)]; b_CL2 = [b_CL, NB("CL_1")]
        cos2 = [cosT, cv.take([Lt_])]; sin2 = [sinT, cv.take([Lt_])]; b_tab2 = [b_tab, NB("tab_1")]
        its = [(fc, d, q4) for fc in range(4) for d in range(2) for q4 in range(4)]
        NI = len(its)

        def stB(k):
            fc, d, q4 = its[k]
            z = k % 2
            if d == 0 and q4 == 0:
                for dd in range(2):
                    LD(Bst[:, dd * 2 + 0, :, :], s5bre[l, dd][fc * 512:(fc + 1) * 512, :].rearrange("(c p) m -> p c m", p=128), b_Bst, group=(dd > 0))
                    LD(Bst[:, dd * 2 + 1, :, :], s5bim[l, dd][fc * 512:(fc + 1) * 512, :].rearrange("(c p) m -> p c m", p=128), b_Bst, group=True)
                    LD(Cn[:, dd * 2 + 0, :], s5cre[l, dd][fc * 128:(fc + 1) * 128, :], b_Cn, group=(dd > 0))
                    LD(Cn[:, dd * 2 + 1, :], s5cim[l, dd][fc * 128:(fc + 1) * 128, :], b_Cn, group=True)
            c = fc * 4 + q4
            dc = d * 16 + c
            cre, cim = pc[:, 8, dc:dc + 1], pc[:, 9, dc:dc + 1]
            Bre, Bim = Bst[:, d * 2 + 0, q4, :], Bst[:, d * 2 + 1, q4, :]
            Bc_, bBc_, Bt_, bBt_ = Bc2[z], b_Bc2[z], Bt2[z], b_Bt2[z]
            TS("dve", Bt_[:], Bim, cim, ALU.mult, [b_Bst, b_pc], [bBt_])
            STT(Bc_[:, 0, :], Bre, cre, Bt_[:], ALU.mult, ALU.subtract, [b_Bst, b_pc, bBt_], [bBc_])
            TS("dve", Bt_[:], Bre, cim, ALU.mult, [b_Bst, b_pc], [bBt_])
            STT(Bc_[:, 1, :], Bim, cre, Bt_[:], ALU.mult, ALU.add, [b_Bst, b_pc, bBt_], [bBc_])
            for ri in range(2):
                TT("pool", Bx2[z][ri].rearrange("p (g m) -> p g m", g=8), maskB[:, q4, :].rearrange("p (g m) -> p g m", g=8),
                   Bc_[:, ri, :].unsqueeze(1).broadcast_to([128, 8, 16]), ALU.mult, [b_maskB, bBc_], [b_Bx2[z][ri]])
                ps, bp = rr.get()
                pv = ps[:].bitcast(BF16)[:, 0:128]
                transpose_to(pv, Bx2[z][ri][:], [b_Bx2[z][ri]], [bp])
                CP("act", BcL2[z][ri][:], pv, [bp], [b_BcL2[z][ri]])
            for ri in range(2):
                TT("pool", Cx2[z][ri].rearrange("p (g n) -> p g n", g=2), maskC[:, q4, :].rearrange("p (g n) -> p g n", g=2),
                   Cn[:, d * 2 + ri, :].unsqueeze(1).broadcast_to([128, 2, 64]), ALU.mult, [b_maskC, b_Cn], [b_Cx2[z][ri]])
                ps, bp = rr.get()
                pv = ps[:].bitcast(BF16)[:, 0:128]
                transpose_to(pv, Cx2[z][ri][:], [b_Cx2[z][ri]], [bp])
                CP("act", CL2[z][:, 2 * ri, :], pv, [bp], [b_CL2[z]])
                ACT(CL2[z][:, 2 * ri + 1, :], pv, AF.Copy, [bp], [b_CL2[z]], scale=-1.0)

        def stT1(k):
            fc, d, q4 = its[k]
            z = k % 2
            dc = d * 16 + fc * 4 + q4
            cT, sT, bt = cos2[z], sin2[z], [b_tab2[z]]
            TS("dve", cT[:], iota[:, 0:Lt_], pc[:, 4, dc:dc + 1], ALU.mult, [b_iota, b_pc], bt)
            TS("dve", sT[:].bitcast(I32), cT[:], 1.0 / (2 * math.pi), ALU.mult, bt, bt)
            CP("act", sT[:], sT[:].bitcast(I32), bt, bt)

        def stT2(k):
            z = k % 2
            cT, sT, bt = cos2[z], sin2[z], [b_tab2[z]]
            STT(cT[:], sT[:], -2 * math.pi, cT[:], ALU.mult, ALU.add, bt, bt)
            TS("dve", cT[:], cT[:], 3.14159, ALU.min, bt, bt, s2=-3.14159, op1=ALU.max)
            ACT(sT[:], cT[:], AF.Sin, bt, bt)
            ACT(cT[:], cT[:], AF.Abs, bt, bt)
            ACT(cT[:], cT[:], AF.Sin, bt, bt, scale=-1.0, bias=math.pi / 2)

        def stX(k):
            fc, d, q4 = its[k]
            z = k % 2
            c = fc * 4 + q4
            dc = d * 16 + c
            cosT_, sinT_, bt = cos2[z], sin2[z], b_tab2[z]
            for h in range(2):
                hs = slice(h * 512, (h + 1) * 512)
                for ri in range(2):
                    MM(xbk[ri][0][:, :], BcL2[z][ri][:], uT[:, fc, hs], True, True, [b_BcL2[z][ri], b_uT], [xbk[ri][1]])
                xre, xim = xbk[0][0][:, :], xbk[1][0][:, :]
                if Lt_ < 512:
                    nr2 = 512 // Lt_
                    cB = cosT_.unsqueeze(1).broadcast_to([128, nr2, Lt_]); sB = sinT_.unsqueeze(1).broadcast_to([128, nr2, Lt_])
                    vv = lambda a, nr2=nr2: a.rearrange("p (s x) -> p s x", s=nr2)
                else:
                    cB, sB = cosT_[:, hs], sinT_[:, hs]
                    vv = lambda a: a
                TT("dve", vv(xr[1][:, hs]), vv(xim), cB, ALU.mult, [xbk[1][1], bt], [b_xr[1]])
                TT("dve", vv(tmpy[:, hs]), vv(xre), sB, ALU.mult, [xbk[0][1], bt], bY)
                TT("pool", xr[1][:, hs], xr[1][:, hs], tmpy[:, hs], ALU.subtract if d == 0 else ALU.add, [b_xr[1]] + bY, [b_xr[1]])
                TT("dve", vv(xr[0][:, hs]), vv(xre), cB, ALU.mult, [xbk[0][1], bt], [b_xr[0]])
                TT("dve", vv(tmpx[:, hs]), vv(xim), sB, ALU.mult, [xbk[1][1], bt], bA)
                TT("pool", xr[0][:, hs], xr[0][:, hs], tmpx[:, hs], ALU.add if d == 0 else ALU.subtract, [b_xr[0]] + bA, [b_xr[0]])
            if is_s:
                sre = s0c[:, (d * 2 + 0) * 16 + c:(d * 2 + 0) * 16 + c + 1]; sim = s0c[:, (d * 2 + 1) * 16 + c:(d * 2 + 1) * 16 + c + 1]
                abre, abim = pc[:, 6, dc:dc + 1], pc[:, 7, dc:dc + 1]
                sm = small
                RS, WS_ = [b_small, b_s0c, b_pc, bt], [b_small]
                TT("dve", sm[:, 20:21], sre, abre, ALU.mult, RS, WS_); TT("dve", sm[:, 21:22], sim, abim, ALU.mult, RS, WS_)
                TT("dve", sm[:, 22:23], sm[:, 20:21], sm[:, 21:22], ALU.subtract, RS, WS_)
                TT("dve", sm[:, 20:21], sre, abim, ALU.mult, RS, WS_); TT("dve", sm[:, 21:22], sim, abre, ALU.mult, RS, WS_)
                TT("dve", sm[:, 23:24], sm[:, 20:21], sm[:, 21:22], ALU.add, RS, WS_)
                if d == 0:
                    TT("dve", xr[0][:, 0:1], xr[0][:, 0:1], sm[:, 22:23], ALU.add, [b_xr[0], b_small], [b_xr[0]])
                    TT("dve", xr[1][:, 0:1], xr[1][:, 0:1], sm[:, 23:24], ALU.add, [b_xr[1], b_small], [b_xr[1]])
                else:
                    cl, sl = cosT_[:, L - 1:L], sinT_[:, L - 1:L]
                    TT("dve", sm[:, 20:21], sm[:, 22:23], cl, ALU.mult, RS, WS_); TT("dve", sm[:, 21:22], sm[:, 23:24], sl, ALU.mult, RS, WS_)
                    TT("dve", sm[:, 24:25], sm[:, 20:21], sm[:, 21:22], ALU.subtract, RS, WS_)
                    TT("dve", sm[:, 20:21], sm[:, 22:23], sl, ALU.mult, RS, WS_); TT("dve", sm[:, 21:22], sm[:, 23:24], cl, ALU.mult, RS, WS_)
                    TT("dve", sm[:, 25:26], sm[:, 20:21], sm[:, 21:22], ALU.add, RS, WS_)
                    TT("dve", xr[0][:, L - 1:L], xr[0][:, L - 1:L], sm[:, 24:25], ALU.add, [b_xr[0], b_small], [b_xr[0]])
                    TT("dve", xr[1][:, L - 1:L], xr[1][:, L - 1:L], sm[:, 25:26], ALU.add, [b_xr[1], b_small], [b_xr[1]])

        def stS(k):
            fc, d, q4 = its[k]
            z = k % 2
            dc = d * 16 + fc * 4 + q4
            cosT_, sinT_, bt = cos2[z], sin2[z], b_tab2[z]
            if is_s:
                rm_ = pc[:, 5, dc:dc + 1].broadcast_to([128, T]); rm_r = rm_; brm = [b_pc]
            else:
                TS("dve", rmt, (rmF if d == 0 else rmB)[:], pc[:, 5, dc:dc + 1], ALU.mult, [b_rmF, b_rmB, b_pc], bB)
                rm_ = rmt; rm_r = rmt[:, ::-1]; brm = bB
            for ri in (1, 0):
                if d == 0:
                    S.op("dve", lambda e, ri=ri, rm_=rm_: e.tensor_tensor_scan(out=xr[ri][:], data0=rm_, data1=xr[ri][:], initial=0.0, op0=ALU.mult, op1=ALU.add),
                         brm + [b_xr[ri]], [b_xr[ri]])
                else:
                    S.op("dve", lambda e, ri=ri, rm_r=rm_r: e.tensor_tensor_scan(out=xr[ri][:, ::-1], data0=rm_r, data1=xr[ri][:, ::-1], initial=0.0, op0=ALU.mult, op1=ALU.add),
                         brm + [b_xr[ri]], [b_xr[ri]])
            if not is_s:
                lpos = L - 1 if d == 0 else 0
                for ri in range(2):
                    w3 = xr[ri].rearrange("p (s x) -> p s x", s=nseq)[:, :, lpos:lpos + 1].rearrange("p s x -> p (s x)")
                    CP("act", wcap[:, ri, dc, :], w3, [b_xr[ri]], [b_wcap])
                CP("act", tcap[:, 0, dc:dc + 1], cosT_[:, lpos:lpos + 1], [bt], [b_wcap])
                CP("act", tcap[:, 1, dc:dc + 1], sinT_[:, lpos:lpos + 1], [bt], [b_wcap])

        def stP(k):
            fc, d, q4 = its[k]
            z = k % 2
            cosT_, sinT_, bt = cos2[z], sin2[z], b_tab2[z]
            cosB = cosT_.unsqueeze(1).broadcast_to([128, nrep, Lt_]) if nrep > 1 else cosT_
            sinB = sinT_.unsqueeze(1).broadcast_to([128, nrep, Lt_]) if nrep > 1 else sinT_
            TT("dve", v3(pr[1]), v3(xr[1][:]), sinB, ALU.mult, [b_xr[1], bt], [b_pr[1]])
            TT("pool", v3(pr[0]), v3(xr[0][:]), cosB, ALU.mult, [b_xr[0], bt], [b_pr[0]])
            TT("dve", v3(pr[3]), v3(xr[1][:]), cosB, ALU.mult, [b_xr[1], bt], [b_pr[3]])
            TT("pool", v3(pr[2]), v3(xr[0][:]), sinB, ALU.mult, [b_xr[0], bt], [b_pr[2]])
            sel = [0, 1, 3, 3] if d == 0 else [0, 0, 2, 3]
            first_y = (d == 0 and q4 == 0)
            last_y = (d == 1 and q4 == 3)
            for h in range(2):
                hs = slice(h * 512, (h + 1) * 512)
                for k4 in range(4):
                    MM(ybk[h][0][:, :], CL2[z][:, sel[k4], :], pr[k4][:, hs], first_y and k4 == 0, last_y and k4 == 3, [b_CL2[z], b_pr[k4]], [ybk[h][1]])
            if last_y:
                for h in range(2):
                    hs = slice(h * 512, (h + 1) * 512)
                    STT(uT[:, fc, hs], uT[:, fc, hs], d5[:, fc:fc + 1], ybk[h][0][:, :], ALU.mult, ALU.add, [b_uT, b_d5, ybk[h][1]], [b_uT])

        Sched.PHASE = _p0 + 'L'
        stB(0); stT1(0); stT2(0)
        for k in range(NI):
            if k + 1 < NI:
                stB(k + 1); stT1(k + 1)
            stX(k)
            if k + 1 < NI:
                stT2(k + 1)
            stS(k)
            stP(k)
        if not is_s:
            cB_ = tcap[:, 0, :].unsqueeze(2).broadcast_to([128, 32, 4]); sB_ = tcap[:, 1, :].unsqueeze(2).broadcast_to([128, 32, 4])
            RW = [b_wcap]
            TT("dve", wtmp[:, 0, :, :], wcap[:, 0, :, :], cB_, ALU.mult, RW, RW)
            TT("dve", wtmp[:, 1, :, :], wcap[:, 1, :, :], sB_, ALU.mult, RW, RW)
            TT("dve", wtmp[:, 2, :, :], wcap[:, 0, :, :], sB_, ALU.mult, RW, RW)
            TT("dve", wtmp[:, 3, :, :], wcap[:, 1, :, :], cB_, ALU.mult, RW, RW)
            f4 = finS.rearrange("p (s d r c) -> p s d r c", s=4, d=2, r=2)
            for d in range(2):
                src = lambda k, d=d: wtmp[:, k, d * 16:(d + 1) * 16, :].rearrange("p c s -> p s c")
                TT("dve", f4[:, :, d, 0, :], src(0), src(1), ALU.subtract if d == 0 else ALU.add, RW, [b_finS])
                TT("dve", f4[:, :, d, 1, :], src(3), src(2), ALU.add if d == 0 else ALU.subtract, RW, [b_finS])
            for half in range(2):
                ps, bp = rr.get()
                transpose_to(ps[:, 0:128], finS[:, half * 128:(half + 1) * 128], [b_finS], [bp], dt=F32)
                CP("act", fT[:], ps[:, 0:128], [bp], [b_fT])
                for s2 in range(2):
                    STO(o_s5[half * 2 + s2, l], fT[s2 * 64:(s2 + 1) * 64, :], b_fT)
        Sched.PHASE = _p0 + 'G'
        for blk in range(2):
            w, bw = wload(w_glu[l][:, blk * 256:(blk + 1) * 256], 4, 256)
            w2, bw2 = wload(w_glu[l][:, (blk + 2) * 256:(blk + 3) * 256], 4, 256)
            for c2 in range(2):
                cc = blk * 2 + c2
                for h in range(2):
                    hs = slice(h * 512, (h + 1) * 512)
                    psg, bpg = rr.get()
                    for k in range(4):
                        MM(psg[:, :], w2[:, k, c2 * 128:(c2 + 1) * 128], y5T[:, k, hs], k == 0, k == 3, [b_y5, bw2], [bpg])
                    ACT(sg[:], psg[:, :], AF.Sigmoid, [bpg, b_d5], [b_sg], bias=bg[:, 4 + cc:5 + cc])
                    psv, bpv = rr.get()
                    for k in range(4):
                        MM(psv[:, :], w[:, k, c2 * 128:(c2 + 1) * 128], y5T[:, k, hs], k == 0, k == 3, [b_y5, bw], [bpv])
                    yt_, by_ = yT(1, cc)
                    STT(yt_[:, hs], psv[:, :], bg[:, cc:cc + 1], sg[:], ALU.add, ALU.mult, [bpv, b_d5, b_sg], [by_])
        S.op("pool", lambda e: e.memset(prow[:, 0:1], 0.0), [], ([b_uT, b_y5, b_prow, b_pc, b_pci, b_Bst, b_Cn, b_Bc, b_Bt, b_CL, b_tab, b_d5, b_finS, b_s0c, b_sg, b_fT]
             + b_Bx + b_BcL + b_Cx + b_xr + b_pr + (bY if not is_s else []) + [b_iota, b_rmF, b_rmB, b_wcap]
             + [b_Bc2[1], b_Bt2[1], b_CL2[1], b_tab2[1]] + b_Bx2[1] + b_BcL2[1] + b_Cx2[1]) + [b_scr_all])

    def rms_groups(ps, bp, ncols, gain_bc, b_gain, qf, b_qf, sq, b_sq, rs, b_rs, t, rope):
        ng = ncols // 64
        CP("act", qf[:, 0:ncols], ps[:, 0:ncols], [bp], [b_qf])
        TT("dve", sq[:, 0:ncols], qf[:, 0:ncols], qf[:, 0:ncols], ALU.mult, [b_qf], [b_sq])
        S.op("dve", lambda e: e.tensor_reduce(out=rs[:, 0:ng], in_=sq[:, 0:ncols].rearrange("p (g x) -> p g x", g=ng), op=ALU.add, axis=AX.X), [b_sq], [b_rs])
        rstd_from_ss(rs[:, 0:ng], 64, rs[:, 0:ng], [b_rs], [b_rs])
        q3 = qf[:, 0:ncols].rearrange("p (g x) -> p g x", g=ng)
        TT("dve", q3, q3, rs[:, 0:ng].unsqueeze(2).broadcast_to([128, ng, 64]), ALU.mult, [b_qf, b_rs], [b_qf])
        TT("dve", q3, q3, gain_bc.unsqueeze(1).broadcast_to([128, ng, 64]), ALU.mult, [b_qf, b_gain], [b_qf])
        if rope:
            s3 = sq[:, 0:ncols].rearrange("p (g a q f) -> p (g a) q f", g=ng, a=2, q=2)
            x4 = qf[:, 0:ncols].rearrange("p (g a q f) -> p (g a) q f", g=ng, a=2, q=2)
            S4 = ropeS[:, t, :].rearrange("p (a q f) -> p a q f", a=2, q=2)
            for pz in range(2):
                TT("dve", s3[:, :, pz, :].rearrange("p (g a) f -> p g a f", g=ng), x4[:, :, 1 - pz, :].rearrange("p (g a) f -> p g a f", g=ng),
                   S4[:, :, pz, :].unsqueeze(1).broadcast_to([128, ng, 2, 16]), ALU.mult, [b_qf, b_ropeS], [b_sq])
            TT("dve", q3, q3, ropeC[:, t, :].unsqueeze(1).broadcast_to([128, ng, 64]), ALU.mult, [b_qf, b_ropeC], [b_qf])
            TT("dve", qf[:, 0:ncols], qf[:, 0:ncols], sq[:, 0:ncols], ALU.add, [b_qf, b_sq], [b_qf])

    def branch_diff(l, path, nseq, L, nt, is_s):
        barrier_begin()
        _p0 = Sched.PHASE
        cv = Carve()
        nk_ctx = 2 if is_s else 0
        NKT = 8 + nk_ctx
        qT = cv.take([4, T], BF16); b_qT = NB("qT")
        kT = cv.take([4, NKT * 128], BF16); b_kT = NB("kT")
        vaug = cv.take([NKT, 4, 130], BF16); b_va = NB("vaug")
        qfL = [cv.take([512]) for _ in range(2)]; b_qfL = [NB(f"qf{i}") for i in range(2)]
        sqL = [cv.take([512]) for _ in range(2)]; b_sqL = [NB(f"sq{i}") for i in range(2)]
        rsL = [cv.take([16]) for _ in range(2)]; b_rsL = [NB(f"rs{i}") for i in range(2)]
        rot_ = [0]

        def nxt():
            i = rot_[0] % 2; rot_[0] += 1
            return qfL[i], b_qfL[i], sqL[i], b_sqL[i], rsL[i], b_rsL[i]
        qb = [cv.take([512], BF16), cv.take([512], BF16)]; b_qb = [NB("qb0"), NB("qb1")]
        gq = cv.take([64]); gk = cv.take([64]); b_g = NB("dg")
        lamt = cv.take([4, 64]); lamc = cv.take([8]); b_lam = NB("lam")
        o1 = cv.take([4, 128]); b_o1 = NB("o1")
        odn = cv.take([8, 512], BF16); b_odn = NB("odn")
        pT = [cv.take([512], BF16) for _ in range(4)]; b_pT = [NB(f"pT{i}") for i in range(4)]
        rd = cv.take([8]); b_rd = NB("rd")
        oh = cv.take([128]); b_oh = NB("oh")
        gsub = cv.take([1]); b_gsub = NB("gsub")
        kstL = [cv.take([512]) for _ in range(2)]; b_kstL = [NB(f"kst{i}") for i in range(2)]
        kst, b_kst = kstL[0], b_kstL[0]
        lam_init = 0.8 - 0.6 * math.exp(-0.3 * l)
        S.op("pool", lambda e: e.memset(gq[:], 0.0), [b_scr_all], [b_g, b_scr_all])
        LD(gq[:], dqg[l:l + 1, :].partition_broadcast(128), b_g); LD(gk[:], dkg[l:l + 1, :].partition_broadcast(128), b_g, group=True)
        TS("dve", gq[:], gq[:], 0.125, ALU.mult, [b_g], [b_g])
        LD(lamt[:].rearrange("p a b -> p (a b)"), dlam[l:l + 1, :].partition_broadcast(128), b_lam)
        LD(gsub[:], dsubc[l], b_gsub)
        TS("dve", gsub[:], gsub[:], 1.0 - lam_init, ALU.mult, [b_gsub], [b_gsub])
        TT("dve", lamt[:, 0, :], lamt[:, 0, :], lamt[:, 1, :], ALU.mult, [b_lam], [b_lam])
        TT("dve", lamt[:, 2, :], lamt[:, 2, :], lamt[:, 3, :], ALU.mult, [b_lam], [b_lam])
        S.op("dve", lambda e: e.tensor_reduce(out=lamc[:, 0:1], in_=lamt[:, 0, :], op=ALU.add, axis=AX.X), [b_lam], [b_lam])
        S.op("dve", lambda e: e.tensor_reduce(out=lamc[:, 1:2], in_=lamt[:, 2, :], op=ALU.add, axis=AX.X), [b_lam], [b_lam])
        ACT(lamc[:, 0:2], lamc[:, 0:2], AF.Exp, [b_lam], [b_lam])
        TT("dve", lamc[:, 2:3], lamc[:, 0:1], lamc[:, 1:2], ALU.subtract, [b_lam], [b_lam])
        TS("dve", lamc[:, 2:3], lamc[:, 2:3], lam_init, ALU.add, [b_lam], [b_lam], s2=-1.0, op1=ALU.mult)
        MSET("pool", vaug[:].rearrange("p a b c -> p (a b c)"), 1.0, [b_va])
        if sub == 1:
            raise _Stop()
        Sched.PHASE = _p0 + 'A'
        stagesA = []
        for which in range(2):
            col0 = C_DQ if which == 0 else C_DK
            wd = {}
            for t in range(8):
                st = {}

                def FA(st=st, which=which, t=t, wd=wd, col0=col0):
                    if t == 0:
                        wd["A"] = wload(w_in[l][:, col0:col0 + 256], 8, 256)
                        wd["B"] = wload(w_in[l][:, col0 + 256:col0 + 512], 8, 256)
                    wA, bwA = wd["A"]; wB, bwB = wd["B"]
                    ps, bp = rr.get()
                    for k in range(8):
                        MM(ps[:, 0:256], hT[:, k, t * 128:(t + 1) * 128], wA[:, k, :], k == 0, k == 7, [b_hT, bwA], [bp])
                    for k in range(8):
                        MM(ps[:, 256:512], hT[:, k, t * 128:(t + 1) * 128], wB[:, k, :], k == 0, k == 7, [b_hT, bwB], [bp])
                    qf, b_qf, sq, b_sq, rs, b_rs = nxt()
                    kst, b_kst = kstL[t % 2], b_kstL[t % 2]
                    if which == 1 and not is_s:
                        rms_groups(ps, bp, 512, gk[:], b_g, kst, b_kst, sq, b_sq, rs, b_rs, t, False)
                        STO(o_dk[t // 2, l, (t % 2) * 128:(t % 2 + 1) * 128, :], kst[:], b_kst)
                        st["src"] = (kst, b_kst)
                    else:
                        rms_groups(ps, bp, 512, (gq if which == 0 else gk)[:], b_g, qf, b_qf, sq, b_sq, rs, b_rs, t, is_s)
                        st["src"] = (qf, b_qf)

                def FB(st=st, which=which, t=t):
                    src, bsrc = st["src"]
                    qb_, bqb_ = qb[t % 2], b_qb[t % 2]
                    CP("pool", qb_[:], src[:], [bsrc], [bqb_])
                    for j4 in range(4):
                        ps2, bp2 = rr.get()
                        pv = ps2[:].bitcast(BF16)[:, 0:128]
                        transpose_to(pv, qb_[:, j4 * 128:(j4 + 1) * 128], [bqb_], [bp2])
                        if which == 0:
                            CP("act", qT[:, j4, t * 128:(t + 1) * 128], pv, [bp2], [b_qT])
                        else:
                            CP("act", kT[:, j4, (nk_ctx + t) * 128:(nk_ctx + t + 1) * 128], pv, [bp2], [b_kT])
                stagesA.append((FA, FB))
        stagesA[0][0]()
        for k in range(len(stagesA)):
            if k + 1 < len(stagesA):
                stagesA[k + 1][0]()
            stagesA[k][1]()
        if is_s:
            for kt in range(2):
                LD(kst[:], cdk[l, kt * 128:(kt + 1) * 128, :], b_kst)
                CP("pool", qb[0][:], kst[:], [b_kst], [b_qb[0]])
                for j4 in range(4):
                    ps2, bp2 = rr.get()
                    pv = ps2[:].bitcast(BF16)[:, 0:128]
                    transpose_to(pv, qb[0][:, j4 * 128:(j4 + 1) * 128], [b_qb[0]], [bp2])
                    CP("act", kT[:, j4, kt * 128:(kt + 1) * 128], pv, [bp2], [b_kT])
                LD(kst[:], cdv[l, kt * 128:(kt + 1) * 128, :], b_kst)
                CP("pool", vaug[:, kt, :, 0:128], kst[:].rearrange("p (h e) -> p h e", h=4), [b_kst], [b_va])
        Sched.PHASE = _p0 + 'B'
        wA, bwA = wload(w_in[l][:, C_DV:C_DV + 256], 8, 256)
        wB, bwB = wload(w_in[l][:, C_DV + 256:C_DV + 512], 8, 256)
        for t in range(8):
            ps, bp = rr.get()
            for k in range(8):
                MM(ps[:, 0:256], hT[:, k, t * 128:(t + 1) * 128], wA[:, k, :], k == 0, k == 7, [b_hT, bwA], [bp])
            for k in range(8):
                MM(ps[:, 256:512], hT[:, k, t * 128:(t + 1) * 128], wB[:, k, :], k == 0, k == 7, [b_hT, bwB], [bp])
            if sub != 31:
                kst, b_kst = kstL[t % 2], b_kstL[t % 2]
            CP("act", vaug[:, nk_ctx + t, :, 0:128], ps[:, :].rearrange("p (h e) -> p h e", h=4), [bp], [b_va])
            if not is_s and sub != 32:
                CP("dve", kst[:], ps[:, :], [bp], [b_kst])
                STO(o_dv[t // 2, l, (t % 2) * 128:(t % 2 + 1) * 128, :], kst[:], b_kst)
        if sub in (3, 31, 32):
            raise _Stop()
        Sched.PHASE = _p0 + 'C'
        obk = [(psum[4 + i], psb[4 + i]) for i in range(4)]
        pc_ = [0]
        stages = []
        for s in range(nseq):
            keyt = list(range(nk_ctx)) + [nk_ctx + s * nt + i for i in range(nt)]
            nq = min(L, 512)
            for qc in range(L // nq):
                q0 = s * L + qc * nq
                nqt = nq // 128
                for h in range(4):
                    for c in range(2):
                        ksl = slice(c * 64, (c + 1) * 64)
                        for ki, kt in enumerate(keyt):
                            st = {}

                            def A(st=st, ksl=ksl, h=h, kt=kt, q0=q0, nq=nq):
                                psS, bpS = rr.get()
                                MM(psS[:, 0:nq], kT[ksl, h, kt * 128:(kt + 1) * 128], qT[ksl, h, q0:q0 + nq], True, True, [b_kT, b_qT], [bpS])
                                z = pc_[0] % 4; pc_[0] += 1
                                st["z"] = z
                                ACT(pT[z][:, 0:nq], psS[:, 0:nq], AF.Exp, [bpS], [b_pT[z]])

                            def B(st=st, h=h, c=c, kt=kt, ki=ki, nk=len(keyt), nqt=nqt, q0=q0):
                                z = st["z"]
                                for qt in range(nqt):
                                    MM(obk[qt][0][:, 0:129], pT[z][:, qt * 128:(qt + 1) * 128], vaug[:, kt, h, 0:129], ki == 0, ki == nk - 1, [b_pT[z], b_va], [obk[qt][1]])
                                if ki != nk - 1:
                                    return
                                for qt in range(nqt):
                                    tq = q0 // 128 + qt
                                    ob, bob = obk[qt]
                                    S.op("dve", lambda e, ob=ob, qt=qt, c=c: e.reciprocal(out=rd[:, qt * 2 + c:qt * 2 + c + 1], in_=ob[:, 128:129]), [bob], [b_rd])
                                    if c == 0:
                                        TS("dve", o1[:, qt, :], ob[:, 0:128], rd[:, qt * 2:qt * 2 + 1], ALU.mult, [bob, b_rd], [b_o1])
                                    else:
                                        TT("dve", rd[:, qt * 2 + 1:qt * 2 + 2], rd[:, qt * 2 + 1:qt * 2 + 2], lamc[:, 2:3], ALU.mult, [b_rd, b_lam], [b_rd])
                                        STT(oh[:], ob[:, 0:128], rd[:, qt * 2 + 1:qt * 2 + 2], o1[:, qt, :], ALU.mult, ALU.add, [bob, b_rd, b_o1], [b_oh])
                                        ACT(junk[:, 0:128], oh[:], AF.Square, [b_oh], [b_junk, b_small], accum=small[:, 40:41])
                                        rstd_from_ss(small[:, 40:41], 128, small[:, 41:42], [b_small], [b_small])
                                        TS("dve", odn[:, tq, h * 128:(h + 1) * 128], oh[:], small[:, 41:42], ALU.mult, [b_oh, b_small], [b_odn])
                            stages.append((A, B))
        LA = 3
        for k in range(min(LA, len(stages))):
            stages[k][0]()
        for k in range(len(stages)):
            if k + LA < len(stages):
                stages[k + LA][0]()
            stages[k][1]()
        if sub == 4:
            raise _Stop()
        Sched.PHASE = _p0 + 'D'
        for t in range(8):
            for h in range(4):
                ps2, bp2 = rr.get()
                pv = ps2[:].bitcast(BF16)[:, 0:128]
                transpose_to(pv, odn[:, t, h * 128:(h + 1) * 128], [b_odn], [bp2])
                yt_, by_ = yT(2, h)
                ACT(yt_[:, t * 128:(t + 1) * 128], pv, AF.Copy, [bp2, b_gsub], [by_], scale=gsub[:, 0:1])
        S.op("pool", lambda e: e.memset(gq[:, 0:1], 0.0), [], ([b_qT, b_kT, b_va, b_g, b_lam, b_o1, b_odn, b_rd, b_oh, b_gsub] + b_kstL + b_qb + b_pT + b_qfL + b_sqL + b_rsL) + [b_scr_all])

    def branch_win(l, path, nseq, L, nt, is_s):
        barrier_begin()
        cv = Carve()
        nk_ctx = 2 if is_s else 0
        NKT = 8 + nk_ctx
        qT = cv.take([4, T], BF16); b_qT = NB("wqT")
        kT = cv.take([2, NKT * 128], BF16); b_kT = NB("wkT")
        vaug = cv.take([NKT, 2, 66], BF16); b_va = NB("wvaug")
        qfL = [cv.take([512]) for _ in range(2)]; b_qfL = [NB(f"wqf{i}") for i in range(2)]
        sqL = [cv.take([512]) for _ in range(2)]; b_sqL = [NB(f"wsq{i}") for i in range(2)]
        rsL = [cv.take([16]) for _ in range(2)]; b_rsL = [NB(f"wrs{i}") for i in range(2)]
        rot_ = [0]

        def nxt():
            i = rot_[0] % 2; rot_[0] += 1
            return qfL[i], b_qfL[i], sqL[i], b_sqL[i], rsL[i], b_rsL[i]
        qb = [cv.take([512], BF16), cv.take([512], BF16)]; b_qb = [NB("wqb0"), NB("wqb1")]
        gq = cv.take([64]); gk = cv.take([64]); b_g = NB("wg")
        snk = cv.take([8]); b_snk = NB("snk")
        on = cv.take([8, 512], BF16); b_on = NB("won")
        pT = [cv.take([512], BF16) for _ in range(4)]; b_pT = [NB(f"wpT{i}") for i in range(4)]
        rd = cv.take([8]); b_rd = NB("wrd")
        kst = cv.take([256]); b_kst = NB("wkst")
        S.op("pool", lambda e: e.memset(gq[:], 0.0), [b_scr_all], [b_g, b_scr_all])
        LD(gq[:], wqg[l:l + 1, :].partition_broadcast(128), b_g); LD(gk[:], wkg[l:l + 1, :].partition_broadcast(128), b_g, group=True)
        TS("dve", gq[:], gq[:], 0.125, ALU.mult, [b_g], [b_g])
        LD(snk[:], wsink[l:l + 1, :].partition_broadcast(128), b_snk)
        ACT(snk[:], snk[:], AF.Exp, [b_snk], [b_snk])
        MSET("pool", vaug[:].rearrange("p a b c -> p (a b c)"), 1.0, [b_va])
        wA, bwA = wload(w_in[l][:, C_WQ:C_WQ + 256], 8, 256)
        wB, bwB = wload(w_in[l][:, C_WQ + 256:C_WQ + 512], 8, 256)
        stq = []
        for t in range(8):
            st = {}

            def QA(st=st, t=t):
                ps, bp = rr.get()
                for k in range(8):
                    MM(ps[:, 0:256], hT[:, k, t * 128:(t + 1) * 128], wA[:, k, :], k == 0, k == 7, [b_hT, bwA], [bp])
                for k in range(8):
                    MM(ps[:, 256:512], hT[:, k, t * 128:(t + 1) * 128], wB[:, k, :], k == 0, k == 7, [b_hT, bwB], [bp])
                qf, b_qf, sq, b_sq, rs, b_rs = nxt()
                rms_groups(ps, bp, 512, gq[:], b_g, qf, b_qf, sq, b_sq, rs, b_rs, t, is_s)
                st["q"] = (qf, b_qf)

            def QB(st=st, t=t):
                qf, b_qf = st["q"]
                qb_, bqb_ = qb[t % 2], b_qb[t % 2]
                CP("pool", qb_[:], qf[:], [b_qf], [bqb_])
                for j4 in range(4):
                    ps2, bp2 = rr.get()
                    pv = ps2[:].bitcast(BF16)[:, 0:128]
                    transpose_to(pv, qb_[:, j4 * 128:(j4 + 1) * 128], [bqb_], [bp2])
                    CP("act", qT[:, j4, t * 128:(t + 1) * 128], pv, [bp2], [b_qT])
            stq.append((QA, QB))
        stq[0][0]()
        for k in range(8):
            if k + 1 < 8:
                stq[k + 1][0]()
            stq[k][1]()
        wK, bwK = wload(w_in[l][:, C_WK:C_WK + 256], 8, 256)

        def put_k(src_f32, bsrc, ktile):
            for n in range(2):
                CP("dve", qb[n][:, 0:128].rearrange("p (r d) -> p r d", r=2), src_f32[:, n * 64:(n + 1) * 64].unsqueeze(1).broadcast_to([128, 2, 64]), [bsrc], [b_qb[n]])
                ps2, bp2 = rr.get()
                pv = ps2[:].bitcast(BF16)[:, 0:128]
                transpose_to(pv, qb[n][:, 0:128], [b_qb[n]], [bp2])
                CP("act", kT[:, n, ktile * 128:(ktile + 1) * 128], pv, [bp2], [b_kT])
        for t in range(8):
            ps, bp = rr.get()
            for k in range(8):
                MM(ps[:, 0:256], hT[:, k, t * 128:(t + 1) * 128], wK[:, k, :], k == 0, k == 7, [b_hT, bwK], [bp])
            CP("act", vaug[:, nk_ctx + t, :, 0:64], ps[:, 128:256].rearrange("p (n e) -> p n e", n=2), [bp], [b_va])
            qf, b_qf, sq, b_sq, rs, b_rs = nxt()
            if not is_s:
                CP("dve", kst[:, 128:256], ps[:, 128:256], [bp], [b_kst])
                STO(o_wv[t // 2, l, (t % 2) * 128:(t % 2 + 1) * 128, :], kst[:, 128:256], b_kst)
                rms_groups(ps, bp, 128, gk[:], b_g, kst, b_kst, sq, b_sq, rs, b_rs, t, False)
                STO(o_wk[t // 2, l, (t % 2) * 128:(t % 2 + 1) * 128, :], kst[:, 0:128], b_kst)
                put_k(kst, b_kst, nk_ctx + t)
            else:
                rms_groups(ps, bp, 128, gk[:], b_g, qf, b_qf, sq, b_sq, rs, b_rs, t, True)
                put_k(qf, b_qf, nk_ctx + t)
        if is_s:
            for kt in range(2):
                LD(kst[:, 0:128], cwk[l, kt * 128:(kt + 1) * 128, :], b_kst)
                put_k(kst, b_kst, kt)
                LD(kst[:, 128:256], cwv[l, kt * 128:(kt + 1) * 128, :], b_kst)
                CP("pool", vaug[:, kt, :, 0:64], kst[:, 128:256].rearrange("p (n e) -> p n e", n=2), [b_kst], [b_va])
        obk = [(psum[4 + i], psb[4 + i]) for i in range(4)]
        pc_ = [0]
        stages = []

        def evac(ob, bob, qt, tq, h):
            TT("dve", rd[:, qt:qt + 1], ob[:, 64:65], snk[:, h:h + 1], ALU.add, [bob, b_snk], [b_rd])
            S.op("dve", lambda e, qt=qt: e.reciprocal(out=rd[:, qt:qt + 1], in_=rd[:, qt:qt + 1]), [b_rd], [b_rd])
            TS("dve", on[:, tq, h * 64:(h + 1) * 64], ob[:, 0:64], rd[:, qt:qt + 1], ALU.mult, [bob, b_rd], [b_on])
        for h in range(8):
            n = h // 4
            j4 = h // 2
            bsl = slice((h % 2) * 64, (h % 2 + 1) * 64)
            if not is_s:
                for s in range(nseq):
                    q0 = s * L
                    keyt = [s * nt + i for i in range(nt)]
                    for ki, kt in enumerate(keyt):
                        st = {}

                        def A(st=st, bsl=bsl, n=n, j4=j4, kt=kt, q0=q0):
                            psS, bpS = rr.get()
                            MM(psS[:, 0:L], kT[bsl, n, kt * 128:(kt + 1) * 128], qT[bsl, j4, q0:q0 + L], True, True, [b_kT, b_qT], [bpS])
                            z = pc_[0] % 4; pc_[0] += 1
                            st["z"] = z
                            ACT(pT[z][:, 0:L], psS[:, 0:L], AF.Exp, [bpS], [b_pT[z]])

                        def B(st=st, n=n, kt=kt, ki=ki, nk=len(keyt), s=s, h=h):
                            z = st["z"]
                            for qt in range(nt):
                                MM(obk[qt][0][:, 0:65], pT[z][:, qt * 128:(qt + 1) * 128], vaug[:, kt, n, 0:65], ki == 0, ki == nk - 1, [b_pT[z], b_va], [obk[qt][1]])
                            if ki == nk - 1:
                                for qt in range(nt):
                                    evac(obk[qt][0], obk[qt][1], qt, s * nt + qt, h)
                        stages.append((A, B))
            else:
                for tq in range(8):
                    qt = tq % 4
                    keys = [(0, None), (1, None)]
                    if tq > 0:
                        keys.append((nk_ctx + tq - 1, "prev"))
                    keys.append((nk_ctx + tq, None))
                    if tq < 7:
                        keys.append((nk_ctx + tq + 1, "next"))
                    for ki, (kt, msk) in enumerate(keys):
                        st = {}

                        def A(st=st, bsl=bsl, n=n, j4=j4, kt=kt, tq=tq, msk=msk):
                            psS, bpS = rr.get()
                            MM(psS[:, 0:128], kT[bsl, n, kt * 128:(kt + 1) * 128], qT[bsl, j4, tq * 128:(tq + 1) * 128], True, True, [b_kT, b_qT], [bpS])
                            z = pc_[0] % 4; pc_[0] += 1
                            st["z"] = z
                            ACT(pT[z][:, 0:128], psS[:, 0:128], AF.Exp, [bpS], [b_pT[z]])
                            if msk is not None:
                                TT("dve", pT[z][:, 0:128], pT[z][:, 0:128], (tril if msk == "prev" else triu)[:], ALU.mult, [b_pT[z], b_tril, b_triu], [b_pT[z]])

                        def B(st=st, n=n, kt=kt, ki=ki, nk=len(keys), qt=qt, tq=tq, h=h):
                            z = st["z"]
                            ob, bob = obk[qt]
                            MM(ob[:, 0:65], pT[z][:, 0:128], vaug[:, kt, n, 0:65], ki == 0, ki == nk - 1, [b_pT[z], b_va], [bob])
                            if ki == nk - 1:
                                evac(ob, bob, qt, tq, h)
                        stages.append((A, B))
        LA = 3
        for k in range(min(LA, len(stages))):
            stages[k][0]()
        for k in range(len(stages)):
            if k + LA < len(stages):
                stages[k + LA][0]()
            stages[k][1]()
        for t in range(8):
            for j4 in range(4):
                ps2, bp2 = rr.get()
                pv = ps2[:].bitcast(BF16)[:, 0:128]
                transpose_to(pv, on[:, t, j4 * 128:(j4 + 1) * 128], [b_on], [bp2])
                yt_, by_ = yT(3, j4)
                CP("act", yt_[:, t * 128:(t + 1) * 128], pv, [bp2], [by_])
        S.op("pool", lambda e: e.memset(gq[:, 0:1], 0.0), [], ([b_qT, b_kT, b_va, b_g, b_snk, b_on, b_rd, b_kst] + b_qb + b_pT + b_qfL + b_sqL + b_rsL) + [b_scr_all])

    def merge(l):
        rr.set(range(8))
        barrier_begin()
        cv = Carve()
        mT = cv.take([8, T], BF16); b_mT = [NB(f"mT{c}") for c in range(8)]
        gs = [cv.take([512]) for _ in range(4)]; b_gs = [NB(f"gs{i}") for i in range(4)]
        tmp = [cv.take([512]) for _ in range(4)]; b_tmp = [NB(f"mt{i}") for i in range(4)]
        bgt = cv.take([32]); b_bgt = NB("bgt")
        S.op("pool", lambda e: e.memset(bgt[:], 0.0), [b_scr_all], [b_bgt, b_scr_all])
        LD(bgt[:], bgatec[l], b_bgt)
        kq = [0]
        for dcp in range(4):
            for br in range(4):
                wg, bwg = wload(w_gate[l][:, br * 1024 + dcp * 256: br * 1024 + (dcp + 1) * 256], 8, 256)
                wb_, bwb_ = wload(w_br[l][br * 512:(br + 1) * 512, dcp * 256:(dcp + 1) * 256], 4, 256)
                for c2 in range(2):
                    dc = dcp * 2 + c2
                    for h in range(2):
                        hs = slice(h * 512, (h + 1) * 512)
                        ti_ = c2 * 2 + h
                        psg, bpg = rr.get()
                        for k in range(8):
                            MM(psg[:, :], wg[:, k, c2 * 128:(c2 + 1) * 128], hT[:, k, hs], k == 0, k == 7, [b_hT, bwg], [bpg])
                        z = kq[0] % 4; kq[0] += 1
                        ACT(gs[z][:], psg[:, :], AF.Sigmoid, [bpg, b_bgt], [b_gs[z]], bias=bgt[:, br * 8 + dc:br * 8 + dc + 1])
                        psb_, bpb_ = rr.get()
                        for k in range(4):
                            yt_, by_ = yT(br, k)
                            MM(psb_[:, :], wb_[:, k, c2 * 128:(c2 + 1) * 128], yt_[:, hs], k == 0, k == 3, [by_, bwb_], [bpb_])
                        if br == 0:
                            TT("dve", tmp[ti_][:], psb_[:, :], gs[z][:], ALU.mult, [bpb_, b_gs[z]], [b_tmp[ti_]])
                        else:
                            TT("dve", gs[z][:], psb_[:, :], gs[z][:], ALU.mult, [bpb_, b_gs[z]], [b_gs[z]])
                            if br < 3:
                                TT("pool", tmp[ti_][:], tmp[ti_][:], gs[z][:], ALU.add, [b_tmp[ti_], b_gs[z]], [b_tmp[ti_]])
                            else:
                                TT("pool", mT[:, dc, hs], tmp[ti_][:], gs[z][:], ALU.add, [b_tmp[ti_], b_gs[z]], [b_mT[dc]])
        if sub == 54:
            raise _Stop()
        for cb4 in range(4):
            w, bw = wload(w_out[l][:, cb4 * 256:(cb4 + 1) * 256], 8, 256)
            for t in range(8):
                ps, bp = rr.get()
                for k in range(8):
                    MM(ps[:, 0:256], mT[:, k, t * 128:(t + 1) * 128], w[:, k, :], k == 0, k == 7, [b_mT[k], bw], [bp])
                cs = slice(cb4 * 256, (cb4 + 1) * 256)
                zz = kq[0] % 4; kq[0] += 1
                TT("dve", gs[zz][:, 0:256], ps[:, 0:256], gbc[:, 0, cs], ALU.mult, [bp, b_gbc[0]], [b_gs[zz]])
                TT("pool", xres[:, t, cs], xres[:, t, cs], gs[zz][:, 0:256], ALU.add, [b_xres[t], b_gs[zz]], [b_xres[t]])
        S.op("pool", lambda e: e.memset(bgt[:, 0:1], 0.0), [], (b_mT + b_gs + b_tmp + [b_bgt]) + [b_scr_all])
        rr.set(range(4))

    def mlp(l):
        barrier_begin()
        cvm = Carve()
        rl = [cvm.take([512]) for _ in range(4)]; b_rl = [NB(f"rl{i}") for i in range(4)]
        rs_ = [cvm.take([256]) for _ in range(4)]; b_rs_ = [NB(f"rsd{i}") for i in range(4)]
        mk = [0, 0]
        for h in range(2):
            hs = slice(h * 512, (h + 1) * 512)

            def aTv(kc):
                return big[:, kc // 2, (kc % 2) * 512:(kc % 2 + 1) * 512], b_big[kc // 2]
            for blk in range(16):
                w, bw = wload(w_fc1[l][:, blk * 256:(blk + 1) * 256], 8, 256)
                for c2 in range(2):
                    kc = blk * 2 + c2
                    ps, bp = rr.get()
                    for k in range(8):
                        MM(ps[:, :], w[:, k, c2 * 128:(c2 + 1) * 128], hT[:, k, hs], k == 0, k == 7, [b_hT, bw], [bp])
                    a_, ba_ = aTv(kc)
                    zr = mk[0] % 4; mk[0] += 1
                    ACT(rl[zr][:], ps[:, :], AF.Relu, [bp], [b_rl[zr]])
                    TT("dve", a_, rl[zr][:], rl[zr][:], ALU.mult, [b_rl[zr]], [ba_])
            for cb4 in range(4):
                cs = slice(cb4 * 256, (cb4 + 1) * 256)
                accb = [(psum[4 + i], psb[4 + i]) for i in range(4)]
                for kg in range(4):
                    w, bw = wload(w_fc2[l][kg * 1024:(kg + 1) * 1024, cs], 8, 256)
                    for tt in range(4):
                        for k in range(8):
                            kc = kg * 8 + k
                            a_, ba_ = aTv(kc)
                            MM(accb[tt][0][:, 0:256], a_[:, tt * 128:(tt + 1) * 128], w[:, k, :], kc == 0, kc == 31, [ba_, bw], [accb[tt][1]])
                for tt in range(4):
                    t = h * 4 + tt
                    zq = mk[1] % 4; mk[1] += 1
                    TT("dve", rs_[zq][:], accb[tt][0][:, 0:256], gbc[:, 1, cs], ALU.mult, [accb[tt][1], b_gbc[1]], [b_rs_[zq]])
                    TT("pool", xres[:, t, cs], xres[:, t, cs], rs_[zq][:], ALU.add, [b_xres[t], b_rs_[zq]], [b_xres[t]])

        S.op("pool", lambda e: e.memset(small[:, 62:63], 0.0), [], b_rl + b_rs_ + [b_scr_all])

    try:
        Sched.PHASE = "prologue"
        adaln_weights(0)
        adaln_weights(1)
        run_pass(0)
        run_pass(1)
    except _Stop:
        pass
    if stop is not None:
        d_hT = nc.dram_tensor("dbg_hT", [128, 8, T], BF16, kind="ExternalOutput").ap()
        d_big = nc.dram_tensor("dbg_big", [128, 16, 1024], BF16, kind="ExternalOutput").ap()
        d_x = nc.dram_tensor("dbg_x", [128, 8, D], F32, kind="ExternalOutput").ap()
        d_modc = nc.dram_tensor("dbg_modc", [128, 48], F32, kind="ExternalOutput").ap()
        S.dma("sp", d_hT[:, :, :], hT[:], reads=[b_hT])
        S.dma("sp", d_big[:, :, :], big[:], reads=b_big, sbuf=b_big[0])
        S.dma("sp", d_x[:, :, :], xres[:], reads=b_xres, sbuf=b_xres[0])
        S.dma("sp", d_modc[:, :], modc[:], reads=[b_modc])
    with nc.Block() as block:
        S.emit(block)
    es.close()
    nc._phases = {e: [o.phase for o in S.ops[e]] for e in ENGS}
    return nc


_NC_CACHE = {}


def _consts():
    bf = ml_dtypes.bfloat16
    c = {}
    c["c_identb"] = np.eye(128, dtype=np.float32).astype(bf)
    c["c_identf"] = np.eye(128, dtype=np.float32)
    c["c_ones"] = np.ones((128, 128), np.float32)
    k = np.arange(128)[:, None]; t = np.arange(128)[None, :]
    c["c_triu"] = (k <= t).astype(np.float32)
    c["c_tril"] = (k >= t).astype(np.float32)
    c["c_mnegF"] = np.where(k <= t, 0.0, -1e30).astype(np.float32)
    c["c_mnegB"] = np.where(k >= t, 0.0, -1e30).astype(np.float32)
    c["c_bprev"] = (t <= k).astype(np.float32)
    c["c_bnext"] = (k <= t).astype(np.float32)
    mB = np.zeros((128, 4, 128), np.float32)
    mC = np.zeros((128, 4, 128), np.float32)
    for q in range(4):
        for gl in range(2):
            g8 = 2 * q + gl
            mB[gl * 64:(gl + 1) * 64, q, g8 * 16:(g8 + 1) * 16] = 1.0
            mC[g8 * 16:(g8 + 1) * 16, q, gl * 64:(gl + 1) * 64] = 1.0
    c["c_maskB"] = mB; c["c_maskC"] = mC
    c["c_iota"] = np.broadcast_to(np.arange(1024, dtype=np.float32)[None, :], (128, 1024)).copy()
    Ls = 1024
    row = np.repeat(np.arange(Ls // 64), 64).astype(np.float32); col = np.tile(np.arange(64), Ls // 64).astype(np.float32)
    nf = 16
    inv = (10000.0 ** (-np.arange(nf, dtype=np.float32) / nf)).astype(np.float32)
    ang = np.concatenate([row[:, None] * inv, col[:, None] * inv], axis=-1).astype(np.float32)
    cs, sn = np.cos(ang).astype(np.float32), np.sin(ang).astype(np.float32)
    C64 = np.zeros((Ls, 2, 2, 16), np.float32); S64 = np.zeros((Ls, 2, 2, 16), np.float32)
    for a in range(2):
        for p in range(2):
            C64[:, a, p, :] = cs[:, a * 16:(a + 1) * 16]
            S64[:, a, p, :] = (-1.0 if p == 0 else 1.0) * sn[:, a * 16:(a + 1) * 16]
    c["c_ropeC"] = C64.reshape(8, 128, 64).transpose(1, 0, 2).copy()
    c["c_ropeS"] = S64.reshape(8, 128, 64).transpose(1, 0, 2).copy()
    lidx = np.arange(1024)
    c["c_rmF"] = np.broadcast_to((lidx % 256 != 0).astype(np.float32)[None, :], (128, 1024)).astype(bf)
    c["c_rmB"] = np.broadcast_to((lidx % 256 != 255).astype(np.float32)[None, :], (128, 1024)).astype(bf)
    return c


def _colmajor(v, nchunk):
    return np.ascontiguousarray(np.swapaxes(v.reshape(v.shape[:-1] + (nchunk, 128)), -1, -2))


def make_in_maps(inp):
    f = lambda a: np.ascontiguousarray(np.asarray(a, dtype=np.float32))
    I = {k: f(v) for k, v in inp.items()}
    shared = dict(_consts())
    shared.update({
        "w_mod": I["w_mod"], "w_in": I["w_in"], "w_gate": I["w_gate"], "w_out": I["w_out"], "w_fc1": I["w_fc1"], "w_fc2": I["w_fc2"],
        "w_glu": I["s5_w_glu"], "w_br": I["w_branch"].reshape(2, 2048, 1024),
        "bmodc": _colmajor(I["b_mod"], 48), "g1c": _colmajor(I["g_norm1"], 8), "g2c": _colmajor(I["g_norm2"], 8),
        "convw": np.ascontiguousarray(I["ssd_conv_w"].transpose(0, 2, 1).reshape(2, 6, 128, 7).transpose(0, 2, 1, 3)),
        "convb": _colmajor(I["ssd_conv_b"], 6),
        "dtb": I["ssd_dt_bias"].reshape(2, 16), "alog": I["ssd_a_log"].reshape(2, 16), "ssdd": I["ssd_d"], "normgc": _colmajor(I["ssd_norm_g"], 4),
        "lamre": I["s5_lam_re"].reshape(2, 32, 128), "lamim": I["s5_lam_im"].reshape(2, 32, 128),
        "lsx": np.ascontiguousarray(np.repeat(I["s5_log_step"].reshape(2, 2, 32, 1), 64, axis=-1).reshape(2, 32, 128)),
        "s5bre": I["s5_b_re"].reshape(2, 2, 2048, 16), "s5bim": I["s5_b_im"].reshape(2, 2, 2048, 16),
        "s5cre": I["s5_c_re"].reshape(2, 2, 512, 64), "s5cim": I["s5_c_im"].reshape(2, 2, 512, 64),
        "s5dc": _colmajor(I["s5_d"], 4), "bgluc": _colmajor(I["s5_b_glu"], 8),
        "dqg": I["diff_qn_g"], "dkg": I["diff_kn_g"], "dlam": I["diff_lambda"].reshape(2, 256), "dsubc": I["diff_subln_g"].reshape(2, 128, 1),
        "wqg": I["win_qn_g"], "wkg": I["win_kn_g"], "wsink": I["win_sink"], "bgatec": _colmajor(I["b_gate"], 32),
    })
    in_maps = []
    for i in range(8):
        b = i // 2
        cv = np.stack([I["c_ctx"], I["c"][b]], axis=0)
        m = dict(shared)
        m.update({
            "xp": I["x_prompt"][4 * i:4 * i + 4].reshape(1024, 1024), "xs": I["x_sample"][b],
            "cvT": np.ascontiguousarray(cv.reshape(2, 8, 128).transpose(2, 1, 0)),
            "st_ssd": I["state_ssd"][b], "st_s5": I["state_s5"][b].reshape(2, 64, 128),
            "cdk": I["cache_diff_k"][b].reshape(2, 256, 512), "cdv": I["cache_diff_v"][b].reshape(2, 256, 512),
            "cwk": I["cache_win_k"][b].reshape(2, 256, 128), "cwv": I["cache_win_v"][b].reshape(2, 256, 128),
        })
        in_maps.append({k: np.ascontiguousarray(v) for k, v in m.items()})
    return in_maps


def kernel(**inp):
    if "nc" not in _NC_CACHE:
        _NC_CACHE["nc"] = build_program()
    nc = _NC_CACHE["nc"]
    in_maps = make_in_maps(inp)
    res = run_bass_kernel_spmd(nc, in_maps, core_ids=list(range(8)))
    R = res.results
    yp = np.concatenate([R[i]["yp"].reshape(4, 256, 1024) for i in range(8)], axis=0)
    ys = np.stack([R[2 * b]["ys"] for b in range(4)], axis=0)
    ssd = np.concatenate([R[i]["o_ssd"] for i in range(8)], axis=0)
    s5 = np.concatenate([R[i]["o_s5"].reshape(4, 2, 2, 2, 32, 64) for i in range(8)], axis=0)
    dk = np.concatenate([R[i]["o_dk"].reshape(4, 2, 256, 4, 2, 64) for i in range(8)], axis=0)
    dv = np.concatenate([R[i]["o_dv"].reshape(4, 2, 256, 4, 128) for i in range(8)], axis=0)
    wk = np.concatenate([R[i]["o_wk"].reshape(4, 2, 256, 2, 64) for i in range(8)], axis=0)
    wv = np.concatenate([R[i]["o_wv"].reshape(4, 2, 256, 2, 64) for i in range(8)], axis=0)
    return tuple(np.ascontiguousarray(a.astype(np.float32)) for a in (yp, ys, ssd, s5, dk, dv, wk, wv))
```
